# Optimizing a Trainium2 kernel written in Bass

```python
import math
import jax, jax.numpy as jnp
from jax import lax
import numpy as np

D_MODEL = 2048
BATCH = 2
SEQ = 4096
DEPTH = 2
DEC_BATCH = 8
DEC_SEQ = 1
PAST_LEN = 16384
PAGE_SIZE = 128

D_MIX = D_MODEL
N_MIXERS = 4
W_BR = D_MIX // N_MIXERS
HEAD_DIM = 64
H_A = W_BR // HEAD_DIM
H_B = W_BR // HEAD_DIM
H_C = W_BR // HEAD_DIM
CONV_W = 4
GDN_CHUNK = 64
LORA_B = 32
D_B_IN = 3 * W_BR + 2 * LORA_B
POOL_WINDOWS = (2, 4, 8, 16)
POOL_GROUPS = len(POOL_WINDOWS)
POOL_GW = W_BR // POOL_GROUPS
POOL_BUF = max(POOL_WINDOWS) - 1
Q_BLOCK = 128
D_IN = (3 * W_BR + 2 * H_A + W_BR) + (D_B_IN + W_BR) + (3 * W_BR + H_C + W_BR) + 2 * W_BR
ALPHA_DN = (2.0 * DEPTH) ** 0.25
BETA_DN = (8.0 * DEPTH) ** -0.25
LN_EPS = 1e-5
GN_EPS = 64e-5
RMS_EPS = 1e-6
L2_EPS = 1e-6
RWKV_DECAY_SCALE = math.exp(-0.5)
NEG_INF = -1e30
F32 = jnp.float32

kernel_name = 'hymba_gdn_rwkv7_fox_pool_step'


def _silu(x):
    return x * jax.nn.sigmoid(x)


def _l2norm(x):
    return x * lax.rsqrt(jnp.sum(x * x, axis=-1, keepdims=True) + L2_EPS)


def _layernorm(x, g, b):
    xf = x.astype(F32)
    mu = jnp.mean(xf, axis=-1, keepdims=True)
    var = jnp.mean(jnp.square(xf - mu), axis=-1, keepdims=True)
    return ((xf - mu) * lax.rsqrt(var + LN_EPS) * g.astype(F32) + b.astype(F32)).astype(x.dtype)


def _split_proj(proj):
    sizes = (3 * W_BR, H_A, H_A, W_BR, D_B_IN, W_BR, 3 * W_BR, H_C, W_BR, W_BR, W_BR)
    idx, acc = [], 0
    for s in sizes[:-1]:
        acc += s
        idx.append(acc)
    return jnp.split(proj, idx, axis=-1)


def _causal_conv(u, buf, w):
    T = u.shape[1]
    ext = jnp.concatenate([buf.astype(F32), u.astype(F32)], axis=1)
    w = w.astype(F32)
    out = ext[:, 0:T] * w[0]
    for i in range(1, CONV_W):
        out = out + ext[:, i:i + T] * w[i]
    return out, ext[:, T:]


def _to_chunks4(a, nc):
    N, T, H, D = a.shape
    a = jnp.pad(a.astype(F32), ((0, 0), (0, nc * GDN_CHUNK - T), (0, 0), (0, 0)))
    return a.reshape(N, nc, GDN_CHUNK, H, D).transpose(1, 0, 3, 2, 4)


def _to_chunks3(a, nc):
    N, T, H = a.shape
    a = jnp.pad(a.astype(F32), ((0, 0), (0, nc * GDN_CHUNK - T), (0, 0)))
    return a.reshape(N, nc, GDN_CHUNK, H).transpose(1, 0, 3, 2)


def _gated_delta_chunked(q, k, v, g, beta, S0):
    N, T, H, Dk = q.shape
    Dv = v.shape[-1]
    C = GDN_CHUNK
    nc = -(-T // C)
    qc, kc, vc = _to_chunks4(q, nc), _to_chunks4(k, nc), _to_chunks4(v, nc)
    gc, bc = _to_chunks3(g, nc), _to_chunks3(beta, nc)
    gcum = jnp.cumsum(gc, axis=-1)
    tril = jnp.tril(jnp.ones((C, C), dtype=bool))
    eye = jnp.eye(C, dtype=F32)
    decay = jnp.exp(jnp.where(tril, gcum[..., :, None] - gcum[..., None, :], -jnp.inf))
    kb = kc * bc[..., None]
    vb = vc * bc[..., None]
    lower = jnp.einsum('cnhik,cnhjk->cnhij', kb, kc) * decay * (1.0 - eye)
    tri = lower + eye
    u = lax.linalg.triangular_solve(tri, vb, left_side=True, lower=True)
    w = lax.linalg.triangular_solve(tri, kb * jnp.exp(gcum)[..., None], left_side=True, lower=True)

    def step(S, inp):
        q_i, k_i, u_i, w_i, g_i, dec_i = inp
        v_new = u_i - jnp.einsum('nhck,nhkv->nhcv', w_i, S)
        attn = jnp.einsum('nhik,nhjk->nhij', q_i, k_i) * dec_i
        o_i = (jnp.einsum('nhck,nhkv->nhcv', q_i * jnp.exp(g_i)[..., None], S)
               + jnp.einsum('nhij,nhjv->nhiv', attn, v_new))
        g_last = g_i[..., -1]
        k_dec = k_i * jnp.exp(g_last[..., None] - g_i)[..., None]
        S = S * jnp.exp(g_last)[..., None, None] + jnp.einsum('nhck,nhcv->nhkv', k_dec, v_new)
        return S, o_i

    S, o = lax.scan(step, S0.astype(F32), (qc, kc, u, w, gcum, decay))
    o = o.transpose(1, 0, 3, 2, 4).reshape(N, nc * C, H, Dv)[:, :T]
    return o, S


def _gdn_branch(qkv, a_col, b_col, conv_buf, S0, conv_w, A_log, dt_bias, norm_g):
    N, T, _ = qkv.shape
    c, new_buf = _causal_conv(qkv, conv_buf, conv_w)
    c = _silu(c)
    q, k, v = jnp.split(c, 3, axis=-1)
    q = _l2norm(q.reshape(N, T, H_A, HEAD_DIM)) * (HEAD_DIM ** -0.5)
    k = _l2norm(k.reshape(N, T, H_A, HEAD_DIM))
    v = v.reshape(N, T, H_A, HEAD_DIM)
    g = -jnp.exp(A_log.astype(F32)) * jax.nn.softplus(a_col.astype(F32) + dt_bias.astype(F32))
    beta = jax.nn.sigmoid(b_col.astype(F32))
    o, S = _gated_delta_chunked(q, k, v, g, beta, S0)
    o = o * lax.rsqrt(jnp.mean(o * o, axis=-1, keepdims=True) + RMS_EPS) * norm_g.astype(F32)
    return o.reshape(N, T, W_BR), new_buf.astype(conv_buf.dtype), S.astype(S0.dtype)


def _rwkv7_scan(r, w, k, v, kk, a, S0):
    def step(S, inp):
        r_t, w_t, k_t, v_t, kk_t, a_t = inp
        s_kk = jnp.einsum('nhvk,nhk->nhv', S, -kk_t)
        S = (S * w_t[:, :, None, :] + s_kk[..., :, None] * (kk_t * a_t)[:, :, None, :]
             + v_t[..., :, None] * k_t[:, :, None, :])
        return S, jnp.einsum('nhvk,nhk->nhv', S, r_t)

    xs = tuple(t.transpose(1, 0, 2, 3) for t in (r, w, k, v, kk, a))
    S, y = lax.scan(step, S0.astype(F32), xs)
    return y.transpose(1, 0, 2, 3), S


def _rwkv7_branch(p, shift_buf, S0, mu, w0, w_up, a0, a_up, xi, alpha, rho, gn_g, gn_b):
    N, T, _ = p.shape
    pf = p.astype(F32)
    prev = jnp.concatenate([shift_buf.astype(F32)[:, None, :], pf[:, :-1]], axis=1)
    ps = pf + (prev - pf) * mu.astype(F32)
    r, k, v, wl, al = jnp.split(ps, [W_BR, 2 * W_BR, 3 * W_BR, 3 * W_BR + LORA_B], axis=-1)
    d = w0.astype(F32) + jnp.tanh(wl) @ w_up.astype(F32)
    decay = jnp.exp(-RWKV_DECAY_SCALE * jax.nn.sigmoid(d))
    a = jax.nn.sigmoid(a0.astype(F32) + al @ a_up.astype(F32))

    def heads(t):
        return t.reshape(N, T, H_B, HEAD_DIM)

    kk = _l2norm(heads(k * xi.astype(F32)))
    k = k * (1.0 + (a - 1.0) * alpha.astype(F32))
    y, S = _rwkv7_scan(heads(r), heads(decay), heads(k), heads(v), kk, heads(a), S0)
    mu_y = jnp.mean(y, axis=-1, keepdims=True)
    var = jnp.mean(jnp.square(y - mu_y), axis=-1, keepdims=True)
    yn = ((y - mu_y) * lax.rsqrt(var + GN_EPS)).reshape(N, T, W_BR) * gn_g.astype(F32) + gn_b.astype(F32)
    bonus = jnp.sum(heads(r * k * rho.astype(F32)), axis=-1, keepdims=True) * heads(v)
    return yn + bonus.reshape(N, T, W_BR), p[:, -1].astype(shift_buf.dtype), S.astype(S0.dtype)


def _fox_block(qb, cqb, qpos, k, v, ck, kpos):
    s = jnp.einsum('nqhd,nshd->nhqs', qb, k) * (HEAD_DIM ** -0.5)
    s = s + jnp.swapaxes(cqb, 1, 2)[..., :, None] - jnp.swapaxes(ck, 1, 2)[..., None, :]
    s = jnp.where((kpos[None, :] <= qpos[:, None])[None, None], s, NEG_INF)
    return jnp.einsum('nhqs,nshd->nqhd', jax.nn.softmax(s, axis=-1), v)


def _fox_prompt(q, k, v, c):
    N, T, H, D = q.shape
    nb = T // Q_BLOCK
    pos = jnp.arange(T)
    qb = q.reshape(N, nb, Q_BLOCK, H, D).transpose(1, 0, 2, 3, 4)
    cb = c.reshape(N, nb, Q_BLOCK, H).transpose(1, 0, 2, 3)
    pb = pos.reshape(nb, Q_BLOCK)
    out = lax.map(lambda t: _fox_block(t[0], t[1], t[2], k, v, c, pos), (qb, cb, pb))
    return out.transpose(1, 0, 2, 3, 4).reshape(N, T, H, D)


def _fox_branch(qkv, f_col, b_f, start_pos, past):
    N, T, _ = qkv.shape
    q, k, v = [t.reshape(N, T, H_C, HEAD_DIM).astype(F32) for t in jnp.split(qkv, 3, axis=-1)]
    logf = jax.nn.log_sigmoid(f_col.astype(F32) + b_f.astype(F32))
    if past is None:
        c = jnp.cumsum(logf, axis=1)
        o = _fox_prompt(q, k, v, c)
    else:
        k_past, v_past, lf_past = past
        k_all = jnp.concatenate([k_past.astype(F32), k], axis=1)
        v_all = jnp.concatenate([v_past.astype(F32), v], axis=1)
        c_all = jnp.cumsum(jnp.concatenate([lf_past.astype(F32), logf], axis=1), axis=1)
        kpos = jnp.arange(k_all.shape[1])
        qpos = start_pos + jnp.arange(T)
        o = _fox_block(q, c_all[:, -T:], qpos, k_all, v_all, c_all, kpos)
    return o.reshape(N, T, W_BR), k, v, logf


def _pool_branch(u, buf, start_pos, pool_w, pool_scale):
    N, T, _ = u.shape
    ext = jnp.concatenate([buf.astype(F32), u.astype(F32)], axis=1)
    cs = jnp.concatenate([jnp.zeros_like(ext[:, :1]), jnp.cumsum(ext, axis=1)], axis=1)
    end = cs[:, POOL_BUF + 1:]
    pos = start_pos + jnp.arange(T)
    groups = []
    for gi, wdw in enumerate(POOL_WINDOWS):
        lo, hi = gi * POOL_GW, (gi + 1) * POOL_GW
        s = end[..., lo:hi] - cs[:, POOL_BUF + 1 - wdw:POOL_BUF + 1 - wdw + T, lo:hi]
        cnt = jnp.minimum(wdw, pos + 1).astype(F32)[None, :, None]
        groups.append(s / cnt)
    pooled = jnp.concatenate(groups, axis=-1) - u.astype(F32)
    mixed = jnp.einsum('ntgc,gcd->ntgd', pooled.reshape(N, T, POOL_GROUPS, POOL_GW), pool_w.astype(F32))
    return mixed.reshape(N, T, W_BR) * pool_scale.astype(F32), ext[:, -POOL_BUF:].astype(buf.dtype)


def _layer(x, start_pos, st, past, P):
    S_A, conv_buf, S_B, shift_buf, pool_buf = st
    proj = jnp.einsum('ntd,de->nte', x, P['w_in'])
    qkv_A, a_A, b_A, z_A, p_B, z_B, qkv_C, f_C, z_C, u_D, z_D = _split_proj(proj)
    o_A, conv_n, S_A_n = _gdn_branch(qkv_A, a_A, b_A, conv_buf, S_A, P['conv_A'], P['A_log'],
                                     P['dt_bias'], P['norm_A'])
    o_B, shift_n, S_B_n = _rwkv7_branch(p_B, shift_buf, S_B, P['mu_B'], P['w0_B'], P['w_up_B'], P['a0_B'],
                                        P['a_up_B'], P['xi_B'], P['alpha_B'], P['rho_B'], P['gn_g_B'], P['gn_b_B'])
    o_C, k_n, v_n, lf_n = _fox_branch(qkv_C, f_C, P['b_f_C'], start_pos, past)
    o_D, pool_n = _pool_branch(u_D, pool_buf, start_pos, P['pool_w_D'], P['pool_scale_D'])
    gated = jnp.concatenate([o_A * _silu(z_A.astype(F32)), o_B * _silu(z_B.astype(F32)),
                             o_C * _silu(z_C.astype(F32)), o_D * _silu(z_D.astype(F32))], axis=-1).astype(x.dtype)
    out = jnp.einsum('nte,ed->ntd', gated, P['w_out'])
    y = _layernorm(ALPHA_DN * x + out, P['ln_g'], P['ln_b'])
    dt = x.dtype
    new = (S_A_n, conv_n, S_B_n, shift_n, k_n.astype(dt), v_n.astype(dt), lf_n.astype(dt), pool_n)
    return y, new


def setup_inputs(seed: int = 0) -> dict:
    key = jax.random.key(seed)
    ks = jax.random.split(key, 40)
    n_pages = PAST_LEN // PAGE_SIZE
    n_phys = (DEC_BATCH * n_pages * 5) // 4

    def nrm(k, shape, s):
        return jax.random.normal(k, shape, F32) * s

    x_prompt = nrm(ks[0], (BATCH, SEQ, D_MODEL), 1.0)
    x_sample = nrm(ks[1], (DEC_BATCH, DEC_SEQ, D_MODEL), 1.0)
    state_A_S = nrm(ks[2], (DEPTH, DEC_BATCH, H_A, HEAD_DIM, HEAD_DIM), 0.5)
    state_A_conv = nrm(ks[3], (DEPTH, DEC_BATCH, CONV_W - 1, 3 * W_BR), 1.0)
    state_B_S = nrm(ks[4], (DEPTH, DEC_BATCH, H_B, HEAD_DIM, HEAD_DIM), 0.5)
    state_B_shift = nrm(ks[5], (DEPTH, DEC_BATCH, D_B_IN), 1.0)
    cache_C_k = nrm(ks[6], (DEPTH, n_phys, PAGE_SIZE, H_C, HEAD_DIM), 1.0)
    cache_C_v = nrm(ks[7], (DEPTH, n_phys, PAGE_SIZE, H_C, HEAD_DIM), 1.0)
    cache_C_logf = jax.nn.log_sigmoid(nrm(ks[8], (DEPTH, n_phys, PAGE_SIZE, H_C), 1.0) + 2.0)
    state_D_buf = nrm(ks[9], (DEPTH, DEC_BATCH, POOL_BUF, W_BR), 1.0)
    page_table = jax.random.permutation(ks[10], n_phys)[:DEC_BATCH * n_pages].reshape(
        DEC_BATCH, n_pages).astype(jnp.int32)

    w_in = nrm(ks[11], (DEPTH, D_MODEL, D_IN), D_MODEL ** -0.5)
    conv_A = nrm(ks[12], (DEPTH, CONV_W, 3 * W_BR), CONV_W ** -0.5)
    A_log = jnp.log(jax.random.uniform(ks[13], (DEPTH, H_A), F32, 1.0, 16.0))
    dt0 = jnp.exp(jax.random.uniform(ks[14], (DEPTH, H_A), F32, math.log(1e-3), math.log(1e-1)))
    dt_bias = dt0 + jnp.log(-jnp.expm1(-dt0))
    norm_A = 1.0 + nrm(ks[15], (DEPTH, HEAD_DIM), 0.05)
    mu_B = jax.random.uniform(ks[16], (DEPTH, D_B_IN), F32)
    w0_B = -0.5 + nrm(ks[17], (DEPTH, W_BR), 0.5)
    w_up_B = nrm(ks[18], (DEPTH, LORA_B, W_BR), LORA_B ** -0.5)
    a0_B = nrm(ks[19], (DEPTH, W_BR), 0.1)
    a_up_B = nrm(ks[20], (DEPTH, LORA_B, W_BR), LORA_B ** -0.5)
    xi_B = 0.85 + nrm(ks[21], (DEPTH, W_BR), 0.05)
    alpha_B = 1.0 + nrm(ks[22], (DEPTH, W_BR), 0.05)
    rho_B = nrm(ks[23], (DEPTH, W_BR), 0.1)
    gn_g_B = 1.0 + nrm(ks[24], (DEPTH, W_BR), 0.05)
    gn_b_B = nrm(ks[25], (DEPTH, W_BR), 0.02)
    b_f_C = 2.0 + nrm(ks[26], (DEPTH, H_C), 0.1)
    pool_w_D = nrm(ks[27], (DEPTH, POOL_GROUPS, POOL_GW, POOL_GW), POOL_GW ** -0.5)
    pool_scale_D = 0.5 + nrm(ks[28], (DEPTH, W_BR), 0.1)
    w_out = nrm(ks[29], (DEPTH, D_MIX, D_MODEL), D_MIX ** -0.5 * BETA_DN)
    ln_g = 1.0 + nrm(ks[30], (DEPTH, D_MODEL), 0.05)
    ln_b = nrm(ks[31], (DEPTH, D_MODEL), 0.02)
    return {'x_prompt': x_prompt, 'x_sample': x_sample, 'state_A_S': state_A_S, 'state_A_conv': state_A_conv,
            'state_B_S': state_B_S, 'state_B_shift': state_B_shift, 'cache_C_k': cache_C_k,
            'cache_C_v': cache_C_v, 'cache_C_logf': cache_C_logf, 'state_D_buf': state_D_buf,
            'page_table': page_table, 'w_in': w_in, 'conv_A': conv_A, 'A_log': A_log, 'dt_bias': dt_bias,
            'norm_A': norm_A, 'mu_B': mu_B, 'w0_B': w0_B, 'w_up_B': w_up_B, 'a0_B': a0_B, 'a_up_B': a_up_B,
            'xi_B': xi_B, 'alpha_B': alpha_B, 'rho_B': rho_B, 'gn_g_B': gn_g_B, 'gn_b_B': gn_b_B,
            'b_f_C': b_f_C, 'pool_w_D': pool_w_D, 'pool_scale_D': pool_scale_D, 'w_out': w_out,
            'ln_g': ln_g, 'ln_b': ln_b}


def reference(x_prompt, x_sample, state_A_S, state_A_conv, state_B_S, state_B_shift, cache_C_k, cache_C_v,
              cache_C_logf, state_D_buf, page_table, w_in, conv_A, A_log, dt_bias, norm_A, mu_B, w0_B, w_up_B,
              a0_B, a_up_B, xi_B, alpha_B, rho_B, gn_g_B, gn_b_B, b_f_C, pool_w_D, pool_scale_D, w_out,
              ln_g, ln_b):
    dt = x_prompt.dtype
    nb = x_prompt.shape[0]
    nd = page_table.shape[0]
    past_len = page_table.shape[1] * PAGE_SIZE
    y_prompt, y_sample = x_prompt, x_sample
    prompt_new, sample_new = [], []
    for l in range(DEPTH):
        P = {'w_in': w_in[l], 'conv_A': conv_A[l], 'A_log': A_log[l], 'dt_bias': dt_bias[l], 'norm_A': norm_A[l],
             'mu_B': mu_B[l], 'w0_B': w0_B[l], 'w_up_B': w_up_B[l], 'a0_B': a0_B[l], 'a_up_B': a_up_B[l],
             'xi_B': xi_B[l], 'alpha_B': alpha_B[l], 'rho_B': rho_B[l], 'gn_g_B': gn_g_B[l], 'gn_b_B': gn_b_B[l],
             'b_f_C': b_f_C[l], 'pool_w_D': pool_w_D[l], 'pool_scale_D': pool_scale_D[l], 'w_out': w_out[l],
             'ln_g': ln_g[l], 'ln_b': ln_b[l]}
        st_p = (jnp.zeros((nb, H_A, HEAD_DIM, HEAD_DIM), dt), jnp.zeros((nb, CONV_W - 1, 3 * W_BR), dt),
                jnp.zeros((nb, H_B, HEAD_DIM, HEAD_DIM), dt), jnp.zeros((nb, D_B_IN), dt),
                jnp.zeros((nb, POOL_BUF, W_BR), dt))
        y_prompt, new_p = _layer(y_prompt, 0, st_p, None, P)
        past = (cache_C_k[l][page_table].reshape(nd, past_len, H_C, HEAD_DIM),
                cache_C_v[l][page_table].reshape(nd, past_len, H_C, HEAD_DIM),
                cache_C_logf[l][page_table].reshape(nd, past_len, H_C))
        st_s = (state_A_S[l], state_A_conv[l], state_B_S[l], state_B_shift[l], state_D_buf[l])
        y_sample, new_s = _layer(y_sample, past_len, st_s, past, P)
        prompt_new.append(new_p)
        sample_new.append(new_s)
    p_A_S, p_A_conv, p_B_S, p_B_shift, p_C_k, p_C_v, p_C_logf, p_D_buf = [
        jnp.stack([n[i] for n in prompt_new]) for i in range(8)]
    s_A_S, s_A_conv, s_B_S, s_B_shift, s_C_k, s_C_v, s_C_logf, s_D_buf = [
        jnp.stack([n[i] for n in sample_new]) for i in range(8)]
    return (y_prompt, y_sample, p_A_S, p_A_conv, p_B_S, p_B_shift, p_C_k, p_C_v, p_C_logf, p_D_buf,
            s_A_S, s_A_conv, s_B_S, s_B_shift, s_C_k, s_C_v, s_C_logf, s_D_buf)
```

```python
import contextlib
import numpy as np
import concourse.bass as bass
import concourse.mybir as mybir
from concourse.bass_utils import run_bass_kernel_spmd

F32, BF16, I32 = mybir.dt.float32, mybir.dt.bfloat16, mybir.dt.int32
AF, ALU, AX = mybir.ActivationFunctionType, mybir.AluOpType, mybir.AxisListType

D, T, TS = 2048, 4096, 4097
DEPTH, NH, HD = 2, 8, 64
NBLK, NROW = 57, 57 * 128
KC = 16
R_AQ, R_AK, R_AV, R_AZ = 0, 512, 1024, 1536
R_BR, R_BK, R_BV, R_BZ = 2048, 2560, 3072, 3584
R_CQ, R_CK, R_CV, R_CZ = 4096, 4608, 5120, 5632
R_DU, R_DZ, R_SM = 6144, 6656, 7168
SM_A, SM_B, SM_F, SM_WL, SM_AL = 0, 8, 16, 32, 64


def _col_perm():
    ref = dict(qkvA=0, aA=1536, bA=1544, zA=1552, rB=2064, kB=2576, vB=3088, wl=3600, al=3632, zB=3664,
               qkvC=4176, fC=5712, zC=5720, uD=6232, zD=6744)
    p = np.full(NROW, -1, np.int64)
    p[0:1536] = ref['qkvA'] + np.arange(1536)
    p[1536:2048] = ref['zA'] + np.arange(512)
    p[2048:3584] = ref['rB'] + np.arange(1536)
    p[3584:4096] = ref['zB'] + np.arange(512)
    p[4096:5632] = ref['qkvC'] + np.arange(1536)
    p[5632:6144] = ref['zC'] + np.arange(512)
    p[6144:6656] = ref['uD'] + np.arange(512)
    p[6656:7168] = ref['zD'] + np.arange(512)
    p[R_SM + SM_A:R_SM + SM_A + 8] = ref['aA'] + np.arange(8)
    p[R_SM + SM_B:R_SM + SM_B + 8] = ref['bA'] + np.arange(8)
    p[R_SM + SM_F:R_SM + SM_F + 8] = ref['fC'] + np.arange(8)
    p[R_SM + SM_WL:R_SM + SM_WL + 32] = ref['wl'] + np.arange(32)
    p[R_SM + SM_AL:R_SM + SM_AL + 32] = ref['al'] + np.arange(32)
    return p


PE, ACT, DVE, POOL, SP = 'pe', 'act', 'dve', 'pool', 'sp'
ENGS = (PE, ACT, DVE, POOL, SP)


class _Op:
    __slots__ = ('eng', 'fn', 'deps', 'is_dma', 'tag', 'sig', 'semkey', 'semval')

    def __init__(self, eng, fn, is_dma, tag):
        self.eng, self.fn, self.is_dma, self.tag = eng, fn, is_dma, tag
        self.deps, self.sig, self.semkey, self.semval = set(), False, None, 0


class Prog:
    n_emitted = 0

    def __init__(self, nc, dma_rot=6):
        self.nc, self.dma_rot = nc, dma_rot
        self.ops = {e: [] for e in ENGS}
        self.lastw, self.readers, self.all_ops = {}, {}, []

    def op(self, eng, fn, reads=(), writes=(), dma=False, tag=None):
        o = _Op(eng, fn, dma, tag)
        wr = list(writes)
        for r in reads:
            if isinstance(r, tuple) and r[0] == 'ps':
                wr.append(r)
                continue
            w = self.lastw.get(r)
            if w is not None:
                o.deps.add(w)
        for r in wr:
            w = self.lastw.get(r)
            if w is not None:
                o.deps.add(w)
            o.deps.update(self.readers.get(r, ()))
        for r in reads:
            if not (isinstance(r, tuple) and r[0] == 'ps'):
                self.readers.setdefault(r, []).append(o)
        for r in wr:
            self.lastw[r] = o
            self.readers[r] = []
        o.deps.discard(o)
        self.ops[eng].append(o)
        self.all_ops.append(o)
        return o

    def dma(self, eng, fn, reads=(), writes=(), tag='d'):
        return self.op(eng, fn, reads, writes, dma=True, tag=tag)

    def mm(self, out, lhsT, rhs, start=True, stop=True, reads=(), writes=()):
        return self.op(PE, lambda e: e.matmul(out, lhsT=lhsT, rhs=rhs, start=start, stop=stop), reads, writes)

    def tr(self, out, in_, identity, reads=(), writes=()):
        return self.op(PE, lambda e: e.transpose(out=out, in_=in_, identity=identity), reads, writes)

    def act(self, out, in_, func, reads=(), writes=(), **kw):
        return self.op(ACT, lambda e: e.activation(out=out, in_=in_, func=func, **kw), reads, writes)

    def cp(self, eng, out, in_, reads=(), writes=()):
        if eng == ACT:
            return self.op(ACT, lambda e: e.activation(out=out, in_=in_, func=AF.Copy), reads, writes)
        return self.op(eng, lambda e: e.tensor_copy(out=out, in_=in_), reads, writes)

    def tt(self, eng, out, in0, in1, op, reads=(), writes=()):
        return self.op(eng, lambda e: e.tensor_tensor(out=out, in0=in0, in1=in1, op=op), reads, writes)

    def ts(self, eng, out, in0, s1, op0, s2=None, op1=None, reads=(), writes=()):
        if op1 is None:
            return self.op(eng, lambda e: e.tensor_scalar(out=out, in0=in0, scalar1=s1, scalar2=None, op0=op0), reads, writes)
        return self.op(eng, lambda e: e.tensor_scalar(out=out, in0=in0, scalar1=s1, scalar2=s2, op0=op0, op1=op1), reads, writes)

    def stt(self, out, in0, scalar, in1, op0, op1, reads=(), writes=()):
        return self.op(DVE, lambda e: e.scalar_tensor_tensor(out=out, in0=in0, scalar=scalar, in1=in1, op0=op0, op1=op1), reads, writes)

    def ms(self, eng, ap, val, reads=(), writes=()):
        return self.op(eng, lambda e: e.memset(ap, val), reads, writes)

    def ld(self, out, in_, reads=(), writes=(), tag='d', slow=False, eng=SP):
        return self.dma(eng, lambda e: e.dma_start(out=out, in_=in_, allow_slow_non_contiguous=slow), reads, writes, tag=tag)

    def emit(self):
        nc = self.nc
        for o in self.all_ops:
            o.deps = {d for d in o.deps if not (d.eng == PE and o.eng == PE and not d.is_dma and not o.is_dma)}
            for d in o.deps:
                d.sig = True
            if o.is_dma:
                o.sig = True
        lasts = [ops[-1] for ops in self.ops.values() if ops]
        for o in lasts:
            o.sig = True
        counters, dma_seq, order = {}, {}, []
        for o in self.all_ops:
            if not o.sig:
                continue
            if o.is_dma:
                k = dma_seq.get((o.eng, o.tag), 0)
                dma_seq[(o.eng, o.tag)] = k + 1
                key, inc = ('dma', o.eng, o.tag, k % self.dma_rot), 16
            else:
                key, inc = ('cmp', o.eng), 1
            if key not in counters:
                order.append(key)
            counters[key] = counters.get(key, 0) + inc
            o.semkey, o.semval = key, counters[key]
        Prog.n_emitted += 1
        sems = {key: nc.alloc_semaphore('p%ds%d' % (Prog.n_emitted, i)) for i, key in enumerate(order)}
        with contextlib.ExitStack() as st:
            block = st.enter_context(nc.Block())
            engmap = {PE: block.tensor, ACT: block.scalar, DVE: block.vector, POOL: block.gpsimd, SP: block.sync}

            def make(eng):
                ops = self.ops[eng]

                def body(e):
                    waited = {}

                    def wait(key, val):
                        if waited.get(key, 0) < val:
                            e.wait_ge(sems[key], val)
                            waited[key] = val
                    for o in ops:
                        need = {}
                        for d in o.deps:
                            if need.get(d.semkey, 0) < d.semval:
                                need[d.semkey] = d.semval
                        for key, val in need.items():
                            wait(key, val)
                        if o.is_dma and o.semval > 16:
                            wait(o.semkey, o.semval - 16)
                        ins = o.fn(e)
                        if o.sig:
                            ins.then_inc(sems[o.semkey], 16 if o.is_dma else 1)
                    for key, val in counters.items():
                        wait(key, val)
                return body
            for eng in ENGS:
                if self.ops[eng]:
                    engmap[eng](make(eng))
        nc.all_engine_barrier()
        nc.clear_and_free_semaphores(list(sems.values()))
        nc.all_engine_barrier()


class Ctx:
    pass


def phase_inproj(nc, cx, l):
    P = Prog(nc)
    ps = cx.ps
    with contextlib.ExitStack() as st:
        sb = lambda name, shape, dt: st.enter_context(nc.sbuf_tensor('i%d_' % l + name, shape, dt))
        xtb = sb('xtb', [128, KC, TS], BF16)
        wst = [sb('wst%d' % i, [128, KC, 128], F32) for i in range(2)]
        wbf = [sb('wbf%d' % i, [128, KC, 128], BF16) for i in range(2)]
        stg = [sb('stg%d' % i, [128, 512], F32) for i in range(4)]
        if l == 0:
            xst = [sb('xst%d' % i, [128, TS], F32) for i in range(2)]
            for kc in range(KC):
                b = kc % 2
                P.dma(SP, lambda e, kc=kc, b=b: e.dma_start(out=xst[b][:], in_=cx.xT[kc * 128:(kc + 1) * 128, :]),
                      writes=[('xst', b)], tag='x')
                eng = (ACT, DVE, POOL)[kc % 3]
                if eng == ACT:
                    P.op(ACT, lambda e, kc=kc, b=b: e.activation(out=xtb[:, kc, :], in_=xst[b][:], func=AF.Copy),
                         reads=[('xst', b)], writes=[('xtb', kc)])
                else:
                    P.op(eng, lambda e, kc=kc, b=b: e.tensor_copy(out=xtb[:, kc, :], in_=xst[b][:]),
                         reads=[('xst', b)], writes=[('xtb', kc)])
        else:
            for kc in range(KC):
                P.dma(SP, lambda e, kc=kc: e.dma_start(out=xtb[:, kc, :], in_=cx.x2T[kc * 128:(kc + 1) * 128, :]),
                      writes=[('xtb', kc)], tag='x')
        xres = [('xtb', kc) for kc in range(KC)]
        ntile = 0
        for blk in range(NBLK):
            b = blk % 2
            P.dma(SP, lambda e, blk=blk, b=b: e.dma_start(
                out=wst[b][:], in_=cx.w_in[l, :, blk * 128:(blk + 1) * 128].rearrange("(kc p) c -> p kc c", p=128)),
                writes=[('wst', b)], tag='w')
            P.op(POOL, lambda e, b=b: e.tensor_copy(out=wbf[b][:], in_=wst[b][:]), reads=[('wst', b)], writes=[('wbf', b)])
            for tt in range(9):
                t0, n = tt * 512, (512 if tt < 8 else 1)
                bank = ntile % 4
                for kc in range(KC):
                    P.op(PE, lambda e, kc=kc, b=b, bank=bank, t0=t0, n=n: e.matmul(
                        ps[:, bank, 0:n], lhsT=wbf[b][:, kc, :], rhs=xtb[:, kc, t0:t0 + n], start=(kc == 0), stop=(kc == KC - 1)),
                        reads=[('wbf', b)] + (xres if kc == 0 else []), writes=[('ps', bank)])
                s = ntile % 4
                if ntile % 2 == 0:
                    P.op(ACT, lambda e, bank=bank, s=s, n=n: e.activation(out=stg[s][:, 0:n], in_=ps[:, bank, 0:n], func=AF.Copy),
                         reads=[('ps', bank)], writes=[('stg', s)])
                else:
                    P.op(DVE, lambda e, bank=bank, s=s, n=n: e.tensor_copy(out=stg[s][:, 0:n], in_=ps[:, bank, 0:n]),
                         reads=[('ps', bank)], writes=[('stg', s)])
                P.dma(SP, lambda e, blk=blk, s=s, t0=t0, n=n: e.dma_start(
                    out=cx.projT[blk * 128:(blk + 1) * 128, t0:t0 + n], in_=stg[s][:, 0:n], allow_slow_non_contiguous=(n == 1)),
                    reads=[('stg', s)], writes=[('projT', blk)], tag='po')
                ntile += 1
        P.emit()


ALPHA_DN = (2.0 * DEPTH) ** 0.25
LN_EPS = 1e-5


def load_bcast(P, nc, eng, dst, src_row, key):
    P.dma(eng, lambda e: e.dma_start(out=dst, in_=src_row.partition_broadcast(dst.shape[0])), writes=[key], tag='bc')


def phase_outproj(nc, cx, l):
    P = Prog(nc)
    ps = cx.ps
    last = (l == DEPTH - 1)
    with contextlib.ExitStack() as st:
        sb = lambda name, shape, dt: st.enter_context(nc.sbuf_tensor('o%d_' % l + name, shape, dt))
        wob = sb('wob', [128, KC, D], BF16)
        wst = [sb('wost%d' % i, [128, D], F32) for i in range(2)]
        lng, lnb = sb('lng', [128, D], F32), sb('lnb', [128, D], F32)
        identb = sb('identb', [128, 128], BF16)
        gT = [sb('gT%d' % i, [128, KC, 128], BF16) for i in range(2)]
        xr = [sb('xr%d' % i, [128, D], F32) for i in range(2)]
        zt = [sb('zt%d' % i, [128, D], F32) for i in range(2)]
        yb = [sb('yb%d' % i, [128, D], BF16) for i in range(2)]
        xT2 = [sb('xT2%d' % i, [128, KC, 128], BF16) for i in range(2)]
        st1 = [sb('st1%d' % i, [128, 4], F32) for i in range(2)]
        sq = sb('sqjunk', [128, D], F32)
        P.dma(SP, lambda e: e.dma_start(out=identb[:], in_=cx.identb), writes=['identb'], tag='c')
        load_bcast(P, nc, SP, lng[:], cx.ln_g[l:l + 1, :], 'lng')
        load_bcast(P, nc, SP, lnb[:], cx.ln_b[l:l + 1, :], 'lnb')
        for kc in range(KC):
            b = kc % 2
            P.dma(SP, lambda e, kc=kc, b=b: e.dma_start(out=wst[b][:], in_=cx.w_out[l, kc * 128:(kc + 1) * 128, :]), writes=[('wst', b)], tag='w')
            if kc % 2 == 0:
                P.op(POOL, lambda e, kc=kc, b=b: e.tensor_copy(out=wob[:, kc, :], in_=wst[b][:]), reads=[('wst', b)], writes=[('wob', kc)])
            else:
                P.op(ACT, lambda e, kc=kc, b=b: e.activation(out=wob[:, kc, :], in_=wst[b][:], func=AF.Copy), reads=[('wst', b)], writes=[('wob', kc)])
        wres = [('wob', kc) for kc in range(KC)]
        xsrc = cx.xtok if l == 0 else cx.y1
        ydst = cx.y if last else cx.y1
        for tb in range(33):
            t0, n = tb * 128, (128 if tb < 32 else 1)
            b = tb % 2
            P.dma(SP, lambda e, b=b, t0=t0, n=n: e.dma_start(
                out=gT[b][:, :, 0:n], in_=cx.gatedT[:, t0:t0 + n].rearrange("(kc p) t -> p kc t", p=128), allow_slow_non_contiguous=(n == 1)),
                writes=[('gT', b)], tag='g')
            P.dma(SP, lambda e, b=b, t0=t0, n=n: e.dma_start(out=xr[b][0:n, :], in_=xsrc[t0:t0 + n, :]), writes=[('xr', b)], tag='xr')
            for cg in range(4):
                for kc in range(KC):
                    P.op(PE, lambda e, b=b, n=n, cg=cg, kc=kc: e.matmul(
                        ps[0:n, cg, :], lhsT=gT[b][:, kc, 0:n], rhs=wob[:, kc, cg * 512:(cg + 1) * 512], start=(kc == 0), stop=(kc == KC - 1)),
                        reads=[('gT', b)] + (wres if kc == 0 and cg == 0 and tb == 0 else []), writes=[('ps', cg)])
                P.op(DVE, lambda e, b=b, n=n, cg=cg: e.scalar_tensor_tensor(
                    out=zt[b][0:n, cg * 512:(cg + 1) * 512], in0=xr[b][0:n, cg * 512:(cg + 1) * 512], scalar=ALPHA_DN,
                    in1=ps[0:n, cg, :], op0=ALU.mult, op1=ALU.add), reads=[('xr', b), ('ps', cg)], writes=[('zt', b)])
            s = st1[b]
            P.op(DVE, lambda e, b=b, n=n, s=s: e.tensor_reduce(out=s[0:n, 0:1], in_=zt[b][0:n, :], axis=AX.X, op=ALU.add),
                 reads=[('zt', b)], writes=[('st1', b)])
            P.op(DVE, lambda e, n=n, s=s: e.tensor_scalar(out=s[0:n, 1:2], in0=s[0:n, 0:1], scalar1=-1.0 / D, scalar2=None, op0=ALU.mult),
                 reads=[('st1', b)], writes=[('st1', b)])
            P.op(POOL, lambda e, b=b, n=n, s=s: e.tensor_scalar(out=zt[b][0:n, :], in0=zt[b][0:n, :], scalar1=s[0:n, 1:2], scalar2=None, op0=ALU.add),
                 reads=[('zt', b), ('st1', b)], writes=[('zt', b)])
            P.op(ACT, lambda e, b=b, n=n, s=s: e.activation(out=sq[0:n, :], in_=zt[b][0:n, :], func=AF.Square, accum_out=s[0:n, 2:3]),
                 reads=[('zt', b)], writes=[('st1', b), 'sq'])
            P.op(DVE, lambda e, n=n, s=s: e.tensor_scalar(out=s[0:n, 3:4], in0=s[0:n, 2:3], scalar1=1.0 / D, scalar2=LN_EPS, op0=ALU.mult, op1=ALU.add),
                 reads=[('st1', b)], writes=[('st1', b)])
            P.op(ACT, lambda e, n=n, s=s: e.activation(out=s[0:n, 3:4], in_=s[0:n, 3:4], func=AF.Ln), reads=[('st1', b)], writes=[('st1', b)])
            P.op(ACT, lambda e, n=n, s=s: e.activation(out=s[0:n, 3:4], in_=s[0:n, 3:4], func=AF.Exp, scale=-0.5), reads=[('st1', b)], writes=[('st1', b)])
            P.op(DVE, lambda e, b=b, n=n, s=s: e.scalar_tensor_tensor(out=zt[b][0:n, :], in0=zt[b][0:n, :], scalar=s[0:n, 3:4], in1=lng[0:n, :],
                                                                    op0=ALU.mult, op1=ALU.mult), reads=[('zt', b), ('st1', b), 'lng'], writes=[('zt', b)])
            P.op(POOL, lambda e, b=b, n=n: e.tensor_tensor(out=zt[b][0:n, :], in0=zt[b][0:n, :], in1=lnb[0:n, :], op=ALU.add),
                 reads=[('zt', b), 'lnb'], writes=[('zt', b)])
            P.dma(SP, lambda e, b=b, t0=t0, n=n: e.dma_start(out=ydst[t0:t0 + n, :], in_=zt[b][0:n, :]), reads=[('zt', b)], writes=['ydst'], tag='yo')
            if not last:
                P.op(ACT, lambda e, b=b, n=n: e.activation(out=yb[b][0:n, :], in_=zt[b][0:n, :], func=AF.Copy), reads=[('zt', b)], writes=[('yb', b)])
                psb = [ps[:, 4 + h, :].bitcast(BF16) for h in range(2)]
                for kc in range(KC):
                    h, j = kc // 8, kc % 8
                    P.op(PE, lambda e, b=b, n=n, kc=kc, h=h, j=j: e.transpose(out=psb[h][:, j * 128:j * 128 + n], in_=yb[b][0:n, kc * 128:(kc + 1) * 128],
                                                                          identity=identb[0:n, 0:n]),
                         reads=[('yb', b), 'identb'], writes=[('ps', 4 + h)])
                for h in range(2):
                    P.op(DVE if h == 0 else ACT, (lambda e, b=b, n=n, h=h: e.tensor_copy(
                        out=xT2[b][:, h * 8:(h + 1) * 8, 0:n], in_=psb[h].rearrange("p (j t) -> p j t", t=128)[:, :, 0:n])) if h == 0 else
                        (lambda e, b=b, n=n, h=h: e.activation(
                            out=xT2[b][:, h * 8:(h + 1) * 8, 0:n], in_=psb[h].rearrange("p (j t) -> p j t", t=128)[:, :, 0:n], func=AF.Copy)),
                        reads=[('ps', 4 + h)], writes=[('xT2', b)])
                P.dma(SP, lambda e, b=b, t0=t0, n=n: e.dma_start(
                    out=cx.x2T[:, t0:t0 + n].rearrange("(kc p) t -> p kc t", p=128), in_=xT2[b][:, :, 0:n], allow_slow_non_contiguous=(n == 1)),
                    reads=[('xT2', b)], writes=['x2T'], tag='x2')
        P.emit()


PP_CONV, PP_NORMA, PP_ALOG, PP_DTB, PP_MU, PP_MUS = 0, 48, 49, 50, 51, 63
PP_W0, PP_A0, PP_XI, PP_ALPHA, PP_RHO, PP_GNG, PP_GNB, PP_BF, PP_PSC, NPP = 64, 68, 72, 76, 80, 84, 88, 92, 93, 100
POOL_W = (2, 4, 8, 16)


def pack_pp(I):
    pp = np.zeros((DEPTH, 128, NPP), np.float32)
    p = np.arange(128)
    for l in range(DEPTH):
        for b12 in range(12):
            for i in range(4):
                pp[l, :, PP_CONV + b12 * 4 + i] = I['conv_A'][l][i, b12 * 128 + p]
            pp[l, :, PP_MU + b12] = I['mu_B'][l][b12 * 128 + p]
        pp[l, :, PP_NORMA] = I['norm_A'][l][p % 64]
        pp[l, 0:8, PP_ALOG] = I['A_log'][l]
        pp[l, 0:8, PP_DTB] = I['dt_bias'][l]
        pp[l, 32:64, PP_MUS] = I['mu_B'][l][1536:1568]
        pp[l, 64:96, PP_MUS] = I['mu_B'][l][1568:1600]
        for j in range(4):
            for col, nm in ((PP_W0, 'w0_B'), (PP_A0, 'a0_B'), (PP_XI, 'xi_B'), (PP_ALPHA, 'alpha_B'), (PP_RHO, 'rho_B'),
                            (PP_GNG, 'gn_g_B'), (PP_GNB, 'gn_b_B'), (PP_PSC, 'pool_scale_D')):
                pp[l, :, col + j] = I[nm][l][j * 128 + p]
        pp[l, 16:24, PP_BF] = I['b_f_C'][l]
    return pp


def phase_mixD(nc, cx, l):
    P = Prog(nc)
    ps = cx.ps
    with contextlib.ExitStack() as st:
        sb = lambda name, shape, dt: st.enter_context(nc.sbuf_tensor('d%d_' % l + name, shape, dt))
        pp = sb('pp', [128, NPP], F32)
        pwf, pwb = sb('pwf', [128, 4, 128], F32), sb('pwb', [128, 4, 128], BF16)
        ext, zt = sb('ext', [128, 15 + T], F32), sb('zt', [128, T], F32)
        sA, sB = sb('sA', [128, 15 + T], F32), sb('sB', [128, 15 + T], F32)
        pb, go = sb('pb', [128, T], BF16), sb('go', [128, T], BF16)
        exs, zs, sAs, sBs = sb('exs', [128, 16], F32), sb('zs', [128, 1], F32), sb('sAs', [128, 16], F32), sb('sBs', [128, 16], F32)
        pbs, gos = sb('pbs', [128, 1], BF16), sb('gos', [128, 1], BF16)
        P.dma(SP, lambda e: e.dma_start(out=pp[:], in_=cx.pp[l]), writes=['pp'], tag='c')
        P.dma(SP, lambda e: e.dma_start(out=pwf[:], in_=cx.pool_w[l].rearrange("g c d -> c g d")), writes=['pwf'], tag='c')
        P.op(POOL, lambda e: e.tensor_copy(out=pwb[:], in_=pwf[:]), reads=['pwf'], writes=['pwb'])
        P.op(POOL, lambda e: e.memset(ext[:, 0:15], 0.0), writes=['ext'])

        def group(g, n, ext, zt, sA, sB, pb, go, fix, sfx):
            w = POOL_W[g]
            r0 = g * 128
            k = lambda nm: nm + sfx
            if n > 1:
                P.dma(SP, lambda e: e.dma_start(out=ext[:, 15:15 + n], in_=cx.projT[R_DU + r0:R_DU + r0 + 128, 0:n]), writes=[k('ext')], tag='m')
                P.dma(SP, lambda e: e.dma_start(out=zt[:, 0:n], in_=cx.projT[R_DZ + r0:R_DZ + r0 + 128, 0:n]), writes=[k('zt')], tag='m')
            else:
                P.dma(SP, lambda e: e.dma_start(out=ext[:, 0:15], in_=cx.dbufT[l, r0:r0 + 128, :]), writes=[k('ext')], tag='m')
                P.dma(SP, lambda e: e.dma_start(out=ext[:, 15:16], in_=cx.projT[R_DU + r0:R_DU + r0 + 128, T:T + 1], allow_slow_non_contiguous=True), writes=[k('ext')], tag='m')
                P.dma(SP, lambda e: e.dma_start(out=zt[:, 0:1], in_=cx.projT[R_DZ + r0:R_DZ + r0 + 128, T:T + 1], allow_slow_non_contiguous=True), writes=[k('zt')], tag='m')
            P.op(ACT, lambda e: e.activation(out=zt[:, 0:n], in_=zt[:, 0:n], func=AF.Silu), reads=[k('zt')], writes=[k('zt')])
            src, srck, sh, E = ext, k('ext'), 1, 15 + n
            for step in range(g + 1):
                dst, dstk = (sA, k('sA')) if step % 2 == 0 else (sB, k('sB'))
                lo = 2 * sh - 1
                P.op(DVE if step % 2 == 0 else POOL, lambda e, src=src, dst=dst, sh=sh, lo=lo: e.tensor_tensor(
                    out=dst[:, lo:E], in0=src[:, lo:E], in1=src[:, lo - sh:E - sh], op=ALU.add), reads=[srck], writes=[dstk])
                src, srck, sh = dst, dstk, sh * 2
            P.op(DVE, lambda e, src=src: e.scalar_tensor_tensor(out=pb[:, 0:n], in0=src[:, 15:15 + n], scalar=1.0 / w, in1=ext[:, 15:15 + n],
                                                              op0=ALU.mult, op1=ALU.subtract), reads=[srck, k('ext')], writes=[k('pb')])
            if fix:
                for t in range(w - 1):
                    P.op(DVE, lambda e, src=src, t=t: e.scalar_tensor_tensor(out=pb[:, t:t + 1], in0=src[:, 15 + t:16 + t], scalar=1.0 / (t + 1),
                                                                            in1=ext[:, 15 + t:16 + t], op0=ALU.mult, op1=ALU.subtract),
                         reads=[srck, k('ext')], writes=[k('pb')])
            for tt in range((n + 511) // 512):
                t0, m = tt * 512, min(512, n - tt * 512)
                bank = tt % 4
                P.op(PE, lambda e, t0=t0, m=m, bank=bank: e.matmul(ps[:, bank, 0:m], lhsT=pwb[:, g, :], rhs=pb[:, t0:t0 + m], start=True, stop=True),
                     reads=['pwb', k('pb')], writes=[('ps', bank)])
                P.op(DVE, lambda e, t0=t0, m=m, bank=bank: e.scalar_tensor_tensor(
                    out=go[:, t0:t0 + m], in0=ps[:, bank, 0:m], scalar=pp[:, PP_PSC + g:PP_PSC + g + 1], in1=zt[:, t0:t0 + m], op0=ALU.mult, op1=ALU.mult),
                    reads=[('ps', bank), 'pp', k('zt')], writes=[k('go')])
            c0 = 0 if n > 1 else T
            P.dma(SP, lambda e: e.dma_start(out=cx.gatedT[1536 + r0:1536 + r0 + 128, c0:c0 + n], in_=go[:, 0:n], allow_slow_non_contiguous=(n == 1)),
                  reads=[k('go')], writes=['gatedT'], tag='go')
        for g in range(4):
            group(g, T, ext, zt, sA, sB, pb, go, True, '')
            group(g, 1, exs, zs, sAs, sBs, pbs, gos, False, 's')
        P.dma(SP, lambda e: e.dma_start(out=cx.o_dbuf[l, 0], in_=cx.projT[R_DU:R_DU + 512, T - 15:T], allow_slow_non_contiguous=True), writes=['o_dbuf0'], tag='so')
        P.dma(SP, lambda e: e.dma_start(out=cx.o_dbuf[l, 1, :, 0:14], in_=cx.dbufT[l, :, 1:15], allow_slow_non_contiguous=True), writes=['o_dbuf1'], tag='so')
        P.dma(SP, lambda e: e.dma_start(out=cx.o_dbuf[l, 1, :, 14:15], in_=cx.projT[R_DU:R_DU + 512, T:T + 1], allow_slow_non_contiguous=True), writes=['o_dbuf2'], tag='so')
        P.emit()


def phase_mixC(nc, cx, l, heads=range(NH), do_sample=True):
    P = Prog(nc)
    ps = cx.ps
    with contextlib.ExitStack() as st:
        sb = lambda name, shape, dt: st.enter_context(nc.sbuf_tensor('c%d_' % l + name, shape, dt))
        cf = sb('cf', [128, 256], F32)
        cb = sb('cb', [128, 128 + 4 * 512], BF16)
        identf, tri = cf[:, 0:128], cf[:, 128:256]
        identb = cb[:, 0:128]
        negb = sb('negb', [128, 8], F32)
        ones32 = sb('ones32', [128, 64], F32)
        onesj = sb('onesj', [128, 32], F32)
        qf, kf, vf, zf = (sb(n, [64, T], F32) for n in ('qf', 'kf', 'vf', 'zf'))
        Qa, Ka = sb('Qa', [70, T], BF16), sb('Ka', [70, T], BF16)
        Vt = sb('Vt', [128, 32, 65], BF16)
        PT = [sb('PT%d' % i, [128, 512], BF16) for i in range(3)]
        f32t, e1, lf, cw, hif, r1 = (sb(n, [128, 32], F32) for n in ('f32t', 'e1', 'lf', 'cw', 'hif', 'r1'))
        offs = sb('offs', [128, 1], F32)
        c3 = sb('c3', [128, 6, 32], BF16)
        oT, t1 = sb('oT', [65, 512], F32), sb('t1', [64, 512], F32)
        rden = sb('rden', [65, 512], F32)
        go = sb('go', [64, T], BF16)
        P.dma(SP, lambda e: e.dma_start(out=cf[:], in_=cx.cf32), writes=['cf'], tag='c')
        P.dma(SP, lambda e: e.dma_start(out=cb[:], in_=cx.cb16), writes=['cb'], tag='c')
        P.dma(SP, lambda e: e.dma_start(out=negb[:], in_=cx.b_f[l:l + 1, :].partition_broadcast(128)), writes=['negb'], tag='c')
        P.op(POOL, lambda e: e.tensor_scalar(out=negb[:], in0=negb[:], scalar1=-1.0, scalar2=None, op0=ALU.mult), reads=['negb'], writes=['negb'])
        P.op(POOL, lambda e: e.memset(ones32[:], 1.0), writes=['ones32'])
        P.op(POOL, lambda e: e.memset(onesj[:], 1.0), writes=['onesj'])
        P.op(POOL, lambda e: e.memset(Vt[:, :, 64:65], 1.0), writes=['Vt'])
        P.op(POOL, lambda e: e.memset(Qa[64:70, :], 1.0), writes=['Qa'])
        P.op(POOL, lambda e: e.memset(Ka[64:70, :], 1.0), writes=['Ka'])
        for h in heads:
            P.dma(SP, lambda e, h=h: e.dma_start(out=f32t[:], in_=cx.projT[R_SM + SM_F + h, 0:T].rearrange("(p j) -> p j", j=32)), writes=['f32t'], tag='m')
            P.op(ACT, lambda e, h=h: e.activation(out=e1[:], in_=f32t[:], func=AF.Exp, scale=-1.0, bias=negb[:, h:h + 1]), reads=['f32t', 'negb'], writes=['e1'])
            P.op(ACT, lambda e: e.activation(out=e1[:], in_=e1[:], func=AF.Ln, bias=1.0), reads=['e1'], writes=['e1'])
            P.op(DVE, lambda e: e.tensor_scalar(out=lf[:], in0=e1[:], scalar1=-1.0, scalar2=None, op0=ALU.mult), reads=['e1'], writes=['lf'])
            P.dma(SP, lambda e, h=h: e.dma_start(out=cx.o_clf[l, h, 0:T].rearrange("(p j) -> p j", j=32), in_=lf[:]), reads=['lf'], writes=['o_clf'], tag='so')
            P.op(DVE, lambda e: e.tensor_tensor_scan(out=cw[:], data0=onesj[:], data1=lf[:], initial=0.0, op0=ALU.mult, op1=ALU.add),
                 reads=['lf', 'onesj'], writes=['cw'])
            P.op(PE, lambda e: e.matmul(ps[:, 4, 0:1], lhsT=tri, rhs=cw[:, 31:32], start=True, stop=True), reads=['cf', 'cw'], writes=[('ps', 4)])
            P.op(DVE, lambda e: e.tensor_copy(out=offs[:], in_=ps[:, 4, 0:1]), reads=[('ps', 4)], writes=['offs'])
            P.op(DVE, lambda e: e.tensor_scalar(out=cw[:], in0=cw[:], scalar1=offs[:, 0:1], scalar2=None, op0=ALU.add), reads=['cw', 'offs'], writes=['cw'])
            src = cw
            for pc in range(3):
                P.op(DVE, lambda e, pc=pc, src=src: e.tensor_copy(out=c3[:, pc, :], in_=src[:]), reads=['cw', 'r1'], writes=['c3'])
                P.op(POOL, lambda e, pc=pc, src=src: e.tensor_scalar(out=c3[:, 3 + pc, :], in0=src[:], scalar1=-1.0, scalar2=None, op0=ALU.mult),
                     reads=['cw', 'r1'], writes=['c3'])
                if pc < 2:
                    P.op(DVE, lambda e, pc=pc: e.tensor_copy(out=hif[:], in_=c3[:, pc, :]), reads=['c3'], writes=['hif'])
                    P.op(DVE, lambda e, src=src: e.tensor_tensor(out=r1[:], in0=src[:], in1=hif[:], op=ALU.subtract), reads=['cw', 'r1', 'hif'], writes=['r1'])
                    src = r1
            P.dma(SP, lambda e: e.dma_start(out=cx.cscr.rearrange("r (p j) -> p r j", j=32), in_=c3[:]), reads=['c3'], writes=['cscr'], tag='cs')
            P.dma(SP, lambda e: e.dma_start(out=Qa[64:67, :], in_=cx.cscr[0:3, :]), reads=['cscr'], writes=['Qa'], tag='cs')
            P.dma(SP, lambda e: e.dma_start(out=Ka[67:70, :], in_=cx.cscr[3:6, :]), reads=['cscr'], writes=['Ka'], tag='cs')
            for tl, r0, key in ((qf, R_CQ, 'qf'), (kf, R_CK, 'kf'), (vf, R_CV, 'vf'), (zf, R_CZ, 'zf')):
                P.dma(SP, lambda e, tl=tl, r0=r0, h=h: e.dma_start(out=tl[:], in_=cx.projT[r0 + h * 64:r0 + (h + 1) * 64, 0:T]), writes=[key], tag='m')
            P.op(ACT, lambda e: e.activation(out=Qa[0:64, :], in_=qf[:], func=AF.Copy, scale=0.125), reads=['qf'], writes=['Qa'])
            P.op(POOL, lambda e: e.tensor_copy(out=Ka[0:64, :], in_=kf[:]), reads=['kf'], writes=['Ka'])
            P.op(ACT, lambda e: e.activation(out=zf[:], in_=zf[:], func=AF.Silu), reads=['zf'], writes=['zf'])
            for r in range(4):
                bank = 5 + r % 2
                for j in range(8):
                    blk = r * 8 + j
                    P.op(PE, lambda e, bank=bank, j=j, blk=blk: e.transpose(out=ps[:, bank, j * 64:(j + 1) * 64], in_=vf[:, blk * 128:(blk + 1) * 128],
                                                                        identity=identf[0:64, 0:64]), reads=['vf', 'cf'], writes=[('ps', bank)])
                P.op(DVE, lambda e, bank=bank, r=r: e.tensor_copy(out=Vt[:, r * 8:(r + 1) * 8, 0:64], in_=ps[:, bank, :].rearrange("p (j d) -> p j d", d=64)),
                     reads=[('ps', bank)], writes=['Vt'])
            it = 0
            for qt in range(8):
                nkb = 4 * qt + 4
                for kb in range(nkb):
                    bank, r = it % 3, it % 3
                    diag = kb >= 4 * qt
                    P.op(PE, lambda e, bank=bank, kb=kb, qt=qt, diag=diag: e.matmul(
                        ps[:, bank, :], lhsT=Ka[0:70, kb * 128:(kb + 1) * 128], rhs=Qa[0:70, qt * 512:(qt + 1) * 512], start=True, stop=not diag),
                        reads=['Ka', 'Qa'], writes=[('ps', bank)])
                    if diag:
                        j = kb - 4 * qt
                        P.op(PE, lambda e, bank=bank, j=j: e.matmul(ps[:, bank, :], lhsT=identb, rhs=cb[:, 128 + j * 512:128 + (j + 1) * 512], start=False, stop=True),
                             reads=['cb'], writes=[('ps', bank)])
                    P.op(ACT, lambda e, bank=bank, r=r: e.activation(out=PT[r][:], in_=ps[:, bank, :], func=AF.Exp), reads=[('ps', bank)], writes=[('PT', r)])
                    P.op(PE, lambda e, r=r, kb=kb, nkb=nkb: e.matmul(ps[0:65, 3, :], lhsT=Vt[:, kb, :], rhs=PT[r][:], start=(kb == 0), stop=(kb == nkb - 1)),
                         reads=['Vt', ('PT', r)], writes=[('ps', 3)])
                    it += 1
                P.op(DVE, lambda e: e.tensor_copy(out=oT[:], in_=ps[0:65, 3, :]), reads=[('ps', 3)], writes=['oT'])
                P.op(DVE, lambda e: e.reciprocal(out=rden[64:65, :], in_=oT[64:65, :]), reads=['oT'], writes=['rden'])
                P.op(PE, lambda e: e.matmul(ps[0:64, 4, :], lhsT=ones32[64:65, 0:64], rhs=rden[64:65, :], start=True, stop=True),
                     reads=['ones32', 'rden'], writes=[('ps', 4)])
                P.op(DVE, lambda e: e.tensor_tensor(out=t1[:], in0=oT[0:64, :], in1=ps[0:64, 4, :], op=ALU.mult), reads=['oT', ('ps', 4)], writes=['t1'])
                P.op(POOL, lambda e, qt=qt: e.tensor_tensor(out=go[:, qt * 512:(qt + 1) * 512], in0=t1[:], in1=zf[:, qt * 512:(qt + 1) * 512], op=ALU.mult),
                     reads=['t1', 'zf'], writes=['go'])
            P.dma(SP, lambda e, h=h: e.dma_start(out=cx.gatedT[1024 + h * 64:1024 + (h + 1) * 64, 0:T], in_=go[:]), reads=['go'], writes=['gatedT'], tag='go')
        for j in range(8):
            P.dma(SP, lambda e, j=j: e.dma_start(out=cx.o_ckv[l, j * 128:(j + 1) * 128, :], in_=cx.projT[R_CK + j * 128:R_CK + (j + 1) * 128, :]),
                  writes=[('o_ckv', j)], tag='so')
        P.emit()


TP, NCH = 33 * 128, 33
A_KINDS = ('k', 'q', 'bz', 'rz', 'k2', 'a2', 'v')
B_KINDS = ('ah', 'bh', 'kh', 'rh', 'k2', 'a2', 'v')
RWKV_DS = float(np.exp(-0.5))


def _l2_rn(P, nc, ps, sqb, onesbd, n, rn, keys, bank0=0):
    for j, t0 in enumerate(range(0, n, 512)):
        m = min(512, n - t0)
        bank = bank0 + j % 2
        P.op(PE, lambda e, t0=t0, m=m, bank=bank: e.matmul(ps[:, bank, 0:m], lhsT=onesbd, rhs=sqb[:, t0:t0 + m], start=True, stop=True),
             reads=[keys[0], 'cb'], writes=[('ps', bank)])
        P.op(ACT, lambda e, t0=t0, m=m, bank=bank: e.activation(out=rn[:, t0:t0 + m], in_=ps[:, bank, 0:m], func=AF.Ln, bias=1e-6),
             reads=[('ps', bank)], writes=[keys[1]])
    P.op(ACT, lambda e: e.activation(out=rn[:, 0:n], in_=rn[:, 0:n], func=AF.Exp, scale=-0.5), reads=[keys[1]], writes=[keys[1]])


def phase_rowsA(nc, cx, l):
    P = Prog(nc)
    with contextlib.ExitStack() as st:
        sb = lambda name, shape, dt: st.enter_context(nc.sbuf_tensor('ra%d_' % l + name, shape, dt))
        pp = sb('pp', [128, NPP], F32)
        P.dma(SP, lambda e: e.dma_start(out=pp[:], in_=cx.pp[l]), writes=['pp'], tag='c')
        g, lb, u1, u0, gl, tmp, rmask, onesr, mA, mB = (sb(n, [8, TP], F32) for n in ('g', 'lb', 'u1', 'u0', 'gl', 'tmp', 'rmask', 'onesr', 'mA', 'mB'))
        nega = sb('nega', [8, 1], F32)
        P.op(POOL, lambda e: e.memset(g[:], 0.0), writes=['g'])
        P.op(POOL, lambda e: e.memset(lb[:], 0.0), writes=['lb'])
        P.op(POOL, lambda e: e.memset(onesr[:], 1.0), writes=['onesr'])
        P.op(POOL, lambda e: e.memset(rmask[:], 1.0), writes=['rmask'])
        P.op(POOL, lambda e: e.memset(rmask[:].rearrange("p (c i) -> p c i", i=128)[:, :, 0:1], 0.0), reads=['rmask'], writes=['rmask'])
        P.dma(SP, lambda e: e.dma_start(out=g[:, 0:TS], in_=cx.projT[R_SM + SM_A:R_SM + SM_A + 8, :]), reads=['g'], writes=['g'], tag='m')
        P.dma(SP, lambda e: e.dma_start(out=lb[:, 0:TS], in_=cx.projT[R_SM + SM_B:R_SM + SM_B + 8, :]), reads=['lb'], writes=['lb'], tag='m')
        P.op(ACT, lambda e: e.activation(out=nega[:], in_=pp[0:8, PP_ALOG:PP_ALOG + 1], func=AF.Exp), reads=['pp'], writes=['nega'])
        P.op(DVE, lambda e: e.tensor_scalar(out=nega[:], in0=nega[:], scalar1=-1.0, scalar2=None, op0=ALU.mult), reads=['nega'], writes=['nega'])
        P.op(ACT, lambda e: e.activation(out=g[:], in_=g[:], func=AF.Exp, bias=pp[0:8, PP_DTB:PP_DTB + 1]), reads=['g', 'pp'], writes=['g'])
        P.op(ACT, lambda e: e.activation(out=g[:], in_=g[:], func=AF.Ln, bias=1.0), reads=['g'], writes=['g'])
        P.op(DVE, lambda e: e.tensor_scalar(out=g[:], in0=g[:], scalar1=nega[:, 0:1], scalar2=None, op0=ALU.mult), reads=['g', 'nega'], writes=['g'])
        P.op(POOL, lambda e: e.memset(g[:, TS:TP], 0.0), reads=['g'], writes=['g'])
        P.op(ACT, lambda e: e.activation(out=lb[:], in_=lb[:], func=AF.Exp, scale=-1.0), reads=['lb'], writes=['lb'])
        P.op(ACT, lambda e: e.activation(out=lb[:], in_=lb[:], func=AF.Ln, bias=1.0), reads=['lb'], writes=['lb'])
        P.op(DVE, lambda e: e.tensor_scalar(out=lb[:], in0=lb[:], scalar1=-1.0, scalar2=None, op0=ALU.mult), reads=['lb'], writes=['lb'])
        P.op(DVE, lambda e: e.tensor_tensor_scan(out=u1[:], data0=rmask[:], data1=g[:], initial=0.0, op0=ALU.mult, op1=ALU.add),
             reads=['g', 'rmask'], writes=['u1'])
        P.op(POOL, lambda e: e.tensor_tensor(out=u0[:], in0=u1[:], in1=g[:], op=ALU.subtract), reads=['u1', 'g'], writes=['u0'])
        P.op(DVE, lambda e: e.tensor_copy(out=gl[:].rearrange("p (c i) -> p c i", i=128),
                                          in_=u1[:].rearrange("p (c i) -> p c i", i=128)[:, :, 127:128].to_broadcast([8, NCH, 128])),
             reads=['u1'], writes=['gl'])
        P.op(POOL, lambda e: e.tensor_tensor(out=tmp[:], in0=lb[:], in1=gl[:], op=ALU.add), reads=['lb', 'gl'], writes=['tmp'])
        stage = [(mA, 'mA'), (mB, 'mB')]
        for i in range(5):
            m, mk = stage[i % 2]
            if i == 0:
                P.op(ACT, lambda e, m=m: e.activation(out=m[:], in_=u0[:], func=AF.Exp), reads=['u0'], writes=[mk])
            elif i == 1:
                P.op(ACT, lambda e, m=m: e.activation(out=m[:], in_=u1[:], func=AF.Exp), reads=['u1'], writes=[mk])
            elif i == 4:
                P.op(ACT, lambda e, m=m: e.activation(out=m[:], in_=gl[:], func=AF.Exp), reads=['gl'], writes=[mk])
            else:
                src = u1 if i == 2 else u0
                P.op(POOL, lambda e, m=m, src=src: e.tensor_tensor(out=m[:], in0=tmp[:], in1=src[:], op=ALU.subtract), reads=['tmp', 'u1', 'u0'], writes=[mk])
                P.op(ACT, lambda e, m=m: e.activation(out=m[:], in_=m[:], func=AF.Exp), reads=[mk], writes=[mk])
            P.dma(SP, lambda e, i=i, m=m: e.dma_start(out=cx.A_mrows[i], in_=m[:]), reads=[mk], writes=['A_mrows'], tag='er')
        for kind, r, tl, key in ((0, 0, u0, 'u0'), (0, 1, onesr, 'onesr'), (3, 0, u1, 'u1'), (3, 1, onesr, 'onesr'), (1, 0, onesr, 'onesr'), (2, 0, onesr, 'onesr')):
            P.dma(SP, lambda e, kind=kind, r=r, tl=tl: e.dma_start(out=cx.A_erows[kind, r], in_=tl[:]), reads=[key], writes=[('A_erows', kind, r)], tag='er')
        P.op(POOL, lambda e: e.tensor_tensor(out=tmp[:], in0=lb[:], in1=u0[:], op=ALU.subtract), reads=['lb', 'u0', 'mA', 'mB'], writes=['tmp'])
        P.dma(SP, lambda e: e.dma_start(out=cx.A_erows[1, 1], in_=tmp[:]), reads=['tmp'], writes=[('A_erows', 1, 1)], tag='er')
        P.op(POOL, lambda e: e.tensor_tensor(out=gl[:], in0=lb[:], in1=u1[:], op=ALU.subtract), reads=['lb', 'u1', 'gl', 'mA', 'mB'], writes=['gl'])
        P.dma(SP, lambda e: e.dma_start(out=cx.A_erows[2, 1], in_=gl[:]), reads=['gl'], writes=[('A_erows', 2, 1)], tag='er')
        P.emit()


def phase_preA(nc, cx, l):
    P = Prog(nc)
    ps = cx.ps
    with contextlib.ExitStack() as st:
        sb = lambda name, shape, dt: st.enter_context(nc.sbuf_tensor('pa%d_' % l + name, shape, dt))
        pp = sb('pp', [128, NPP], F32)
        cb = sb('cb', [128, 256], BF16)
        onesbd = cb[:, 128:256]
        sel = sb('sel', [8, 512], F32)
        P.dma(SP, lambda e: e.dma_start(out=pp[:], in_=cx.pp[l]), writes=['pp'], tag='c')
        P.dma(SP, lambda e: e.dma_start(out=cb[:], in_=cx.cb2), writes=['cb'], tag='c')
        P.dma(SP, lambda e: e.dma_start(out=sel[:], in_=cx.sel8), writes=['sel'], tag='c')
        W = 1024
        raw, acc, rn, kn, qn = (sb(n, [128, W + 3], F32) for n in ('raw', 'acc', 'rn', 'kn', 'qn'))
        sqb, ob = sb('sqb', [128, W], BF16), [sb('ob%d' % i, [128, W], BF16) for i in range(2)]
        gct = sb('gct', [128, NCH], F32)
        mr = sb('mr', [8, 4, W], F32)
        m4c = sb('m4c', [8, NCH], F32)
        P.dma(SP, lambda e: e.dma_start(out=m4c[:], in_=cx.A_mrows[4].rearrange("h (c i) -> h c i", i=128)[:, :, 0], allow_slow_non_contiguous=True), writes=['m4c'], tag='c')
        nob = [0]

        def store(kind, pb, c0, n, src_f32, mul_row=None, srckey=None):
            o = ob[nob[0] % 2]
            ok = ('ob', nob[0] % 2)
            nob[0] += 1
            if mul_row is None:
                P.op(POOL, lambda e: e.tensor_copy(out=o[:, 0:n], in_=src_f32[:, 0:n]), reads=[srckey], writes=[ok])
            else:
                for j, t0 in enumerate(range(0, n, 512)):
                    m = min(512, n - t0)
                    bank = 2 + j % 2
                    P.op(PE, lambda e, t0=t0, m=m, bank=bank: e.matmul(ps[:, bank, 0:m], lhsT=sel[:, pb * 128:(pb + 1) * 128],
                                                                    rhs=mr[:, mul_row, t0:t0 + m], start=True, stop=True),
                         reads=['sel', 'mr'], writes=[('ps', bank)])
                    P.op(DVE, lambda e, t0=t0, m=m, bank=bank: e.tensor_tensor(out=o[:, t0:t0 + m], in0=src_f32[:, t0:t0 + m], in1=ps[:, bank, 0:m], op=ALU.mult),
                         reads=[srckey, ('ps', bank)], writes=[ok])
            P.dma(SP, lambda e: e.dma_start(out=cx.A_sc[kind][pb * 128:(pb + 1) * 128, c0:c0 + n], in_=o[:, 0:n], allow_slow_non_contiguous=(n == 1)),
                  reads=[ok], writes=[('A_sc', kind)], tag='sc')

        def conv_block(br, pb, c0, n, dst, dstkey):
            r0 = br * 512 + pb * 128
            b12 = br * 4 + pb
            if c0 == 0:
                P.op(POOL, lambda e: e.memset(raw[:, 0:3], 0.0), reads=['raw'], writes=['raw'])
                P.dma(SP, lambda e: e.dma_start(out=raw[:, 3:3 + n], in_=cx.projT[r0:r0 + 128, 0:n]), writes=['raw'], tag='m')
            elif c0 == T:
                P.dma(SP, lambda e: e.dma_start(out=raw[:, 0:3], in_=cx.aconvT[l, r0:r0 + 128, :]), writes=['raw'], tag='m')
                P.dma(SP, lambda e: e.dma_start(out=raw[:, 3:4], in_=cx.projT[r0:r0 + 128, T:T + 1], allow_slow_non_contiguous=True), writes=['raw'], tag='m')
            else:
                P.dma(SP, lambda e: e.dma_start(out=raw[:, 0:3 + n], in_=cx.projT[r0:r0 + 128, c0 - 3:c0 + n]), writes=['raw'], tag='m')
            cw = lambda i: pp[:, PP_CONV + b12 * 4 + i:PP_CONV + b12 * 4 + i + 1]
            P.op(DVE, lambda e: e.tensor_scalar(out=acc[:, 0:n], in0=raw[:, 0:n], scalar1=cw(0), scalar2=None, op0=ALU.mult), reads=['raw', 'pp'], writes=['acc'])
            for i in range(1, 4):
                P.op(DVE, lambda e, i=i: e.scalar_tensor_tensor(out=acc[:, 0:n], in0=raw[:, i:i + n], scalar=cw(i), in1=acc[:, 0:n], op0=ALU.mult, op1=ALU.add),
                     reads=['raw', 'acc', 'pp'], writes=['acc'])
            P.op(ACT, lambda e: e.activation(out=dst[:, 0:n], in_=acc[:, 0:n], func=AF.Silu), reads=['acc'], writes=[dstkey])

        for pb in range(4):
            for (c0, n) in [(i * W, W) for i in range(T // W)] + [(T, 1)]:
                P.dma(SP, lambda e, c0=c0, n=n: e.dma_start(out=mr[:, :, 0:n], in_=cx.A_mrows[0:4, :, c0:c0 + n].rearrange("k h t -> h k t"),
                                                          allow_slow_non_contiguous=(n == 1)), writes=['mr'], tag='m')
                for br, dst, dk, sc in ((0, qn, 'qn', 0.125), (1, kn, 'kn', 1.0)):
                    conv_block(br, pb, c0, n, dst, dk)
                    P.act(sqb[:, 0:n], dst[:, 0:n], AF.Square, reads=[dk], writes=['sqb'])
                    _l2_rn(P, nc, ps, sqb, onesbd, n, rn, ('sqb', 'rn'))
                    P.stt(dst[:, 0:n], dst[:, 0:n], sc, rn[:, 0:n], ALU.mult, ALU.mult, reads=[dk, 'rn'], writes=[dk])
                store('q', pb, c0, n, qn, None, 'qn')
                store('rz', pb, c0, n, qn, 1, 'qn')
                store('k', pb, c0, n, kn, None, 'kn')
                store('bz', pb, c0, n, kn, 0, 'kn')
                store('k2', pb, c0, n, kn, 2, 'kn')
                store('a2', pb, c0, n, kn, 3, 'kn')
                conv_block(2, pb, c0, n, qn, 'qn')
                store('v', pb, c0, n, qn, None, 'qn')
            for j in range(NCH):
                pass
            P.op(PE, lambda e, pb=pb: e.matmul(ps[:, 4, 0:NCH], lhsT=sel[:, pb * 128:(pb + 1) * 128],
                                               rhs=m4c[:], start=True, stop=True),
                 reads=['sel', 'm4c'], writes=[('ps', 4)])
            P.op(DVE, lambda e: e.tensor_copy(out=gct[:], in_=ps[:, 4, 0:NCH]), reads=[('ps', 4)], writes=['gct'])
            P.dma(SP, lambda e, pb=pb: e.dma_start(out=cx.A_gc[pb * 128:(pb + 1) * 128, :], in_=gct[:]), reads=['gct'], writes=['A_gc'], tag='sc')
        P.dma(SP, lambda e: e.dma_start(out=cx.o_aconv[l, 0], in_=cx.projT[0:1536, T - 3:T], allow_slow_non_contiguous=True), writes=['oac0'], tag='so')
        P.dma(SP, lambda e: e.dma_start(out=cx.o_aconv[l, 1, :, 0:2], in_=cx.aconvT[l, :, 1:3], allow_slow_non_contiguous=True), writes=['oac1'], tag='so')
        P.dma(SP, lambda e: e.dma_start(out=cx.o_aconv[l, 1, :, 2:3], in_=cx.projT[0:1536, T:T + 1], allow_slow_non_contiguous=True), writes=['oac2'], tag='so')
        P.emit()


def phase_preB(nc, cx, l):
    P = Prog(nc)
    ps = cx.ps
    W = 1024
    with contextlib.ExitStack() as st:
        sb = lambda name, shape, dt: st.enter_context(nc.sbuf_tensor('pb%d_' % l + name, shape, dt))
        pp = sb('pp', [128, NPP], F32)
        cb = sb('cb', [128, 256], BF16)
        onesbd = cb[:, 128:256]
        lwf, lwb = sb('lwf', [128, 512], F32), sb('lwb', [128, 512], BF16)
        oma = sb('oma', [128, 4], F32)
        rmask = sb('rmask', [128, W], F32)
        P.ld(pp[:], cx.pp[l], writes=['pp'], tag='c')
        P.ld(cb[:], cx.cb2, writes=['cb'], tag='c')
        P.ld(lwf[:], cx.lora_w[l], writes=['lwf'], tag='c')
        P.cp(POOL, lwb[:], lwf[:], reads=['lwf'], writes=['lwb'])
        P.ts(DVE, oma[:], pp[:, PP_ALPHA:PP_ALPHA + 4], -1.0, ALU.mult, 1.0, ALU.add, reads=['pp'], writes=['oma'])
        P.ms(POOL, rmask[:], 1.0, writes=['rmask'])
        P.ms(POOL, rmask[:].rearrange("p (c i) -> p c i", i=128)[:, :, 0:1], 0.0, reads=['rmask'], writes=['rmask'])
        sm, smd = sb('sm', [128, W + 1], F32), sb('smd', [128, W], F32)
        lwin = sb('lwin', [128, W], BF16)
        raw = {k: sb('raw_' + k, [128, W + 1], F32) for k in 'rkv'}
        sh = {k: sb('sh_' + k, [128, W], F32) for k in 'rkv'}
        dtmp = sb('dtmp', [128, W], F32)
        lw, av, kx, rn, kt, t1, Lc, G, Gi, Gm, Ah, Kh = (sb(n, [128, W], F32) for n in ('lw', 'av', 'kx', 'rn', 'kt', 't1', 'Lc', 'G', 'Gi', 'Gm', 'Ah', 'Kh'))
        sqb = sb('sqb', [128, W], BF16)
        ob = [sb('ob%d' % i, [128, W], BF16) for i in range(3)]
        bo = sb('bo', [128, W], F32)
        gct = [sb('gct%d' % i, [128, NCH], F32) for i in range(4)]
        nob = [0]

        def store(kind, pb, c0, n, fn):
            j = nob[0] % 3
            nob[0] += 1
            fn(ob[j][:, 0:n], ('ob', j))
            P.ld(cx.B_sc[kind][pb * 128:(pb + 1) * 128, c0:c0 + n], ob[j][:, 0:n], reads=[('ob', j)], writes=[('B_sc', kind)], tag='sc', slow=(n == 1))

        def halo_load(tile, key, r0, c0, n, col12):
            if c0 == 0:
                P.ms(POOL, tile[:, 0:1], 0.0, reads=[key], writes=[key])
                P.ld(tile[:, 1:1 + n], cx.projT[r0:r0 + 128, 0:n], writes=[key], tag='m')
            elif c0 == T:
                P.ld(tile[:, 0:1], cx.bshiftT[l, :, col12:col12 + 1], writes=[key], tag='m', slow=True)
                P.ld(tile[:, 1:2], cx.projT[r0:r0 + 128, T:T + 1], writes=[key], tag='m', slow=True)
            else:
                P.ld(tile[:, 0:1 + n], cx.projT[r0:r0 + 128, c0 - 1:c0 + n], writes=[key], tag='m')

        def shift(dst, dkey, tile, key, n, mucol):
            P.tt(POOL, dtmp[:, 0:n], tile[:, 0:n], tile[:, 1:1 + n], ALU.subtract, reads=[key], writes=['dtmp'])
            P.stt(dst[:, 0:n], dtmp[:, 0:n], pp[:, mucol:mucol + 1], tile[:, 1:1 + n], ALU.mult, ALU.add, reads=['dtmp', key, 'pp'], writes=[dkey])

        for (c0, n) in [(i * W, W) for i in range(T // W)] + [(T, 1)]:
            nch = max(1, n // 128)
            halo_load(sm, 'sm', R_SM, c0, n, 12)
            shift(smd, 'smd', sm, 'sm', n, PP_MUS)
            P.act(lwin[32:64, 0:n], smd[32:64, 0:n], AF.Tanh, reads=['smd'], writes=['lwin'])
            P.cp(POOL, lwin[64:96, 0:n], smd[64:96, 0:n], reads=['smd'], writes=['lwin'])
            for pb in range(4):
                for j, k in enumerate('rkv'):
                    halo_load(raw[k], 'raw_' + k, R_BR + j * 512 + pb * 128, c0, n, j * 4 + pb)
                    shift(sh[k], 'sh_' + k, raw[k], 'raw_' + k, n, PP_MU + j * 4 + pb)
                rs, ks, vs = sh['r'], sh['k'], sh['v']
                for jt, t0 in enumerate(range(0, n, 512)):
                    m = min(512, n - t0)
                    cs = slice(t0, t0 + m)
                    P.mm(ps[:, 2, 0:m], lwb[32:64, pb * 128:(pb + 1) * 128], lwin[32:64, cs], reads=['lwb', 'lwin'], writes=[('ps', 2)])
                    P.mm(ps[:, 3, 0:m], lwb[64:96, pb * 128:(pb + 1) * 128], lwin[64:96, cs], reads=['lwb', 'lwin'], writes=[('ps', 3)])
                    P.act(lw[:, cs], ps[:, 2, 0:m], AF.Sigmoid, reads=[('ps', 2), 'pp'], writes=['lw'], bias=pp[:, PP_W0 + pb:PP_W0 + pb + 1])
                    P.act(av[:, cs], ps[:, 3, 0:m], AF.Sigmoid, reads=[('ps', 3), 'pp'], writes=['av'], bias=pp[:, PP_A0 + pb:PP_A0 + pb + 1])
                P.ts(DVE, lw[:, 0:n], lw[:, 0:n], -RWKV_DS, ALU.mult, reads=['lw'], writes=['lw'])
                P.ts(DVE, kx[:, 0:n], ks[:, 0:n], pp[:, PP_XI + pb:PP_XI + pb + 1], ALU.mult, reads=['sh_k', 'pp'], writes=['kx'])
                P.act(sqb[:, 0:n], kx[:, 0:n], AF.Square, reads=['kx'], writes=['sqb'])
                _l2_rn(P, nc, ps, sqb, onesbd, n, rn, ('sqb', 'rn'))
                P.tt(DVE, kx[:, 0:n], kx[:, 0:n], rn[:, 0:n], ALU.mult, reads=['kx', 'rn'], writes=['kx'])
                P.ts(DVE, t1[:, 0:n], av[:, 0:n], pp[:, PP_ALPHA + pb:PP_ALPHA + pb + 1], ALU.mult, oma[:, pb:pb + 1], ALU.add, reads=['av', 'pp', 'oma'], writes=['t1'])
                P.tt(POOL, kt[:, 0:n], ks[:, 0:n], t1[:, 0:n], ALU.mult, reads=['sh_k', 't1'], writes=['kt'])
                P.tt(POOL, t1[:, 0:n], rs[:, 0:n], kt[:, 0:n], ALU.mult, reads=['sh_r', 'kt', 't1'], writes=['t1'])
                P.ts(DVE, sqb[:, 0:n], t1[:, 0:n], pp[:, PP_RHO + pb:PP_RHO + pb + 1], ALU.mult, reads=['t1', 'pp', 'rn'], writes=['sqb'])
                for jt, t0 in enumerate(range(0, n, 512)):
                    m = min(512, n - t0)
                    cs = slice(t0, t0 + m)
                    P.mm(ps[:, 4 + jt % 2, 0:m], onesbd, sqb[:, cs], reads=['sqb', 'cb'], writes=[('ps', 4 + jt % 2)])
                    P.tt(DVE, bo[:, cs], ps[:, 4 + jt % 2, 0:m], vs[:, cs], ALU.mult, reads=[('ps', 4 + jt % 2), 'sh_v'], writes=['bo'])
                P.ld(cx.B_bonus[pb * 128:(pb + 1) * 128, c0:c0 + n], bo[:, 0:n], reads=['bo'], writes=['B_bonus'], tag='sc', slow=(n == 1))
                P.op(DVE, lambda e, n=n: e.tensor_tensor_scan(out=Lc[:, 0:n], data0=rmask[:, 0:n], data1=lw[:, 0:n], initial=0.0, op0=ALU.mult, op1=ALU.add),
                     reads=['lw', 'rmask'], writes=['Lc'])
                P.act(G[:, 0:n], Lc[:, 0:n], AF.Exp, reads=['Lc'], writes=['G'])
                P.act(Gi[:, 0:n], Lc[:, 0:n], AF.Exp, reads=['Lc'], writes=['Gi'], scale=-1.0)
                P.tt(POOL, Gm[:, 0:n], Lc[:, 0:n], lw[:, 0:n], ALU.subtract, reads=['Lc', 'lw'], writes=['Gm'])
                P.act(Gm[:, 0:n], Gm[:, 0:n], AF.Exp, reads=['Gm'], writes=['Gm'])
                if n > 1:
                    Gv = G[:, 0:n].rearrange("p (c i) -> p c i", i=128)
                    GCb = Gv[:, :, 127:128].to_broadcast([128, nch, 128])
                    P.cp(POOL, gct[pb][:, c0 // 128:c0 // 128 + nch], Gv[:, :, 127], reads=['G'], writes=[('gct', pb)])
                    v3 = lambda ap: ap.rearrange("p (c i) -> p c i", i=128)
                else:
                    GCb = G[:, 0:1]
                    P.cp(POOL, gct[pb][:, 32:33], G[:, 0:1], reads=['G'], writes=[('gct', pb)])
                    v3 = lambda ap: ap
                P.tt(POOL, t1[:, 0:n], av[:, 0:n], kx[:, 0:n], ALU.mult, reads=['av', 'kx', 'sqb'], writes=['t1'])
                P.tt(DVE, Ah[:, 0:n], t1[:, 0:n], Gi[:, 0:n], ALU.mult, reads=['t1', 'Gi'], writes=['Ah'])
                P.tt(DVE, Kh[:, 0:n], kt[:, 0:n], Gi[:, 0:n], ALU.mult, reads=['kt', 'Gi'], writes=['Kh'])
                store('ah', pb, c0, n, lambda o, ok: P.cp(POOL, o, Ah[:, 0:n], reads=['Ah'], writes=[ok]))
                store('kh', pb, c0, n, lambda o, ok: P.cp(ACT, o, Kh[:, 0:n], reads=['Kh'], writes=[ok]))
                store('bh', pb, c0, n, lambda o, ok: P.tt(DVE, o, kx[:, 0:n], Gm[:, 0:n], ALU.mult, reads=['kx', 'Gm'], writes=[ok]))
                store('rh', pb, c0, n, lambda o, ok: P.tt(POOL, o, rs[:, 0:n], G[:, 0:n], ALU.mult, reads=['sh_r', 'G'], writes=[ok]))
                store('k2', pb, c0, n, lambda o, ok: P.tt(DVE, v3(o), v3(Kh[:, 0:n]), GCb, ALU.mult, reads=['Kh', 'G'], writes=[ok]))
                store('a2', pb, c0, n, lambda o, ok: P.tt(POOL, v3(o), v3(Ah[:, 0:n]), GCb, ALU.mult, reads=['Ah', 'G'], writes=[ok]))
                store('v', pb, c0, n, lambda o, ok: P.cp(ACT, o, vs[:, 0:n], reads=['sh_v'], writes=[ok]))
        for pb in range(4):
            P.ld(cx.B_gc[pb * 128:(pb + 1) * 128, :], gct[pb][:], reads=[('gct', pb)], writes=['B_gc'], tag='sc')
        for j, col in ((0, T - 1), (1, T)):
            P.ld(cx.o_bshift[l, j, :, 0:12], cx.projT[R_BR:R_BR + 1536, col:col + 1].rearrange("(b p) o -> p (b o)", p=128), writes=[('obs', j)], tag='so', slow=True)
            P.ld(cx.o_bshift[l, j, :, 12:13], cx.projT[R_SM:R_SM + 128, col:col + 1], writes=[('obs2', j)], tag='so', slow=True)
        P.emit()


def phase_scan(nc, cx, l, br, chunks=range(NCH)):
    P = Prog(nc)
    ps = cx.ps
    isA = br == 'A'
    SC = cx.A_sc if isA else cx.B_sc
    GC = cx.A_gc if isA else cx.B_gc
    names = dict(XA='k', XB='k', XK='k', XR='q', XBz='bz', XRz='rz', K2='k2', A2='a2', V='v') if isA else \
        dict(XA='ah', XB='bh', XK='kh', XR='rh', XBz='bh', XRz='rh', K2='k2', A2='a2', V='v')
    kinds = sorted(set(names.values()))
    with contextlib.ExitStack() as st:
        sb = lambda name, shape, dt: st.enter_context(nc.sbuf_tensor('s%s%d_' % (br, l) + name, shape, dt))
        cb = sb('cb', [128, 128 + 3 * 128], BF16)
        identb = cb[:, 0:128]
        cf = sb('cf', [128, 4 * 4 * 128], F32)
        ident4 = cf[:, 0:512].rearrange("p (h t) -> p h t", h=4)
        k01 = [cf[:, 512 * (1 + i):512 * (2 + i)].rearrange("p (h t) -> p h t", h=4) for i in range(3)]
        pp = sb('pp', [128, NPP], F32)
        P.dma(SP, lambda e: e.dma_start(out=cb[:], in_=cx.cb3), writes=['cb'], tag='c')
        P.dma(SP, lambda e: e.dma_start(out=cf[:], in_=cx.cf3), writes=['cf'], tag='c')
        P.dma(SP, lambda e: e.dma_start(out=pp[:], in_=cx.pp[l]), writes=['pp'], tag='c')
        op_t = {k: [sb('x_%s%d' % (k, i), [64, 8, 128], BF16) for i in range(2)] for k in kinds}
        tok = {k: [sb('t_%s%d' % (k, i), [128, 8, 64], BF16) for i in range(2)] for k in ('k2', 'a2', 'v')}
        er = [sb('er%d' % i, [2, 4, 8, 128], F32) for i in range(2)] if isA else None
        E = [[sb('E%d_%d' % (k, g), [128, 4, 128], F32) for g in range(2)] for k in range(5)] if isA else None
        NDT = F32
        Lb = [[sb('Lb%d_%d' % (g, i), [128, 4, 128], NDT) for i in range(2)] for g in range(2)]
        Uf = [sb('Uf%d' % g, [128, 4, 128], F32) for g in range(2)]
        Ub = [[Uf[g] if i == 0 else sb('Ub%d_%d' % (g, i), [128, 4, 128], NDT) for i in range(3)] for g in range(2)]
        Pf = [sb('Pf%d' % g, [128, 4, 128], F32) for g in range(2)]
        Pb = [sb('Pb%d' % g, [128, 4, 128], BF16) for g in range(2)]
        Mkb, RKb, RAb = ([sb('%s%d' % (n, g), [128, 4, 128], BF16) for g in range(2)] for n in ('Mkb', 'RKb', 'RAb'))
        Xb, Pnb = ([sb('%s%d' % (n, g), [128, 4, 64], BF16) for g in range(2)] for n in ('Xb', 'Pnb'))
        Zf, Zb = sb('Zf', [64, 8, 64], F32), sb('Zb', [64, 8, 64], BF16)
        Zt = sb('Ztmp', [64, 8, 64], F32)
        gct = sb('gct', [64, 8, NCH], F32)
        oT = [sb('oT%d' % i, [64, 8, 128], F32) for i in range(2)]
        zt = [sb('zt%d' % i, [64, 8, 128], F32) for i in range(2)]
        w1, w2 = sb('w1', [64, 8, 128], F32), sb('w2', [64, 8, 128], F32)
        wb = sb('wb', [64, 8, 128], BF16)
        gob = [sb('gob%d' % i, [64, 8, 128], BF16) for i in range(2)]
        ones64b = sb('ones64b', [64, 64], BF16)
        ones64f = sb('ones64f', [64, 64], F32)
        P.op(POOL, lambda e: e.memset(ones64b[:], 1.0), writes=['ones64b'])
        P.op(POOL, lambda e: e.memset(ones64f[:], 1.0), writes=['ones64f'])
        P.dma(SP, lambda e: e.dma_start(out=gct[:], in_=GC.rearrange("(h d) c -> d h c", d=64)), writes=['gct'], tag='c')
        if not isA:
            bon = [sb('bon%d' % i, [64, 8, 128], F32) for i in range(2)]
            gng, gnb = sb('gng', [64, 8], F32), sb('gnb', [64, 8], F32)
            P.dma(SP, lambda e: e.dma_start(out=gng[:], in_=cx.gn_g[l].rearrange("(h d) -> d h", d=64), allow_slow_non_contiguous=True), writes=['gng'], tag='c')
            P.dma(SP, lambda e: e.dma_start(out=gnb[:], in_=cx.gn_b[l].rearrange("(h d) -> d h", d=64), allow_slow_non_contiguous=True), writes=['gnb'], tag='c')
        P.op(POOL, lambda e: e.memset(Zf[:], 0.0), writes=['Zf'])
        P.op(POOL, lambda e: e.memset(Zb[:], 0.0), writes=['Zb'])
        zrow = R_AZ if isA else R_BZ
        grow = 0 if isA else 512
        st_in = cx.sA_in if isA else cx.sB_in
        st_out = cx.o_AS if isA else cx.o_BS
        flat = lambda ap: ap.rearrange("p h t -> p (h t)")
        b4 = lambda ap: ap.rearrange("p (h t) -> p h t", h=4)
        for c in chunks:
            b = c % 2
            c0 = c * 128
            n = 128 if c < 32 else 1
            slow = (n == 1)
            for k in kinds:
                if n == 1:
                    P.ms(POOL, op_t[k][b][:], 0.0, writes=[('x', k, b)])
                P.ld(op_t[k][b][:, :, 0:n], SC[k][:, c0:c0 + n].rearrange("(h d) t -> d h t", d=64), writes=[('x', k, b)], tag='ld', slow=slow)
            if n == 1:
                P.ms(POOL, zt[b][:], 0.0, writes=[('zt', b)])
            P.ld(zt[b][:, :, 0:n], cx.projT[zrow:zrow + 512, c0:c0 + n].rearrange("(h d) t -> d h t", d=64), writes=[('zt', b)], tag='ld', slow=slow)
            if not isA:
                if n == 1:
                    P.ms(POOL, bon[b][:], 0.0, writes=[('bon', b)])
                P.ld(bon[b][:, :, 0:n], cx.B_bonus[:, c0:c0 + n].rearrange("(h d) t -> d h t", d=64), writes=[('bon', b)], tag='ld', slow=slow)
            if isA:
                for kd in range(4):
                    P.ld(er[b][:, kd], cx.A_erows[kd, :, :, c0:c0 + 128], writes=[('er', b)], tag='ld')
            if c == 32:
                P.ld(st_out[l, 0], Zf[:], reads=['Zf'], writes=['st_out0'], tag='so')
                P.ld(Zf[:], st_in[l], reads=['st_out0'], writes=['Zf'], tag='ld')
                P.cp(ACT, Zb[:], Zf[:], reads=['Zf'], writes=['Zb'])
            X = {role: op_t[k][b] for role, k in names.items()}
            xk = {role: ('x', k, b) for role, k in names.items()}
            for j, k in enumerate(('k2', 'a2', 'v')):
                bank = 5 + j % 2
                pst = ps[:, bank, :].bitcast(BF16)[:, 0:512].rearrange("p (h d) -> p h d", d=64)
                for h in range(8):
                    P.tr(pst[:, h, :], op_t[k][b][:, h, :], identb[0:64, 0:64], reads=[('x', k, b), 'cb'], writes=[('ps', bank)])
                P.cp(ACT if j != 1 else DVE, tok[k][b][:], pst, reads=[('ps', bank)], writes=[('tok', k, b)])
            K2t, A2t, Vt = tok['k2'][b], tok['a2'][b], tok['v'][b]
            tkk = [('tok', 'k2', b), ('tok', 'a2', b), ('tok', 'v', b)]
            for g in range(2):
                hs = list(range(4 * g, 4 * g + 4))
                if isA:
                    combos = ((0, 1, 1), (1, 0, 2), (2, 0, 2), (2, 3, 3), (1, 3, 3))
                    for k5, (ka, kb_, mi) in enumerate(combos):
                        bank = k5 % 2
                        for hh, h in enumerate(hs):
                            o_ = ps[:, bank, hh * 128:(hh + 1) * 128]
                            P.mm(o_, er[b][0:2, ka, h, :], er[b][0:2, kb_, h, :], True, False, reads=[('er', b)], writes=[('ps', bank)])
                            P.mm(o_, identb, cb[:, 128 * mi:128 * (mi + 1)], False, True, reads=['cb'], writes=[('ps', bank)])
                        P.act(E[k5][g][:], b4(ps[:, bank, :]), AF.Exp, reads=[('ps', bank)], writes=[('E', k5, g)])
                    mult = [E[k5][g][:] for k5 in range(5)]
                    mkeys = [('E', k5, g) for k5 in range(5)]
                else:
                    mult = [k01[0], k01[1], k01[1], k01[2], k01[2]]
                    mkeys = ['cf'] * 5
                specs = (('XB', 'XA'), ('XA', 'XB'), ('XK', 'XB'), ('XK', 'XR'), ('XA', 'XR'))
                dsts = ((Lb[g][0], ('Lb', g, 0)), (Uf[g], ('Uf', g)), (Mkb[g], ('Mkb', g)), (RKb[g], ('RKb', g)), (RAb[g], ('RAb', g)))
                for k5, ((ra, rb), (dst, dkey)) in enumerate(zip(specs, dsts)):
                    bank = 2 + k5 % 3
                    for hh, h in enumerate(hs):
                        P.mm(ps[:, bank, hh * 128:(hh + 1) * 128], X[ra][:, h, :], X[rb][:, h, :], reads=[xk[ra], xk[rb]], writes=[('ps', bank)])
                    P.tt(DVE, dst[:], b4(ps[:, bank, :]), mult[k5], ALU.mult, reads=[('ps', bank), mkeys[k5]], writes=[dkey])
                P.tt(POOL, Pf[g][:], ident4, Uf[g][:], ALU.subtract, reads=['cf', ('Uf', g)], writes=[('Pf', g)])
            ukey = lambda g, i: ('Uf', g) if i == 0 else ('Ub', g, i)
            for lvl in range(1, 7):
                i0, i1 = (lvl - 1) % 2, lvl % 2
                u0i, u1i = (0 if lvl == 1 else 1 + lvl % 2), 1 + (lvl + 1) % 2
                for g in range(2):
                    for hh in range(4):
                        P.mm(ps[:, g, hh * 128:(hh + 1) * 128], Ub[g][u0i][:, hh, :], Lb[g][i0][:, hh, :], reads=[ukey(g, u0i), ('Lb', g, i0)], writes=[('ps', g)])
                    if lvl < 6:
                        for hh in range(4):
                            P.mm(ps[:, 2 + g, hh * 128:(hh + 1) * 128], Lb[g][i0][:, hh, :], Ub[g][u0i][:, hh, :], reads=[ukey(g, u0i), ('Lb', g, i0)], writes=[('ps', 2 + g)])
                    P.cp(ACT, Lb[g][i1][:], b4(ps[:, g, :]), reads=[('ps', g)], writes=[('Lb', g, i1)])
                    if lvl < 6:
                        P.cp(ACT, Ub[g][u1i][:], b4(ps[:, 2 + g, :]), reads=[('ps', 2 + g)], writes=[ukey(g, u1i)])
                for g in range(2):
                    for hh in range(4):
                        P.mm(ps[:, 4 + g, hh * 128:(hh + 1) * 128], Lb[g][i1][:, hh, :], Pf[g][:, hh, :], reads=[('Lb', g, i1), ('Pf', g)], writes=[('ps', 4 + g)])
                    P.tt(DVE, Pf[g][:], Pf[g][:], b4(ps[:, 4 + g, :]), ALU.add, reads=[('ps', 4 + g), ('Pf', g)], writes=[('Pf', g)])
            for g in range(2):
                P.cp(POOL, Pb[g][:], Pf[g][:], reads=[('Pf', g)], writes=[('Pb', g)])
            for g in range(2):
                hs = list(range(4 * g, 4 * g + 4))
                gs = slice(4 * g, 4 * g + 4)
                bx = 6 + g
                psX = ps[:, bx, 0:256].rearrange("p (h v) -> p h v", v=64)
                for hh, h in enumerate(hs):
                    P.mm(psX[:, hh, :], X['XBz'][:, h, :], Zb[:, h, :], True, False, reads=[xk['XBz'], 'Zb'], writes=[('ps', bx)])
                    P.mm(psX[:, hh, :], Mkb[g][:, hh, :], Vt[:, h, :], False, True, reads=[('Mkb', g), tkk[2]], writes=[('ps', bx)])
                P.cp(ACT, Xb[g][:], psX, reads=[('ps', bx)], writes=[('Xb', g)])
                psP = ps[:, bx, 256:512].rearrange("p (h v) -> p h v", v=64)
                for hh in range(4):
                    P.mm(psP[:, hh, :], Pb[g][:, hh, :], Xb[g][:, hh, :], reads=[('Pb', g), ('Xb', g)], writes=[('ps', bx)])
                P.act(Pnb[g][:], psP, AF.Copy, reads=[('ps', bx)], writes=[('Pnb', g)], scale=-1.0)
                bo = g
                psO = b4(ps[0:64, bo, :])
                for hh, h in enumerate(hs):
                    P.mm(psO[:, hh, :], Zb[:, h, :], X['XRz'][:, h, :], True, False, reads=['Zb', xk['XRz']], writes=[('ps', bo)])
                    P.mm(psO[:, hh, :], Vt[:, h, :], RKb[g][:, hh, :], False, False, reads=[tkk[2], ('RKb', g)], writes=[('ps', bo)])
                    P.mm(psO[:, hh, :], Pnb[g][:, hh, :], RAb[g][:, hh, :], False, True, reads=[('Pnb', g), ('RAb', g)], writes=[('ps', bo)])
                P.cp(DVE, oT[b][:, gs, :], psO, reads=[('ps', bo)], writes=[('oT', b)])
                bz = 2 + g
                psZ = ps[0:64, bz, 0:256].rearrange("p (h v) -> p h v", v=64)
                for hh, h in enumerate(hs):
                    P.mm(psZ[:, hh, :], K2t[:, h, :], Vt[:, h, :], True, False, reads=[tkk[0], tkk[2]], writes=[('ps', bz)])
                    P.mm(psZ[:, hh, :], A2t[:, h, :], Pnb[g][:, hh, :], False, True, reads=[tkk[1], ('Pnb', g)], writes=[('ps', bz)])
                P.tt(DVE, Zt[:, gs, :], Zf[:, gs, :], gct[:, gs, c:c + 1].to_broadcast([64, 4, 64]), ALU.mult, reads=['Zf', 'gct'], writes=['Zt'])
                P.tt(DVE, Zf[:, gs, :], Zt[:, gs, :], psZ, ALU.add, reads=['Zt', ('ps', bz)], writes=['Zf'])
            P.cp(ACT, Zb[:], Zf[:], reads=['Zf'], writes=['Zb'])
            o = oT[b]
            P.act(zt[b][:], zt[b][:], AF.Silu, reads=[('zt', b)], writes=[('zt', b)])
            if isA:
                P.act(wb[:], o[:], AF.Square, reads=[('oT', b)], writes=['wb'])
                for j in range(2):
                    js = slice(j * 512, (j + 1) * 512)
                    P.mm(ps[0:64, 4 + j, :], ones64b[:], flat(wb[:])[:, js], reads=['wb', 'ones64b'], writes=[('ps', 4 + j)])
                    P.act(flat(w1[:])[:, js], ps[0:64, 4 + j, :], AF.Ln, reads=[('ps', 4 + j)], writes=['w1'], scale=1.0 / 64, bias=1e-6)
                P.act(w1[:], w1[:], AF.Exp, reads=['w1'], writes=['w1'], scale=-0.5)
                P.stt(w2[:], o[:], pp[0:64, PP_NORMA:PP_NORMA + 1], w1[:], ALU.mult, ALU.mult, reads=[('oT', b), 'pp', 'w1'], writes=['w2'])
            else:
                for j in range(2):
                    js = slice(j * 512, (j + 1) * 512)
                    P.mm(ps[0:64, 4 + j, :], ones64f[:], flat(o[:])[:, js], reads=[('oT', b), 'ones64f'], writes=[('ps', 4 + j)])
                    P.stt(flat(w2[:])[:, js], ps[0:64, 4 + j, :], -1.0 / 64, flat(o[:])[:, js], ALU.mult, ALU.add, reads=[('ps', 4 + j), ('oT', b)], writes=['w2'])
                P.act(w1[:], w2[:], AF.Square, reads=['w2'], writes=['w1'])
                for j in range(2):
                    js = slice(j * 512, (j + 1) * 512)
                    P.mm(ps[0:64, 4 + j, :], ones64f[:], flat(w1[:])[:, js], reads=['w1', 'ones64f'], writes=[('ps', 4 + j)])
                    P.act(flat(w1[:])[:, js], ps[0:64, 4 + j, :], AF.Ln, reads=[('ps', 4 + j)], writes=['w1'], scale=1.0 / 64, bias=64e-5)
                P.act(w1[:], w1[:], AF.Exp, reads=['w1'], writes=['w1'], scale=-0.5)
                P.tt(DVE, w2[:], w2[:], w1[:], ALU.mult, reads=['w2', 'w1'], writes=['w2'])
                P.tt(POOL, w2[:], w2[:], gng[:].unsqueeze(2).to_broadcast([64, 8, 128]), ALU.mult, reads=['w2', 'gng'], writes=['w2'])
                P.tt(POOL, w2[:], w2[:], gnb[:].unsqueeze(2).to_broadcast([64, 8, 128]), ALU.add, reads=['w2', 'gnb'], writes=['w2'])
                P.tt(POOL, w2[:], w2[:], bon[b][:], ALU.add, reads=['w2', ('bon', b)], writes=['w2'])
            P.tt(DVE, gob[b][:], w2[:], zt[b][:], ALU.mult, reads=['w2', ('zt', b)], writes=[('gob', b)])
            P.ld(cx.gatedT[grow:grow + 512, c0:c0 + n].rearrange("(h d) t -> d h t", d=64), gob[b][:, :, 0:n], reads=[('gob', b)], writes=['gatedT'], tag='go', slow=slow)
        if NCH - 1 in chunks:
            P.dma(SP, lambda e: e.dma_start(out=st_out[l, 1], in_=Zf[:]), reads=['Zf'], writes=['st_out1'], tag='so')
        P.emit()


def phase_mixCs(nc, cx, l):
    P = Prog(nc)
    ps = cx.ps
    NPG = 128
    with contextlib.ExitStack() as st:
        sb = lambda name, shape, dt: st.enter_context(nc.sbuf_tensor('cs%d_' % l + name, shape, dt))
        cf = sb('cf', [128, 384], F32)
        identf, tri_gt = cf[:, 0:128], cf[:, 256:384]
        bm = sb('bm', [8, 512], F32)
        idx = sb('idx', [128, 1], I32)
        rows = sb('rows', [128, 2056], F32)
        qb, kb, vb, fb = rows[:, 0:512], rows[:, 512:1024], rows[:, 1024:1536], rows[:, 2048:2056]
        negb = sb('negb', [128, 8], F32)
        ones = sb('ones', [128, 128], F32)
        lfp, Cw, sc, pr = (sb(n, [128, 128, 8], F32) for n in ('lfp', 'Cw', 'sc', 'pr'))
        kch = [sb('kch%d' % i, [128, 16, 512], F32) for i in range(2)]
        lfn, tot, base, mloc, snew, pnew, den, rden, mxb = (sb(n, [128, 8], F32) for n in ('lfn', 'tot', 'base', 'mloc', 'snew', 'pnew', 'den', 'rden', 'mxb'))
        tmp512 = sb('tmp512', [128, 512], F32)
        mx8, D8 = sb('mx8', [8, 1], F32), sb('D8', [8, 8], F32)
        accs = sb('accs', [8, 512], F32)
        oc, zc = sb('oc', [128, 4], F32), sb('zc', [128, 4], F32)
        gb = sb('gb', [128, 4], BF16)
        P.ld(cf[:], cx.cf32s, writes=['cf'], tag='c')
        P.ld(bm[:], cx.bmask, writes=['bm'], tag='c')
        P.ld(idx[:], cx.ptab, writes=['idx'], tag='c')
        P.ld(negb[:], cx.b_f[l:l + 1, :].partition_broadcast(128), writes=['negb'], tag='c')
        P.ts(POOL, negb[:], negb[:], -1.0, ALU.mult, reads=['negb'], writes=['negb'])
        P.ms(POOL, ones[:], 1.0, writes=['ones'])
        for j, r0 in enumerate((R_CQ, R_CK, R_CV, R_CZ)):
            P.ld(cx.srow[0:1, j * 512:(j + 1) * 512].rearrange("o c -> c o"), cx.projT[r0:r0 + 512, T:T + 1], writes=[('srow', j)], tag='sr', slow=True)
        P.ld(cx.srow[0:1, 2048:2056].rearrange("o c -> c o"), cx.projT[R_SM + SM_F:R_SM + SM_F + 8, T:T + 1], writes=[('srow', 4)], tag='sr', slow=True)
        P.ld(rows[:], cx.srow[0:1, :].partition_broadcast(128), reads=[('srow', j) for j in range(5)], writes=['rows'], tag='c')
        P.tt(DVE, lfn[:], fb, negb[:], ALU.subtract, reads=['rows', 'negb'], writes=['lfn'])
        P.act(lfn[:], lfn[:], AF.Exp, reads=['lfn'], writes=['lfn'], scale=-1.0)
        P.act(lfn[:], lfn[:], AF.Ln, reads=['lfn'], writes=['lfn'], bias=1.0)
        P.ts(DVE, lfn[:], lfn[:], -1.0, ALU.mult, reads=['lfn'], writes=['lfn'])
        P.ld(cx.o_clf[l, :, T:T + 1].rearrange("h o -> o h"), lfn[0:1, :], reads=['lfn'], writes=['o_clf_s'], tag='so', slow=True)
        ioff = bass.IndirectOffsetOnAxis(ap=idx[:, 0:1], axis=0)
        P.dma(POOL, lambda e: e.indirect_dma_start(out=lfp[:].rearrange("p t h -> p (t h)"), out_offset=None,
                                                   in_=cx.cache_lf.rearrange("l n t h -> (l n) (t h)"), in_offset=ioff,
                                                   element_offset=l * 1280 * 1024), reads=['idx'], writes=['lfp'], tag='ig')
        for h in range(8):
            P.op(DVE, lambda e, h=h: e.tensor_tensor_scan(out=Cw[:, :, h], data0=ones[:, :], data1=lfp[:, :, h], initial=0.0, op0=ALU.mult, op1=ALU.add),
                 reads=['lfp', 'ones'], writes=['Cw'])
        P.cp(DVE, tot[:], Cw[:, 127, :], reads=['Cw'], writes=['tot'])
        P.mm(ps[:, 0, 0:8], tri_gt, tot[:], reads=['cf', 'tot'], writes=[('ps', 0)])
        P.tt(DVE, base[:], tot[:], ps[:, 0, 0:8], ALU.add, reads=['tot', ('ps', 0)], writes=['base'])
        P.tt(DVE, base[:], base[:], lfn[:], ALU.add, reads=['base', 'lfn'], writes=['base'])
        P.tt(DVE, Cw[:], base[:].unsqueeze(1).to_broadcast([128, 128, 8]), Cw[:], ALU.subtract, reads=['base', 'Cw'], writes=['Cw'])
        ck = cx.cache_k.rearrange("l n t h d -> (l n) (t h d)")
        cv = cx.cache_v.rearrange("l n t h d -> (l n) (t h d)")
        ng = [0]

        def gather(src, tc):
            b = ng[0] % 2
            ng[0] += 1
            P.dma(POOL, lambda e: e.indirect_dma_start(out=kch[b][:].rearrange("p t c -> p (t c)"), out_offset=None, in_=src, in_offset=ioff,
                                                       element_offset=l * 1280 * 65536 + tc * 8192),
                  reads=['idx'], writes=[('kch', b)], tag='ig')
            return b
        for tc in range(8):
            b = gather(ck, tc)
            eng = DVE if tc % 2 == 0 else POOL
            P.tt(eng, kch[b][:], kch[b][:], qb.unsqueeze(1).to_broadcast([128, 16, 512]), ALU.mult, reads=[('kch', b), 'rows'], writes=[('kch', b)])
            P.op(DVE, lambda e, b=b, tc=tc: e.tensor_reduce(out=sc[:, tc * 16:(tc + 1) * 16, :].rearrange("p t h -> p (t h)"),
                                                           in_=kch[b][:].rearrange("p t (h d) -> p (t h) d", d=64), axis=AX.X, op=ALU.add),
                 reads=[('kch', b)], writes=['sc'])
        P.stt(sc[:], sc[:], 0.125, Cw[:], ALU.mult, ALU.add, reads=['sc', 'Cw'], writes=['sc'])
        P.tt(DVE, tmp512[:], qb, kb, ALU.mult, reads=['rows'], writes=['tmp512'])
        P.op(DVE, lambda e: e.tensor_reduce(out=snew[:], in_=tmp512[:].rearrange("p (h d) -> p h d", d=64), axis=AX.X, op=ALU.add), reads=['tmp512'], writes=['snew'])
        P.ts(DVE, snew[:], snew[:], 0.125, ALU.mult, reads=['snew'], writes=['snew'])
        P.op(DVE, lambda e: e.tensor_reduce(out=mloc[:], in_=sc[:].rearrange("p t h -> p h t"), axis=AX.X, op=ALU.max), reads=['sc'], writes=['mloc'])
        P.tt(DVE, mloc[:], mloc[:], snew[:], ALU.max, reads=['mloc', 'snew'], writes=['mloc'])
        P.tr(ps[0:8, 1, 0:128], mloc[:], identf, reads=['mloc', 'cf'], writes=[('ps', 1)])
        P.op(DVE, lambda e: e.tensor_reduce(out=mx8[:], in_=ps[0:8, 1, 0:128], axis=AX.X, op=ALU.max), reads=[('ps', 1)], writes=['mx8'])
        P.ts(DVE, D8[:], identf[0:8, 0:8], mx8[:, 0:1], ALU.mult, reads=['cf', 'mx8'], writes=['D8'])
        P.mm(ps[:, 2, 0:8], ones[0:8, :], D8[:], reads=['ones', 'D8'], writes=[('ps', 2)])
        P.cp(DVE, mxb[:], ps[:, 2, 0:8], reads=[('ps', 2)], writes=['mxb'])
        P.tt(DVE, sc[:], sc[:], mxb[:].unsqueeze(1).to_broadcast([128, 128, 8]), ALU.subtract, reads=['sc', 'mxb'], writes=['sc'])
        P.act(pr[:], sc[:], AF.Exp, reads=['sc'], writes=['pr'])
        P.tt(DVE, pnew[:], snew[:], mxb[:], ALU.subtract, reads=['snew', 'mxb'], writes=['pnew'])
        P.act(pnew[:], pnew[:], AF.Exp, reads=['pnew'], writes=['pnew'])
        P.op(DVE, lambda e: e.tensor_reduce(out=den[:], in_=pr[:].rearrange("p t h -> p h t"), axis=AX.X, op=ALU.add), reads=['pr'], writes=['den'])
        P.mm(ps[:, 3, 0:8], ones[:, :], den[:], reads=['ones', 'den'], writes=[('ps', 3)])
        P.tt(DVE, den[:], ps[:, 3, 0:8], pnew[:], ALU.add, reads=[('ps', 3), 'pnew', 'den'], writes=['den'])
        P.op(DVE, lambda e: e.reciprocal(out=rden[:], in_=den[:]), reads=['den'], writes=['rden'])
        P.tt(DVE, pr[:], pr[:], rden[:].unsqueeze(1).to_broadcast([128, 128, 8]), ALU.mult, reads=['pr', 'rden'], writes=['pr'])
        P.tt(DVE, pnew[:], pnew[:], rden[:], ALU.mult, reads=['pnew', 'rden'], writes=['pnew'])
        first = True
        for tc in range(8):
            b = gather(cv, tc)
            for t in range(16):
                P.mm(ps[0:8, 4, :], pr[:, tc * 16 + t, :], kch[b][:, t, :], first, False, reads=['pr', ('kch', b)], writes=[('ps', 4)])
                first = False
        P.mm(ps[0:8, 4, :], pnew[0:1, :], vb[0:1, :], False, True, reads=['pnew', 'rows'], writes=[('ps', 4)])
        P.tt(DVE, accs[:], ps[0:8, 4, :], bm[:], ALU.mult, reads=[('ps', 4), 'bm'], writes=['accs'])
        for j in range(4):
            P.mm(ps[:, 5, j:j + 1], accs[:, j * 128:(j + 1) * 128], ones[0:8, 0:1], reads=['accs', 'ones'], writes=[('ps', 5)])
        P.cp(DVE, oc[:], ps[:, 5, 0:4], reads=[('ps', 5)], writes=['oc'])
        P.ld(zc[:], cx.projT[R_CZ:R_CZ + 512, T:T + 1].rearrange("(j p) o -> p (j o)", p=128), writes=['zc'], tag='c', slow=True)
        P.act(zc[:], zc[:], AF.Silu, reads=['zc'], writes=['zc'])
        P.tt(DVE, gb[:], oc[:], zc[:], ALU.mult, reads=['oc', 'zc'], writes=['gb'])
        P.ld(cx.gatedT[1024:1536, T:T + 1].rearrange("(j p) o -> p (j o)", p=128), gb[:], reads=['gb'], writes=['gatedT'], tag='go', slow=True)
        P.emit()


def build(debug=None, phases=None, ext_in=()):
    nc = bass.Bass("TRN2", target_bir_lowering=False)
    cx = Ctx()
    di = lambda name, shape, dt=F32: nc.dram_tensor(name, shape, dt, kind="ExternalInput").ap()
    do = lambda name, shape, dt=F32: nc.dram_tensor(name, shape, dt, kind="ExternalOutput").ap()
    dbg = debug is not None

    def scratch(name, shape, dt=F32):
        if name in ext_in:
            return di(name, shape, dt)
        return do(name, shape, dt) if dbg else nc.dram_tensor(name, shape, dt).ap()
    cx.xT = di("xT", [D, TS])
    cx.xtok = di("xtok", [TS, D])
    cx.w_in = di("w_in", [DEPTH, D, NROW])
    cx.w_out = di("w_out", [DEPTH, D, D])
    cx.ln_g, cx.ln_b = di("ln_g", [DEPTH, D]), di("ln_b", [DEPTH, D])
    cx.identb = di("identb", [128, 128], BF16)
    cx.pp = di("pp", [DEPTH, 128, NPP])
    cx.pool_w = di("pool_w", [DEPTH, 4, 128, 128])
    cx.dbufT = di("dbufT", [DEPTH, 512, 15])
    cx.o_dbuf = do("o_dbuf", [DEPTH, 2, 512, 15])
    cx.cf32 = di("cf32", [128, 256])
    cx.cb16 = di("cb16", [128, 128 + 4 * 512], BF16)
    cx.b_f = di("b_f", [DEPTH, 8])
    cx.o_clf = do("o_clf", [DEPTH, 8, TS])
    cx.o_ckv = do("o_ckv", [DEPTH, 1024, TS])
    cx.cscr = scratch("cscr", [6, T], BF16)
    cx.cb2 = di("cb2", [128, 256], BF16)
    cx.cb3 = di("cb3", [128, 512], BF16)
    cx.cf3 = di("cf3", [128, 2048])
    cx.sel8 = di("sel8", [8, 512])
    cx.aconvT = di("aconvT", [DEPTH, 1536, 3])
    cx.sA_in = di("sA_in", [DEPTH, 64, 8, 64])
    cx.sB_in = di("sB_in", [DEPTH, 64, 8, 64])
    cx.gn_g, cx.gn_b = di("gn_g", [DEPTH, 512]), di("gn_b", [DEPTH, 512])
    cx.cf32s = di("cf32s", [128, 384])
    cx.bmask = di("bmask", [8, 512])
    cx.ptab = di("ptab", [128, 1], I32)
    cx.cache_k = di("cache_k", [DEPTH, 1280, 128, 8, 64])
    cx.cache_v = di("cache_v", [DEPTH, 1280, 128, 8, 64])
    cx.cache_lf = di("cache_lf", [DEPTH, 1280, 128, 8])
    cx.srow = scratch("srow", [1, 2056])
    cx.lora_w = di("lora_w", [DEPTH, 128, 512])
    cx.bshiftT = di("bshiftT", [DEPTH, 128, 13])
    cx.o_bshift = do("o_bshift", [DEPTH, 2, 128, 13])
    cx.o_aconv = do("o_aconv", [DEPTH, 2, 1536, 3])
    cx.o_AS = do("o_AS", [DEPTH, 2, 64, 8, 64])
    cx.o_BS = do("o_BS", [DEPTH, 2, 64, 8, 64])
    cx.A_sc = {k: scratch("A_" + k, [512, TP], BF16) for k in A_KINDS}
    cx.B_sc = {k: scratch("B_" + k, [512, TP], BF16) for k in B_KINDS}
    cx.A_erows = scratch("A_erows", [4, 2, 8, TP])
    cx.A_gc, cx.B_gc = scratch("A_gc", [512, NCH]), scratch("B_gc", [512, NCH])
    cx.A_mrows = scratch("A_mrows", [5, 8, TP])
    cx.B_bonus = scratch("B_bonus", [512, TP])
    cx.y = do("y", [TS, D])
    cx.projT = scratch("projT", [NROW, TS])
    cx.gatedT = scratch("gatedT", [D, TS], BF16)
    cx.y1 = scratch("y1", [TS, D])
    cx.x2T = scratch("x2T", [D, TS], BF16)
    cx.ps = nc.alloc_psum_tensor("ps", [128, 8, 512], F32)
    if phases is None:
        phases = [('in', 0), ('out', 0), ('in', 1), ('out', 1)]
    for ph, l in phases:
        if ph == 'in':
            phase_inproj(nc, cx, l)
        elif ph == 'out':
            phase_outproj(nc, cx, l)
        elif ph == 'mixD':
            phase_mixD(nc, cx, l)
        elif ph == 'mixC':
            phase_mixC(nc, cx, l)
        elif ph == 'preA':
            phase_rowsA(nc, cx, l)
            phase_preA(nc, cx, l)
        elif ph == 'preB':
            phase_preB(nc, cx, l)
        elif ph == 'scanA':
            phase_scan(nc, cx, l, 'A')
        elif ph == 'scanA_few':
            phase_scan(nc, cx, l, 'A', chunks=[0, 1, 32])
        elif ph == 'scanB_few':
            phase_scan(nc, cx, l, 'B', chunks=[0, 1, 32])
        elif ph == 'scanB':
            phase_scan(nc, cx, l, 'B')
        elif ph == 'mixCs':
            phase_mixCs(nc, cx, l)
        elif ph == 'mixC1':
            phase_mixC(nc, cx, l, heads=[0], do_sample=False)
    return nc


def _cf32():
    a = np.zeros((128, 256), np.float32)
    a[:, 0:128] = np.eye(128)
    a[:, 128:256] = np.triu(np.ones((128, 128)), 1)
    return a


def _cb16():
    import ml_dtypes
    a = np.zeros((128, 128 + 4 * 512), np.float32)
    a[:, 0:128] = np.eye(128)
    k, q = np.arange(128)[:, None], np.arange(512)[None, :]
    for j in range(4):
        a[:, 128 + j * 512:128 + (j + 1) * 512] = np.where(j * 128 + k > q, -30000.0, 0.0)
    return a.astype(ml_dtypes.bfloat16)


def _cb2():
    import ml_dtypes
    a = np.zeros((128, 256), np.float32)
    a[:, 0:128] = np.eye(128)
    a[:, 128:256] = np.kron(np.eye(2), np.ones((64, 64)))
    return a.astype(ml_dtypes.bfloat16)


def _cb3():
    import ml_dtypes
    p, f = np.arange(128)[:, None], np.arange(128)[None, :]
    a = np.zeros((128, 512), np.float32)
    a[:, 0:128] = np.eye(128)
    a[:, 128:256] = np.where(f >= p, -30000.0, 0.0)
    a[:, 256:384] = np.where(p >= f, -30000.0, 0.0)
    a[:, 384:512] = np.where(p > f, -30000.0, 0.0)
    return a.astype(ml_dtypes.bfloat16)


def _cf3():
    p, f = np.arange(128)[:, None], np.arange(128)[None, :]
    mats = [np.eye(128), (f < p) * 1.0, (p < f) * 1.0, (p <= f) * 1.0]
    return np.concatenate([np.tile(m, (1, 4)) for m in mats], axis=1).astype(np.float32)


def _lora_w(I):
    a = np.zeros((DEPTH, 128, 512), np.float32)
    a[:, 32:64] = I['w_up_B']
    a[:, 64:96] = I['a_up_B']
    return a


def _bshiftT(I, c):
    sh = I['state_B_shift'][:, c]
    a = np.zeros((DEPTH, 128, 13), np.float32)
    a[:, :, 0:12] = sh[:, 0:1536].reshape(DEPTH, 12, 128).transpose(0, 2, 1)
    a[:, 32:64, 12] = sh[:, 1536:1568]
    a[:, 64:96, 12] = sh[:, 1568:1600]
    return a


def host_inputs(c, I):
    import ml_dtypes
    n = c % 2
    perm = _col_perm()
    w = I['w_in']
    w_p = np.zeros((DEPTH, D, NROW), np.float32)
    w_p[:, :, perm >= 0] = w[:, :, perm[perm >= 0]]
    xtok = np.concatenate([I['x_prompt'][n], I['x_sample'][c]], axis=0)
    return {"xT": np.ascontiguousarray(xtok.T), "xtok": np.ascontiguousarray(xtok), "w_in": w_p, "w_out": I['w_out'],
            "ln_g": I['ln_g'], "ln_b": I['ln_b'], "identb": np.eye(128, dtype=np.float32).astype(ml_dtypes.bfloat16),
            "pp": pack_pp(I), "pool_w": I['pool_w_D'], "cf32": _cf32(), "cb16": _cb16(), "b_f": I['b_f_C'],
            "cb2": _cb2(), "cb3": _cb3(), "cf3": _cf3(), "sel8": np.kron(np.eye(8, dtype=np.float32), np.ones((1, 64), np.float32)),
            "aconvT": np.ascontiguousarray(I['state_A_conv'][:, c].transpose(0, 2, 1)),
            "sA_in": np.ascontiguousarray(I['state_A_S'][:, c].transpose(0, 2, 1, 3)),
            "sB_in": np.ascontiguousarray(I['state_B_S'][:, c].transpose(0, 3, 1, 2)),
            "cf32s": np.concatenate([_cf32(), np.tril(np.ones((128, 128), np.float32), -1)], axis=1),
            "bmask": np.kron(np.eye(8, dtype=np.float32), np.ones((1, 64), np.float32)),
            "ptab": np.ascontiguousarray(I['page_table'][c].reshape(128, 1).astype(np.int32)),
            "cache_k": I['cache_C_k'], "cache_v": I['cache_C_v'], "cache_lf": I['cache_C_logf'],
            "gn_g": I['gn_g_B'], "gn_b": I['gn_b_B'], "lora_w": _lora_w(I), "bshiftT": _bshiftT(I, c),
            "dbufT": np.ascontiguousarray(I['state_D_buf'][:, c].transpose(0, 2, 1))}


FULL_PHASES = [(ph, l) for l in range(DEPTH) for ph in ('in', 'preA', 'scanA', 'preB', 'scanB', 'mixC', 'mixCs', 'mixD', 'out')]


def _unpack_bshift(a):
    o = np.zeros(1600, np.float32)
    o[0:1536] = a[:, 0:12].T.reshape(1536)
    o[1536:1568] = a[32:64, 12]
    o[1568:1600] = a[64:96, 12]
    return o


def kernel(**I):
    I = {k: np.asarray(v) for k, v in I.items()}
    nc = build(phases=FULL_PHASES)
    shared = None
    in_maps = []
    for c in range(8):
        m = host_inputs(c, I)
        if shared is None:
            shared = m
        else:
            for k in ('w_in', 'w_out', 'ln_g', 'ln_b', 'identb', 'pp', 'pool_w', 'cf32', 'cb16', 'b_f', 'cb2', 'cb3', 'cf3', 'sel8',
                      'cf32s', 'bmask', 'cache_k', 'cache_v', 'cache_lf', 'gn_g', 'gn_b', 'lora_w'):
                m[k] = shared[k]
        in_maps.append(m)
    res = run_bass_kernel_spmd(nc, in_maps, core_ids=list(range(8)))
    R = res.results
    f = np.float32
    NB, ND = 2, 8
    y_prompt = np.stack([R[n]['y'][0:T] for n in range(NB)]).astype(f)
    y_sample = np.stack([R[c]['y'][T:T + 1] for c in range(ND)]).astype(f)

    def per(fn, j):
        cores = range(NB) if j == 0 else range(ND)
        return np.stack([np.stack([fn(R[c], l) for c in cores]) for l in range(DEPTH)]).astype(f)
    outs = {}
    for j, pre in ((0, 'p'), (1, 's')):
        outs[pre + '_A_S'] = per(lambda r, l: r['o_AS'][l, j].transpose(1, 0, 2), j)
        outs[pre + '_A_conv'] = per(lambda r, l: r['o_aconv'][l, j].T, j)
        outs[pre + '_B_S'] = per(lambda r, l: r['o_BS'][l, j].transpose(1, 2, 0), j)
        outs[pre + '_B_shift'] = per(lambda r, l: _unpack_bshift(r['o_bshift'][l, j]), j)
        outs[pre + '_D_buf'] = per(lambda r, l: r['o_dbuf'][l, j].T, j)
    cs = {0: slice(0, T), 1: slice(T, T + 1)}
    for j, pre in ((0, 'p'), (1, 's')):
        n = T if j == 0 else 1
        outs[pre + '_C_k'] = per(lambda r, l: r['o_ckv'][l, 0:512, cs[j]].T.reshape(n, NH, HD), j)
        outs[pre + '_C_v'] = per(lambda r, l: r['o_ckv'][l, 512:1024, cs[j]].T.reshape(n, NH, HD), j)
        outs[pre + '_C_logf'] = per(lambda r, l: r['o_clf'][l, :, cs[j]].T, j)
    order = ['p_A_S', 'p_A_conv', 'p_B_S', 'p_B_shift', 'p_C_k', 'p_C_v', 'p_C_logf', 'p_D_buf',
             's_A_S', 's_A_conv', 's_B_S', 's_B_shift', 's_C_k', 's_C_v', 's_C_logf', 's_D_buf']
    return (y_prompt, y_sample) + tuple(np.ascontiguousarray(outs[k]) for k in order)
```

```python
import contextlib
import numpy as np
import concourse.bass as bass
import concourse.mybir as mybir
from concourse.bass_utils import run_bass_kernel_spmd

F32, BF16, I32 = mybir.dt.float32, mybir.dt.bfloat16, mybir.dt.int32
AF, ALU, AX = mybir.ActivationFunctionType, mybir.AluOpType, mybir.AxisListType

D, T, TS = 2048, 4096, 4097
DEPTH, NH, HD = 2, 8, 64
NBLK, NROW = 57, 57 * 128
KC = 16
R_AQ, R_AK, R_AV, R_AZ = 0, 512, 1024, 1536
R_BR, R_BK, R_BV, R_BZ = 2048, 2560, 3072, 3584
R_CQ, R_CK, R_CV, R_CZ = 4096, 4608, 5120, 5632
R_DU, R_DZ, R_SM = 6144, 6656, 7168
SM_A, SM_B, SM_F, SM_WL, SM_AL = 0, 8, 16, 32, 64


def _col_perm():
    ref = dict(qkvA=0, aA=1536, bA=1544, zA=1552, rB=2064, kB=2576, vB=3088, wl=3600, al=3632, zB=3664,
               qkvC=4176, fC=5712, zC=5720, uD=6232, zD=6744)
    p = np.full(NROW, -1, np.int64)
    p[0:1536] = ref['qkvA'] + np.arange(1536)
    p[1536:2048] = ref['zA'] + np.arange(512)
    p[2048:3584] = ref['rB'] + np.arange(1536)
    p[3584:4096] = ref['zB'] + np.arange(512)
    p[4096:5632] = ref['qkvC'] + np.arange(1536)
    p[5632:6144] = ref['zC'] + np.arange(512)
    p[6144:6656] = ref['uD'] + np.arange(512)
    p[6656:7168] = ref['zD'] + np.arange(512)
    p[R_SM + SM_A:R_SM + SM_A + 8] = ref['aA'] + np.arange(8)
    p[R_SM + SM_B:R_SM + SM_B + 8] = ref['bA'] + np.arange(8)
    p[R_SM + SM_F:R_SM + SM_F + 8] = ref['fC'] + np.arange(8)
    p[R_SM + SM_WL:R_SM + SM_WL + 32] = ref['wl'] + np.arange(32)
    p[R_SM + SM_AL:R_SM + SM_AL + 32] = ref['al'] + np.arange(32)
    return p


PE, ACT, DVE, POOL, SP = 'pe', 'act', 'dve', 'pool', 'sp'
ENGS = (PE, ACT, DVE, POOL, SP)


class _Op:
    __slots__ = ('eng', 'fn', 'deps', 'is_dma', 'tag', 'sig', 'semkey', 'semval')

    def __init__(self, eng, fn, is_dma, tag):
        self.eng, self.fn, self.is_dma, self.tag = eng, fn, is_dma, tag
        self.deps, self.sig, self.semkey, self.semval = set(), False, None, 0


class Prog:
    n_emitted = 0

    def __init__(self, nc, dma_rot=6):
        self.nc, self.dma_rot = nc, dma_rot
        self.ops = {e: [] for e in ENGS}
        self.lastw, self.readers, self.all_ops = {}, {}, []

    def op(self, eng, fn, reads=(), writes=(), dma=False, tag=None):
        o = _Op(eng, fn, dma, tag)
        wr = list(writes)
        for r in reads:
            if isinstance(r, tuple) and r[0] == 'ps':
                wr.append(r)
                continue
            w = self.lastw.get(r)
            if w is not None:
                o.deps.add(w)
        for r in wr:
            w = self.lastw.get(r)
            if w is not None:
                o.deps.add(w)
            o.deps.update(self.readers.get(r, ()))
        for r in reads:
            if not (isinstance(r, tuple) and r[0] == 'ps'):
                self.readers.setdefault(r, []).append(o)
        for r in wr:
            self.lastw[r] = o
            self.readers[r] = []
        o.deps.discard(o)
        self.ops[eng].append(o)
        self.all_ops.append(o)
        return o

    def dma(self, eng, fn, reads=(), writes=(), tag='d'):
        return self.op(eng, fn, reads, writes, dma=True, tag=tag)

    def mm(self, out, lhsT, rhs, start=True, stop=True, reads=(), writes=()):
        return self.op(PE, lambda e: e.matmul(out, lhsT=lhsT, rhs=rhs, start=start, stop=stop), reads, writes)

    def tr(self, out, in_, identity, reads=(), writes=()):
        return self.op(PE, lambda e: e.transpose(out=out, in_=in_, identity=identity), reads, writes)

    def act(self, out, in_, func, reads=(), writes=(), **kw):
        return self.op(ACT, lambda e: e.activation(out=out, in_=in_, func=func, **kw), reads, writes)

    def cp(self, eng, out, in_, reads=(), writes=()):
        if eng == ACT:
            return self.op(ACT, lambda e: e.activation(out=out, in_=in_, func=AF.Copy), reads, writes)
        return self.op(eng, lambda e: e.tensor_copy(out=out, in_=in_), reads, writes)

    def tt(self, eng, out, in0, in1, op, reads=(), writes=()):
        return self.op(eng, lambda e: e.tensor_tensor(out=out, in0=in0, in1=in1, op=op), reads, writes)

    def ts(self, eng, out, in0, s1, op0, s2=None, op1=None, reads=(), writes=()):
        if op1 is None:
            return self.op(eng, lambda e: e.tensor_scalar(out=out, in0=in0, scalar1=s1, scalar2=None, op0=op0), reads, writes)
        return self.op(eng, lambda e: e.tensor_scalar(out=out, in0=in0, scalar1=s1, scalar2=s2, op0=op0, op1=op1), reads, writes)

    def stt(self, out, in0, scalar, in1, op0, op1, reads=(), writes=()):
        return self.op(DVE, lambda e: e.scalar_tensor_tensor(out=out, in0=in0, scalar=scalar, in1=in1, op0=op0, op1=op1), reads, writes)

    def ms(self, eng, ap, val, reads=(), writes=()):
        return self.op(eng, lambda e: e.memset(ap, val), reads, writes)

    def ld(self, out, in_, reads=(), writes=(), tag='d', slow=False, eng=SP):
        return self.dma(eng, lambda e: e.dma_start(out=out, in_=in_, allow_slow_non_contiguous=slow), reads, writes, tag=tag)

    def emit(self):
        nc = self.nc
        for o in self.all_ops:
            o.deps = {d for d in o.deps if not (d.eng == PE and o.eng == PE and not d.is_dma and not o.is_dma)}
            for d in o.deps:
                d.sig = True
            if o.is_dma:
                o.sig = True
        lasts = [ops[-1] for ops in self.ops.values() if ops]
        for o in lasts:
            o.sig = True
        counters, dma_seq, order = {}, {}, []
        for o in self.all_ops:
            if not o.sig:
                continue
            if o.is_dma:
                k = dma_seq.get((o.eng, o.tag), 0)
                dma_seq[(o.eng, o.tag)] = k + 1
                key, inc = ('dma', o.eng, o.tag, k % self.dma_rot), 16
            else:
                key, inc = ('cmp', o.eng), 1
            if key not in counters:
                order.append(key)
            counters[key] = counters.get(key, 0) + inc
            o.semkey, o.semval = key, counters[key]
        Prog.n_emitted += 1
        sems = {key: nc.alloc_semaphore('p%ds%d' % (Prog.n_emitted, i)) for i, key in enumerate(order)}
        with contextlib.ExitStack() as st:
            block = st.enter_context(nc.Block())
            engmap = {PE: block.tensor, ACT: block.scalar, DVE: block.vector, POOL: block.gpsimd, SP: block.sync}

            def make(eng):
                ops = self.ops[eng]

                def body(e):
                    waited = {}

                    def wait(key, val):
                        if waited.get(key, 0) < val:
                            e.wait_ge(sems[key], val)
                            waited[key] = val
                    for o in ops:
                        need = {}
                        for d in o.deps:
                            if need.get(d.semkey, 0) < d.semval:
                                need[d.semkey] = d.semval
                        for key, val in need.items():
                            wait(key, val)
                        if o.is_dma and o.semval > 16:
                            wait(o.semkey, o.semval - 16)
                        ins = o.fn(e)
                        if o.sig:
                            ins.then_inc(sems[o.semkey], 16 if o.is_dma else 1)
                    for key, val in counters.items():
                        wait(key, val)
                return body
            for eng in ENGS:
                if self.ops[eng]:
                    engmap[eng](make(eng))
        nc.all_engine_barrier()
        nc.clear_and_free_semaphores(list(sems.values()))
        nc.all_engine_barrier()


class Ctx:
    pass


def phase_inproj(nc, cx, l):
    P = Prog(nc)
    ps = cx.ps
    with contextlib.ExitStack() as st:
        sb = lambda name, shape, dt: st.enter_context(nc.sbuf_tensor('i%d_' % l + name, shape, dt))
        xtb = sb('xtb', [128, KC, TS], BF16)
        wst = [sb('wst%d' % i, [128, KC, 128], F32) for i in range(2)]
        wbf = [sb('wbf%d' % i, [128, KC, 128], BF16) for i in range(2)]
        stg = [sb('stg%d' % i, [128, 512], F32) for i in range(4)]
        if l == 0:
            xst = [sb('xst%d' % i, [128, TS], F32) for i in range(2)]
            for kc in range(KC):
                b = kc % 2
                P.dma(SP, lambda e, kc=kc, b=b: e.dma_start(out=xst[b][:], in_=cx.xT[kc * 128:(kc + 1) * 128, :]),
                      writes=[('xst', b)], tag='x')
                eng = (ACT, DVE, POOL)[kc % 3]
                if eng == ACT:
                    P.op(ACT, lambda e, kc=kc, b=b: e.activation(out=xtb[:, kc, :], in_=xst[b][:], func=AF.Copy),
                         reads=[('xst', b)], writes=[('xtb', kc)])
                else:
                    P.op(eng, lambda e, kc=kc, b=b: e.tensor_copy(out=xtb[:, kc, :], in_=xst[b][:]),
                         reads=[('xst', b)], writes=[('xtb', kc)])
        else:
            for kc in range(KC):
                P.dma(SP, lambda e, kc=kc: e.dma_start(out=xtb[:, kc, :], in_=cx.x2T[kc * 128:(kc + 1) * 128, :]),
                      writes=[('xtb', kc)], tag='x')
        xres = [('xtb', kc) for kc in range(KC)]
        ntile = 0
        for blk in range(NBLK):
            b = blk % 2
            P.dma(SP, lambda e, blk=blk, b=b: e.dma_start(
                out=wst[b][:], in_=cx.w_in[l, :, blk * 128:(blk + 1) * 128].rearrange("(kc p) c -> p kc c", p=128)),
                writes=[('wst', b)], tag='w')
            P.op(POOL, lambda e, b=b: e.tensor_copy(out=wbf[b][:], in_=wst[b][:]), reads=[('wst', b)], writes=[('wbf', b)])
            for tt in range(9):
                t0, n = tt * 512, (512 if tt < 8 else 1)
                bank = ntile % 4
                for kc in range(KC):
                    P.op(PE, lambda e, kc=kc, b=b, bank=bank, t0=t0, n=n: e.matmul(
                        ps[:, bank, 0:n], lhsT=wbf[b][:, kc, :], rhs=xtb[:, kc, t0:t0 + n], start=(kc == 0), stop=(kc == KC - 1)),
                        reads=[('wbf', b)] + (xres if kc == 0 else []), writes=[('ps', bank)])
                s = ntile % 4
                if ntile % 2 == 0:
                    P.op(ACT, lambda e, bank=bank, s=s, n=n: e.activation(out=stg[s][:, 0:n], in_=ps[:, bank, 0:n], func=AF.Copy),
                         reads=[('ps', bank)], writes=[('stg', s)])
                else:
                    P.op(DVE, lambda e, bank=bank, s=s, n=n: e.tensor_copy(out=stg[s][:, 0:n], in_=ps[:, bank, 0:n]),
                         reads=[('ps', bank)], writes=[('stg', s)])
                P.dma(SP, lambda e, blk=blk, s=s, t0=t0, n=n: e.dma_start(
                    out=cx.projT[blk * 128:(blk + 1) * 128, t0:t0 + n], in_=stg[s][:, 0:n], allow_slow_non_contiguous=(n == 1)),
                    reads=[('stg', s)], writes=[('projT', blk)], tag='po')
                ntile += 1
        P.emit()


ALPHA_DN = (2.0 * DEPTH) ** 0.25
LN_EPS = 1e-5


def load_bcast(P, nc, eng, dst, src_row, key):
    P.dma(eng, lambda e: e.dma_start(out=dst, in_=src_row.partition_broadcast(dst.shape[0])), writes=[key], tag='bc')


def phase_outproj(nc, cx, l):
    P = Prog(nc)
    ps = cx.ps
    last = (l == DEPTH - 1)
    with contextlib.ExitStack() as st:
        sb = lambda name, shape, dt: st.enter_context(nc.sbuf_tensor('o%d_' % l + name, shape, dt))
        wob = sb('wob', [128, KC, D], BF16)
        wst = [sb('wost%d' % i, [128, D], F32) for i in range(2)]
        lng, lnb = sb('lng', [128, D], F32), sb('lnb', [128, D], F32)
        identb = sb('identb', [128, 128], BF16)
        gT = [sb('gT%d' % i, [128, KC, 128], BF16) for i in range(2)]
        xr = [sb('xr%d' % i, [128, D], F32) for i in range(2)]
        zt = [sb('zt%d' % i, [128, D], F32) for i in range(2)]
        yb = [sb('yb%d' % i, [128, D], BF16) for i in range(2)]
        xT2 = [sb('xT2%d' % i, [128, KC, 128], BF16) for i in range(2)]
        st1 = [sb('st1%d' % i, [128, 4], F32) for i in range(2)]
        sq = sb('sqjunk', [128, D], F32)
        P.dma(SP, lambda e: e.dma_start(out=identb[:], in_=cx.identb), writes=['identb'], tag='c')
        load_bcast(P, nc, SP, lng[:], cx.ln_g[l:l + 1, :], 'lng')
        load_bcast(P, nc, SP, lnb[:], cx.ln_b[l:l + 1, :], 'lnb')
        for kc in range(KC):
            b = kc % 2
            P.dma(SP, lambda e, kc=kc, b=b: e.dma_start(out=wst[b][:], in_=cx.w_out[l, kc * 128:(kc + 1) * 128, :]), writes=[('wst', b)], tag='w')
            if kc % 2 == 0:
                P.op(POOL, lambda e, kc=kc, b=b: e.tensor_copy(out=wob[:, kc, :], in_=wst[b][:]), reads=[('wst', b)], writes=[('wob', kc)])
            else:
                P.op(ACT, lambda e, kc=kc, b=b: e.activation(out=wob[:, kc, :], in_=wst[b][:], func=AF.Copy), reads=[('wst', b)], writes=[('wob', kc)])
        wres = [('wob', kc) for kc in range(KC)]
        xsrc = cx.xtok if l == 0 else cx.y1
        ydst = cx.y if last else cx.y1
        for tb in range(33):
            t0, n = tb * 128, (128 if tb < 32 else 1)
            b = tb % 2
            P.dma(SP, lambda e, b=b, t0=t0, n=n: e.dma_start(
                out=gT[b][:, :, 0:n], in_=cx.gatedT[:, t0:t0 + n].rearrange("(kc p) t -> p kc t", p=128), allow_slow_non_contiguous=(n == 1)),
                writes=[('gT', b)], tag='g')
            P.dma(SP, lambda e, b=b, t0=t0, n=n: e.dma_start(out=xr[b][0:n, :], in_=xsrc[t0:t0 + n, :]), writes=[('xr', b)], tag='xr')
            for cg in range(4):
                for kc in range(KC):
                    P.op(PE, lambda e, b=b, n=n, cg=cg, kc=kc: e.matmul(
                        ps[0:n, cg, :], lhsT=gT[b][:, kc, 0:n], rhs=wob[:, kc, cg * 512:(cg + 1) * 512], start=(kc == 0), stop=(kc == KC - 1)),
                        reads=[('gT', b)] + (wres if kc == 0 and cg == 0 and tb == 0 else []), writes=[('ps', cg)])
                P.op(DVE, lambda e, b=b, n=n, cg=cg: e.scalar_tensor_tensor(
                    out=zt[b][0:n, cg * 512:(cg + 1) * 512], in0=xr[b][0:n, cg * 512:(cg + 1) * 512], scalar=ALPHA_DN,
                    in1=ps[0:n, cg, :], op0=ALU.mult, op1=ALU.add), reads=[('xr', b), ('ps', cg)], writes=[('zt', b)])
            s = st1[b]
            P.op(DVE, lambda e, b=b, n=n, s=s: e.tensor_reduce(out=s[0:n, 0:1], in_=zt[b][0:n, :], axis=AX.X, op=ALU.add),
                 reads=[('zt', b)], writes=[('st1', b)])
            P.op(DVE, lambda e, n=n, s=s: e.tensor_scalar(out=s[0:n, 1:2], in0=s[0:n, 0:1], scalar1=-1.0 / D, scalar2=None, op0=ALU.mult),
                 reads=[('st1', b)], writes=[('st1', b)])
            P.op(ACT, lambda e, b=b, n=n, s=s: e.activation(out=sq[0:n, :], in_=zt[b][0:n, :], func=AF.Square, bias=s[0:n, 1:2], accum_out=s[0:n, 2:3]),
                 reads=[('zt', b), ('st1', b)], writes=[('st1', b), 'sq'])
            P.op(DVE, lambda e, n=n, s=s: e.tensor_scalar(out=s[0:n, 3:4], in0=s[0:n, 2:3], scalar1=1.0 / D, scalar2=LN_EPS, op0=ALU.mult, op1=ALU.add),
                 reads=[('st1', b)], writes=[('st1', b)])
            P.op(ACT, lambda e, n=n, s=s: e.activation(out=s[0:n, 3:4], in_=s[0:n, 3:4], func=AF.Ln), reads=[('st1', b)], writes=[('st1', b)])
            P.op(ACT, lambda e, n=n, s=s: e.activation(out=s[0:n, 3:4], in_=s[0:n, 3:4], func=AF.Exp, scale=-0.5), reads=[('st1', b)], writes=[('st1', b)])
            P.op(DVE, lambda e, b=b, n=n, s=s: e.tensor_scalar(out=zt[b][0:n, :], in0=zt[b][0:n, :], scalar1=s[0:n, 1:2], scalar2=s[0:n, 3:4],
                                                               op0=ALU.add, op1=ALU.mult), reads=[('zt', b), ('st1', b)], writes=[('zt', b)])
            P.op(DVE, lambda e, b=b, n=n: e.tensor_tensor(out=zt[b][0:n, :], in0=zt[b][0:n, :], in1=lng[0:n, :], op=ALU.mult),
                 reads=[('zt', b), 'lng'], writes=[('zt', b)])
            P.op(POOL, lambda e, b=b, n=n: e.tensor_tensor(out=zt[b][0:n, :], in0=zt[b][0:n, :], in1=lnb[0:n, :], op=ALU.add),
                 reads=[('zt', b), 'lnb'], writes=[('zt', b)])
            P.dma(SP, lambda e, b=b, t0=t0, n=n: e.dma_start(out=ydst[t0:t0 + n, :], in_=zt[b][0:n, :]), reads=[('zt', b)], writes=['ydst'], tag='yo')
            if not last:
                P.op(ACT, lambda e, b=b, n=n: e.activation(out=yb[b][0:n, :], in_=zt[b][0:n, :], func=AF.Copy), reads=[('zt', b)], writes=[('yb', b)])
                psb = [ps[:, 4 + h, :].bitcast(BF16) for h in range(2)]
                for kc in range(KC):
                    h, j = kc // 8, kc % 8
                    P.op(PE, lambda e, b=b, n=n, kc=kc, h=h, j=j: e.transpose(out=psb[h][:, j * 128:j * 128 + n], in_=yb[b][0:n, kc * 128:(kc + 1) * 128],
                                                                          identity=identb[0:n, 0:n]),
                         reads=[('yb', b), 'identb'], writes=[('ps', 4 + h)])
                for h in range(2):
                    P.op(DVE if h == 0 else ACT, (lambda e, b=b, n=n, h=h: e.tensor_copy(
                        out=xT2[b][:, h * 8:(h + 1) * 8, 0:n], in_=psb[h].rearrange("p (j t) -> p j t", t=128)[:, :, 0:n])) if h == 0 else
                        (lambda e, b=b, n=n, h=h: e.activation(
                            out=xT2[b][:, h * 8:(h + 1) * 8, 0:n], in_=psb[h].rearrange("p (j t) -> p j t", t=128)[:, :, 0:n], func=AF.Copy)),
                        reads=[('ps', 4 + h)], writes=[('xT2', b)])
                P.dma(SP, lambda e, b=b, t0=t0, n=n: e.dma_start(
                    out=cx.x2T[:, t0:t0 + n].rearrange("(kc p) t -> p kc t", p=128), in_=xT2[b][:, :, 0:n], allow_slow_non_contiguous=(n == 1)),
                    reads=[('xT2', b)], writes=['x2T'], tag='x2')
        P.emit()


PP_CONV, PP_NORMA, PP_ALOG, PP_DTB, PP_MU, PP_MUS = 0, 48, 49, 50, 51, 63
PP_W0, PP_A0, PP_XI, PP_ALPHA, PP_RHO, PP_GNG, PP_GNB, PP_BF, PP_PSC, NPP = 64, 68, 72, 76, 80, 84, 88, 92, 93, 100
POOL_W = (2, 4, 8, 16)


def pack_pp(I):
    pp = np.zeros((DEPTH, 128, NPP), np.float32)
    p = np.arange(128)
    for l in range(DEPTH):
        for b12 in range(12):
            for i in range(4):
                pp[l, :, PP_CONV + b12 * 4 + i] = I['conv_A'][l][i, b12 * 128 + p]
            pp[l, :, PP_MU + b12] = I['mu_B'][l][b12 * 128 + p]
        pp[l, :, PP_NORMA] = I['norm_A'][l][p % 64]
        pp[l, 0:8, PP_ALOG] = I['A_log'][l]
        pp[l, 0:8, PP_DTB] = I['dt_bias'][l]
        pp[l, 32:64, PP_MUS] = I['mu_B'][l][1536:1568]
        pp[l, 64:96, PP_MUS] = I['mu_B'][l][1568:1600]
        for j in range(4):
            for col, nm in ((PP_W0, 'w0_B'), (PP_A0, 'a0_B'), (PP_XI, 'xi_B'), (PP_ALPHA, 'alpha_B'), (PP_RHO, 'rho_B'),
                            (PP_GNG, 'gn_g_B'), (PP_GNB, 'gn_b_B'), (PP_PSC, 'pool_scale_D')):
                pp[l, :, col + j] = I[nm][l][j * 128 + p]
        pp[l, 16:24, PP_BF] = I['b_f_C'][l]
    return pp


def phase_mixD(nc, cx, l):
    P = Prog(nc)
    ps = cx.ps
    with contextlib.ExitStack() as st:
        sb = lambda name, shape, dt: st.enter_context(nc.sbuf_tensor('d%d_' % l + name, shape, dt))
        pp = sb('pp', [128, NPP], F32)
        pwf, pwb = sb('pwf', [128, 4, 128], F32), sb('pwb', [128, 4, 128], BF16)
        ext, zt = sb('ext', [128, 15 + T], F32), sb('zt', [128, T], F32)
        sA, sB = sb('sA', [128, 15 + T], F32), sb('sB', [128, 15 + T], F32)
        pb, go = sb('pb', [128, T], BF16), sb('go', [128, T], BF16)
        exs, zs, sAs, sBs = sb('exs', [128, 16], F32), sb('zs', [128, 1], F32), sb('sAs', [128, 16], F32), sb('sBs', [128, 16], F32)
        pbs, gos = sb('pbs', [128, 1], BF16), sb('gos', [128, 1], BF16)
        P.dma(SP, lambda e: e.dma_start(out=pp[:], in_=cx.pp[l]), writes=['pp'], tag='c')
        P.dma(SP, lambda e: e.dma_start(out=pwf[:], in_=cx.pool_w[l].rearrange("g c d -> c g d")), writes=['pwf'], tag='c')
        P.op(POOL, lambda e: e.tensor_copy(out=pwb[:], in_=pwf[:]), reads=['pwf'], writes=['pwb'])
        P.op(POOL, lambda e: e.memset(ext[:, 0:15], 0.0), writes=['ext'])

        def group(g, n, ext, zt, sA, sB, pb, go, fix, sfx):
            w = POOL_W[g]
            r0 = g * 128
            k = lambda nm: nm + sfx
            if n > 1:
                P.dma(SP, lambda e: e.dma_start(out=ext[:, 15:15 + n], in_=cx.projT[R_DU + r0:R_DU + r0 + 128, 0:n]), writes=[k('ext')], tag='m')
                P.dma(SP, lambda e: e.dma_start(out=zt[:, 0:n], in_=cx.projT[R_DZ + r0:R_DZ + r0 + 128, 0:n]), writes=[k('zt')], tag='m')
            else:
                P.dma(SP, lambda e: e.dma_start(out=ext[:, 0:15], in_=cx.dbufT[l, r0:r0 + 128, :]), writes=[k('ext')], tag='m')
                P.dma(SP, lambda e: e.dma_start(out=ext[:, 15:16], in_=cx.projT[R_DU + r0:R_DU + r0 + 128, T:T + 1], allow_slow_non_contiguous=True), writes=[k('ext')], tag='m')
                P.dma(SP, lambda e: e.dma_start(out=zt[:, 0:1], in_=cx.projT[R_DZ + r0:R_DZ + r0 + 128, T:T + 1], allow_slow_non_contiguous=True), writes=[k('zt')], tag='m')
            P.op(ACT, lambda e: e.activation(out=zt[:, 0:n], in_=zt[:, 0:n], func=AF.Silu), reads=[k('zt')], writes=[k('zt')])
            src, srck, sh, E = ext, k('ext'), 1, 15 + n
            for step in range(g + 1):
                dst, dstk = (sA, k('sA')) if step % 2 == 0 else (sB, k('sB'))
                lo = 2 * sh - 1
                P.op(DVE if step % 2 == 0 else POOL, lambda e, src=src, dst=dst, sh=sh, lo=lo: e.tensor_tensor(
                    out=dst[:, lo:E], in0=src[:, lo:E], in1=src[:, lo - sh:E - sh], op=ALU.add), reads=[srck], writes=[dstk])
                src, srck, sh = dst, dstk, sh * 2
            P.op(DVE, lambda e, src=src: e.scalar_tensor_tensor(out=pb[:, 0:n], in0=src[:, 15:15 + n], scalar=1.0 / w, in1=ext[:, 15:15 + n],
                                                              op0=ALU.mult, op1=ALU.subtract), reads=[srck, k('ext')], writes=[k('pb')])
            if fix:
                for t in range(w - 1):
                    P.op(DVE, lambda e, src=src, t=t: e.scalar_tensor_tensor(out=pb[:, t:t + 1], in0=src[:, 15 + t:16 + t], scalar=1.0 / (t + 1),
                                                                            in1=ext[:, 15 + t:16 + t], op0=ALU.mult, op1=ALU.subtract),
                         reads=[srck, k('ext')], writes=[k('pb')])
            for tt in range((n + 511) // 512):
                t0, m = tt * 512, min(512, n - tt * 512)
                bank = tt % 4
                P.op(PE, lambda e, t0=t0, m=m, bank=bank: e.matmul(ps[:, bank, 0:m], lhsT=pwb[:, g, :], rhs=pb[:, t0:t0 + m], start=True, stop=True),
                     reads=['pwb', k('pb')], writes=[('ps', bank)])
                P.op(DVE, lambda e, t0=t0, m=m, bank=bank: e.scalar_tensor_tensor(
                    out=go[:, t0:t0 + m], in0=ps[:, bank, 0:m], scalar=pp[:, PP_PSC + g:PP_PSC + g + 1], in1=zt[:, t0:t0 + m], op0=ALU.mult, op1=ALU.mult),
                    reads=[('ps', bank), 'pp', k('zt')], writes=[k('go')])
            c0 = 0 if n > 1 else T
            P.dma(SP, lambda e: e.dma_start(out=cx.gatedT[1536 + r0:1536 + r0 + 128, c0:c0 + n], in_=go[:, 0:n], allow_slow_non_contiguous=(n == 1)),
                  reads=[k('go')], writes=['gatedT'], tag='go')
        for g in range(4):
            group(g, T, ext, zt, sA, sB, pb, go, True, '')
            group(g, 1, exs, zs, sAs, sBs, pbs, gos, False, 's')
        P.dma(SP, lambda e: e.dma_start(out=cx.o_dbuf[l, 0], in_=cx.projT[R_DU:R_DU + 512, T - 15:T], allow_slow_non_contiguous=True), writes=['o_dbuf0'], tag='so')
        P.dma(SP, lambda e: e.dma_start(out=cx.o_dbuf[l, 1, :, 0:14], in_=cx.dbufT[l, :, 1:15], allow_slow_non_contiguous=True), writes=['o_dbuf1'], tag='so')
        P.dma(SP, lambda e: e.dma_start(out=cx.o_dbuf[l, 1, :, 14:15], in_=cx.projT[R_DU:R_DU + 512, T:T + 1], allow_slow_non_contiguous=True), writes=['o_dbuf2'], tag='so')
        P.emit()


def phase_mixC(nc, cx, l, heads=range(NH), do_sample=True):
    P = Prog(nc)
    ps = cx.ps
    with contextlib.ExitStack() as st:
        sb = lambda name, shape, dt: st.enter_context(nc.sbuf_tensor('c%d_' % l + name, shape, dt))
        cf = sb('cf', [128, 256], F32)
        cb = sb('cb', [128, 128 + 4 * 512], BF16)
        identf, tri = cf[:, 0:128], cf[:, 128:256]
        identb = cb[:, 0:128]
        negb = sb('negb', [128, 8], F32)
        ones32 = sb('ones32', [128, 64], F32)
        onesj = sb('onesj', [128, 32], F32)
        qf, kf, vf, zf = (sb(n, [64, T], F32) for n in ('qf', 'kf', 'vf', 'zf'))
        Qa, Ka = sb('Qa', [70, T], BF16), sb('Ka', [70, T], BF16)
        Vt = sb('Vt', [128, 32, 65], BF16)
        PT = [sb('PT%d' % i, [128, 512], BF16) for i in range(3)]
        f32t, e1, lf, cw, hif, r1 = (sb(n, [128, 32], F32) for n in ('f32t', 'e1', 'lf', 'cw', 'hif', 'r1'))
        offs = sb('offs', [128, 1], F32)
        c3 = sb('c3', [128, 6, 32], BF16)
        oT, t1 = sb('oT', [65, 512], F32), sb('t1', [64, 512], F32)
        rden = sb('rden', [65, 512], F32)
        go = sb('go', [64, T], BF16)
        P.dma(SP, lambda e: e.dma_start(out=cf[:], in_=cx.cf32), writes=['cf'], tag='c')
        P.dma(SP, lambda e: e.dma_start(out=cb[:], in_=cx.cb16), writes=['cb'], tag='c')
        P.dma(SP, lambda e: e.dma_start(out=negb[:], in_=cx.b_f[l:l + 1, :].partition_broadcast(128)), writes=['negb'], tag='c')
        P.op(POOL, lambda e: e.tensor_scalar(out=negb[:], in0=negb[:], scalar1=-1.0, scalar2=None, op0=ALU.mult), reads=['negb'], writes=['negb'])
        P.op(POOL, lambda e: e.memset(ones32[:], 1.0), writes=['ones32'])
        P.op(POOL, lambda e: e.memset(onesj[:], 1.0), writes=['onesj'])
        P.op(POOL, lambda e: e.memset(Vt[:, :, 64:65], 1.0), writes=['Vt'])
        P.op(POOL, lambda e: e.memset(Qa[64:70, :], 1.0), writes=['Qa'])
        P.op(POOL, lambda e: e.memset(Ka[64:70, :], 1.0), writes=['Ka'])
        zfs = [zf, sb('zf1', [64, T], F32)]

        def load_head(h):
            zfh, zk = zfs[h % 2], ('zf', h % 2)
            P.dma(SP, lambda e, h=h: e.dma_start(out=f32t[:], in_=cx.projT[R_SM + SM_F + h, 0:T].rearrange("(p j) -> p j", j=32)), writes=['f32t'], tag='m')
            for tl, r0, key in ((qf, R_CQ, 'qf'), (kf, R_CK, 'kf'), (vf, R_CV, 'vf'), (zfh, R_CZ, zk)):
                P.dma(SP, lambda e, tl=tl, r0=r0, h=h: e.dma_start(out=tl[:], in_=cx.projT[r0 + h * 64:r0 + (h + 1) * 64, 0:T]), writes=[key], tag='m')
            P.op(ACT, lambda e, h=h: e.activation(out=e1[:], in_=f32t[:], func=AF.Exp, scale=-1.0, bias=negb[:, h:h + 1]), reads=['f32t', 'negb'], writes=['e1'])
            P.op(ACT, lambda e: e.activation(out=e1[:], in_=e1[:], func=AF.Ln, bias=1.0), reads=['e1'], writes=['e1'])
            P.op(DVE, lambda e: e.tensor_scalar(out=lf[:], in0=e1[:], scalar1=-1.0, scalar2=None, op0=ALU.mult), reads=['e1'], writes=['lf'])
            P.dma(SP, lambda e, h=h: e.dma_start(out=cx.o_clf[l, h, 0:T].rearrange("(p j) -> p j", j=32), in_=lf[:]), reads=['lf'], writes=['o_clf'], tag='so')
            P.op(DVE, lambda e: e.tensor_tensor_scan(out=cw[:], data0=onesj[:], data1=lf[:], initial=0.0, op0=ALU.mult, op1=ALU.add),
                 reads=['lf', 'onesj'], writes=['cw'])
            P.op(PE, lambda e: e.matmul(ps[:, 4, 0:1], lhsT=tri, rhs=cw[:, 31:32], start=True, stop=True), reads=['cf', 'cw'], writes=[('ps', 4)])
            P.op(DVE, lambda e: e.tensor_copy(out=offs[:], in_=ps[:, 4, 0:1]), reads=[('ps', 4)], writes=['offs'])
            P.op(DVE, lambda e: e.tensor_scalar(out=cw[:], in0=cw[:], scalar1=offs[:, 0:1], scalar2=None, op0=ALU.add), reads=['cw', 'offs'], writes=['cw'])
            src = cw
            for pc in range(3):
                P.op(DVE, lambda e, pc=pc, src=src: e.tensor_copy(out=c3[:, pc, :], in_=src[:]), reads=['cw', 'r1'], writes=['c3'])
                P.op(POOL, lambda e, pc=pc, src=src: e.tensor_scalar(out=c3[:, 3 + pc, :], in0=src[:], scalar1=-1.0, scalar2=None, op0=ALU.mult),
                     reads=['cw', 'r1'], writes=['c3'])
                if pc < 2:
                    P.op(DVE, lambda e, pc=pc: e.tensor_copy(out=hif[:], in_=c3[:, pc, :]), reads=['c3'], writes=['hif'])
                    P.op(DVE, lambda e, src=src: e.tensor_tensor(out=r1[:], in0=src[:], in1=hif[:], op=ALU.subtract), reads=['cw', 'r1', 'hif'], writes=['r1'])
                    src = r1
            P.dma(SP, lambda e: e.dma_start(out=cx.cscr.rearrange("r (p j) -> p r j", j=32), in_=c3[:]), reads=['c3'], writes=['cscr'], tag='cs')

        def conv_head(h):
            zfh, zk = zfs[h % 2], ('zf', h % 2)
            P.dma(SP, lambda e: e.dma_start(out=Qa[64:67, :], in_=cx.cscr[0:3, :]), reads=['cscr'], writes=['Qa'], tag='cs')
            P.dma(SP, lambda e: e.dma_start(out=Ka[67:70, :], in_=cx.cscr[3:6, :]), reads=['cscr'], writes=['Ka'], tag='cs')
            P.op(ACT, lambda e: e.activation(out=Qa[0:64, :], in_=qf[:], func=AF.Copy, scale=0.125), reads=['qf'], writes=['Qa'])
            P.op(POOL, lambda e: e.tensor_copy(out=Ka[0:64, :], in_=kf[:]), reads=['kf'], writes=['Ka'])
            P.op(ACT, lambda e: e.activation(out=zfh[:], in_=zfh[:], func=AF.Silu), reads=[zk], writes=[zk])
            for r in range(4):
                bank = 5 + r % 2
                for j in range(8):
                    blk = r * 8 + j
                    P.op(PE, lambda e, bank=bank, j=j, blk=blk: e.transpose(out=ps[:, bank, j * 64:(j + 1) * 64], in_=vf[:, blk * 128:(blk + 1) * 128],
                                                                        identity=identf[0:64, 0:64]), reads=['vf', 'cf'], writes=[('ps', bank)])
                P.op(DVE, lambda e, bank=bank, r=r: e.tensor_copy(out=Vt[:, r * 8:(r + 1) * 8, 0:64], in_=ps[:, bank, :].rearrange("p (j d) -> p j d", d=64)),
                     reads=[('ps', bank)], writes=['Vt'])

        itc = [0]

        def attend(h):
            zfh, zk = zfs[h % 2], ('zf', h % 2)
            for qt in range(8):
                nkb = 4 * qt + 4
                for kb in range(nkb):
                    it = itc[0]
                    bank, r = it % 3, it % 3
                    diag = kb >= 4 * qt
                    P.op(PE, lambda e, bank=bank, kb=kb, qt=qt, diag=diag: e.matmul(
                        ps[:, bank, :], lhsT=Ka[0:70, kb * 128:(kb + 1) * 128], rhs=Qa[0:70, qt * 512:(qt + 1) * 512], start=True, stop=not diag),
                        reads=['Ka', 'Qa'], writes=[('ps', bank)])
                    if diag:
                        j = kb - 4 * qt
                        P.op(PE, lambda e, bank=bank, j=j: e.matmul(ps[:, bank, :], lhsT=identb, rhs=cb[:, 128 + j * 512:128 + (j + 1) * 512], start=False, stop=True),
                             reads=['cb'], writes=[('ps', bank)])
                    P.op(ACT, lambda e, bank=bank, r=r: e.activation(out=PT[r][:], in_=ps[:, bank, :], func=AF.Exp), reads=[('ps', bank)], writes=[('PT', r)])
                    P.op(PE, lambda e, r=r, kb=kb, nkb=nkb: e.matmul(ps[0:65, 3, :], lhsT=Vt[:, kb, :], rhs=PT[r][:], start=(kb == 0), stop=(kb == nkb - 1)),
                         reads=['Vt', ('PT', r)], writes=[('ps', 3)])
                    itc[0] += 1
                P.op(DVE, lambda e: e.tensor_copy(out=oT[:], in_=ps[0:65, 3, :]), reads=[('ps', 3)], writes=['oT'])
                P.op(DVE, lambda e: e.reciprocal(out=rden[64:65, :], in_=oT[64:65, :]), reads=['oT'], writes=['rden'])
                P.op(PE, lambda e: e.matmul(ps[0:64, 4, :], lhsT=ones32[64:65, 0:64], rhs=rden[64:65, :], start=True, stop=True),
                     reads=['ones32', 'rden'], writes=[('ps', 4)])
                P.op(DVE, lambda e: e.tensor_tensor(out=t1[:], in0=oT[0:64, :], in1=ps[0:64, 4, :], op=ALU.mult), reads=['oT', ('ps', 4)], writes=['t1'])
                P.op(POOL, lambda e, qt=qt: e.tensor_tensor(out=go[:, qt * 512:(qt + 1) * 512], in0=t1[:], in1=zfh[:, qt * 512:(qt + 1) * 512], op=ALU.mult),
                     reads=['t1', zk], writes=['go'])
            P.dma(SP, lambda e, h=h: e.dma_start(out=cx.gatedT[1024 + h * 64:1024 + (h + 1) * 64, 0:T], in_=go[:]), reads=['go'], writes=['gatedT'], tag='go')

        hl = list(heads)
        load_head(hl[0])
        conv_head(hl[0])
        for i, h in enumerate(hl):
            if i + 1 < len(hl):
                load_head(hl[i + 1])
            attend(h)
            if i + 1 < len(hl):
                conv_head(hl[i + 1])
        for j in range(8):
            P.dma(SP, lambda e, j=j: e.dma_start(out=cx.o_ckv[l, j * 128:(j + 1) * 128, :], in_=cx.projT[R_CK + j * 128:R_CK + (j + 1) * 128, :]),
                  writes=[('o_ckv', j)], tag='so')
        P.emit()


TP, NCH = 33 * 128, 33
A_KINDS = ('k', 'q', 'bz', 'rz', 'k2', 'a2', 'v')
B_KINDS = ('ah', 'bh', 'kh', 'rh', 'k2', 'a2', 'v')
RWKV_DS = float(np.exp(-0.5))


def _l2_rn(P, nc, ps, sqb, onesbd, n, rn, keys, bank0=0):
    for j, t0 in enumerate(range(0, n, 512)):
        m = min(512, n - t0)
        bank = bank0 + j % 2
        P.op(PE, lambda e, t0=t0, m=m, bank=bank: e.matmul(ps[:, bank, 0:m], lhsT=onesbd, rhs=sqb[:, t0:t0 + m], start=True, stop=True),
             reads=[keys[0], 'cb'], writes=[('ps', bank)])
        P.op(ACT, lambda e, t0=t0, m=m, bank=bank: e.activation(out=rn[:, t0:t0 + m], in_=ps[:, bank, 0:m], func=AF.Ln, bias=1e-6),
             reads=[('ps', bank)], writes=[keys[1]])
    P.op(ACT, lambda e: e.activation(out=rn[:, 0:n], in_=rn[:, 0:n], func=AF.Exp, scale=-0.5), reads=[keys[1]], writes=[keys[1]])


def phase_rowsA(nc, cx, l):
    P = Prog(nc)
    with contextlib.ExitStack() as st:
        sb = lambda name, shape, dt: st.enter_context(nc.sbuf_tensor('ra%d_' % l + name, shape, dt))
        pp = sb('pp', [128, NPP], F32)
        P.dma(SP, lambda e: e.dma_start(out=pp[:], in_=cx.pp[l]), writes=['pp'], tag='c')
        g, lb, u1, u0, gl, tmp, rmask, onesr, mA, mB = (sb(n, [8, TP], F32) for n in ('g', 'lb', 'u1', 'u0', 'gl', 'tmp', 'rmask', 'onesr', 'mA', 'mB'))
        nega = sb('nega', [8, 1], F32)
        P.op(POOL, lambda e: e.memset(g[:], 0.0), writes=['g'])
        P.op(POOL, lambda e: e.memset(lb[:], 0.0), writes=['lb'])
        P.op(POOL, lambda e: e.memset(onesr[:], 1.0), writes=['onesr'])
        P.op(POOL, lambda e: e.memset(rmask[:], 1.0), writes=['rmask'])
        P.op(POOL, lambda e: e.memset(rmask[:].rearrange("p (c i) -> p c i", i=128)[:, :, 0:1], 0.0), reads=['rmask'], writes=['rmask'])
        P.dma(SP, lambda e: e.dma_start(out=g[:, 0:TS], in_=cx.projT[R_SM + SM_A:R_SM + SM_A + 8, :]), reads=['g'], writes=['g'], tag='m')
        P.dma(SP, lambda e: e.dma_start(out=lb[:, 0:TS], in_=cx.projT[R_SM + SM_B:R_SM + SM_B + 8, :]), reads=['lb'], writes=['lb'], tag='m')
        P.op(ACT, lambda e: e.activation(out=nega[:], in_=pp[0:8, PP_ALOG:PP_ALOG + 1], func=AF.Exp), reads=['pp'], writes=['nega'])
        P.op(DVE, lambda e: e.tensor_scalar(out=nega[:], in0=nega[:], scalar1=-1.0, scalar2=None, op0=ALU.mult), reads=['nega'], writes=['nega'])
        P.op(ACT, lambda e: e.activation(out=g[:], in_=g[:], func=AF.Exp, bias=pp[0:8, PP_DTB:PP_DTB + 1]), reads=['g', 'pp'], writes=['g'])
        P.op(ACT, lambda e: e.activation(out=g[:], in_=g[:], func=AF.Ln, bias=1.0), reads=['g'], writes=['g'])
        P.op(DVE, lambda e: e.tensor_scalar(out=g[:], in0=g[:], scalar1=nega[:, 0:1], scalar2=None, op0=ALU.mult), reads=['g', 'nega'], writes=['g'])
        P.op(POOL, lambda e: e.memset(g[:, TS:TP], 0.0), reads=['g'], writes=['g'])
        P.op(ACT, lambda e: e.activation(out=lb[:], in_=lb[:], func=AF.Exp, scale=-1.0), reads=['lb'], writes=['lb'])
        P.op(ACT, lambda e: e.activation(out=lb[:], in_=lb[:], func=AF.Ln, bias=1.0), reads=['lb'], writes=['lb'])
        P.op(DVE, lambda e: e.tensor_scalar(out=lb[:], in0=lb[:], scalar1=-1.0, scalar2=None, op0=ALU.mult), reads=['lb'], writes=['lb'])
        P.op(DVE, lambda e: e.tensor_tensor_scan(out=u1[:], data0=rmask[:], data1=g[:], initial=0.0, op0=ALU.mult, op1=ALU.add),
             reads=['g', 'rmask'], writes=['u1'])
        P.op(POOL, lambda e: e.tensor_tensor(out=u0[:], in0=u1[:], in1=g[:], op=ALU.subtract), reads=['u1', 'g'], writes=['u0'])
        P.op(DVE, lambda e: e.tensor_copy(out=gl[:].rearrange("p (c i) -> p c i", i=128),
                                          in_=u1[:].rearrange("p (c i) -> p c i", i=128)[:, :, 127:128].to_broadcast([8, NCH, 128])),
             reads=['u1'], writes=['gl'])
        P.op(POOL, lambda e: e.tensor_tensor(out=tmp[:], in0=lb[:], in1=gl[:], op=ALU.add), reads=['lb', 'gl'], writes=['tmp'])
        stage = [(mA, 'mA'), (mB, 'mB')]
        for i in range(5):
            m, mk = stage[i % 2]
            if i == 0:
                P.op(ACT, lambda e, m=m: e.activation(out=m[:], in_=u0[:], func=AF.Exp), reads=['u0'], writes=[mk])
            elif i == 1:
                P.op(ACT, lambda e, m=m: e.activation(out=m[:], in_=u1[:], func=AF.Exp), reads=['u1'], writes=[mk])
            elif i == 4:
                P.op(ACT, lambda e, m=m: e.activation(out=m[:], in_=gl[:], func=AF.Exp), reads=['gl'], writes=[mk])
            else:
                src = u1 if i == 2 else u0
                P.op(POOL, lambda e, m=m, src=src: e.tensor_tensor(out=m[:], in0=tmp[:], in1=src[:], op=ALU.subtract), reads=['tmp', 'u1', 'u0'], writes=[mk])
                P.op(ACT, lambda e, m=m: e.activation(out=m[:], in_=m[:], func=AF.Exp), reads=[mk], writes=[mk])
            P.dma(SP, lambda e, i=i, m=m: e.dma_start(out=cx.A_mrows[i], in_=m[:]), reads=[mk], writes=['A_mrows'], tag='er')
        for kind, r, tl, key in ((0, 0, u0, 'u0'), (0, 1, onesr, 'onesr'), (3, 0, u1, 'u1'), (3, 1, onesr, 'onesr'), (1, 0, onesr, 'onesr'), (2, 0, onesr, 'onesr')):
            P.dma(SP, lambda e, kind=kind, r=r, tl=tl: e.dma_start(out=cx.A_erows[kind, r], in_=tl[:]), reads=[key], writes=[('A_erows', kind, r)], tag='er')
        P.op(POOL, lambda e: e.tensor_tensor(out=tmp[:], in0=lb[:], in1=u0[:], op=ALU.subtract), reads=['lb', 'u0', 'mA', 'mB'], writes=['tmp'])
        P.dma(SP, lambda e: e.dma_start(out=cx.A_erows[1, 1], in_=tmp[:]), reads=['tmp'], writes=[('A_erows', 1, 1)], tag='er')
        P.op(POOL, lambda e: e.tensor_tensor(out=gl[:], in0=lb[:], in1=u1[:], op=ALU.subtract), reads=['lb', 'u1', 'gl', 'mA', 'mB'], writes=['gl'])
        P.dma(SP, lambda e: e.dma_start(out=cx.A_erows[2, 1], in_=gl[:]), reads=['gl'], writes=[('A_erows', 2, 1)], tag='er')
        P.emit()


def phase_preA(nc, cx, l):
    P = Prog(nc)
    ps = cx.ps
    with contextlib.ExitStack() as st:
        sb = lambda name, shape, dt: st.enter_context(nc.sbuf_tensor('pa%d_' % l + name, shape, dt))
        pp = sb('pp', [128, NPP], F32)
        cb = sb('cb', [128, 256], BF16)
        onesbd = cb[:, 128:256]
        sel = sb('sel', [8, 512], F32)
        P.dma(SP, lambda e: e.dma_start(out=pp[:], in_=cx.pp[l]), writes=['pp'], tag='c')
        P.dma(SP, lambda e: e.dma_start(out=cb[:], in_=cx.cb2), writes=['cb'], tag='c')
        P.dma(SP, lambda e: e.dma_start(out=sel[:], in_=cx.sel8), writes=['sel'], tag='c')
        W = 1024
        raw, acc, rn, kn, qn = (sb(n, [128, W + 3], F32) for n in ('raw', 'acc', 'rn', 'kn', 'qn'))
        sqb, ob = sb('sqb', [128, W], BF16), [sb('ob%d' % i, [128, W], BF16) for i in range(2)]
        gct = sb('gct', [128, NCH], F32)
        mr = sb('mr', [8, 4, W], F32)
        m4c = sb('m4c', [8, NCH], F32)
        P.dma(SP, lambda e: e.dma_start(out=m4c[:], in_=cx.A_mrows[4].rearrange("h (c i) -> h c i", i=128)[:, :, 0], allow_slow_non_contiguous=True), writes=['m4c'], tag='c')
        nob = [0]

        def store(kind, pb, c0, n, src_f32, mul_row=None, srckey=None):
            o = ob[nob[0] % 2]
            ok = ('ob', nob[0] % 2)
            nob[0] += 1
            if mul_row is None:
                P.op(POOL, lambda e: e.tensor_copy(out=o[:, 0:n], in_=src_f32[:, 0:n]), reads=[srckey], writes=[ok])
            else:
                for j, t0 in enumerate(range(0, n, 512)):
                    m = min(512, n - t0)
                    bank = 2 + j % 2
                    P.op(PE, lambda e, t0=t0, m=m, bank=bank: e.matmul(ps[:, bank, 0:m], lhsT=sel[:, pb * 128:(pb + 1) * 128],
                                                                    rhs=mr[:, mul_row, t0:t0 + m], start=True, stop=True),
                         reads=['sel', 'mr'], writes=[('ps', bank)])
                    P.op(DVE, lambda e, t0=t0, m=m, bank=bank: e.tensor_tensor(out=o[:, t0:t0 + m], in0=src_f32[:, t0:t0 + m], in1=ps[:, bank, 0:m], op=ALU.mult),
                         reads=[srckey, ('ps', bank)], writes=[ok])
            P.dma(SP, lambda e: e.dma_start(out=cx.A_sc[kind][pb * 128:(pb + 1) * 128, c0:c0 + n], in_=o[:, 0:n], allow_slow_non_contiguous=(n == 1)),
                  reads=[ok], writes=[('A_sc', kind)], tag='sc')

        def conv_block(br, pb, c0, n, dst, dstkey):
            r0 = br * 512 + pb * 128
            b12 = br * 4 + pb
            if c0 == 0:
                P.op(POOL, lambda e: e.memset(raw[:, 0:3], 0.0), reads=['raw'], writes=['raw'])
                P.dma(SP, lambda e: e.dma_start(out=raw[:, 3:3 + n], in_=cx.projT[r0:r0 + 128, 0:n]), writes=['raw'], tag='m')
            elif c0 == T:
                P.dma(SP, lambda e: e.dma_start(out=raw[:, 0:3], in_=cx.aconvT[l, r0:r0 + 128, :]), writes=['raw'], tag='m')
                P.dma(SP, lambda e: e.dma_start(out=raw[:, 3:4], in_=cx.projT[r0:r0 + 128, T:T + 1], allow_slow_non_contiguous=True), writes=['raw'], tag='m')
            else:
                P.dma(SP, lambda e: e.dma_start(out=raw[:, 0:3 + n], in_=cx.projT[r0:r0 + 128, c0 - 3:c0 + n]), writes=['raw'], tag='m')
            cw = lambda i: pp[:, PP_CONV + b12 * 4 + i:PP_CONV + b12 * 4 + i + 1]
            P.op(DVE, lambda e: e.tensor_scalar(out=acc[:, 0:n], in0=raw[:, 0:n], scalar1=cw(0), scalar2=None, op0=ALU.mult), reads=['raw', 'pp'], writes=['acc'])
            for i in range(1, 4):
                P.op(DVE, lambda e, i=i: e.scalar_tensor_tensor(out=acc[:, 0:n], in0=raw[:, i:i + n], scalar=cw(i), in1=acc[:, 0:n], op0=ALU.mult, op1=ALU.add),
                     reads=['raw', 'acc', 'pp'], writes=['acc'])
            P.op(ACT, lambda e: e.activation(out=dst[:, 0:n], in_=acc[:, 0:n], func=AF.Silu), reads=['acc'], writes=[dstkey])

        for pb in range(4):
            for (c0, n) in [(i * W, W) for i in range(T // W)] + [(T, 1)]:
                P.dma(SP, lambda e, c0=c0, n=n: e.dma_start(out=mr[:, :, 0:n], in_=cx.A_mrows[0:4, :, c0:c0 + n].rearrange("k h t -> h k t"),
                                                          allow_slow_non_contiguous=(n == 1)), writes=['mr'], tag='m')
                for br, dst, dk, sc in ((0, qn, 'qn', 0.125), (1, kn, 'kn', 1.0)):
                    conv_block(br, pb, c0, n, dst, dk)
                    P.act(sqb[:, 0:n], dst[:, 0:n], AF.Square, reads=[dk], writes=['sqb'])
                    _l2_rn(P, nc, ps, sqb, onesbd, n, rn, ('sqb', 'rn'))
                    P.stt(dst[:, 0:n], dst[:, 0:n], sc, rn[:, 0:n], ALU.mult, ALU.mult, reads=[dk, 'rn'], writes=[dk])
                store('q', pb, c0, n, qn, None, 'qn')
                store('rz', pb, c0, n, qn, 1, 'qn')
                store('k', pb, c0, n, kn, None, 'kn')
                store('bz', pb, c0, n, kn, 0, 'kn')
                store('k2', pb, c0, n, kn, 2, 'kn')
                store('a2', pb, c0, n, kn, 3, 'kn')
                conv_block(2, pb, c0, n, qn, 'qn')
                store('v', pb, c0, n, qn, None, 'qn')
            for j in range(NCH):
                pass
            P.op(PE, lambda e, pb=pb: e.matmul(ps[:, 4, 0:NCH], lhsT=sel[:, pb * 128:(pb + 1) * 128],
                                               rhs=m4c[:], start=True, stop=True),
                 reads=['sel', 'm4c'], writes=[('ps', 4)])
            P.op(DVE, lambda e: e.tensor_copy(out=gct[:], in_=ps[:, 4, 0:NCH]), reads=[('ps', 4)], writes=['gct'])
            P.dma(SP, lambda e, pb=pb: e.dma_start(out=cx.A_gc[pb * 128:(pb + 1) * 128, :], in_=gct[:]), reads=['gct'], writes=['A_gc'], tag='sc')
        P.dma(SP, lambda e: e.dma_start(out=cx.o_aconv[l, 0], in_=cx.projT[0:1536, T - 3:T], allow_slow_non_contiguous=True), writes=['oac0'], tag='so')
        P.dma(SP, lambda e: e.dma_start(out=cx.o_aconv[l, 1, :, 0:2], in_=cx.aconvT[l, :, 1:3], allow_slow_non_contiguous=True), writes=['oac1'], tag='so')
        P.dma(SP, lambda e: e.dma_start(out=cx.o_aconv[l, 1, :, 2:3], in_=cx.projT[0:1536, T:T + 1], allow_slow_non_contiguous=True), writes=['oac2'], tag='so')
        P.emit()


def phase_preB(nc, cx, l):
    P = Prog(nc)
    ps = cx.ps
    W = 1024
    with contextlib.ExitStack() as st:
        sb = lambda name, shape, dt: st.enter_context(nc.sbuf_tensor('pb%d_' % l + name, shape, dt))
        pp = sb('pp', [128, NPP], F32)
        cb = sb('cb', [128, 256], BF16)
        onesbd = cb[:, 128:256]
        lwf, lwb = sb('lwf', [128, 512], F32), sb('lwb', [128, 512], BF16)
        oma = sb('oma', [128, 4], F32)
        rmask = sb('rmask', [128, W], F32)
        P.ld(pp[:], cx.pp[l], writes=['pp'], tag='c')
        P.ld(cb[:], cx.cb2, writes=['cb'], tag='c')
        P.ld(lwf[:], cx.lora_w[l], writes=['lwf'], tag='c')
        P.cp(POOL, lwb[:], lwf[:], reads=['lwf'], writes=['lwb'])
        P.ts(DVE, oma[:], pp[:, PP_ALPHA:PP_ALPHA + 4], -1.0, ALU.mult, 1.0, ALU.add, reads=['pp'], writes=['oma'])
        P.ms(POOL, rmask[:], 1.0, writes=['rmask'])
        P.ms(POOL, rmask[:].rearrange("p (c i) -> p c i", i=128)[:, :, 0:1], 0.0, reads=['rmask'], writes=['rmask'])
        sm, smd = sb('sm', [128, W + 1], F32), sb('smd', [128, W], F32)
        lwin = sb('lwin', [128, W], BF16)
        raw = {k: sb('raw_' + k, [128, W + 1], F32) for k in 'rkv'}
        sh = {k: sb('sh_' + k, [128, W], F32) for k in 'rkv'}
        dtmp = sb('dtmp', [128, W], F32)
        lw, av, kx, rn, kt, t1, Lc, G, Gi, Gm, Ah, Kh = (sb(n, [128, W], F32) for n in ('lw', 'av', 'kx', 'rn', 'kt', 't1', 'Lc', 'G', 'Gi', 'Gm', 'Ah', 'Kh'))
        sqb = sb('sqb', [128, W], BF16)
        ob = [sb('ob%d' % i, [128, W], BF16) for i in range(3)]
        bo = sb('bo', [128, W], F32)
        gct = [sb('gct%d' % i, [128, NCH], F32) for i in range(4)]
        nob = [0]

        def store(kind, pb, c0, n, fn):
            j = nob[0] % 3
            nob[0] += 1
            fn(ob[j][:, 0:n], ('ob', j))
            P.ld(cx.B_sc[kind][pb * 128:(pb + 1) * 128, c0:c0 + n], ob[j][:, 0:n], reads=[('ob', j)], writes=[('B_sc', kind)], tag='sc', slow=(n == 1))

        def halo_load(tile, key, r0, c0, n, col12):
            if c0 == 0:
                P.ms(POOL, tile[:, 0:1], 0.0, reads=[key], writes=[key])
                P.ld(tile[:, 1:1 + n], cx.projT[r0:r0 + 128, 0:n], writes=[key], tag='m')
            elif c0 == T:
                P.ld(tile[:, 0:1], cx.bshiftT[l, :, col12:col12 + 1], writes=[key], tag='m', slow=True)
                P.ld(tile[:, 1:2], cx.projT[r0:r0 + 128, T:T + 1], writes=[key], tag='m', slow=True)
            else:
                P.ld(tile[:, 0:1 + n], cx.projT[r0:r0 + 128, c0 - 1:c0 + n], writes=[key], tag='m')

        def shift(dst, dkey, tile, key, n, mucol):
            P.tt(POOL, dtmp[:, 0:n], tile[:, 0:n], tile[:, 1:1 + n], ALU.subtract, reads=[key], writes=['dtmp'])
            P.stt(dst[:, 0:n], dtmp[:, 0:n], pp[:, mucol:mucol + 1], tile[:, 1:1 + n], ALU.mult, ALU.add, reads=['dtmp', key, 'pp'], writes=[dkey])

        for (c0, n) in [(i * W, W) for i in range(T // W)] + [(T, 1)]:
            nch = max(1, n // 128)
            halo_load(sm, 'sm', R_SM, c0, n, 12)
            shift(smd, 'smd', sm, 'sm', n, PP_MUS)
            P.act(lwin[32:64, 0:n], smd[32:64, 0:n], AF.Tanh, reads=['smd'], writes=['lwin'])
            P.cp(POOL, lwin[64:96, 0:n], smd[64:96, 0:n], reads=['smd'], writes=['lwin'])
            for pb in range(4):
                for j, k in enumerate('rkv'):
                    halo_load(raw[k], 'raw_' + k, R_BR + j * 512 + pb * 128, c0, n, j * 4 + pb)
                    shift(sh[k], 'sh_' + k, raw[k], 'raw_' + k, n, PP_MU + j * 4 + pb)
                rs, ks, vs = sh['r'], sh['k'], sh['v']
                for jt, t0 in enumerate(range(0, n, 512)):
                    m = min(512, n - t0)
                    cs = slice(t0, t0 + m)
                    P.mm(ps[:, 2, 0:m], lwb[32:64, pb * 128:(pb + 1) * 128], lwin[32:64, cs], reads=['lwb', 'lwin'], writes=[('ps', 2)])
                    P.mm(ps[:, 3, 0:m], lwb[64:96, pb * 128:(pb + 1) * 128], lwin[64:96, cs], reads=['lwb', 'lwin'], writes=[('ps', 3)])
                    P.act(lw[:, cs], ps[:, 2, 0:m], AF.Sigmoid, reads=[('ps', 2), 'pp'], writes=['lw'], bias=pp[:, PP_W0 + pb:PP_W0 + pb + 1])
                    P.act(av[:, cs], ps[:, 3, 0:m], AF.Sigmoid, reads=[('ps', 3), 'pp'], writes=['av'], bias=pp[:, PP_A0 + pb:PP_A0 + pb + 1])
                P.ts(DVE, lw[:, 0:n], lw[:, 0:n], -RWKV_DS, ALU.mult, reads=['lw'], writes=['lw'])
                P.ts(DVE, kx[:, 0:n], ks[:, 0:n], pp[:, PP_XI + pb:PP_XI + pb + 1], ALU.mult, reads=['sh_k', 'pp'], writes=['kx'])
                P.act(sqb[:, 0:n], kx[:, 0:n], AF.Square, reads=['kx'], writes=['sqb'])
                _l2_rn(P, nc, ps, sqb, onesbd, n, rn, ('sqb', 'rn'))
                P.tt(DVE, kx[:, 0:n], kx[:, 0:n], rn[:, 0:n], ALU.mult, reads=['kx', 'rn'], writes=['kx'])
                P.ts(DVE, t1[:, 0:n], av[:, 0:n], pp[:, PP_ALPHA + pb:PP_ALPHA + pb + 1], ALU.mult, oma[:, pb:pb + 1], ALU.add, reads=['av', 'pp', 'oma'], writes=['t1'])
                P.tt(POOL, kt[:, 0:n], ks[:, 0:n], t1[:, 0:n], ALU.mult, reads=['sh_k', 't1'], writes=['kt'])
                P.tt(POOL, t1[:, 0:n], rs[:, 0:n], kt[:, 0:n], ALU.mult, reads=['sh_r', 'kt', 't1'], writes=['t1'])
                P.ts(DVE, sqb[:, 0:n], t1[:, 0:n], pp[:, PP_RHO + pb:PP_RHO + pb + 1], ALU.mult, reads=['t1', 'pp', 'rn'], writes=['sqb'])
                for jt, t0 in enumerate(range(0, n, 512)):
                    m = min(512, n - t0)
                    cs = slice(t0, t0 + m)
                    P.mm(ps[:, 4 + jt % 2, 0:m], onesbd, sqb[:, cs], reads=['sqb', 'cb'], writes=[('ps', 4 + jt % 2)])
                    P.tt(DVE, bo[:, cs], ps[:, 4 + jt % 2, 0:m], vs[:, cs], ALU.mult, reads=[('ps', 4 + jt % 2), 'sh_v'], writes=['bo'])
                P.ld(cx.B_bonus[pb * 128:(pb + 1) * 128, c0:c0 + n], bo[:, 0:n], reads=['bo'], writes=['B_bonus'], tag='sc', slow=(n == 1))
                P.op(DVE, lambda e, n=n: e.tensor_tensor_scan(out=Lc[:, 0:n], data0=rmask[:, 0:n], data1=lw[:, 0:n], initial=0.0, op0=ALU.mult, op1=ALU.add),
                     reads=['lw', 'rmask'], writes=['Lc'])
                P.act(G[:, 0:n], Lc[:, 0:n], AF.Exp, reads=['Lc'], writes=['G'])
                P.act(Gi[:, 0:n], Lc[:, 0:n], AF.Exp, reads=['Lc'], writes=['Gi'], scale=-1.0)
                P.tt(POOL, Gm[:, 0:n], Lc[:, 0:n], lw[:, 0:n], ALU.subtract, reads=['Lc', 'lw'], writes=['Gm'])
                P.act(Gm[:, 0:n], Gm[:, 0:n], AF.Exp, reads=['Gm'], writes=['Gm'])
                if n > 1:
                    Gv = G[:, 0:n].rearrange("p (c i) -> p c i", i=128)
                    GCb = Gv[:, :, 127:128].to_broadcast([128, nch, 128])
                    P.cp(POOL, gct[pb][:, c0 // 128:c0 // 128 + nch], Gv[:, :, 127], reads=['G'], writes=[('gct', pb)])
                    v3 = lambda ap: ap.rearrange("p (c i) -> p c i", i=128)
                else:
                    GCb = G[:, 0:1]
                    P.cp(POOL, gct[pb][:, 32:33], G[:, 0:1], reads=['G'], writes=[('gct', pb)])
                    v3 = lambda ap: ap
                P.tt(POOL, t1[:, 0:n], av[:, 0:n], kx[:, 0:n], ALU.mult, reads=['av', 'kx', 'sqb'], writes=['t1'])
                P.tt(DVE, Ah[:, 0:n], t1[:, 0:n], Gi[:, 0:n], ALU.mult, reads=['t1', 'Gi'], writes=['Ah'])
                P.tt(DVE, Kh[:, 0:n], kt[:, 0:n], Gi[:, 0:n], ALU.mult, reads=['kt', 'Gi'], writes=['Kh'])
                store('ah', pb, c0, n, lambda o, ok: P.cp(POOL, o, Ah[:, 0:n], reads=['Ah'], writes=[ok]))
                store('kh', pb, c0, n, lambda o, ok: P.cp(ACT, o, Kh[:, 0:n], reads=['Kh'], writes=[ok]))
                store('bh', pb, c0, n, lambda o, ok: P.tt(DVE, o, kx[:, 0:n], Gm[:, 0:n], ALU.mult, reads=['kx', 'Gm'], writes=[ok]))
                store('rh', pb, c0, n, lambda o, ok: P.tt(POOL, o, rs[:, 0:n], G[:, 0:n], ALU.mult, reads=['sh_r', 'G'], writes=[ok]))
                store('k2', pb, c0, n, lambda o, ok: P.tt(DVE, v3(o), v3(Kh[:, 0:n]), GCb, ALU.mult, reads=['Kh', 'G'], writes=[ok]))
                store('a2', pb, c0, n, lambda o, ok: P.tt(POOL, v3(o), v3(Ah[:, 0:n]), GCb, ALU.mult, reads=['Ah', 'G'], writes=[ok]))
                store('v', pb, c0, n, lambda o, ok: P.cp(ACT, o, vs[:, 0:n], reads=['sh_v'], writes=[ok]))
        for pb in range(4):
            P.ld(cx.B_gc[pb * 128:(pb + 1) * 128, :], gct[pb][:], reads=[('gct', pb)], writes=['B_gc'], tag='sc')
        for j, col in ((0, T - 1), (1, T)):
            P.ld(cx.o_bshift[l, j, :, 0:12], cx.projT[R_BR:R_BR + 1536, col:col + 1].rearrange("(b p) o -> p (b o)", p=128), writes=[('obs', j)], tag='so', slow=True)
            P.ld(cx.o_bshift[l, j, :, 12:13], cx.projT[R_SM:R_SM + 128, col:col + 1], writes=[('obs2', j)], tag='so', slow=True)
        P.emit()


def phase_scan(nc, cx, l, br, chunks=range(NCH)):
    P = Prog(nc)
    ps = cx.ps
    isA = br == 'A'
    SC = cx.A_sc if isA else cx.B_sc
    GC = cx.A_gc if isA else cx.B_gc
    names = dict(XA='k', XB='k', XK='k', XR='q', XBz='bz', XRz='rz', K2='k2', A2='a2', V='v') if isA else \
        dict(XA='ah', XB='bh', XK='kh', XR='rh', XBz='bh', XRz='rh', K2='k2', A2='a2', V='v')
    kinds = sorted(set(names.values()))
    with contextlib.ExitStack() as st:
        sb = lambda name, shape, dt: st.enter_context(nc.sbuf_tensor('s%s%d_' % (br, l) + name, shape, dt))
        cb = sb('cb', [128, 128 + 3 * 128], BF16)
        identb = cb[:, 0:128]
        cf = sb('cf', [128, 4 * 4 * 128], F32)
        ident4 = cf[:, 0:512].rearrange("p (h t) -> p h t", h=4)
        k01 = [cf[:, 512 * (1 + i):512 * (2 + i)].rearrange("p (h t) -> p h t", h=4) for i in range(3)]
        pp = sb('pp', [128, NPP], F32)
        P.dma(SP, lambda e: e.dma_start(out=cb[:], in_=cx.cb3), writes=['cb'], tag='c')
        P.dma(SP, lambda e: e.dma_start(out=cf[:], in_=cx.cf3), writes=['cf'], tag='c')
        P.dma(SP, lambda e: e.dma_start(out=pp[:], in_=cx.pp[l]), writes=['pp'], tag='c')
        op_t = {k: [sb('x_%s%d' % (k, i), [64, 8, 128], BF16) for i in range(2)] for k in kinds}
        tok = {k: [sb('t_%s%d' % (k, i), [128, 8, 64], BF16) for i in range(2)] for k in ('k2', 'a2', 'v')}
        er = [sb('er%d' % i, [2, 4, 8, 128], F32) for i in range(2)] if isA else None
        E = [[sb('E%d_%d' % (k, g), [128, 4, 128], F32) for g in range(2)] for k in range(5)] if isA else None
        NDT = F32
        Lb = [[sb('Lb%d_%d' % (g, i), [128, 4, 128], NDT) for i in range(2)] for g in range(2)]
        Uf = [sb('Uf%d' % g, [128, 4, 128], F32) for g in range(2)]
        Ub = [[Uf[g] if i == 0 else sb('Ub%d_%d' % (g, i), [128, 4, 128], NDT) for i in range(3)] for g in range(2)]
        Pf = [sb('Pf%d' % g, [128, 4, 128], F32) for g in range(2)]
        Pb = [sb('Pb%d' % g, [128, 4, 128], BF16) for g in range(2)]
        Mkb, RKb, RAb = ([sb('%s%d' % (n, g), [128, 4, 128], BF16) for g in range(2)] for n in ('Mkb', 'RKb', 'RAb'))
        Xb, Pnb = ([sb('%s%d' % (n, g), [128, 4, 64], BF16) for g in range(2)] for n in ('Xb', 'Pnb'))
        Zf, Zb = sb('Zf', [64, 8, 64], F32), sb('Zb', [64, 8, 64], BF16)
        Zt = sb('Ztmp', [64, 8, 64], F32)
        gct = sb('gct', [64, 8, NCH], F32)
        oT = [sb('oT%d' % i, [64, 8, 128], F32) for i in range(2)]
        zt = [sb('zt%d' % i, [64, 8, 128], F32) for i in range(2)]
        w1, w2 = sb('w1', [64, 8, 128], F32), sb('w2', [64, 8, 128], F32)
        wb = sb('wb', [64, 8, 128], BF16)
        gob = [sb('gob%d' % i, [64, 8, 128], BF16) for i in range(2)]
        ones64b = sb('ones64b', [64, 64], BF16)
        ones64f = sb('ones64f', [64, 64], F32)
        P.op(POOL, lambda e: e.memset(ones64b[:], 1.0), writes=['ones64b'])
        P.op(POOL, lambda e: e.memset(ones64f[:], 1.0), writes=['ones64f'])
        P.dma(SP, lambda e: e.dma_start(out=gct[:], in_=GC.rearrange("(h d) c -> d h c", d=64)), writes=['gct'], tag='c')
        if not isA:
            bon = [sb('bon%d' % i, [64, 8, 128], F32) for i in range(2)]
            gng, gnb = sb('gng', [64, 8], F32), sb('gnb', [64, 8], F32)
            P.dma(SP, lambda e: e.dma_start(out=gng[:], in_=cx.gn_g[l].rearrange("(h d) -> d h", d=64), allow_slow_non_contiguous=True), writes=['gng'], tag='c')
            P.dma(SP, lambda e: e.dma_start(out=gnb[:], in_=cx.gn_b[l].rearrange("(h d) -> d h", d=64), allow_slow_non_contiguous=True), writes=['gnb'], tag='c')
        P.op(POOL, lambda e: e.memset(Zf[:], 0.0), writes=['Zf'])
        P.op(POOL, lambda e: e.memset(Zb[:], 0.0), writes=['Zb'])
        zrow = R_AZ if isA else R_BZ
        grow = 0 if isA else 512
        st_in = cx.sA_in if isA else cx.sB_in
        st_out = cx.o_AS if isA else cx.o_BS
        flat = lambda ap: ap.rearrange("p h t -> p (h t)")
        b4 = lambda ap: ap.rearrange("p (h t) -> p h t", h=4)
        for c in chunks:
            b = c % 2
            c0 = c * 128
            n = 128 if c < 32 else 1
            slow = (n == 1)
            for k in kinds:
                if n == 1:
                    P.ms(POOL, op_t[k][b][:], 0.0, writes=[('x', k, b)])
                P.ld(op_t[k][b][:, :, 0:n], SC[k][:, c0:c0 + n].rearrange("(h d) t -> d h t", d=64), writes=[('x', k, b)], tag='ld', slow=slow)
            if n == 1:
                P.ms(POOL, zt[b][:], 0.0, writes=[('zt', b)])
            P.ld(zt[b][:, :, 0:n], cx.projT[zrow:zrow + 512, c0:c0 + n].rearrange("(h d) t -> d h t", d=64), writes=[('zt', b)], tag='ld', slow=slow)
            if not isA:
                if n == 1:
                    P.ms(POOL, bon[b][:], 0.0, writes=[('bon', b)])
                P.ld(bon[b][:, :, 0:n], cx.B_bonus[:, c0:c0 + n].rearrange("(h d) t -> d h t", d=64), writes=[('bon', b)], tag='ld', slow=slow)
            if isA:
                for kd in range(4):
                    P.ld(er[b][:, kd], cx.A_erows[kd, :, :, c0:c0 + 128], writes=[('er', b)], tag='ld')
            if c == 32:
                P.ld(st_out[l, 0], Zf[:], reads=['Zf'], writes=['st_out0'], tag='so')
                P.ld(Zf[:], st_in[l], reads=['st_out0'], writes=['Zf'], tag='ld')
                P.cp(ACT, Zb[:], Zf[:], reads=['Zf'], writes=['Zb'])
            X = {role: op_t[k][b] for role, k in names.items()}
            xk = {role: ('x', k, b) for role, k in names.items()}
            for j, k in enumerate(('k2', 'a2', 'v')):
                bank = 5 + j % 2
                pst = ps[:, bank, :].bitcast(BF16)[:, 0:512].rearrange("p (h d) -> p h d", d=64)
                for h in range(8):
                    P.tr(pst[:, h, :], op_t[k][b][:, h, :], identb[0:64, 0:64], reads=[('x', k, b), 'cb'], writes=[('ps', bank)])
                P.cp(ACT if j != 1 else DVE, tok[k][b][:], pst, reads=[('ps', bank)], writes=[('tok', k, b)])
            K2t, A2t, Vt = tok['k2'][b], tok['a2'][b], tok['v'][b]
            tkk = [('tok', 'k2', b), ('tok', 'a2', b), ('tok', 'v', b)]
            for g in range(2):
                hs = list(range(4 * g, 4 * g + 4))
                if isA:
                    combos = ((0, 1, 1), (1, 0, 2), (2, 0, 2), (2, 3, 3), (1, 3, 3))
                    for k5, (ka, kb_, mi) in enumerate(combos):
                        bank = k5 % 2
                        for hh, h in enumerate(hs):
                            o_ = ps[:, bank, hh * 128:(hh + 1) * 128]
                            P.mm(o_, er[b][0:2, ka, h, :], er[b][0:2, kb_, h, :], True, False, reads=[('er', b)], writes=[('ps', bank)])
                            P.mm(o_, identb, cb[:, 128 * mi:128 * (mi + 1)], False, True, reads=['cb'], writes=[('ps', bank)])
                        P.act(E[k5][g][:], b4(ps[:, bank, :]), AF.Exp, reads=[('ps', bank)], writes=[('E', k5, g)])
                    mult = [E[k5][g][:] for k5 in range(5)]
                    mkeys = [('E', k5, g) for k5 in range(5)]
                else:
                    mult = [k01[0], k01[1], k01[1], k01[2], k01[2]]
                    mkeys = ['cf'] * 5
                specs = (('XB', 'XA'), ('XA', 'XB'), ('XK', 'XB'), ('XK', 'XR'), ('XA', 'XR'))
                dsts = ((Lb[g][0], ('Lb', g, 0)), (Uf[g], ('Uf', g)), (Mkb[g], ('Mkb', g)), (RKb[g], ('RKb', g)), (RAb[g], ('RAb', g)))
                for k5, ((ra, rb), (dst, dkey)) in enumerate(zip(specs, dsts)):
                    bank = 2 + k5 % 3
                    for hh, h in enumerate(hs):
                        P.mm(ps[:, bank, hh * 128:(hh + 1) * 128], X[ra][:, h, :], X[rb][:, h, :], reads=[xk[ra], xk[rb]], writes=[('ps', bank)])
                    P.tt(DVE, dst[:], b4(ps[:, bank, :]), mult[k5], ALU.mult, reads=[('ps', bank), mkeys[k5]], writes=[dkey])
                P.tt(POOL, Pf[g][:], ident4, Uf[g][:], ALU.subtract, reads=['cf', ('Uf', g)], writes=[('Pf', g)])
            ukey = lambda g, i: ('Uf', g) if i == 0 else ('Ub', g, i)
            for lvl in range(1, 7):
                i0, i1 = (lvl - 1) % 2, lvl % 2
                u0i, u1i = (0 if lvl == 1 else 1 + lvl % 2), 1 + (lvl + 1) % 2
                for g in range(2):
                    for hh in range(4):
                        P.mm(ps[:, g, hh * 128:(hh + 1) * 128], Ub[g][u0i][:, hh, :], Lb[g][i0][:, hh, :], reads=[ukey(g, u0i), ('Lb', g, i0)], writes=[('ps', g)])
                    if lvl < 6:
                        for hh in range(4):
                            P.mm(ps[:, 2 + g, hh * 128:(hh + 1) * 128], Lb[g][i0][:, hh, :], Ub[g][u0i][:, hh, :], reads=[ukey(g, u0i), ('Lb', g, i0)], writes=[('ps', 2 + g)])
                    P.cp(ACT, Lb[g][i1][:], b4(ps[:, g, :]), reads=[('ps', g)], writes=[('Lb', g, i1)])
                    if lvl < 6:
                        P.cp(ACT, Ub[g][u1i][:], b4(ps[:, 2 + g, :]), reads=[('ps', 2 + g)], writes=[ukey(g, u1i)])
                for g in range(2):
                    for hh in range(4):
                        P.mm(ps[:, 4 + g, hh * 128:(hh + 1) * 128], Lb[g][i1][:, hh, :], Pf[g][:, hh, :], reads=[('Lb', g, i1), ('Pf', g)], writes=[('ps', 4 + g)])
                    P.tt(DVE, Pf[g][:], Pf[g][:], b4(ps[:, 4 + g, :]), ALU.add, reads=[('ps', 4 + g), ('Pf', g)], writes=[('Pf', g)])
            for g in range(2):
                P.cp(POOL, Pb[g][:], Pf[g][:], reads=[('Pf', g)], writes=[('Pb', g)])
            for g in range(2):
                hs = list(range(4 * g, 4 * g + 4))
                gs = slice(4 * g, 4 * g + 4)
                bx = 6 + g
                psX = ps[:, bx, 0:256].rearrange("p (h v) -> p h v", v=64)
                for hh, h in enumerate(hs):
                    P.mm(psX[:, hh, :], X['XBz'][:, h, :], Zb[:, h, :], True, False, reads=[xk['XBz'], 'Zb'], writes=[('ps', bx)])
                    P.mm(psX[:, hh, :], Mkb[g][:, hh, :], Vt[:, h, :], False, True, reads=[('Mkb', g), tkk[2]], writes=[('ps', bx)])
                P.cp(ACT, Xb[g][:], psX, reads=[('ps', bx)], writes=[('Xb', g)])
                psP = ps[:, bx, 256:512].rearrange("p (h v) -> p h v", v=64)
                for hh in range(4):
                    P.mm(psP[:, hh, :], Pb[g][:, hh, :], Xb[g][:, hh, :], reads=[('Pb', g), ('Xb', g)], writes=[('ps', bx)])
                P.act(Pnb[g][:], psP, AF.Copy, reads=[('ps', bx)], writes=[('Pnb', g)], scale=-1.0)
                bo = g
                psO = b4(ps[0:64, bo, :])
                for hh, h in enumerate(hs):
                    P.mm(psO[:, hh, :], Zb[:, h, :], X['XRz'][:, h, :], True, False, reads=['Zb', xk['XRz']], writes=[('ps', bo)])
                    P.mm(psO[:, hh, :], Vt[:, h, :], RKb[g][:, hh, :], False, False, reads=[tkk[2], ('RKb', g)], writes=[('ps', bo)])
                    P.mm(psO[:, hh, :], Pnb[g][:, hh, :], RAb[g][:, hh, :], False, True, reads=[('Pnb', g), ('RAb', g)], writes=[('ps', bo)])
                P.cp(DVE, oT[b][:, gs, :], psO, reads=[('ps', bo)], writes=[('oT', b)])
                bz = 2 + g
                psZ = ps[0:64, bz, 0:256].rearrange("p (h v) -> p h v", v=64)
                for hh, h in enumerate(hs):
                    P.mm(psZ[:, hh, :], K2t[:, h, :], Vt[:, h, :], True, False, reads=[tkk[0], tkk[2]], writes=[('ps', bz)])
                    P.mm(psZ[:, hh, :], A2t[:, h, :], Pnb[g][:, hh, :], False, True, reads=[tkk[1], ('Pnb', g)], writes=[('ps', bz)])
                P.tt(DVE, Zt[:, gs, :], Zf[:, gs, :], gct[:, gs, c:c + 1].to_broadcast([64, 4, 64]), ALU.mult, reads=['Zf', 'gct'], writes=['Zt'])
                P.tt(DVE, Zf[:, gs, :], Zt[:, gs, :], psZ, ALU.add, reads=['Zt', ('ps', bz)], writes=['Zf'])
            P.cp(ACT, Zb[:], Zf[:], reads=['Zf'], writes=['Zb'])
            o = oT[b]
            P.act(zt[b][:], zt[b][:], AF.Silu, reads=[('zt', b)], writes=[('zt', b)])
            if isA:
                P.act(wb[:], o[:], AF.Square, reads=[('oT', b)], writes=['wb'])
                for j in range(2):
                    js = slice(j * 512, (j + 1) * 512)
                    P.mm(ps[0:64, 4 + j, :], ones64b[:], flat(wb[:])[:, js], reads=['wb', 'ones64b'], writes=[('ps', 4 + j)])
                    P.act(flat(w1[:])[:, js], ps[0:64, 4 + j, :], AF.Ln, reads=[('ps', 4 + j)], writes=['w1'], scale=1.0 / 64, bias=1e-6)
                P.act(w1[:], w1[:], AF.Exp, reads=['w1'], writes=['w1'], scale=-0.5)
                P.stt(w2[:], o[:], pp[0:64, PP_NORMA:PP_NORMA + 1], w1[:], ALU.mult, ALU.mult, reads=[('oT', b), 'pp', 'w1'], writes=['w2'])
            else:
                for j in range(2):
                    js = slice(j * 512, (j + 1) * 512)
                    P.mm(ps[0:64, 4 + j, :], ones64f[:], flat(o[:])[:, js], reads=[('oT', b), 'ones64f'], writes=[('ps', 4 + j)])
                    P.stt(flat(w2[:])[:, js], ps[0:64, 4 + j, :], -1.0 / 64, flat(o[:])[:, js], ALU.mult, ALU.add, reads=[('ps', 4 + j), ('oT', b)], writes=['w2'])
                P.act(w1[:], w2[:], AF.Square, reads=['w2'], writes=['w1'])
                for j in range(2):
                    js = slice(j * 512, (j + 1) * 512)
                    P.mm(ps[0:64, 4 + j, :], ones64f[:], flat(w1[:])[:, js], reads=['w1', 'ones64f'], writes=[('ps', 4 + j)])
                    P.act(flat(w1[:])[:, js], ps[0:64, 4 + j, :], AF.Ln, reads=[('ps', 4 + j)], writes=['w1'], scale=1.0 / 64, bias=64e-5)
                P.act(w1[:], w1[:], AF.Exp, reads=['w1'], writes=['w1'], scale=-0.5)
                P.tt(DVE, w2[:], w2[:], w1[:], ALU.mult, reads=['w2', 'w1'], writes=['w2'])
                P.tt(POOL, w2[:], w2[:], gng[:].unsqueeze(2).to_broadcast([64, 8, 128]), ALU.mult, reads=['w2', 'gng'], writes=['w2'])
                P.tt(POOL, w2[:], w2[:], gnb[:].unsqueeze(2).to_broadcast([64, 8, 128]), ALU.add, reads=['w2', 'gnb'], writes=['w2'])
                P.tt(POOL, w2[:], w2[:], bon[b][:], ALU.add, reads=['w2', ('bon', b)], writes=['w2'])
            P.tt(DVE, gob[b][:], w2[:], zt[b][:], ALU.mult, reads=['w2', ('zt', b)], writes=[('gob', b)])
            P.ld(cx.gatedT[grow:grow + 512, c0:c0 + n].rearrange("(h d) t -> d h t", d=64), gob[b][:, :, 0:n], reads=[('gob', b)], writes=['gatedT'], tag='go', slow=slow)
        if NCH - 1 in chunks:
            P.dma(SP, lambda e: e.dma_start(out=st_out[l, 1], in_=Zf[:]), reads=['Zf'], writes=['st_out1'], tag='so')
        P.emit()


def phase_mixCs(nc, cx, l):
    P = Prog(nc)
    ps = cx.ps
    NPG = 128
    with contextlib.ExitStack() as st:
        sb = lambda name, shape, dt: st.enter_context(nc.sbuf_tensor('cs%d_' % l + name, shape, dt))
        cf = sb('cf', [128, 384], F32)
        identf, tri_gt = cf[:, 0:128], cf[:, 256:384]
        bm = sb('bm', [8, 512], F32)
        idx = sb('idx', [128, 1], I32)
        rows = sb('rows', [128, 2056], F32)
        qb, kb, vb, fb = rows[:, 0:512], rows[:, 512:1024], rows[:, 1024:1536], rows[:, 2048:2056]
        negb = sb('negb', [128, 8], F32)
        ones = sb('ones', [128, 128], F32)
        lfp, Cw, sc, pr = (sb(n, [128, 128, 8], F32) for n in ('lfp', 'Cw', 'sc', 'pr'))
        kch = [sb('kch%d' % i, [128, 16, 512], F32) for i in range(2)]
        lfn, tot, base, mloc, snew, pnew, den, rden, mxb = (sb(n, [128, 8], F32) for n in ('lfn', 'tot', 'base', 'mloc', 'snew', 'pnew', 'den', 'rden', 'mxb'))
        tmp512 = sb('tmp512', [128, 512], F32)
        mx8, D8 = sb('mx8', [8, 1], F32), sb('D8', [8, 8], F32)
        accs = sb('accs', [8, 512], F32)
        oc, zc = sb('oc', [128, 4], F32), sb('zc', [128, 4], F32)
        gb = sb('gb', [128, 4], BF16)
        P.ld(cf[:], cx.cf32s, writes=['cf'], tag='c')
        P.ld(bm[:], cx.bmask, writes=['bm'], tag='c')
        P.ld(idx[:], cx.ptab, writes=['idx'], tag='c')
        P.ld(negb[:], cx.b_f[l:l + 1, :].partition_broadcast(128), writes=['negb'], tag='c')
        P.ts(POOL, negb[:], negb[:], -1.0, ALU.mult, reads=['negb'], writes=['negb'])
        P.ms(POOL, ones[:], 1.0, writes=['ones'])
        for j, r0 in enumerate((R_CQ, R_CK, R_CV, R_CZ)):
            P.ld(cx.srow[0:1, j * 512:(j + 1) * 512].rearrange("o c -> c o"), cx.projT[r0:r0 + 512, T:T + 1], writes=[('srow', j)], tag='sr', slow=True)
        P.ld(cx.srow[0:1, 2048:2056].rearrange("o c -> c o"), cx.projT[R_SM + SM_F:R_SM + SM_F + 8, T:T + 1], writes=[('srow', 4)], tag='sr', slow=True)
        P.ld(rows[:], cx.srow[0:1, :].partition_broadcast(128), reads=[('srow', j) for j in range(5)], writes=['rows'], tag='c')
        P.tt(DVE, lfn[:], fb, negb[:], ALU.subtract, reads=['rows', 'negb'], writes=['lfn'])
        P.act(lfn[:], lfn[:], AF.Exp, reads=['lfn'], writes=['lfn'], scale=-1.0)
        P.act(lfn[:], lfn[:], AF.Ln, reads=['lfn'], writes=['lfn'], bias=1.0)
        P.ts(DVE, lfn[:], lfn[:], -1.0, ALU.mult, reads=['lfn'], writes=['lfn'])
        P.ld(cx.o_clf[l, :, T:T + 1].rearrange("h o -> o h"), lfn[0:1, :], reads=['lfn'], writes=['o_clf_s'], tag='so', slow=True)
        ioff = bass.IndirectOffsetOnAxis(ap=idx[:, 0:1], axis=0)
        P.dma(POOL, lambda e: e.indirect_dma_start(out=lfp[:].rearrange("p t h -> p (t h)"), out_offset=None,
                                                   in_=cx.cache_lf.rearrange("l n t h -> (l n) (t h)"), in_offset=ioff,
                                                   element_offset=l * 1280 * 1024), reads=['idx'], writes=['lfp'], tag='ig')
        for h in range(8):
            P.op(DVE, lambda e, h=h: e.tensor_tensor_scan(out=Cw[:, :, h], data0=ones[:, :], data1=lfp[:, :, h], initial=0.0, op0=ALU.mult, op1=ALU.add),
                 reads=['lfp', 'ones'], writes=['Cw'])
        P.cp(DVE, tot[:], Cw[:, 127, :], reads=['Cw'], writes=['tot'])
        P.mm(ps[:, 0, 0:8], tri_gt, tot[:], reads=['cf', 'tot'], writes=[('ps', 0)])
        P.tt(DVE, base[:], tot[:], ps[:, 0, 0:8], ALU.add, reads=['tot', ('ps', 0)], writes=['base'])
        P.tt(DVE, base[:], base[:], lfn[:], ALU.add, reads=['base', 'lfn'], writes=['base'])
        P.tt(DVE, Cw[:], base[:].unsqueeze(1).to_broadcast([128, 128, 8]), Cw[:], ALU.subtract, reads=['base', 'Cw'], writes=['Cw'])
        ck = cx.cache_k.rearrange("l n t h d -> (l n) (t h d)")
        cv = cx.cache_v.rearrange("l n t h d -> (l n) (t h d)")
        ng = [0]

        def gather(src, tc):
            b = ng[0] % 2
            ng[0] += 1
            P.dma(POOL, lambda e: e.indirect_dma_start(out=kch[b][:].rearrange("p t c -> p (t c)"), out_offset=None, in_=src, in_offset=ioff,
                                                       element_offset=l * 1280 * 65536 + tc * 8192),
                  reads=['idx'], writes=[('kch', b)], tag='ig')
            return b
        for tc in range(8):
            b = gather(ck, tc)
            eng = DVE if tc % 2 == 0 else POOL
            P.tt(eng, kch[b][:], kch[b][:], qb.unsqueeze(1).to_broadcast([128, 16, 512]), ALU.mult, reads=[('kch', b), 'rows'], writes=[('kch', b)])
            P.op(DVE, lambda e, b=b, tc=tc: e.tensor_reduce(out=sc[:, tc * 16:(tc + 1) * 16, :].rearrange("p t h -> p (t h)"),
                                                           in_=kch[b][:].rearrange("p t (h d) -> p (t h) d", d=64), axis=AX.X, op=ALU.add),
                 reads=[('kch', b)], writes=['sc'])
        P.stt(sc[:], sc[:], 0.125, Cw[:], ALU.mult, ALU.add, reads=['sc', 'Cw'], writes=['sc'])
        P.tt(DVE, tmp512[:], qb, kb, ALU.mult, reads=['rows'], writes=['tmp512'])
        P.op(DVE, lambda e: e.tensor_reduce(out=snew[:], in_=tmp512[:].rearrange("p (h d) -> p h d", d=64), axis=AX.X, op=ALU.add), reads=['tmp512'], writes=['snew'])
        P.ts(DVE, snew[:], snew[:], 0.125, ALU.mult, reads=['snew'], writes=['snew'])
        P.op(DVE, lambda e: e.tensor_reduce(out=mloc[:], in_=sc[:].rearrange("p t h -> p h t"), axis=AX.X, op=ALU.max), reads=['sc'], writes=['mloc'])
        P.tt(DVE, mloc[:], mloc[:], snew[:], ALU.max, reads=['mloc', 'snew'], writes=['mloc'])
        P.tr(ps[0:8, 1, 0:128], mloc[:], identf, reads=['mloc', 'cf'], writes=[('ps', 1)])
        P.op(DVE, lambda e: e.tensor_reduce(out=mx8[:], in_=ps[0:8, 1, 0:128], axis=AX.X, op=ALU.max), reads=[('ps', 1)], writes=['mx8'])
        P.ts(DVE, D8[:], identf[0:8, 0:8], mx8[:, 0:1], ALU.mult, reads=['cf', 'mx8'], writes=['D8'])
        P.mm(ps[:, 2, 0:8], ones[0:8, :], D8[:], reads=['ones', 'D8'], writes=[('ps', 2)])
        P.cp(DVE, mxb[:], ps[:, 2, 0:8], reads=[('ps', 2)], writes=['mxb'])
        P.tt(DVE, sc[:], sc[:], mxb[:].unsqueeze(1).to_broadcast([128, 128, 8]), ALU.subtract, reads=['sc', 'mxb'], writes=['sc'])
        P.act(pr[:], sc[:], AF.Exp, reads=['sc'], writes=['pr'])
        P.tt(DVE, pnew[:], snew[:], mxb[:], ALU.subtract, reads=['snew', 'mxb'], writes=['pnew'])
        P.act(pnew[:], pnew[:], AF.Exp, reads=['pnew'], writes=['pnew'])
        P.op(DVE, lambda e: e.tensor_reduce(out=den[:], in_=pr[:].rearrange("p t h -> p h t"), axis=AX.X, op=ALU.add), reads=['pr'], writes=['den'])
        P.mm(ps[:, 3, 0:8], ones[:, :], den[:], reads=['ones', 'den'], writes=[('ps', 3)])
        P.tt(DVE, den[:], ps[:, 3, 0:8], pnew[:], ALU.add, reads=[('ps', 3), 'pnew', 'den'], writes=['den'])
        P.op(DVE, lambda e: e.reciprocal(out=rden[:], in_=den[:]), reads=['den'], writes=['rden'])
        P.tt(DVE, pr[:], pr[:], rden[:].unsqueeze(1).to_broadcast([128, 128, 8]), ALU.mult, reads=['pr', 'rden'], writes=['pr'])
        P.tt(DVE, pnew[:], pnew[:], rden[:], ALU.mult, reads=['pnew', 'rden'], writes=['pnew'])
        first = True
        for tc in range(8):
            b = gather(cv, tc)
            for t in range(16):
                P.mm(ps[0:8, 4, :], pr[:, tc * 16 + t, :], kch[b][:, t, :], first, False, reads=['pr', ('kch', b)], writes=[('ps', 4)])
                first = False
        P.mm(ps[0:8, 4, :], pnew[0:1, :], vb[0:1, :], False, True, reads=['pnew', 'rows'], writes=[('ps', 4)])
        P.tt(DVE, accs[:], ps[0:8, 4, :], bm[:], ALU.mult, reads=[('ps', 4), 'bm'], writes=['accs'])
        for j in range(4):
            P.mm(ps[:, 5, j:j + 1], accs[:, j * 128:(j + 1) * 128], ones[0:8, 0:1], reads=['accs', 'ones'], writes=[('ps', 5)])
        P.cp(DVE, oc[:], ps[:, 5, 0:4], reads=[('ps', 5)], writes=['oc'])
        P.ld(zc[:], cx.projT[R_CZ:R_CZ + 512, T:T + 1].rearrange("(j p) o -> p (j o)", p=128), writes=['zc'], tag='c', slow=True)
        P.act(zc[:], zc[:], AF.Silu, reads=['zc'], writes=['zc'])
        P.tt(DVE, gb[:], oc[:], zc[:], ALU.mult, reads=['oc', 'zc'], writes=['gb'])
        P.ld(cx.gatedT[1024:1536, T:T + 1].rearrange("(j p) o -> p (j o)", p=128), gb[:], reads=['gb'], writes=['gatedT'], tag='go', slow=True)
        P.emit()


def build(debug=None, phases=None, ext_in=()):
    nc = bass.Bass("TRN2", target_bir_lowering=False)
    cx = Ctx()
    di = lambda name, shape, dt=F32: nc.dram_tensor(name, shape, dt, kind="ExternalInput").ap()
    do = lambda name, shape, dt=F32: nc.dram_tensor(name, shape, dt, kind="ExternalOutput").ap()
    dbg = debug is not None

    def scratch(name, shape, dt=F32):
        if name in ext_in:
            return di(name, shape, dt)
        return do(name, shape, dt) if dbg else nc.dram_tensor(name, shape, dt).ap()
    cx.xT = di("xT", [D, TS])
    cx.xtok = di("xtok", [TS, D])
    cx.w_in = di("w_in", [DEPTH, D, NROW])
    cx.w_out = di("w_out", [DEPTH, D, D])
    cx.ln_g, cx.ln_b = di("ln_g", [DEPTH, D]), di("ln_b", [DEPTH, D])
    cx.identb = di("identb", [128, 128], BF16)
    cx.pp = di("pp", [DEPTH, 128, NPP])
    cx.pool_w = di("pool_w", [DEPTH, 4, 128, 128])
    cx.dbufT = di("dbufT", [DEPTH, 512, 15])
    cx.o_dbuf = do("o_dbuf", [DEPTH, 2, 512, 15])
    cx.cf32 = di("cf32", [128, 256])
    cx.cb16 = di("cb16", [128, 128 + 4 * 512], BF16)
    cx.b_f = di("b_f", [DEPTH, 8])
    cx.o_clf = do("o_clf", [DEPTH, 8, TS])
    cx.o_ckv = do("o_ckv", [DEPTH, 1024, TS])
    cx.cscr = scratch("cscr", [6, T], BF16)
    cx.cb2 = di("cb2", [128, 256], BF16)
    cx.cb3 = di("cb3", [128, 512], BF16)
    cx.cf3 = di("cf3", [128, 2048])
    cx.sel8 = di("sel8", [8, 512])
    cx.aconvT = di("aconvT", [DEPTH, 1536, 3])
    cx.sA_in = di("sA_in", [DEPTH, 64, 8, 64])
    cx.sB_in = di("sB_in", [DEPTH, 64, 8, 64])
    cx.gn_g, cx.gn_b = di("gn_g", [DEPTH, 512]), di("gn_b", [DEPTH, 512])
    cx.cf32s = di("cf32s", [128, 384])
    cx.bmask = di("bmask", [8, 512])
    cx.ptab = di("ptab", [128, 1], I32)
    cx.cache_k = di("cache_k", [DEPTH, 1280, 128, 8, 64])
    cx.cache_v = di("cache_v", [DEPTH, 1280, 128, 8, 64])
    cx.cache_lf = di("cache_lf", [DEPTH, 1280, 128, 8])
    cx.srow = scratch("srow", [1, 2056])
    cx.lora_w = di("lora_w", [DEPTH, 128, 512])
    cx.bshiftT = di("bshiftT", [DEPTH, 128, 13])
    cx.o_bshift = do("o_bshift", [DEPTH, 2, 128, 13])
    cx.o_aconv = do("o_aconv", [DEPTH, 2, 1536, 3])
    cx.o_AS = do("o_AS", [DEPTH, 2, 64, 8, 64])
    cx.o_BS = do("o_BS", [DEPTH, 2, 64, 8, 64])
    cx.A_sc = {k: scratch("A_" + k, [512, TP], BF16) for k in A_KINDS}
    cx.B_sc = {k: scratch("B_" + k, [512, TP], BF16) for k in B_KINDS}
    cx.A_erows = scratch("A_erows", [4, 2, 8, TP])
    cx.A_gc, cx.B_gc = scratch("A_gc", [512, NCH]), scratch("B_gc", [512, NCH])
    cx.A_mrows = scratch("A_mrows", [5, 8, TP])
    cx.B_bonus = scratch("B_bonus", [512, TP])
    cx.y = do("y", [TS, D])
    cx.projT = scratch("projT", [NROW, TS])
    cx.gatedT = scratch("gatedT", [D, TS], BF16)
    cx.y1 = scratch("y1", [TS, D])
    cx.x2T = scratch("x2T", [D, TS], BF16)
    cx.ps = nc.alloc_psum_tensor("ps", [128, 8, 512], F32)
    if phases is None:
        phases = [('in', 0), ('out', 0), ('in', 1), ('out', 1)]
    for ph, l in phases:
        if ph == 'in':
            phase_inproj(nc, cx, l)
        elif ph == 'out':
            phase_outproj(nc, cx, l)
        elif ph == 'mixD':
            phase_mixD(nc, cx, l)
        elif ph == 'mixC':
            phase_mixC(nc, cx, l)
        elif ph == 'preA':
            phase_rowsA(nc, cx, l)
            phase_preA(nc, cx, l)
        elif ph == 'preB':
            phase_preB(nc, cx, l)
        elif ph == 'scanA':
            phase_scan(nc, cx, l, 'A')
        elif ph == 'scanA_few':
            phase_scan(nc, cx, l, 'A', chunks=[0, 1, 32])
        elif ph == 'scanB_few':
            phase_scan(nc, cx, l, 'B', chunks=[0, 1, 32])
        elif ph == 'scanB':
            phase_scan(nc, cx, l, 'B')
        elif ph == 'mixCs':
            phase_mixCs(nc, cx, l)
        elif ph == 'mixC1':
            phase_mixC(nc, cx, l, heads=[0], do_sample=False)
    return nc


def _cf32():
    a = np.zeros((128, 256), np.float32)
    a[:, 0:128] = np.eye(128)
    a[:, 128:256] = np.triu(np.ones((128, 128)), 1)
    return a


def _cb16():
    import ml_dtypes
    a = np.zeros((128, 128 + 4 * 512), np.float32)
    a[:, 0:128] = np.eye(128)
    k, q = np.arange(128)[:, None], np.arange(512)[None, :]
    for j in range(4):
        a[:, 128 + j * 512:128 + (j + 1) * 512] = np.where(j * 128 + k > q, -30000.0, 0.0)
    return a.astype(ml_dtypes.bfloat16)


def _cb2():
    import ml_dtypes
    a = np.zeros((128, 256), np.float32)
    a[:, 0:128] = np.eye(128)
    a[:, 128:256] = np.kron(np.eye(2), np.ones((64, 64)))
    return a.astype(ml_dtypes.bfloat16)


def _cb3():
    import ml_dtypes
    p, f = np.arange(128)[:, None], np.arange(128)[None, :]
    a = np.zeros((128, 512), np.float32)
    a[:, 0:128] = np.eye(128)
    a[:, 128:256] = np.where(f >= p, -30000.0, 0.0)
    a[:, 256:384] = np.where(p >= f, -30000.0, 0.0)
    a[:, 384:512] = np.where(p > f, -30000.0, 0.0)
    return a.astype(ml_dtypes.bfloat16)


def _cf3():
    p, f = np.arange(128)[:, None], np.arange(128)[None, :]
    mats = [np.eye(128), (f < p) * 1.0, (p < f) * 1.0, (p <= f) * 1.0]
    return np.concatenate([np.tile(m, (1, 4)) for m in mats], axis=1).astype(np.float32)


def _lora_w(I):
    a = np.zeros((DEPTH, 128, 512), np.float32)
    a[:, 32:64] = I['w_up_B']
    a[:, 64:96] = I['a_up_B']
    return a


def _bshiftT(I, c):
    sh = I['state_B_shift'][:, c]
    a = np.zeros((DEPTH, 128, 13), np.float32)
    a[:, :, 0:12] = sh[:, 0:1536].reshape(DEPTH, 12, 128).transpose(0, 2, 1)
    a[:, 32:64, 12] = sh[:, 1536:1568]
    a[:, 64:96, 12] = sh[:, 1568:1600]
    return a


def host_inputs(c, I):
    import ml_dtypes
    n = c % 2
    perm = _col_perm()
    w = I['w_in']
    w_p = np.zeros((DEPTH, D, NROW), np.float32)
    w_p[:, :, perm >= 0] = w[:, :, perm[perm >= 0]]
    xtok = np.concatenate([I['x_prompt'][n], I['x_sample'][c]], axis=0)
    return {"xT": np.ascontiguousarray(xtok.T), "xtok": np.ascontiguousarray(xtok), "w_in": w_p, "w_out": I['w_out'],
            "ln_g": I['ln_g'], "ln_b": I['ln_b'], "identb": np.eye(128, dtype=np.float32).astype(ml_dtypes.bfloat16),
            "pp": pack_pp(I), "pool_w": I['pool_w_D'], "cf32": _cf32(), "cb16": _cb16(), "b_f": I['b_f_C'],
            "cb2": _cb2(), "cb3": _cb3(), "cf3": _cf3(), "sel8": np.kron(np.eye(8, dtype=np.float32), np.ones((1, 64), np.float32)),
            "aconvT": np.ascontiguousarray(I['state_A_conv'][:, c].transpose(0, 2, 1)),
            "sA_in": np.ascontiguousarray(I['state_A_S'][:, c].transpose(0, 2, 1, 3)),
            "sB_in": np.ascontiguousarray(I['state_B_S'][:, c].transpose(0, 3, 1, 2)),
            "cf32s": np.concatenate([_cf32(), np.tril(np.ones((128, 128), np.float32), -1)], axis=1),
            "bmask": np.kron(np.eye(8, dtype=np.float32), np.ones((1, 64), np.float32)),
            "ptab": np.ascontiguousarray(I['page_table'][c].reshape(128, 1).astype(np.int32)),
            "cache_k": I['cache_C_k'], "cache_v": I['cache_C_v'], "cache_lf": I['cache_C_logf'],
            "gn_g": I['gn_g_B'], "gn_b": I['gn_b_B'], "lora_w": _lora_w(I), "bshiftT": _bshiftT(I, c),
            "dbufT": np.ascontiguousarray(I['state_D_buf'][:, c].transpose(0, 2, 1))}


FULL_PHASES = [(ph, l) for l in range(DEPTH) for ph in ('in', 'preA', 'scanA', 'preB', 'scanB', 'mixC', 'mixCs', 'mixD', 'out')]


def _unpack_bshift(a):
    o = np.zeros(1600, np.float32)
    o[0:1536] = a[:, 0:12].T.reshape(1536)
    o[1536:1568] = a[32:64, 12]
    o[1568:1600] = a[64:96, 12]
    return o


def kernel(**I):
    I = {k: np.asarray(v) for k, v in I.items()}
    nc = build(phases=FULL_PHASES)
    shared = None
    in_maps = []
    for c in range(8):
        m = host_inputs(c, I)
        if shared is None:
            shared = m
        else:
            for k in ('w_in', 'w_out', 'ln_g', 'ln_b', 'identb', 'pp', 'pool_w', 'cf32', 'cb16', 'b_f', 'cb2', 'cb3', 'cf3', 'sel8',
                      'cf32s', 'bmask', 'cache_k', 'cache_v', 'cache_lf', 'gn_g', 'gn_b', 'lora_w'):
                m[k] = shared[k]
        in_maps.append(m)
    res = run_bass_kernel_spmd(nc, in_maps, core_ids=list(range(8)))
    R = res.results
    f = np.float32
    NB, ND = 2, 8
    y_prompt = np.stack([R[n]['y'][0:T] for n in range(NB)]).astype(f)
    y_sample = np.stack([R[c]['y'][T:T + 1] for c in range(ND)]).astype(f)

    def per(fn, j):
        cores = range(NB) if j == 0 else range(ND)
        return np.stack([np.stack([fn(R[c], l) for c in cores]) for l in range(DEPTH)]).astype(f)
    outs = {}
    for j, pre in ((0, 'p'), (1, 's')):
        outs[pre + '_A_S'] = per(lambda r, l: r['o_AS'][l, j].transpose(1, 0, 2), j)
        outs[pre + '_A_conv'] = per(lambda r, l: r['o_aconv'][l, j].T, j)
        outs[pre + '_B_S'] = per(lambda r, l: r['o_BS'][l, j].transpose(1, 2, 0), j)
        outs[pre + '_B_shift'] = per(lambda r, l: _unpack_bshift(r['o_bshift'][l, j]), j)
        outs[pre + '_D_buf'] = per(lambda r, l: r['o_dbuf'][l, j].T, j)
    cs = {0: slice(0, T), 1: slice(T, T + 1)}
    for j, pre in ((0, 'p'), (1, 's')):
        n = T if j == 0 else 1
        outs[pre + '_C_k'] = per(lambda r, l: r['o_ckv'][l, 0:512, cs[j]].T.reshape(n, NH, HD), j)
        outs[pre + '_C_v'] = per(lambda r, l: r['o_ckv'][l, 512:1024, cs[j]].T.reshape(n, NH, HD), j)
        outs[pre + '_C_logf'] = per(lambda r, l: r['o_clf'][l, :, cs[j]].T, j)
    order = ['p_A_S', 'p_A_conv', 'p_B_S', 'p_B_shift', 'p_C_k', 'p_C_v', 'p_C_logf', 'p_D_buf',
             's_A_S', 's_A_conv', 's_B_S', 's_B_shift', 's_C_k', 's_C_v', 's_C_logf', 's_D_buf']
    return (y_prompt, y_sample) + tuple(np.ascontiguousarray(outs[k]) for k in order)
```

```python
import contextlib
import numpy as np
import concourse.bass as bass
import concourse.mybir as mybir
from concourse.bass_utils import run_bass_kernel_spmd

F32, BF16, I32 = mybir.dt.float32, mybir.dt.bfloat16, mybir.dt.int32
AF, ALU, AX = mybir.ActivationFunctionType, mybir.AluOpType, mybir.AxisListType

D, T, TS = 2048, 4096, 4097
DEPTH, NH, HD = 2, 8, 64
NBLK, NROW = 57, 57 * 128
KC = 16
R_AQ, R_AK, R_AV, R_AZ = 0, 512, 1024, 1536
R_BR, R_BK, R_BV, R_BZ = 2048, 2560, 3072, 3584
R_CQ, R_CK, R_CV, R_CZ = 4096, 4608, 5120, 5632
R_DU, R_DZ, R_SM = 6144, 6656, 7168
SM_A, SM_B, SM_F, SM_WL, SM_AL = 0, 8, 16, 32, 64


def _col_perm():
    ref = dict(qkvA=0, aA=1536, bA=1544, zA=1552, rB=2064, kB=2576, vB=3088, wl=3600, al=3632, zB=3664,
               qkvC=4176, fC=5712, zC=5720, uD=6232, zD=6744)
    p = np.full(NROW, -1, np.int64)
    p[0:1536] = ref['qkvA'] + np.arange(1536)
    p[1536:2048] = ref['zA'] + np.arange(512)
    p[2048:3584] = ref['rB'] + np.arange(1536)
    p[3584:4096] = ref['zB'] + np.arange(512)
    p[4096:5632] = ref['qkvC'] + np.arange(1536)
    p[5632:6144] = ref['zC'] + np.arange(512)
    p[6144:6656] = ref['uD'] + np.arange(512)
    p[6656:7168] = ref['zD'] + np.arange(512)
    p[R_SM + SM_A:R_SM + SM_A + 8] = ref['aA'] + np.arange(8)
    p[R_SM + SM_B:R_SM + SM_B + 8] = ref['bA'] + np.arange(8)
    p[R_SM + SM_F:R_SM + SM_F + 8] = ref['fC'] + np.arange(8)
    p[R_SM + SM_WL:R_SM + SM_WL + 32] = ref['wl'] + np.arange(32)
    p[R_SM + SM_AL:R_SM + SM_AL + 32] = ref['al'] + np.arange(32)
    return p


PE, ACT, DVE, POOL, SP = 'pe', 'act', 'dve', 'pool', 'sp'
ENGS = (PE, ACT, DVE, POOL, SP)


class _Op:
    __slots__ = ('eng', 'fn', 'deps', 'is_dma', 'tag', 'sig', 'semkey', 'semval')

    def __init__(self, eng, fn, is_dma, tag):
        self.eng, self.fn, self.is_dma, self.tag = eng, fn, is_dma, tag
        self.deps, self.sig, self.semkey, self.semval = set(), False, None, 0


class Prog:
    n_emitted = 0

    def __init__(self, nc, dma_rot=6):
        self.nc, self.dma_rot = nc, dma_rot
        self.ops = {e: [] for e in ENGS}
        self.lastw, self.readers, self.all_ops = {}, {}, []

    def op(self, eng, fn, reads=(), writes=(), dma=False, tag=None):
        o = _Op(eng, fn, dma, tag)
        wr = list(writes)
        for r in reads:
            if isinstance(r, tuple) and r[0] == 'ps':
                wr.append(r)
                continue
            w = self.lastw.get(r)
            if w is not None:
                o.deps.add(w)
        for r in wr:
            w = self.lastw.get(r)
            if w is not None:
                o.deps.add(w)
            o.deps.update(self.readers.get(r, ()))
        for r in reads:
            if not (isinstance(r, tuple) and r[0] == 'ps'):
                self.readers.setdefault(r, []).append(o)
        for r in wr:
            self.lastw[r] = o
            self.readers[r] = []
        o.deps.discard(o)
        self.ops[eng].append(o)
        self.all_ops.append(o)
        return o

    def dma(self, eng, fn, reads=(), writes=(), tag='d'):
        return self.op(eng, fn, reads, writes, dma=True, tag=tag)

    def mm(self, out, lhsT, rhs, start=True, stop=True, reads=(), writes=()):
        return self.op(PE, lambda e: e.matmul(out, lhsT=lhsT, rhs=rhs, start=start, stop=stop), reads, writes)

    def tr(self, out, in_, identity, reads=(), writes=()):
        return self.op(PE, lambda e: e.transpose(out=out, in_=in_, identity=identity), reads, writes)

    def act(self, out, in_, func, reads=(), writes=(), **kw):
        return self.op(ACT, lambda e: e.activation(out=out, in_=in_, func=func, **kw), reads, writes)

    def cp(self, eng, out, in_, reads=(), writes=()):
        if eng == ACT:
            return self.op(ACT, lambda e: e.activation(out=out, in_=in_, func=AF.Copy), reads, writes)
        return self.op(eng, lambda e: e.tensor_copy(out=out, in_=in_), reads, writes)

    def tt(self, eng, out, in0, in1, op, reads=(), writes=()):
        return self.op(eng, lambda e: e.tensor_tensor(out=out, in0=in0, in1=in1, op=op), reads, writes)

    def ts(self, eng, out, in0, s1, op0, s2=None, op1=None, reads=(), writes=()):
        if op1 is None:
            return self.op(eng, lambda e: e.tensor_scalar(out=out, in0=in0, scalar1=s1, scalar2=None, op0=op0), reads, writes)
        return self.op(eng, lambda e: e.tensor_scalar(out=out, in0=in0, scalar1=s1, scalar2=s2, op0=op0, op1=op1), reads, writes)

    def stt(self, out, in0, scalar, in1, op0, op1, reads=(), writes=()):
        return self.op(DVE, lambda e: e.scalar_tensor_tensor(out=out, in0=in0, scalar=scalar, in1=in1, op0=op0, op1=op1), reads, writes)

    def ms(self, eng, ap, val, reads=(), writes=()):
        return self.op(eng, lambda e: e.memset(ap, val), reads, writes)

    def ld(self, out, in_, reads=(), writes=(), tag='d', slow=False, eng=SP):
        return self.dma(eng, lambda e: e.dma_start(out=out, in_=in_, allow_slow_non_contiguous=slow), reads, writes, tag=tag)

    def emit(self):
        nc = self.nc
        for o in self.all_ops:
            o.deps = {d for d in o.deps if not (d.eng == PE and o.eng == PE and not d.is_dma and not o.is_dma)}
            for d in o.deps:
                d.sig = True
            if o.is_dma:
                o.sig = True
        lasts = [ops[-1] for ops in self.ops.values() if ops]
        for o in lasts:
            o.sig = True
        counters, dma_seq, order = {}, {}, []
        for o in self.all_ops:
            if not o.sig:
                continue
            if o.is_dma:
                k = dma_seq.get((o.eng, o.tag), 0)
                dma_seq[(o.eng, o.tag)] = k + 1
                key, inc = ('dma', o.eng, o.tag, k % self.dma_rot), 16
            else:
                key, inc = ('cmp', o.eng), 1
            if key not in counters:
                order.append(key)
            counters[key] = counters.get(key, 0) + inc
            o.semkey, o.semval = key, counters[key]
        Prog.n_emitted += 1
        sems = {key: nc.alloc_semaphore('p%ds%d' % (Prog.n_emitted, i)) for i, key in enumerate(order)}
        with contextlib.ExitStack() as st:
            block = st.enter_context(nc.Block())
            engmap = {PE: block.tensor, ACT: block.scalar, DVE: block.vector, POOL: block.gpsimd, SP: block.sync}

            def make(eng):
                ops = self.ops[eng]

                def body(e):
                    waited = {}

                    def wait(key, val):
                        if waited.get(key, 0) < val:
                            e.wait_ge(sems[key], val)
                            waited[key] = val
                    for o in ops:
                        need = {}
                        for d in o.deps:
                            if need.get(d.semkey, 0) < d.semval:
                                need[d.semkey] = d.semval
                        for key, val in need.items():
                            wait(key, val)
                        if o.is_dma and o.semval > 16:
                            wait(o.semkey, o.semval - 16)
                        ins = o.fn(e)
                        if o.sig:
                            ins.then_inc(sems[o.semkey], 16 if o.is_dma else 1)
                    for key, val in counters.items():
                        wait(key, val)
                return body
            for eng in ENGS:
                if self.ops[eng]:
                    engmap[eng](make(eng))
        nc.all_engine_barrier()
        nc.clear_and_free_semaphores(list(sems.values()))
        nc.all_engine_barrier()


class Ctx:
    pass


def phase_inproj(nc, cx, l):
    P = Prog(nc)
    ps = cx.ps
    with contextlib.ExitStack() as st:
        sb = lambda name, shape, dt: st.enter_context(nc.sbuf_tensor('i%d_' % l + name, shape, dt))
        xtb = sb('xtb', [128, KC, TS], BF16)
        wst = [sb('wst%d' % i, [128, KC, 128], F32) for i in range(2)]
        wbf = [sb('wbf%d' % i, [128, KC, 128], BF16) for i in range(2)]
        stg = [sb('stg%d' % i, [128, 512], F32) for i in range(4)]
        if l == 0:
            xst = [sb('xst%d' % i, [128, TS], F32) for i in range(2)]
            for kc in range(KC):
                b = kc % 2
                P.dma(SP, lambda e, kc=kc, b=b: e.dma_start(out=xst[b][:], in_=cx.xT[kc * 128:(kc + 1) * 128, :]),
                      writes=[('xst', b)], tag='x')
                eng = (ACT, DVE, POOL)[kc % 3]
                if eng == ACT:
                    P.op(ACT, lambda e, kc=kc, b=b: e.activation(out=xtb[:, kc, :], in_=xst[b][:], func=AF.Copy),
                         reads=[('xst', b)], writes=[('xtb', kc)])
                else:
                    P.op(eng, lambda e, kc=kc, b=b: e.tensor_copy(out=xtb[:, kc, :], in_=xst[b][:]),
                         reads=[('xst', b)], writes=[('xtb', kc)])
        else:
            for kc in range(KC):
                P.dma(SP, lambda e, kc=kc: e.dma_start(out=xtb[:, kc, :], in_=cx.x2T[kc * 128:(kc + 1) * 128, :]),
                      writes=[('xtb', kc)], tag='x')
        xres = [('xtb', kc) for kc in range(KC)]
        ntile = 0
        for blk in range(NBLK):
            b = blk % 2
            P.dma(SP, lambda e, blk=blk, b=b: e.dma_start(
                out=wst[b][:], in_=cx.w_in[l, :, blk * 128:(blk + 1) * 128].rearrange("(kc p) c -> p kc c", p=128)),
                writes=[('wst', b)], tag='w')
            P.op(POOL, lambda e, b=b: e.tensor_copy(out=wbf[b][:], in_=wst[b][:]), reads=[('wst', b)], writes=[('wbf', b)])
            for tt in range(9):
                t0, n = tt * 512, (512 if tt < 8 else 1)
                bank = ntile % 4
                for kc in range(KC):
                    P.op(PE, lambda e, kc=kc, b=b, bank=bank, t0=t0, n=n: e.matmul(
                        ps[:, bank, 0:n], lhsT=wbf[b][:, kc, :], rhs=xtb[:, kc, t0:t0 + n], start=(kc == 0), stop=(kc == KC - 1)),
                        reads=[('wbf', b)] + (xres if kc == 0 else []), writes=[('ps', bank)])
                s = ntile % 4
                if ntile % 2 == 0:
                    P.op(ACT, lambda e, bank=bank, s=s, n=n: e.activation(out=stg[s][:, 0:n], in_=ps[:, bank, 0:n], func=AF.Copy),
                         reads=[('ps', bank)], writes=[('stg', s)])
                else:
                    P.op(DVE, lambda e, bank=bank, s=s, n=n: e.tensor_copy(out=stg[s][:, 0:n], in_=ps[:, bank, 0:n]),
                         reads=[('ps', bank)], writes=[('stg', s)])
                P.dma(SP, lambda e, blk=blk, s=s, t0=t0, n=n: e.dma_start(
                    out=cx.projT[blk * 128:(blk + 1) * 128, t0:t0 + n], in_=stg[s][:, 0:n], allow_slow_non_contiguous=(n == 1)),
                    reads=[('stg', s)], writes=[('projT', blk)], tag='po')
                ntile += 1
        P.emit()


ALPHA_DN = (2.0 * DEPTH) ** 0.25
LN_EPS = 1e-5


def load_bcast(P, nc, eng, dst, src_row, key):
    P.dma(eng, lambda e: e.dma_start(out=dst, in_=src_row.partition_broadcast(dst.shape[0])), writes=[key], tag='bc')


def phase_outproj(nc, cx, l):
    P = Prog(nc)
    ps = cx.ps
    last = (l == DEPTH - 1)
    with contextlib.ExitStack() as st:
        sb = lambda name, shape, dt: st.enter_context(nc.sbuf_tensor('o%d_' % l + name, shape, dt))
        wob = sb('wob', [128, KC, D], BF16)
        wst = [sb('wost%d' % i, [128, D], F32) for i in range(2)]
        lng, lnb = sb('lng', [128, D], F32), sb('lnb', [128, D], F32)
        identb = sb('identb', [128, 128], BF16)
        gT = [sb('gT%d' % i, [128, KC, 128], BF16) for i in range(2)]
        xr = [sb('xr%d' % i, [128, D], F32) for i in range(2)]
        zt = [sb('zt%d' % i, [128, D], F32) for i in range(2)]
        yb = [sb('yb%d' % i, [128, D], BF16) for i in range(2)]
        xT2 = [sb('xT2%d' % i, [128, KC, 128], BF16) for i in range(2)]
        st1 = [sb('st1%d' % i, [128, 4], F32) for i in range(2)]
        sq = sb('sqjunk', [128, D], F32)
        P.dma(SP, lambda e: e.dma_start(out=identb[:], in_=cx.identb), writes=['identb'], tag='c')
        load_bcast(P, nc, SP, lng[:], cx.ln_g[l:l + 1, :], 'lng')
        load_bcast(P, nc, SP, lnb[:], cx.ln_b[l:l + 1, :], 'lnb')
        for kc in range(KC):
            b = kc % 2
            P.dma(SP, lambda e, kc=kc, b=b: e.dma_start(out=wst[b][:], in_=cx.w_out[l, kc * 128:(kc + 1) * 128, :]), writes=[('wst', b)], tag='w')
            if kc % 2 == 0:
                P.op(POOL, lambda e, kc=kc, b=b: e.tensor_copy(out=wob[:, kc, :], in_=wst[b][:]), reads=[('wst', b)], writes=[('wob', kc)])
            else:
                P.op(ACT, lambda e, kc=kc, b=b: e.activation(out=wob[:, kc, :], in_=wst[b][:], func=AF.Copy), reads=[('wst', b)], writes=[('wob', kc)])
        wres = [('wob', kc) for kc in range(KC)]
        xsrc = cx.xtok if l == 0 else cx.y1
        ydst = cx.y if last else cx.y1
        deferred = []
        for tb in range(33):
            t0, n = tb * 128, (128 if tb < 32 else 1)
            b = tb % 2
            P.dma(SP, lambda e, b=b, t0=t0, n=n: e.dma_start(
                out=gT[b][:, :, 0:n], in_=cx.gatedT[:, t0:t0 + n].rearrange("(kc p) t -> p kc t", p=128), allow_slow_non_contiguous=(n == 1)),
                writes=[('gT', b)], tag='g')
            P.dma(SP, lambda e, b=b, t0=t0, n=n: e.dma_start(out=xr[b][0:n, :], in_=xsrc[t0:t0 + n, :]), writes=[('xr', b)], tag='xr')
            for cg in range(4):
                for kc in range(KC):
                    P.op(PE, lambda e, b=b, n=n, cg=cg, kc=kc: e.matmul(
                        ps[0:n, cg, :], lhsT=gT[b][:, kc, 0:n], rhs=wob[:, kc, cg * 512:(cg + 1) * 512], start=(kc == 0), stop=(kc == KC - 1)),
                        reads=[('gT', b)] + (wres if kc == 0 and cg == 0 and tb == 0 else []), writes=[('ps', cg)])
                P.op(DVE, lambda e, b=b, n=n, cg=cg: e.scalar_tensor_tensor(
                    out=zt[b][0:n, cg * 512:(cg + 1) * 512], in0=xr[b][0:n, cg * 512:(cg + 1) * 512], scalar=ALPHA_DN,
                    in1=ps[0:n, cg, :], op0=ALU.mult, op1=ALU.add), reads=[('xr', b), ('ps', cg)], writes=[('zt', b)])
            s = st1[b]
            P.op(DVE, lambda e, b=b, n=n, s=s: e.tensor_reduce(out=s[0:n, 0:1], in_=zt[b][0:n, :], axis=AX.X, op=ALU.add),
                 reads=[('zt', b)], writes=[('st1', b)])
            P.op(DVE, lambda e, n=n, s=s: e.tensor_scalar(out=s[0:n, 1:2], in0=s[0:n, 0:1], scalar1=-1.0 / D, scalar2=None, op0=ALU.mult),
                 reads=[('st1', b)], writes=[('st1', b)])
            P.op(ACT, lambda e, b=b, n=n, s=s: e.activation(out=sq[0:n, :], in_=zt[b][0:n, :], func=AF.Square, bias=s[0:n, 1:2], accum_out=s[0:n, 2:3]),
                 reads=[('zt', b), ('st1', b)], writes=[('st1', b), 'sq'])
            P.op(DVE, lambda e, n=n, s=s: e.tensor_scalar(out=s[0:n, 3:4], in0=s[0:n, 2:3], scalar1=1.0 / D, scalar2=LN_EPS, op0=ALU.mult, op1=ALU.add),
                 reads=[('st1', b)], writes=[('st1', b)])
            P.op(ACT, lambda e, n=n, s=s: e.activation(out=s[0:n, 3:4], in_=s[0:n, 3:4], func=AF.Ln), reads=[('st1', b)], writes=[('st1', b)])
            P.op(ACT, lambda e, n=n, s=s: e.activation(out=s[0:n, 3:4], in_=s[0:n, 3:4], func=AF.Exp, scale=-0.5), reads=[('st1', b)], writes=[('st1', b)])
            P.op(DVE, lambda e, b=b, n=n, s=s: e.tensor_scalar(out=zt[b][0:n, :], in0=zt[b][0:n, :], scalar1=s[0:n, 1:2], scalar2=s[0:n, 3:4],
                                                               op0=ALU.add, op1=ALU.mult), reads=[('zt', b), ('st1', b)], writes=[('zt', b)])
            P.op(DVE, lambda e, b=b, n=n: e.tensor_tensor(out=zt[b][0:n, :], in0=zt[b][0:n, :], in1=lng[0:n, :], op=ALU.mult),
                 reads=[('zt', b), 'lng'], writes=[('zt', b)])
            P.op(POOL, lambda e, b=b, n=n: e.tensor_tensor(out=zt[b][0:n, :], in0=zt[b][0:n, :], in1=lnb[0:n, :], op=ALU.add),
                 reads=[('zt', b), 'lnb'], writes=[('zt', b)])
            P.dma(SP, lambda e, b=b, t0=t0, n=n: e.dma_start(out=ydst[t0:t0 + n, :], in_=zt[b][0:n, :]), reads=[('zt', b)], writes=['ydst'], tag='yo')
            if not last:
                P.op(ACT, lambda e, b=b, n=n: e.activation(out=yb[b][0:n, :], in_=zt[b][0:n, :], func=AF.Copy), reads=[('zt', b)], writes=[('yb', b)])

                def stage2(b=b, n=n, t0=t0):
                    psb = [ps[:, 4 + h, :].bitcast(BF16) for h in range(2)]
                    for kc in range(KC):
                        h, j = kc // 8, kc % 8
                        P.op(PE, lambda e, kc=kc, h=h, j=j: e.transpose(out=psb[h][:, j * 128:j * 128 + n], in_=yb[b][0:n, kc * 128:(kc + 1) * 128],
                                                                      identity=identb[0:n, 0:n]),
                             reads=[('yb', b), 'identb'], writes=[('ps', 4 + h)])
                    for h in range(2):
                        P.op(DVE if h == 0 else ACT, (lambda e, h=h: e.tensor_copy(
                            out=xT2[b][:, h * 8:(h + 1) * 8, 0:n], in_=psb[h].rearrange("p (j t) -> p j t", t=128)[:, :, 0:n])) if h == 0 else
                            (lambda e, h=h: e.activation(
                                out=xT2[b][:, h * 8:(h + 1) * 8, 0:n], in_=psb[h].rearrange("p (j t) -> p j t", t=128)[:, :, 0:n], func=AF.Copy)),
                            reads=[('ps', 4 + h)], writes=[('xT2', b)])
                    P.dma(SP, lambda e: e.dma_start(
                        out=cx.x2T[:, t0:t0 + n].rearrange("(kc p) t -> p kc t", p=128), in_=xT2[b][:, :, 0:n], allow_slow_non_contiguous=(n == 1)),
                        reads=[('xT2', b)], writes=['x2T'], tag='x2')
                if deferred:
                    deferred.pop()()
                deferred.append(stage2)
        while deferred:
            deferred.pop()()
        P.emit()


PP_CONV, PP_NORMA, PP_ALOG, PP_DTB, PP_MU, PP_MUS = 0, 48, 49, 50, 51, 63
PP_W0, PP_A0, PP_XI, PP_ALPHA, PP_RHO, PP_GNG, PP_GNB, PP_BF, PP_PSC, NPP = 64, 68, 72, 76, 80, 84, 88, 92, 93, 100
POOL_W = (2, 4, 8, 16)


def pack_pp(I):
    pp = np.zeros((DEPTH, 128, NPP), np.float32)
    p = np.arange(128)
    for l in range(DEPTH):
        for b12 in range(12):
            for i in range(4):
                pp[l, :, PP_CONV + b12 * 4 + i] = I['conv_A'][l][i, b12 * 128 + p]
            pp[l, :, PP_MU + b12] = I['mu_B'][l][b12 * 128 + p]
        pp[l, :, PP_NORMA] = I['norm_A'][l][p % 64]
        pp[l, 0:8, PP_ALOG] = I['A_log'][l]
        pp[l, 0:8, PP_DTB] = I['dt_bias'][l]
        pp[l, 32:64, PP_MUS] = I['mu_B'][l][1536:1568]
        pp[l, 64:96, PP_MUS] = I['mu_B'][l][1568:1600]
        for j in range(4):
            for col, nm in ((PP_W0, 'w0_B'), (PP_A0, 'a0_B'), (PP_XI, 'xi_B'), (PP_ALPHA, 'alpha_B'), (PP_RHO, 'rho_B'),
                            (PP_GNG, 'gn_g_B'), (PP_GNB, 'gn_b_B'), (PP_PSC, 'pool_scale_D')):
                pp[l, :, col + j] = I[nm][l][j * 128 + p]
        pp[l, 16:24, PP_BF] = I['b_f_C'][l]
    return pp


def phase_mixD(nc, cx, l):
    P = Prog(nc)
    ps = cx.ps
    with contextlib.ExitStack() as st:
        sb = lambda name, shape, dt: st.enter_context(nc.sbuf_tensor('d%d_' % l + name, shape, dt))
        pp = sb('pp', [128, NPP], F32)
        pwf, pwb = sb('pwf', [128, 4, 128], F32), sb('pwb', [128, 4, 128], BF16)
        ext, zt = sb('ext', [128, 15 + T], F32), sb('zt', [128, T], F32)
        sA, sB = sb('sA', [128, 15 + T], F32), sb('sB', [128, 15 + T], F32)
        pb, go = sb('pb', [128, T], BF16), sb('go', [128, T], BF16)
        exs, zs, sAs, sBs = sb('exs', [128, 16], F32), sb('zs', [128, 1], F32), sb('sAs', [128, 16], F32), sb('sBs', [128, 16], F32)
        pbs, gos = sb('pbs', [128, 1], BF16), sb('gos', [128, 1], BF16)
        P.dma(SP, lambda e: e.dma_start(out=pp[:], in_=cx.pp[l]), writes=['pp'], tag='c')
        P.dma(SP, lambda e: e.dma_start(out=pwf[:], in_=cx.pool_w[l].rearrange("g c d -> c g d")), writes=['pwf'], tag='c')
        P.op(POOL, lambda e: e.tensor_copy(out=pwb[:], in_=pwf[:]), reads=['pwf'], writes=['pwb'])
        P.op(POOL, lambda e: e.memset(ext[:, 0:15], 0.0), writes=['ext'])

        def group(g, n, ext, zt, sA, sB, pb, go, fix, sfx):
            w = POOL_W[g]
            r0 = g * 128
            k = lambda nm: nm + sfx
            if n > 1:
                P.dma(SP, lambda e: e.dma_start(out=ext[:, 15:15 + n], in_=cx.projT[R_DU + r0:R_DU + r0 + 128, 0:n]), writes=[k('ext')], tag='m')
                P.dma(SP, lambda e: e.dma_start(out=zt[:, 0:n], in_=cx.projT[R_DZ + r0:R_DZ + r0 + 128, 0:n]), writes=[k('zt')], tag='m')
            else:
                P.dma(SP, lambda e: e.dma_start(out=ext[:, 0:15], in_=cx.dbufT[l, r0:r0 + 128, :]), writes=[k('ext')], tag='m')
                P.dma(SP, lambda e: e.dma_start(out=ext[:, 15:16], in_=cx.projT[R_DU + r0:R_DU + r0 + 128, T:T + 1], allow_slow_non_contiguous=True), writes=[k('ext')], tag='m')
                P.dma(SP, lambda e: e.dma_start(out=zt[:, 0:1], in_=cx.projT[R_DZ + r0:R_DZ + r0 + 128, T:T + 1], allow_slow_non_contiguous=True), writes=[k('zt')], tag='m')
            P.op(ACT, lambda e: e.activation(out=zt[:, 0:n], in_=zt[:, 0:n], func=AF.Silu), reads=[k('zt')], writes=[k('zt')])
            src, srck, sh, E = ext, k('ext'), 1, 15 + n
            for step in range(g + 1):
                dst, dstk = (sA, k('sA')) if step % 2 == 0 else (sB, k('sB'))
                lo = 2 * sh - 1
                P.op(DVE if step % 2 == 0 else POOL, lambda e, src=src, dst=dst, sh=sh, lo=lo: e.tensor_tensor(
                    out=dst[:, lo:E], in0=src[:, lo:E], in1=src[:, lo - sh:E - sh], op=ALU.add), reads=[srck], writes=[dstk])
                src, srck, sh = dst, dstk, sh * 2
            P.op(DVE, lambda e, src=src: e.scalar_tensor_tensor(out=pb[:, 0:n], in0=src[:, 15:15 + n], scalar=1.0 / w, in1=ext[:, 15:15 + n],
                                                              op0=ALU.mult, op1=ALU.subtract), reads=[srck, k('ext')], writes=[k('pb')])
            if fix:
                for t in range(w - 1):
                    P.op(DVE, lambda e, src=src, t=t: e.scalar_tensor_tensor(out=pb[:, t:t + 1], in0=src[:, 15 + t:16 + t], scalar=1.0 / (t + 1),
                                                                            in1=ext[:, 15 + t:16 + t], op0=ALU.mult, op1=ALU.subtract),
                         reads=[srck, k('ext')], writes=[k('pb')])
            for tt in range((n + 511) // 512):
                t0, m = tt * 512, min(512, n - tt * 512)
                bank = tt % 4
                P.op(PE, lambda e, t0=t0, m=m, bank=bank: e.matmul(ps[:, bank, 0:m], lhsT=pwb[:, g, :], rhs=pb[:, t0:t0 + m], start=True, stop=True),
                     reads=['pwb', k('pb')], writes=[('ps', bank)])
                P.op(DVE, lambda e, t0=t0, m=m, bank=bank: e.scalar_tensor_tensor(
                    out=go[:, t0:t0 + m], in0=ps[:, bank, 0:m], scalar=pp[:, PP_PSC + g:PP_PSC + g + 1], in1=zt[:, t0:t0 + m], op0=ALU.mult, op1=ALU.mult),
                    reads=[('ps', bank), 'pp', k('zt')], writes=[k('go')])
            c0 = 0 if n > 1 else T
            P.dma(SP, lambda e: e.dma_start(out=cx.gatedT[1536 + r0:1536 + r0 + 128, c0:c0 + n], in_=go[:, 0:n], allow_slow_non_contiguous=(n == 1)),
                  reads=[k('go')], writes=['gatedT'], tag='go')
        for g in range(4):
            group(g, T, ext, zt, sA, sB, pb, go, True, '')
            group(g, 1, exs, zs, sAs, sBs, pbs, gos, False, 's')
        P.dma(SP, lambda e: e.dma_start(out=cx.o_dbuf[l, 0], in_=cx.projT[R_DU:R_DU + 512, T - 15:T], allow_slow_non_contiguous=True), writes=['o_dbuf0'], tag='so')
        P.dma(SP, lambda e: e.dma_start(out=cx.o_dbuf[l, 1, :, 0:14], in_=cx.dbufT[l, :, 1:15], allow_slow_non_contiguous=True), writes=['o_dbuf1'], tag='so')
        P.dma(SP, lambda e: e.dma_start(out=cx.o_dbuf[l, 1, :, 14:15], in_=cx.projT[R_DU:R_DU + 512, T:T + 1], allow_slow_non_contiguous=True), writes=['o_dbuf2'], tag='so')
        P.emit()


def phase_mixC(nc, cx, l, heads=range(NH), do_sample=True):
    P = Prog(nc)
    ps = cx.ps
    with contextlib.ExitStack() as st:
        sb = lambda name, shape, dt: st.enter_context(nc.sbuf_tensor('c%d_' % l + name, shape, dt))
        cf = sb('cf', [128, 256], F32)
        cb = sb('cb', [128, 128 + 4 * 512], BF16)
        identf, tri = cf[:, 0:128], cf[:, 128:256]
        identb = cb[:, 0:128]
        negb = sb('negb', [128, 8], F32)
        ones32 = sb('ones32', [128, 64], F32)
        onesj = sb('onesj', [128, 32], F32)
        qf, kf, vf, zf = (sb(n, [64, T], F32) for n in ('qf', 'kf', 'vf', 'zf'))
        Qa, Ka = sb('Qa', [70, T], BF16), sb('Ka', [70, T], BF16)
        Vt = sb('Vt', [128, 32, 65], BF16)
        PT = [sb('PT%d' % i, [128, 512], BF16) for i in range(3)]
        f32t, e1, lf, cw, hif, r1 = (sb(n, [128, 32], F32) for n in ('f32t', 'e1', 'lf', 'cw', 'hif', 'r1'))
        offs = sb('offs', [128, 1], F32)
        c3 = sb('c3', [128, 6, 32], BF16)
        oT, t1 = sb('oT', [65, 512], F32), sb('t1', [64, 512], F32)
        rden = sb('rden', [65, 512], F32)
        go = sb('go', [64, T], BF16)
        P.dma(SP, lambda e: e.dma_start(out=cf[:], in_=cx.cf32), writes=['cf'], tag='c')
        P.dma(SP, lambda e: e.dma_start(out=cb[:], in_=cx.cb16), writes=['cb'], tag='c')
        P.dma(SP, lambda e: e.dma_start(out=negb[:], in_=cx.b_f[l:l + 1, :].partition_broadcast(128)), writes=['negb'], tag='c')
        P.op(POOL, lambda e: e.tensor_scalar(out=negb[:], in0=negb[:], scalar1=-1.0, scalar2=None, op0=ALU.mult), reads=['negb'], writes=['negb'])
        P.op(POOL, lambda e: e.memset(ones32[:], 1.0), writes=['ones32'])
        P.op(POOL, lambda e: e.memset(onesj[:], 1.0), writes=['onesj'])
        P.op(POOL, lambda e: e.memset(Vt[:, :, 64:65], 1.0), writes=['Vt'])
        P.op(POOL, lambda e: e.memset(Qa[64:70, :], 1.0), writes=['Qa'])
        P.op(POOL, lambda e: e.memset(Ka[64:70, :], 1.0), writes=['Ka'])
        zfs = [zf, sb('zf1', [64, T], F32)]

        def load_head(h):
            zfh, zk = zfs[h % 2], ('zf', h % 2)
            P.dma(SP, lambda e, h=h: e.dma_start(out=f32t[:], in_=cx.projT[R_SM + SM_F + h, 0:T].rearrange("(p j) -> p j", j=32)), writes=['f32t'], tag='m')
            for tl, r0, key in ((qf, R_CQ, 'qf'), (kf, R_CK, 'kf'), (vf, R_CV, 'vf'), (zfh, R_CZ, zk)):
                P.dma(SP, lambda e, tl=tl, r0=r0, h=h: e.dma_start(out=tl[:], in_=cx.projT[r0 + h * 64:r0 + (h + 1) * 64, 0:T]), writes=[key], tag='m')
            P.op(ACT, lambda e, h=h: e.activation(out=e1[:], in_=f32t[:], func=AF.Exp, scale=-1.0, bias=negb[:, h:h + 1]), reads=['f32t', 'negb'], writes=['e1'])
            P.op(ACT, lambda e: e.activation(out=e1[:], in_=e1[:], func=AF.Ln, bias=1.0), reads=['e1'], writes=['e1'])
            P.op(DVE, lambda e: e.tensor_scalar(out=lf[:], in0=e1[:], scalar1=-1.0, scalar2=None, op0=ALU.mult), reads=['e1'], writes=['lf'])
            P.dma(SP, lambda e, h=h: e.dma_start(out=cx.o_clf[l, h, 0:T].rearrange("(p j) -> p j", j=32), in_=lf[:]), reads=['lf'], writes=['o_clf'], tag='so')
            P.op(DVE, lambda e: e.tensor_tensor_scan(out=cw[:], data0=onesj[:], data1=lf[:], initial=0.0, op0=ALU.mult, op1=ALU.add),
                 reads=['lf', 'onesj'], writes=['cw'])
            P.op(PE, lambda e: e.matmul(ps[:, 4, 0:1], lhsT=tri, rhs=cw[:, 31:32], start=True, stop=True), reads=['cf', 'cw'], writes=[('ps', 4)])
            P.op(DVE, lambda e: e.tensor_copy(out=offs[:], in_=ps[:, 4, 0:1]), reads=[('ps', 4)], writes=['offs'])
            P.op(DVE, lambda e: e.tensor_scalar(out=cw[:], in0=cw[:], scalar1=offs[:, 0:1], scalar2=None, op0=ALU.add), reads=['cw', 'offs'], writes=['cw'])
            src = cw
            for pc in range(3):
                P.op(DVE, lambda e, pc=pc, src=src: e.tensor_copy(out=c3[:, pc, :], in_=src[:]), reads=['cw', 'r1'], writes=['c3'])
                P.op(POOL, lambda e, pc=pc, src=src: e.tensor_scalar(out=c3[:, 3 + pc, :], in0=src[:], scalar1=-1.0, scalar2=None, op0=ALU.mult),
                     reads=['cw', 'r1'], writes=['c3'])
                if pc < 2:
                    P.op(DVE, lambda e, pc=pc: e.tensor_copy(out=hif[:], in_=c3[:, pc, :]), reads=['c3'], writes=['hif'])
                    P.op(DVE, lambda e, src=src: e.tensor_tensor(out=r1[:], in0=src[:], in1=hif[:], op=ALU.subtract), reads=['cw', 'r1', 'hif'], writes=['r1'])
                    src = r1
            P.dma(SP, lambda e: e.dma_start(out=cx.cscr.rearrange("r (p j) -> p r j", j=32), in_=c3[:]), reads=['c3'], writes=['cscr'], tag='cs')

        def conv_head(h):
            zfh, zk = zfs[h % 2], ('zf', h % 2)
            P.dma(SP, lambda e: e.dma_start(out=Qa[64:67, :], in_=cx.cscr[0:3, :]), reads=['cscr'], writes=['Qa'], tag='cs')
            P.dma(SP, lambda e: e.dma_start(out=Ka[67:70, :], in_=cx.cscr[3:6, :]), reads=['cscr'], writes=['Ka'], tag='cs')
            P.op(ACT, lambda e: e.activation(out=Qa[0:64, :], in_=qf[:], func=AF.Copy, scale=0.125), reads=['qf'], writes=['Qa'])
            P.op(POOL, lambda e: e.tensor_copy(out=Ka[0:64, :], in_=kf[:]), reads=['kf'], writes=['Ka'])
            P.op(ACT, lambda e: e.activation(out=zfh[:], in_=zfh[:], func=AF.Silu), reads=[zk], writes=[zk])
            for r in range(4):
                bank = 5 + r % 2
                for j in range(8):
                    blk = r * 8 + j
                    P.op(PE, lambda e, bank=bank, j=j, blk=blk: e.transpose(out=ps[:, bank, j * 64:(j + 1) * 64], in_=vf[:, blk * 128:(blk + 1) * 128],
                                                                        identity=identf[0:64, 0:64]), reads=['vf', 'cf'], writes=[('ps', bank)])
                P.op(DVE, lambda e, bank=bank, r=r: e.tensor_copy(out=Vt[:, r * 8:(r + 1) * 8, 0:64], in_=ps[:, bank, :].rearrange("p (j d) -> p j d", d=64)),
                     reads=[('ps', bank)], writes=['Vt'])

        itc = [0]

        def attend(h):
            zfh, zk = zfs[h % 2], ('zf', h % 2)
            units = [(qt, kb) for qt in range(8) for kb in range(4 * qt + 4)]
            LOOK = 2

            def qk(i):
                qt, kb = units[i]
                bank = (itc[0] + i) % 3
                diag = kb >= 4 * qt
                P.op(PE, lambda e: e.matmul(ps[:, bank, :], lhsT=Ka[0:70, kb * 128:(kb + 1) * 128], rhs=Qa[0:70, qt * 512:(qt + 1) * 512], start=True, stop=not diag),
                     reads=['Ka', 'Qa'], writes=[('ps', bank)])
                if diag:
                    j = kb - 4 * qt
                    P.op(PE, lambda e: e.matmul(ps[:, bank, :], lhsT=identb, rhs=cb[:, 128 + j * 512:128 + (j + 1) * 512], start=False, stop=True),
                         reads=['cb'], writes=[('ps', bank)])

            def pv(i):
                qt, kb = units[i]
                nkb = 4 * qt + 4
                bank = r = (itc[0] + i) % 3
                ob = 3 if qt % 2 == 0 else 7
                P.op(ACT, lambda e: e.activation(out=PT[r][:], in_=ps[:, bank, :], func=AF.Exp), reads=[('ps', bank)], writes=[('PT', r)])
                P.op(PE, lambda e: e.matmul(ps[0:65, ob, :], lhsT=Vt[:, kb, :], rhs=PT[r][:], start=(kb == 0), stop=(kb == nkb - 1)),
                     reads=['Vt', ('PT', r)], writes=[('ps', ob)])
                if kb == nkb - 1:
                    P.op(DVE, lambda e: e.tensor_copy(out=oT[:], in_=ps[0:65, ob, :]), reads=[('ps', ob)], writes=['oT'])
                    P.op(DVE, lambda e: e.reciprocal(out=rden[64:65, :], in_=oT[64:65, :]), reads=['oT'], writes=['rden'])

                    def epi():
                        P.op(PE, lambda e: e.matmul(ps[0:64, 4, :], lhsT=ones32[64:65, 0:64], rhs=rden[64:65, :], start=True, stop=True),
                             reads=['ones32', 'rden'], writes=[('ps', 4)])
                        P.op(DVE, lambda e: e.tensor_tensor(out=t1[:], in0=oT[0:64, :], in1=ps[0:64, 4, :], op=ALU.mult), reads=['oT', ('ps', 4)], writes=['t1'])
                        P.op(POOL, lambda e: e.tensor_tensor(out=go[:, qt * 512:(qt + 1) * 512], in0=t1[:], in1=zfh[:, qt * 512:(qt + 1) * 512], op=ALU.mult),
                             reads=['t1', zk], writes=['go'])
                    pending.append((i + 2, epi))
            pending = []
            for i in range(len(units) + LOOK):
                if i < len(units):
                    qk(i)
                if i >= LOOK:
                    pv(i - LOOK)
                    while pending and pending[0][0] <= i - LOOK:
                        pending.pop(0)[1]()
            while pending:
                pending.pop(0)[1]()
            itc[0] += len(units)
            P.dma(SP, lambda e, h=h: e.dma_start(out=cx.gatedT[1024 + h * 64:1024 + (h + 1) * 64, 0:T], in_=go[:]), reads=['go'], writes=['gatedT'], tag='go')

        hl = list(heads)
        load_head(hl[0])
        conv_head(hl[0])
        for i, h in enumerate(hl):
            if i + 1 < len(hl):
                load_head(hl[i + 1])
            attend(h)
            if i + 1 < len(hl):
                conv_head(hl[i + 1])
        for j in range(8):
            P.dma(SP, lambda e, j=j: e.dma_start(out=cx.o_ckv[l, j * 128:(j + 1) * 128, :], in_=cx.projT[R_CK + j * 128:R_CK + (j + 1) * 128, :]),
                  writes=[('o_ckv', j)], tag='so')
        P.emit()


TP, NCH = 33 * 128, 33
A_KINDS = ('k', 'q', 'bz', 'rz', 'k2', 'a2', 'v')
B_KINDS = ('ah', 'bh', 'kh', 'rh', 'k2', 'a2', 'v')
RWKV_DS = float(np.exp(-0.5))


def _l2_rn(P, nc, ps, sqb, onesbd, n, rn, keys, bank0=0):
    for j, t0 in enumerate(range(0, n, 512)):
        m = min(512, n - t0)
        bank = bank0 + j % 2
        P.op(PE, lambda e, t0=t0, m=m, bank=bank: e.matmul(ps[:, bank, 0:m], lhsT=onesbd, rhs=sqb[:, t0:t0 + m], start=True, stop=True),
             reads=[keys[0], 'cb'], writes=[('ps', bank)])
        P.op(ACT, lambda e, t0=t0, m=m, bank=bank: e.activation(out=rn[:, t0:t0 + m], in_=ps[:, bank, 0:m], func=AF.Ln, bias=1e-6),
             reads=[('ps', bank)], writes=[keys[1]])
    P.op(ACT, lambda e: e.activation(out=rn[:, 0:n], in_=rn[:, 0:n], func=AF.Exp, scale=-0.5), reads=[keys[1]], writes=[keys[1]])


def phase_rowsA(nc, cx, l):
    P = Prog(nc)
    with contextlib.ExitStack() as st:
        sb = lambda name, shape, dt: st.enter_context(nc.sbuf_tensor('ra%d_' % l + name, shape, dt))
        pp = sb('pp', [128, NPP], F32)
        P.dma(SP, lambda e: e.dma_start(out=pp[:], in_=cx.pp[l]), writes=['pp'], tag='c')
        g, lb, u1, u0, gl, tmp, rmask, onesr, mA, mB = (sb(n, [8, TP], F32) for n in ('g', 'lb', 'u1', 'u0', 'gl', 'tmp', 'rmask', 'onesr', 'mA', 'mB'))
        nega = sb('nega', [8, 1], F32)
        P.op(POOL, lambda e: e.memset(g[:], 0.0), writes=['g'])
        P.op(POOL, lambda e: e.memset(lb[:], 0.0), writes=['lb'])
        P.op(POOL, lambda e: e.memset(onesr[:], 1.0), writes=['onesr'])
        P.op(POOL, lambda e: e.memset(rmask[:], 1.0), writes=['rmask'])
        P.op(POOL, lambda e: e.memset(rmask[:].rearrange("p (c i) -> p c i", i=128)[:, :, 0:1], 0.0), reads=['rmask'], writes=['rmask'])
        P.dma(SP, lambda e: e.dma_start(out=g[:, 0:TS], in_=cx.projT[R_SM + SM_A:R_SM + SM_A + 8, :]), reads=['g'], writes=['g'], tag='m')
        P.dma(SP, lambda e: e.dma_start(out=lb[:, 0:TS], in_=cx.projT[R_SM + SM_B:R_SM + SM_B + 8, :]), reads=['lb'], writes=['lb'], tag='m')
        P.op(ACT, lambda e: e.activation(out=nega[:], in_=pp[0:8, PP_ALOG:PP_ALOG + 1], func=AF.Exp), reads=['pp'], writes=['nega'])
        P.op(DVE, lambda e: e.tensor_scalar(out=nega[:], in0=nega[:], scalar1=-1.0, scalar2=None, op0=ALU.mult), reads=['nega'], writes=['nega'])
        P.op(ACT, lambda e: e.activation(out=g[:], in_=g[:], func=AF.Exp, bias=pp[0:8, PP_DTB:PP_DTB + 1]), reads=['g', 'pp'], writes=['g'])
        P.op(ACT, lambda e: e.activation(out=g[:], in_=g[:], func=AF.Ln, bias=1.0), reads=['g'], writes=['g'])
        P.op(DVE, lambda e: e.tensor_scalar(out=g[:], in0=g[:], scalar1=nega[:, 0:1], scalar2=None, op0=ALU.mult), reads=['g', 'nega'], writes=['g'])
        P.op(POOL, lambda e: e.memset(g[:, TS:TP], 0.0), reads=['g'], writes=['g'])
        P.op(ACT, lambda e: e.activation(out=lb[:], in_=lb[:], func=AF.Exp, scale=-1.0), reads=['lb'], writes=['lb'])
        P.op(ACT, lambda e: e.activation(out=lb[:], in_=lb[:], func=AF.Ln, bias=1.0), reads=['lb'], writes=['lb'])
        P.op(DVE, lambda e: e.tensor_scalar(out=lb[:], in0=lb[:], scalar1=-1.0, scalar2=None, op0=ALU.mult), reads=['lb'], writes=['lb'])
        P.op(DVE, lambda e: e.tensor_tensor_scan(out=u1[:], data0=rmask[:], data1=g[:], initial=0.0, op0=ALU.mult, op1=ALU.add),
             reads=['g', 'rmask'], writes=['u1'])
        P.op(POOL, lambda e: e.tensor_tensor(out=u0[:], in0=u1[:], in1=g[:], op=ALU.subtract), reads=['u1', 'g'], writes=['u0'])
        P.op(DVE, lambda e: e.tensor_copy(out=gl[:].rearrange("p (c i) -> p c i", i=128),
                                          in_=u1[:].rearrange("p (c i) -> p c i", i=128)[:, :, 127:128].to_broadcast([8, NCH, 128])),
             reads=['u1'], writes=['gl'])
        P.op(POOL, lambda e: e.tensor_tensor(out=tmp[:], in0=lb[:], in1=gl[:], op=ALU.add), reads=['lb', 'gl'], writes=['tmp'])
        stage = [(mA, 'mA'), (mB, 'mB')]
        for i in range(5):
            m, mk = stage[i % 2]
            if i == 0:
                P.op(ACT, lambda e, m=m: e.activation(out=m[:], in_=u0[:], func=AF.Exp), reads=['u0'], writes=[mk])
            elif i == 1:
                P.op(ACT, lambda e, m=m: e.activation(out=m[:], in_=u1[:], func=AF.Exp), reads=['u1'], writes=[mk])
            elif i == 4:
                P.op(ACT, lambda e, m=m: e.activation(out=m[:], in_=gl[:], func=AF.Exp), reads=['gl'], writes=[mk])
            else:
                src = u1 if i == 2 else u0
                P.op(POOL, lambda e, m=m, src=src: e.tensor_tensor(out=m[:], in0=tmp[:], in1=src[:], op=ALU.subtract), reads=['tmp', 'u1', 'u0'], writes=[mk])
                P.op(ACT, lambda e, m=m: e.activation(out=m[:], in_=m[:], func=AF.Exp), reads=[mk], writes=[mk])
            P.dma(SP, lambda e, i=i, m=m: e.dma_start(out=cx.A_mrows[i], in_=m[:]), reads=[mk], writes=['A_mrows'], tag='er')
        for kind, r, tl, key in ((0, 0, u0, 'u0'), (0, 1, onesr, 'onesr'), (3, 0, u1, 'u1'), (3, 1, onesr, 'onesr'), (1, 0, onesr, 'onesr'), (2, 0, onesr, 'onesr')):
            P.dma(SP, lambda e, kind=kind, r=r, tl=tl: e.dma_start(out=cx.A_erows[kind, r], in_=tl[:]), reads=[key], writes=[('A_erows', kind, r)], tag='er')
        P.op(POOL, lambda e: e.tensor_tensor(out=tmp[:], in0=lb[:], in1=u0[:], op=ALU.subtract), reads=['lb', 'u0', 'mA', 'mB'], writes=['tmp'])
        P.dma(SP, lambda e: e.dma_start(out=cx.A_erows[1, 1], in_=tmp[:]), reads=['tmp'], writes=[('A_erows', 1, 1)], tag='er')
        P.op(POOL, lambda e: e.tensor_tensor(out=gl[:], in0=lb[:], in1=u1[:], op=ALU.subtract), reads=['lb', 'u1', 'gl', 'mA', 'mB'], writes=['gl'])
        P.dma(SP, lambda e: e.dma_start(out=cx.A_erows[2, 1], in_=gl[:]), reads=['gl'], writes=[('A_erows', 2, 1)], tag='er')
        P.emit()


def phase_preA(nc, cx, l):
    P = Prog(nc)
    ps = cx.ps
    with contextlib.ExitStack() as st:
        sb = lambda name, shape, dt: st.enter_context(nc.sbuf_tensor('pa%d_' % l + name, shape, dt))
        pp = sb('pp', [128, NPP], F32)
        cb = sb('cb', [128, 256], BF16)
        onesbd = cb[:, 128:256]
        sel = sb('sel', [8, 512], F32)
        P.dma(SP, lambda e: e.dma_start(out=pp[:], in_=cx.pp[l]), writes=['pp'], tag='c')
        P.dma(SP, lambda e: e.dma_start(out=cb[:], in_=cx.cb2), writes=['cb'], tag='c')
        P.dma(SP, lambda e: e.dma_start(out=sel[:], in_=cx.sel8), writes=['sel'], tag='c')
        W = 1024
        raw, acc, rn, kn, qn = (sb(n, [128, W + 3], F32) for n in ('raw', 'acc', 'rn', 'kn', 'qn'))
        sqb, ob = sb('sqb', [128, W], BF16), [sb('ob%d' % i, [128, W], BF16) for i in range(2)]
        gct = sb('gct', [128, NCH], F32)
        mr = sb('mr', [8, 4, W], F32)
        m4c = sb('m4c', [8, NCH], F32)
        P.dma(SP, lambda e: e.dma_start(out=m4c[:], in_=cx.A_mrows[4].rearrange("h (c i) -> h c i", i=128)[:, :, 0], allow_slow_non_contiguous=True), writes=['m4c'], tag='c')
        nob = [0]

        def store(kind, pb, c0, n, src_f32, mul_row=None, srckey=None):
            o = ob[nob[0] % 2]
            ok = ('ob', nob[0] % 2)
            nob[0] += 1
            if mul_row is None:
                P.op(POOL, lambda e: e.tensor_copy(out=o[:, 0:n], in_=src_f32[:, 0:n]), reads=[srckey], writes=[ok])
            else:
                for j, t0 in enumerate(range(0, n, 512)):
                    m = min(512, n - t0)
                    bank = 2 + j % 2
                    P.op(PE, lambda e, t0=t0, m=m, bank=bank: e.matmul(ps[:, bank, 0:m], lhsT=sel[:, pb * 128:(pb + 1) * 128],
                                                                    rhs=mr[:, mul_row, t0:t0 + m], start=True, stop=True),
                         reads=['sel', 'mr'], writes=[('ps', bank)])
                    P.op(DVE, lambda e, t0=t0, m=m, bank=bank: e.tensor_tensor(out=o[:, t0:t0 + m], in0=src_f32[:, t0:t0 + m], in1=ps[:, bank, 0:m], op=ALU.mult),
                         reads=[srckey, ('ps', bank)], writes=[ok])
            P.dma(SP, lambda e: e.dma_start(out=cx.A_sc[kind][pb * 128:(pb + 1) * 128, c0:c0 + n], in_=o[:, 0:n], allow_slow_non_contiguous=(n == 1)),
                  reads=[ok], writes=[('A_sc', kind)], tag='sc')

        def conv_block(br, pb, c0, n, dst, dstkey):
            r0 = br * 512 + pb * 128
            b12 = br * 4 + pb
            if c0 == 0:
                P.op(POOL, lambda e: e.memset(raw[:, 0:3], 0.0), reads=['raw'], writes=['raw'])
                P.dma(SP, lambda e: e.dma_start(out=raw[:, 3:3 + n], in_=cx.projT[r0:r0 + 128, 0:n]), writes=['raw'], tag='m')
            elif c0 == T:
                P.dma(SP, lambda e: e.dma_start(out=raw[:, 0:3], in_=cx.aconvT[l, r0:r0 + 128, :]), writes=['raw'], tag='m')
                P.dma(SP, lambda e: e.dma_start(out=raw[:, 3:4], in_=cx.projT[r0:r0 + 128, T:T + 1], allow_slow_non_contiguous=True), writes=['raw'], tag='m')
            else:
                P.dma(SP, lambda e: e.dma_start(out=raw[:, 0:3 + n], in_=cx.projT[r0:r0 + 128, c0 - 3:c0 + n]), writes=['raw'], tag='m')
            cw = lambda i: pp[:, PP_CONV + b12 * 4 + i:PP_CONV + b12 * 4 + i + 1]
            P.op(DVE, lambda e: e.tensor_scalar(out=acc[:, 0:n], in0=raw[:, 0:n], scalar1=cw(0), scalar2=None, op0=ALU.mult), reads=['raw', 'pp'], writes=['acc'])
            for i in range(1, 4):
                P.op(DVE, lambda e, i=i: e.scalar_tensor_tensor(out=acc[:, 0:n], in0=raw[:, i:i + n], scalar=cw(i), in1=acc[:, 0:n], op0=ALU.mult, op1=ALU.add),
                     reads=['raw', 'acc', 'pp'], writes=['acc'])
            P.op(ACT, lambda e: e.activation(out=dst[:, 0:n], in_=acc[:, 0:n], func=AF.Silu), reads=['acc'], writes=[dstkey])

        for pb in range(4):
            for (c0, n) in [(i * W, W) for i in range(T // W)] + [(T, 1)]:
                P.dma(SP, lambda e, c0=c0, n=n: e.dma_start(out=mr[:, :, 0:n], in_=cx.A_mrows[0:4, :, c0:c0 + n].rearrange("k h t -> h k t"),
                                                          allow_slow_non_contiguous=(n == 1)), writes=['mr'], tag='m')
                for br, dst, dk, sc in ((0, qn, 'qn', 0.125), (1, kn, 'kn', 1.0)):
                    conv_block(br, pb, c0, n, dst, dk)
                    P.act(sqb[:, 0:n], dst[:, 0:n], AF.Square, reads=[dk], writes=['sqb'])
                    _l2_rn(P, nc, ps, sqb, onesbd, n, rn, ('sqb', 'rn'))
                    P.stt(dst[:, 0:n], dst[:, 0:n], sc, rn[:, 0:n], ALU.mult, ALU.mult, reads=[dk, 'rn'], writes=[dk])
                store('q', pb, c0, n, qn, None, 'qn')
                store('rz', pb, c0, n, qn, 1, 'qn')
                store('k', pb, c0, n, kn, None, 'kn')
                store('bz', pb, c0, n, kn, 0, 'kn')
                store('k2', pb, c0, n, kn, 2, 'kn')
                store('a2', pb, c0, n, kn, 3, 'kn')
                conv_block(2, pb, c0, n, qn, 'qn')
                store('v', pb, c0, n, qn, None, 'qn')
            for j in range(NCH):
                pass
            P.op(PE, lambda e, pb=pb: e.matmul(ps[:, 4, 0:NCH], lhsT=sel[:, pb * 128:(pb + 1) * 128],
                                               rhs=m4c[:], start=True, stop=True),
                 reads=['sel', 'm4c'], writes=[('ps', 4)])
            P.op(DVE, lambda e: e.tensor_copy(out=gct[:], in_=ps[:, 4, 0:NCH]), reads=[('ps', 4)], writes=['gct'])
            P.dma(SP, lambda e, pb=pb: e.dma_start(out=cx.A_gc[pb * 128:(pb + 1) * 128, :], in_=gct[:]), reads=['gct'], writes=['A_gc'], tag='sc')
        P.dma(SP, lambda e: e.dma_start(out=cx.o_aconv[l, 0], in_=cx.projT[0:1536, T - 3:T], allow_slow_non_contiguous=True), writes=['oac0'], tag='so')
        P.dma(SP, lambda e: e.dma_start(out=cx.o_aconv[l, 1, :, 0:2], in_=cx.aconvT[l, :, 1:3], allow_slow_non_contiguous=True), writes=['oac1'], tag='so')
        P.dma(SP, lambda e: e.dma_start(out=cx.o_aconv[l, 1, :, 2:3], in_=cx.projT[0:1536, T:T + 1], allow_slow_non_contiguous=True), writes=['oac2'], tag='so')
        P.emit()


def phase_preB(nc, cx, l):
    P = Prog(nc)
    ps = cx.ps
    W = 1024
    with contextlib.ExitStack() as st:
        sb = lambda name, shape, dt: st.enter_context(nc.sbuf_tensor('pb%d_' % l + name, shape, dt))
        pp = sb('pp', [128, NPP], F32)
        cb = sb('cb', [128, 256], BF16)
        onesbd = cb[:, 128:256]
        lwf, lwb = sb('lwf', [128, 512], F32), sb('lwb', [128, 512], BF16)
        oma = sb('oma', [128, 4], F32)
        rmask = sb('rmask', [128, W], F32)
        P.ld(pp[:], cx.pp[l], writes=['pp'], tag='c')
        P.ld(cb[:], cx.cb2, writes=['cb'], tag='c')
        P.ld(lwf[:], cx.lora_w[l], writes=['lwf'], tag='c')
        P.cp(POOL, lwb[:], lwf[:], reads=['lwf'], writes=['lwb'])
        P.ts(DVE, oma[:], pp[:, PP_ALPHA:PP_ALPHA + 4], -1.0, ALU.mult, 1.0, ALU.add, reads=['pp'], writes=['oma'])
        P.ms(POOL, rmask[:], 1.0, writes=['rmask'])
        P.ms(POOL, rmask[:].rearrange("p (c i) -> p c i", i=128)[:, :, 0:1], 0.0, reads=['rmask'], writes=['rmask'])
        sm, smd = sb('sm', [128, W + 1], F32), sb('smd', [128, W], F32)
        lwin = sb('lwin', [128, W], BF16)
        raw = {k: sb('raw_' + k, [128, W + 1], F32) for k in 'rkv'}
        sh = {k: sb('sh_' + k, [128, W], F32) for k in 'rkv'}
        dtmp = sb('dtmp', [128, W], F32)
        lw, av, kx, rn, kt, t1, Lc, G, Gi, Gm, Ah, Kh = (sb(n, [128, W], F32) for n in ('lw', 'av', 'kx', 'rn', 'kt', 't1', 'Lc', 'G', 'Gi', 'Gm', 'Ah', 'Kh'))
        sqb = sb('sqb', [128, W], BF16)
        ob = [sb('ob%d' % i, [128, W], BF16) for i in range(3)]
        bo = sb('bo', [128, W], F32)
        gct = [sb('gct%d' % i, [128, NCH], F32) for i in range(4)]
        nob = [0]

        def store(kind, pb, c0, n, fn):
            j = nob[0] % 3
            nob[0] += 1
            fn(ob[j][:, 0:n], ('ob', j))
            P.ld(cx.B_sc[kind][pb * 128:(pb + 1) * 128, c0:c0 + n], ob[j][:, 0:n], reads=[('ob', j)], writes=[('B_sc', kind)], tag='sc', slow=(n == 1))

        def halo_load(tile, key, r0, c0, n, col12):
            if c0 == 0:
                P.ms(POOL, tile[:, 0:1], 0.0, reads=[key], writes=[key])
                P.ld(tile[:, 1:1 + n], cx.projT[r0:r0 + 128, 0:n], writes=[key], tag='m')
            elif c0 == T:
                P.ld(tile[:, 0:1], cx.bshiftT[l, :, col12:col12 + 1], writes=[key], tag='m', slow=True)
                P.ld(tile[:, 1:2], cx.projT[r0:r0 + 128, T:T + 1], writes=[key], tag='m', slow=True)
            else:
                P.ld(tile[:, 0:1 + n], cx.projT[r0:r0 + 128, c0 - 1:c0 + n], writes=[key], tag='m')

        def shift(dst, dkey, tile, key, n, mucol):
            P.tt(POOL, dtmp[:, 0:n], tile[:, 0:n], tile[:, 1:1 + n], ALU.subtract, reads=[key], writes=['dtmp'])
            P.stt(dst[:, 0:n], dtmp[:, 0:n], pp[:, mucol:mucol + 1], tile[:, 1:1 + n], ALU.mult, ALU.add, reads=['dtmp', key, 'pp'], writes=[dkey])

        for (c0, n) in [(i * W, W) for i in range(T // W)] + [(T, 1)]:
            nch = max(1, n // 128)
            halo_load(sm, 'sm', R_SM, c0, n, 12)
            shift(smd, 'smd', sm, 'sm', n, PP_MUS)
            P.act(lwin[32:64, 0:n], smd[32:64, 0:n], AF.Tanh, reads=['smd'], writes=['lwin'])
            P.cp(POOL, lwin[64:96, 0:n], smd[64:96, 0:n], reads=['smd'], writes=['lwin'])
            for pb in range(4):
                for j, k in enumerate('rkv'):
                    halo_load(raw[k], 'raw_' + k, R_BR + j * 512 + pb * 128, c0, n, j * 4 + pb)
                    shift(sh[k], 'sh_' + k, raw[k], 'raw_' + k, n, PP_MU + j * 4 + pb)
                rs, ks, vs = sh['r'], sh['k'], sh['v']
                for jt, t0 in enumerate(range(0, n, 512)):
                    m = min(512, n - t0)
                    cs = slice(t0, t0 + m)
                    P.mm(ps[:, 2, 0:m], lwb[32:64, pb * 128:(pb + 1) * 128], lwin[32:64, cs], reads=['lwb', 'lwin'], writes=[('ps', 2)])
                    P.mm(ps[:, 3, 0:m], lwb[64:96, pb * 128:(pb + 1) * 128], lwin[64:96, cs], reads=['lwb', 'lwin'], writes=[('ps', 3)])
                    P.act(lw[:, cs], ps[:, 2, 0:m], AF.Sigmoid, reads=[('ps', 2), 'pp'], writes=['lw'], bias=pp[:, PP_W0 + pb:PP_W0 + pb + 1])
                    P.act(av[:, cs], ps[:, 3, 0:m], AF.Sigmoid, reads=[('ps', 3), 'pp'], writes=['av'], bias=pp[:, PP_A0 + pb:PP_A0 + pb + 1])
                P.ts(DVE, lw[:, 0:n], lw[:, 0:n], -RWKV_DS, ALU.mult, reads=['lw'], writes=['lw'])
                P.ts(DVE, kx[:, 0:n], ks[:, 0:n], pp[:, PP_XI + pb:PP_XI + pb + 1], ALU.mult, reads=['sh_k', 'pp'], writes=['kx'])
                P.act(sqb[:, 0:n], kx[:, 0:n], AF.Square, reads=['kx'], writes=['sqb'])
                _l2_rn(P, nc, ps, sqb, onesbd, n, rn, ('sqb', 'rn'))
                P.tt(DVE, kx[:, 0:n], kx[:, 0:n], rn[:, 0:n], ALU.mult, reads=['kx', 'rn'], writes=['kx'])
                P.ts(DVE, t1[:, 0:n], av[:, 0:n], pp[:, PP_ALPHA + pb:PP_ALPHA + pb + 1], ALU.mult, oma[:, pb:pb + 1], ALU.add, reads=['av', 'pp', 'oma'], writes=['t1'])
                P.tt(POOL, kt[:, 0:n], ks[:, 0:n], t1[:, 0:n], ALU.mult, reads=['sh_k', 't1'], writes=['kt'])
                P.tt(POOL, t1[:, 0:n], rs[:, 0:n], kt[:, 0:n], ALU.mult, reads=['sh_r', 'kt', 't1'], writes=['t1'])
                P.ts(DVE, sqb[:, 0:n], t1[:, 0:n], pp[:, PP_RHO + pb:PP_RHO + pb + 1], ALU.mult, reads=['t1', 'pp', 'rn'], writes=['sqb'])
                for jt, t0 in enumerate(range(0, n, 512)):
                    m = min(512, n - t0)
                    cs = slice(t0, t0 + m)
                    P.mm(ps[:, 4 + jt % 2, 0:m], onesbd, sqb[:, cs], reads=['sqb', 'cb'], writes=[('ps', 4 + jt % 2)])
                    P.tt(DVE, bo[:, cs], ps[:, 4 + jt % 2, 0:m], vs[:, cs], ALU.mult, reads=[('ps', 4 + jt % 2), 'sh_v'], writes=['bo'])
                P.ld(cx.B_bonus[pb * 128:(pb + 1) * 128, c0:c0 + n], bo[:, 0:n], reads=['bo'], writes=['B_bonus'], tag='sc', slow=(n == 1))
                P.op(DVE, lambda e, n=n: e.tensor_tensor_scan(out=Lc[:, 0:n], data0=rmask[:, 0:n], data1=lw[:, 0:n], initial=0.0, op0=ALU.mult, op1=ALU.add),
                     reads=['lw', 'rmask'], writes=['Lc'])
                P.act(G[:, 0:n], Lc[:, 0:n], AF.Exp, reads=['Lc'], writes=['G'])
                P.act(Gi[:, 0:n], Lc[:, 0:n], AF.Exp, reads=['Lc'], writes=['Gi'], scale=-1.0)
                P.tt(POOL, Gm[:, 0:n], Lc[:, 0:n], lw[:, 0:n], ALU.subtract, reads=['Lc', 'lw'], writes=['Gm'])
                P.act(Gm[:, 0:n], Gm[:, 0:n], AF.Exp, reads=['Gm'], writes=['Gm'])
                if n > 1:
                    Gv = G[:, 0:n].rearrange("p (c i) -> p c i", i=128)
                    GCb = Gv[:, :, 127:128].to_broadcast([128, nch, 128])
                    P.cp(POOL, gct[pb][:, c0 // 128:c0 // 128 + nch], Gv[:, :, 127], reads=['G'], writes=[('gct', pb)])
                    v3 = lambda ap: ap.rearrange("p (c i) -> p c i", i=128)
                else:
                    GCb = G[:, 0:1]
                    P.cp(POOL, gct[pb][:, 32:33], G[:, 0:1], reads=['G'], writes=[('gct', pb)])
                    v3 = lambda ap: ap
                P.tt(POOL, t1[:, 0:n], av[:, 0:n], kx[:, 0:n], ALU.mult, reads=['av', 'kx', 'sqb'], writes=['t1'])
                P.tt(DVE, Ah[:, 0:n], t1[:, 0:n], Gi[:, 0:n], ALU.mult, reads=['t1', 'Gi'], writes=['Ah'])
                P.tt(DVE, Kh[:, 0:n], kt[:, 0:n], Gi[:, 0:n], ALU.mult, reads=['kt', 'Gi'], writes=['Kh'])
                store('ah', pb, c0, n, lambda o, ok: P.cp(POOL, o, Ah[:, 0:n], reads=['Ah'], writes=[ok]))
                store('kh', pb, c0, n, lambda o, ok: P.cp(ACT, o, Kh[:, 0:n], reads=['Kh'], writes=[ok]))
                store('bh', pb, c0, n, lambda o, ok: P.tt(DVE, o, kx[:, 0:n], Gm[:, 0:n], ALU.mult, reads=['kx', 'Gm'], writes=[ok]))
                store('rh', pb, c0, n, lambda o, ok: P.tt(POOL, o, rs[:, 0:n], G[:, 0:n], ALU.mult, reads=['sh_r', 'G'], writes=[ok]))
                store('k2', pb, c0, n, lambda o, ok: P.tt(DVE, v3(o), v3(Kh[:, 0:n]), GCb, ALU.mult, reads=['Kh', 'G'], writes=[ok]))
                store('a2', pb, c0, n, lambda o, ok: P.tt(POOL, v3(o), v3(Ah[:, 0:n]), GCb, ALU.mult, reads=['Ah', 'G'], writes=[ok]))
                store('v', pb, c0, n, lambda o, ok: P.cp(ACT, o, vs[:, 0:n], reads=['sh_v'], writes=[ok]))
        for pb in range(4):
            P.ld(cx.B_gc[pb * 128:(pb + 1) * 128, :], gct[pb][:], reads=[('gct', pb)], writes=['B_gc'], tag='sc')
        for j, col in ((0, T - 1), (1, T)):
            P.ld(cx.o_bshift[l, j, :, 0:12], cx.projT[R_BR:R_BR + 1536, col:col + 1].rearrange("(b p) o -> p (b o)", p=128), writes=[('obs', j)], tag='so', slow=True)
            P.ld(cx.o_bshift[l, j, :, 12:13], cx.projT[R_SM:R_SM + 128, col:col + 1], writes=[('obs2', j)], tag='so', slow=True)
        P.emit()


def phase_scan(nc, cx, l, br, chunks=range(NCH)):
    P = Prog(nc)
    ps = cx.ps
    isA = br == 'A'
    SC = cx.A_sc if isA else cx.B_sc
    GC = cx.A_gc if isA else cx.B_gc
    names = dict(XA='k', XB='k', XK='k', XR='q', XBz='bz', XRz='rz', K2='k2', A2='a2', V='v') if isA else \
        dict(XA='ah', XB='bh', XK='kh', XR='rh', XBz='bh', XRz='rh', K2='k2', A2='a2', V='v')
    kinds = sorted(set(names.values()))
    with contextlib.ExitStack() as st:
        sb = lambda name, shape, dt: st.enter_context(nc.sbuf_tensor('s%s%d_' % (br, l) + name, shape, dt))
        cb = sb('cb', [128, 128 + 3 * 128], BF16)
        identb = cb[:, 0:128]
        cf = sb('cf', [128, 4 * 4 * 128], F32)
        ident4 = cf[:, 0:512].rearrange("p (h t) -> p h t", h=4)
        k01 = [cf[:, 512 * (1 + i):512 * (2 + i)].rearrange("p (h t) -> p h t", h=4) for i in range(3)]
        pp = sb('pp', [128, NPP], F32)
        P.dma(SP, lambda e: e.dma_start(out=cb[:], in_=cx.cb3), writes=['cb'], tag='c')
        P.dma(SP, lambda e: e.dma_start(out=cf[:], in_=cx.cf3), writes=['cf'], tag='c')
        P.dma(SP, lambda e: e.dma_start(out=pp[:], in_=cx.pp[l]), writes=['pp'], tag='c')
        op_t = {k: [sb('x_%s%d' % (k, i), [64, 8, 128], BF16) for i in range(2)] for k in kinds}
        tok = {k: [sb('t_%s%d' % (k, i), [128, 8, 64], BF16) for i in range(2)] for k in ('k2', 'a2', 'v')}
        er = [sb('er%d' % i, [2, 4, 8, 128], F32) for i in range(2)] if isA else None
        E = [[sb('E%d_%d' % (k, g), [128, 4, 128], F32) for g in range(2)] for k in range(5)] if isA else None
        NDT = F32
        Lb = [[sb('Lb%d_%d' % (g, i), [128, 4, 128], NDT) for i in range(2)] for g in range(2)]
        Uf = [sb('Uf%d' % g, [128, 4, 128], F32) for g in range(2)]
        Ub = [[Uf[g] if i == 0 else sb('Ub%d_%d' % (g, i), [128, 4, 128], NDT) for i in range(3)] for g in range(2)]
        Pf = [sb('Pf%d' % g, [128, 4, 128], F32) for g in range(2)]
        Pb = [sb('Pb%d' % g, [128, 4, 128], BF16) for g in range(2)]
        Mkb, RKb, RAb = ([sb('%s%d' % (n, g), [128, 4, 128], BF16) for g in range(2)] for n in ('Mkb', 'RKb', 'RAb'))
        Xb, Pnb = ([sb('%s%d' % (n, g), [128, 4, 64], BF16) for g in range(2)] for n in ('Xb', 'Pnb'))
        Zf, Zb = sb('Zf', [64, 8, 64], F32), sb('Zb', [64, 8, 64], BF16)
        Zt = sb('Ztmp', [64, 8, 64], F32)
        gct = sb('gct', [64, 8, NCH], F32)
        oT = [sb('oT%d' % i, [64, 8, 128], F32) for i in range(2)]
        zt = [sb('zt%d' % i, [64, 8, 128], F32) for i in range(2)]
        w1, w2 = sb('w1', [64, 8, 128], F32), sb('w2', [64, 8, 128], F32)
        wb = sb('wb', [64, 8, 128], BF16)
        gob = [sb('gob%d' % i, [64, 8, 128], BF16) for i in range(2)]
        ones64b = sb('ones64b', [64, 64], BF16)
        ones64f = sb('ones64f', [64, 64], F32)
        P.op(POOL, lambda e: e.memset(ones64b[:], 1.0), writes=['ones64b'])
        P.op(POOL, lambda e: e.memset(ones64f[:], 1.0), writes=['ones64f'])
        P.dma(SP, lambda e: e.dma_start(out=gct[:], in_=GC.rearrange("(h d) c -> d h c", d=64)), writes=['gct'], tag='c')
        if not isA:
            bon = [sb('bon%d' % i, [64, 8, 128], F32) for i in range(2)]
            gng, gnb = sb('gng', [64, 8], F32), sb('gnb', [64, 8], F32)
            P.dma(SP, lambda e: e.dma_start(out=gng[:], in_=cx.gn_g[l].rearrange("(h d) -> d h", d=64), allow_slow_non_contiguous=True), writes=['gng'], tag='c')
            P.dma(SP, lambda e: e.dma_start(out=gnb[:], in_=cx.gn_b[l].rearrange("(h d) -> d h", d=64), allow_slow_non_contiguous=True), writes=['gnb'], tag='c')
        P.op(POOL, lambda e: e.memset(Zf[:], 0.0), writes=['Zf'])
        P.op(POOL, lambda e: e.memset(Zb[:], 0.0), writes=['Zb'])
        zrow = R_AZ if isA else R_BZ
        grow = 0 if isA else 512
        st_in = cx.sA_in if isA else cx.sB_in
        st_out = cx.o_AS if isA else cx.o_BS
        flat = lambda ap: ap.rearrange("p h t -> p (h t)")
        b4 = lambda ap: ap.rearrange("p (h t) -> p h t", h=4)
        for c in chunks:
            b = c % 2
            c0 = c * 128
            n = 128 if c < 32 else 1
            slow = (n == 1)
            for k in kinds:
                if n == 1:
                    P.ms(POOL, op_t[k][b][:], 0.0, writes=[('x', k, b)])
                P.ld(op_t[k][b][:, :, 0:n], SC[k][:, c0:c0 + n].rearrange("(h d) t -> d h t", d=64), writes=[('x', k, b)], tag='ld', slow=slow)
            if n == 1:
                P.ms(POOL, zt[b][:], 0.0, writes=[('zt', b)])
            P.ld(zt[b][:, :, 0:n], cx.projT[zrow:zrow + 512, c0:c0 + n].rearrange("(h d) t -> d h t", d=64), writes=[('zt', b)], tag='ld', slow=slow)
            if not isA:
                if n == 1:
                    P.ms(POOL, bon[b][:], 0.0, writes=[('bon', b)])
                P.ld(bon[b][:, :, 0:n], cx.B_bonus[:, c0:c0 + n].rearrange("(h d) t -> d h t", d=64), writes=[('bon', b)], tag='ld', slow=slow)
            if isA:
                for kd in range(4):
                    P.ld(er[b][:, kd], cx.A_erows[kd, :, :, c0:c0 + 128], writes=[('er', b)], tag='ld')
            if c == 32:
                P.ld(st_out[l, 0], Zf[:], reads=['Zf'], writes=['st_out0'], tag='so')
                P.ld(Zf[:], st_in[l], reads=['st_out0'], writes=['Zf'], tag='ld')
                P.cp(ACT, Zb[:], Zf[:], reads=['Zf'], writes=['Zb'])
            X = {role: op_t[k][b] for role, k in names.items()}
            xk = {role: ('x', k, b) for role, k in names.items()}
            for j, k in enumerate(('k2', 'a2', 'v')):
                bank = 5 + j % 2
                pst = ps[:, bank, :].bitcast(BF16)[:, 0:512].rearrange("p (h d) -> p h d", d=64)
                for h in range(8):
                    P.tr(pst[:, h, :], op_t[k][b][:, h, :], identb[0:64, 0:64], reads=[('x', k, b), 'cb'], writes=[('ps', bank)])
                P.cp(ACT if j != 1 else DVE, tok[k][b][:], pst, reads=[('ps', bank)], writes=[('tok', k, b)])
            K2t, A2t, Vt = tok['k2'][b], tok['a2'][b], tok['v'][b]
            tkk = [('tok', 'k2', b), ('tok', 'a2', b), ('tok', 'v', b)]
            for g in range(2):
                hs = list(range(4 * g, 4 * g + 4))
                if isA:
                    combos = ((0, 1, 1), (1, 0, 2), (2, 0, 2), (2, 3, 3), (1, 3, 3))
                    for k5, (ka, kb_, mi) in enumerate(combos):
                        bank = k5 % 2
                        for hh, h in enumerate(hs):
                            o_ = ps[:, bank, hh * 128:(hh + 1) * 128]
                            P.mm(o_, er[b][0:2, ka, h, :], er[b][0:2, kb_, h, :], True, False, reads=[('er', b)], writes=[('ps', bank)])
                            P.mm(o_, identb, cb[:, 128 * mi:128 * (mi + 1)], False, True, reads=['cb'], writes=[('ps', bank)])
                        P.act(E[k5][g][:], b4(ps[:, bank, :]), AF.Exp, reads=[('ps', bank)], writes=[('E', k5, g)])
                    mult = [E[k5][g][:] for k5 in range(5)]
                    mkeys = [('E', k5, g) for k5 in range(5)]
                else:
                    mult = [k01[0], k01[1], k01[1], k01[2], k01[2]]
                    mkeys = ['cf'] * 5
                specs = (('XB', 'XA'), ('XA', 'XB'), ('XK', 'XB'), ('XK', 'XR'), ('XA', 'XR'))
                dsts = ((Lb[g][0], ('Lb', g, 0)), (Uf[g], ('Uf', g)), (Mkb[g], ('Mkb', g)), (RKb[g], ('RKb', g)), (RAb[g], ('RAb', g)))
                for k5, ((ra, rb), (dst, dkey)) in enumerate(zip(specs, dsts)):
                    bank = 2 + k5 % 3
                    for hh, h in enumerate(hs):
                        P.mm(ps[:, bank, hh * 128:(hh + 1) * 128], X[ra][:, h, :], X[rb][:, h, :], reads=[xk[ra], xk[rb]], writes=[('ps', bank)])
                    P.tt(DVE, dst[:], b4(ps[:, bank, :]), mult[k5], ALU.mult, reads=[('ps', bank), mkeys[k5]], writes=[dkey])
                P.tt(POOL, Pf[g][:], ident4, Uf[g][:], ALU.subtract, reads=['cf', ('Uf', g)], writes=[('Pf', g)])
            ukey = lambda g, i: ('Uf', g) if i == 0 else ('Ub', g, i)
            for lvl in range(1, 7):
                i0, i1 = (lvl - 1) % 2, lvl % 2
                u0i, u1i = (0 if lvl == 1 else 1 + lvl % 2), 1 + (lvl + 1) % 2
                for g in range(2):
                    for hh in range(4):
                        P.mm(ps[:, g, hh * 128:(hh + 1) * 128], Ub[g][u0i][:, hh, :], Lb[g][i0][:, hh, :], reads=[ukey(g, u0i), ('Lb', g, i0)], writes=[('ps', g)])
                    if lvl < 6:
                        for hh in range(4):
                            P.mm(ps[:, 2 + g, hh * 128:(hh + 1) * 128], Lb[g][i0][:, hh, :], Ub[g][u0i][:, hh, :], reads=[ukey(g, u0i), ('Lb', g, i0)], writes=[('ps', 2 + g)])
                    P.cp(ACT, Lb[g][i1][:], b4(ps[:, g, :]), reads=[('ps', g)], writes=[('Lb', g, i1)])
                    if lvl < 6:
                        P.cp(ACT, Ub[g][u1i][:], b4(ps[:, 2 + g, :]), reads=[('ps', 2 + g)], writes=[ukey(g, u1i)])
                for g in range(2):
                    for hh in range(4):
                        P.mm(ps[:, 4 + g, hh * 128:(hh + 1) * 128], Lb[g][i1][:, hh, :], Pf[g][:, hh, :], reads=[('Lb', g, i1), ('Pf', g)], writes=[('ps', 4 + g)])
                    P.tt(DVE, Pf[g][:], Pf[g][:], b4(ps[:, 4 + g, :]), ALU.add, reads=[('ps', 4 + g), ('Pf', g)], writes=[('Pf', g)])
            for g in range(2):
                P.cp(POOL, Pb[g][:], Pf[g][:], reads=[('Pf', g)], writes=[('Pb', g)])
            for g in range(2):
                hs = list(range(4 * g, 4 * g + 4))
                gs = slice(4 * g, 4 * g + 4)
                bx = 6 + g
                psX = ps[:, bx, 0:256].rearrange("p (h v) -> p h v", v=64)
                for hh, h in enumerate(hs):
                    P.mm(psX[:, hh, :], X['XBz'][:, h, :], Zb[:, h, :], True, False, reads=[xk['XBz'], 'Zb'], writes=[('ps', bx)])
                    P.mm(psX[:, hh, :], Mkb[g][:, hh, :], Vt[:, h, :], False, True, reads=[('Mkb', g), tkk[2]], writes=[('ps', bx)])
                P.cp(ACT, Xb[g][:], psX, reads=[('ps', bx)], writes=[('Xb', g)])
                psP = ps[:, bx, 256:512].rearrange("p (h v) -> p h v", v=64)
                for hh in range(4):
                    P.mm(psP[:, hh, :], Pb[g][:, hh, :], Xb[g][:, hh, :], reads=[('Pb', g), ('Xb', g)], writes=[('ps', bx)])
                P.act(Pnb[g][:], psP, AF.Copy, reads=[('ps', bx)], writes=[('Pnb', g)], scale=-1.0)
                bo = g
                psO = b4(ps[0:64, bo, :])
                for hh, h in enumerate(hs):
                    P.mm(psO[:, hh, :], Zb[:, h, :], X['XRz'][:, h, :], True, False, reads=['Zb', xk['XRz']], writes=[('ps', bo)])
                    P.mm(psO[:, hh, :], Vt[:, h, :], RKb[g][:, hh, :], False, False, reads=[tkk[2], ('RKb', g)], writes=[('ps', bo)])
                    P.mm(psO[:, hh, :], Pnb[g][:, hh, :], RAb[g][:, hh, :], False, True, reads=[('Pnb', g), ('RAb', g)], writes=[('ps', bo)])
                P.cp(DVE, oT[b][:, gs, :], psO, reads=[('ps', bo)], writes=[('oT', b)])
                bz = 2 + g
                psZ = ps[0:64, bz, 0:256].rearrange("p (h v) -> p h v", v=64)
                for hh, h in enumerate(hs):
                    P.mm(psZ[:, hh, :], K2t[:, h, :], Vt[:, h, :], True, False, reads=[tkk[0], tkk[2]], writes=[('ps', bz)])
                    P.mm(psZ[:, hh, :], A2t[:, h, :], Pnb[g][:, hh, :], False, True, reads=[tkk[1], ('Pnb', g)], writes=[('ps', bz)])
                P.tt(DVE, Zt[:, gs, :], Zf[:, gs, :], gct[:, gs, c:c + 1].to_broadcast([64, 4, 64]), ALU.mult, reads=['Zf', 'gct'], writes=['Zt'])
                P.tt(DVE, Zf[:, gs, :], Zt[:, gs, :], psZ, ALU.add, reads=['Zt', ('ps', bz)], writes=['Zf'])
            P.cp(ACT, Zb[:], Zf[:], reads=['Zf'], writes=['Zb'])
            o = oT[b]
            P.act(zt[b][:], zt[b][:], AF.Silu, reads=[('zt', b)], writes=[('zt', b)])
            if isA:
                P.act(wb[:], o[:], AF.Square, reads=[('oT', b)], writes=['wb'])
                for j in range(2):
                    js = slice(j * 512, (j + 1) * 512)
                    P.mm(ps[0:64, 4 + j, :], ones64b[:], flat(wb[:])[:, js], reads=['wb', 'ones64b'], writes=[('ps', 4 + j)])
                    P.act(flat(w1[:])[:, js], ps[0:64, 4 + j, :], AF.Ln, reads=[('ps', 4 + j)], writes=['w1'], scale=1.0 / 64, bias=1e-6)
                P.act(w1[:], w1[:], AF.Exp, reads=['w1'], writes=['w1'], scale=-0.5)
                P.stt(w2[:], o[:], pp[0:64, PP_NORMA:PP_NORMA + 1], w1[:], ALU.mult, ALU.mult, reads=[('oT', b), 'pp', 'w1'], writes=['w2'])
            else:
                for j in range(2):
                    js = slice(j * 512, (j + 1) * 512)
                    P.mm(ps[0:64, 4 + j, :], ones64f[:], flat(o[:])[:, js], reads=[('oT', b), 'ones64f'], writes=[('ps', 4 + j)])
                    P.stt(flat(w2[:])[:, js], ps[0:64, 4 + j, :], -1.0 / 64, flat(o[:])[:, js], ALU.mult, ALU.add, reads=[('ps', 4 + j), ('oT', b)], writes=['w2'])
                P.act(w1[:], w2[:], AF.Square, reads=['w2'], writes=['w1'])
                for j in range(2):
                    js = slice(j * 512, (j + 1) * 512)
                    P.mm(ps[0:64, 4 + j, :], ones64f[:], flat(w1[:])[:, js], reads=['w1', 'ones64f'], writes=[('ps', 4 + j)])
                    P.act(flat(w1[:])[:, js], ps[0:64, 4 + j, :], AF.Ln, reads=[('ps', 4 + j)], writes=['w1'], scale=1.0 / 64, bias=64e-5)
                P.act(w1[:], w1[:], AF.Exp, reads=['w1'], writes=['w1'], scale=-0.5)
                P.tt(DVE, w2[:], w2[:], w1[:], ALU.mult, reads=['w2', 'w1'], writes=['w2'])
                P.tt(POOL, w2[:], w2[:], gng[:].unsqueeze(2).to_broadcast([64, 8, 128]), ALU.mult, reads=['w2', 'gng'], writes=['w2'])
                P.tt(POOL, w2[:], w2[:], gnb[:].unsqueeze(2).to_broadcast([64, 8, 128]), ALU.add, reads=['w2', 'gnb'], writes=['w2'])
                P.tt(POOL, w2[:], w2[:], bon[b][:], ALU.add, reads=['w2', ('bon', b)], writes=['w2'])
            P.tt(DVE, gob[b][:], w2[:], zt[b][:], ALU.mult, reads=['w2', ('zt', b)], writes=[('gob', b)])
            P.ld(cx.gatedT[grow:grow + 512, c0:c0 + n].rearrange("(h d) t -> d h t", d=64), gob[b][:, :, 0:n], reads=[('gob', b)], writes=['gatedT'], tag='go', slow=slow)
        if NCH - 1 in chunks:
            P.dma(SP, lambda e: e.dma_start(out=st_out[l, 1], in_=Zf[:]), reads=['Zf'], writes=['st_out1'], tag='so')
        P.emit()


def phase_mixCs(nc, cx, l):
    P = Prog(nc)
    ps = cx.ps
    NPG = 128
    with contextlib.ExitStack() as st:
        sb = lambda name, shape, dt: st.enter_context(nc.sbuf_tensor('cs%d_' % l + name, shape, dt))
        cf = sb('cf', [128, 384], F32)
        identf, tri_gt = cf[:, 0:128], cf[:, 256:384]
        bm = sb('bm', [8, 512], F32)
        idx = sb('idx', [128, 1], I32)
        rows = sb('rows', [128, 2056], F32)
        qb, kb, vb, fb = rows[:, 0:512], rows[:, 512:1024], rows[:, 1024:1536], rows[:, 2048:2056]
        negb = sb('negb', [128, 8], F32)
        ones = sb('ones', [128, 128], F32)
        lfp, Cw, sc, pr = (sb(n, [128, 128, 8], F32) for n in ('lfp', 'Cw', 'sc', 'pr'))
        kch = [sb('kch%d' % i, [128, 16, 512], F32) for i in range(2)]
        lfn, tot, base, mloc, snew, pnew, den, rden, mxb = (sb(n, [128, 8], F32) for n in ('lfn', 'tot', 'base', 'mloc', 'snew', 'pnew', 'den', 'rden', 'mxb'))
        tmp512 = sb('tmp512', [128, 512], F32)
        mx8, D8 = sb('mx8', [8, 1], F32), sb('D8', [8, 8], F32)
        accs = sb('accs', [8, 512], F32)
        oc, zc = sb('oc', [128, 4], F32), sb('zc', [128, 4], F32)
        gb = sb('gb', [128, 4], BF16)
        P.ld(cf[:], cx.cf32s, writes=['cf'], tag='c')
        P.ld(bm[:], cx.bmask, writes=['bm'], tag='c')
        P.ld(idx[:], cx.ptab, writes=['idx'], tag='c')
        P.ld(negb[:], cx.b_f[l:l + 1, :].partition_broadcast(128), writes=['negb'], tag='c')
        P.ts(POOL, negb[:], negb[:], -1.0, ALU.mult, reads=['negb'], writes=['negb'])
        P.ms(POOL, ones[:], 1.0, writes=['ones'])
        for j, r0 in enumerate((R_CQ, R_CK, R_CV, R_CZ)):
            P.ld(cx.srow[0:1, j * 512:(j + 1) * 512].rearrange("o c -> c o"), cx.projT[r0:r0 + 512, T:T + 1], writes=[('srow', j)], tag='sr', slow=True)
        P.ld(cx.srow[0:1, 2048:2056].rearrange("o c -> c o"), cx.projT[R_SM + SM_F:R_SM + SM_F + 8, T:T + 1], writes=[('srow', 4)], tag='sr', slow=True)
        P.ld(rows[:], cx.srow[0:1, :].partition_broadcast(128), reads=[('srow', j) for j in range(5)], writes=['rows'], tag='c')
        P.tt(DVE, lfn[:], fb, negb[:], ALU.subtract, reads=['rows', 'negb'], writes=['lfn'])
        P.act(lfn[:], lfn[:], AF.Exp, reads=['lfn'], writes=['lfn'], scale=-1.0)
        P.act(lfn[:], lfn[:], AF.Ln, reads=['lfn'], writes=['lfn'], bias=1.0)
        P.ts(DVE, lfn[:], lfn[:], -1.0, ALU.mult, reads=['lfn'], writes=['lfn'])
        P.ld(cx.o_clf[l, :, T:T + 1].rearrange("h o -> o h"), lfn[0:1, :], reads=['lfn'], writes=['o_clf_s'], tag='so', slow=True)
        ioff = bass.IndirectOffsetOnAxis(ap=idx[:, 0:1], axis=0)
        P.dma(POOL, lambda e: e.indirect_dma_start(out=lfp[:].rearrange("p t h -> p (t h)"), out_offset=None,
                                                   in_=cx.cache_lf.rearrange("l n t h -> (l n) (t h)"), in_offset=ioff,
                                                   element_offset=l * 1280 * 1024), reads=['idx'], writes=['lfp'], tag='ig')
        for h in range(8):
            P.op(DVE, lambda e, h=h: e.tensor_tensor_scan(out=Cw[:, :, h], data0=ones[:, :], data1=lfp[:, :, h], initial=0.0, op0=ALU.mult, op1=ALU.add),
                 reads=['lfp', 'ones'], writes=['Cw'])
        P.cp(DVE, tot[:], Cw[:, 127, :], reads=['Cw'], writes=['tot'])
        P.mm(ps[:, 0, 0:8], tri_gt, tot[:], reads=['cf', 'tot'], writes=[('ps', 0)])
        P.tt(DVE, base[:], tot[:], ps[:, 0, 0:8], ALU.add, reads=['tot', ('ps', 0)], writes=['base'])
        P.tt(DVE, base[:], base[:], lfn[:], ALU.add, reads=['base', 'lfn'], writes=['base'])
        P.tt(DVE, Cw[:], base[:].unsqueeze(1).to_broadcast([128, 128, 8]), Cw[:], ALU.subtract, reads=['base', 'Cw'], writes=['Cw'])
        ck = cx.cache_k.rearrange("l n t h d -> (l n) (t h d)")
        cv = cx.cache_v.rearrange("l n t h d -> (l n) (t h d)")
        ng = [0]

        def gather(src, tc):
            b = ng[0] % 2
            ng[0] += 1
            P.dma(POOL, lambda e: e.indirect_dma_start(out=kch[b][:].rearrange("p t c -> p (t c)"), out_offset=None, in_=src, in_offset=ioff,
                                                       element_offset=l * 1280 * 65536 + tc * 8192),
                  reads=['idx'], writes=[('kch', b)], tag='ig')
            return b
        for tc in range(8):
            b = gather(ck, tc)
            eng = DVE if tc % 2 == 0 else POOL
            P.tt(eng, kch[b][:], kch[b][:], qb.unsqueeze(1).to_broadcast([128, 16, 512]), ALU.mult, reads=[('kch', b), 'rows'], writes=[('kch', b)])
            P.op(DVE, lambda e, b=b, tc=tc: e.tensor_reduce(out=sc[:, tc * 16:(tc + 1) * 16, :].rearrange("p t h -> p (t h)"),
                                                           in_=kch[b][:].rearrange("p t (h d) -> p (t h) d", d=64), axis=AX.X, op=ALU.add),
                 reads=[('kch', b)], writes=['sc'])
        P.stt(sc[:], sc[:], 0.125, Cw[:], ALU.mult, ALU.add, reads=['sc', 'Cw'], writes=['sc'])
        P.tt(DVE, tmp512[:], qb, kb, ALU.mult, reads=['rows'], writes=['tmp512'])
        P.op(DVE, lambda e: e.tensor_reduce(out=snew[:], in_=tmp512[:].rearrange("p (h d) -> p h d", d=64), axis=AX.X, op=ALU.add), reads=['tmp512'], writes=['snew'])
        P.ts(DVE, snew[:], snew[:], 0.125, ALU.mult, reads=['snew'], writes=['snew'])
        P.op(DVE, lambda e: e.tensor_reduce(out=mloc[:], in_=sc[:].rearrange("p t h -> p h t"), axis=AX.X, op=ALU.max), reads=['sc'], writes=['mloc'])
        P.tt(DVE, mloc[:], mloc[:], snew[:], ALU.max, reads=['mloc', 'snew'], writes=['mloc'])
        P.tr(ps[0:8, 1, 0:128], mloc[:], identf, reads=['mloc', 'cf'], writes=[('ps', 1)])
        P.op(DVE, lambda e: e.tensor_reduce(out=mx8[:], in_=ps[0:8, 1, 0:128], axis=AX.X, op=ALU.max), reads=[('ps', 1)], writes=['mx8'])
        P.ts(DVE, D8[:], identf[0:8, 0:8], mx8[:, 0:1], ALU.mult, reads=['cf', 'mx8'], writes=['D8'])
        P.mm(ps[:, 2, 0:8], ones[0:8, :], D8[:], reads=['ones', 'D8'], writes=[('ps', 2)])
        P.cp(DVE, mxb[:], ps[:, 2, 0:8], reads=[('ps', 2)], writes=['mxb'])
        P.tt(DVE, sc[:], sc[:], mxb[:].unsqueeze(1).to_broadcast([128, 128, 8]), ALU.subtract, reads=['sc', 'mxb'], writes=['sc'])
        P.act(pr[:], sc[:], AF.Exp, reads=['sc'], writes=['pr'])
        P.tt(DVE, pnew[:], snew[:], mxb[:], ALU.subtract, reads=['snew', 'mxb'], writes=['pnew'])
        P.act(pnew[:], pnew[:], AF.Exp, reads=['pnew'], writes=['pnew'])
        P.op(DVE, lambda e: e.tensor_reduce(out=den[:], in_=pr[:].rearrange("p t h -> p h t"), axis=AX.X, op=ALU.add), reads=['pr'], writes=['den'])
        P.mm(ps[:, 3, 0:8], ones[:, :], den[:], reads=['ones', 'den'], writes=[('ps', 3)])
        P.tt(DVE, den[:], ps[:, 3, 0:8], pnew[:], ALU.add, reads=[('ps', 3), 'pnew', 'den'], writes=['den'])
        P.op(DVE, lambda e: e.reciprocal(out=rden[:], in_=den[:]), reads=['den'], writes=['rden'])
        P.tt(DVE, pr[:], pr[:], rden[:].unsqueeze(1).to_broadcast([128, 128, 8]), ALU.mult, reads=['pr', 'rden'], writes=['pr'])
        P.tt(DVE, pnew[:], pnew[:], rden[:], ALU.mult, reads=['pnew', 'rden'], writes=['pnew'])
        first = True
        for tc in range(8):
            b = gather(cv, tc)
            for t in range(16):
                P.mm(ps[0:8, 4, :], pr[:, tc * 16 + t, :], kch[b][:, t, :], first, False, reads=['pr', ('kch', b)], writes=[('ps', 4)])
                first = False
        P.mm(ps[0:8, 4, :], pnew[0:1, :], vb[0:1, :], False, True, reads=['pnew', 'rows'], writes=[('ps', 4)])
        P.tt(DVE, accs[:], ps[0:8, 4, :], bm[:], ALU.mult, reads=[('ps', 4), 'bm'], writes=['accs'])
        for j in range(4):
            P.mm(ps[:, 5, j:j + 1], accs[:, j * 128:(j + 1) * 128], ones[0:8, 0:1], reads=['accs', 'ones'], writes=[('ps', 5)])
        P.cp(DVE, oc[:], ps[:, 5, 0:4], reads=[('ps', 5)], writes=['oc'])
        P.ld(zc[:], cx.projT[R_CZ:R_CZ + 512, T:T + 1].rearrange("(j p) o -> p (j o)", p=128), writes=['zc'], tag='c', slow=True)
        P.act(zc[:], zc[:], AF.Silu, reads=['zc'], writes=['zc'])
        P.tt(DVE, gb[:], oc[:], zc[:], ALU.mult, reads=['oc', 'zc'], writes=['gb'])
        P.ld(cx.gatedT[1024:1536, T:T + 1].rearrange("(j p) o -> p (j o)", p=128), gb[:], reads=['gb'], writes=['gatedT'], tag='go', slow=True)
        P.emit()


def build(debug=None, phases=None, ext_in=()):
    nc = bass.Bass("TRN2", target_bir_lowering=False)
    cx = Ctx()
    di = lambda name, shape, dt=F32: nc.dram_tensor(name, shape, dt, kind="ExternalInput").ap()
    do = lambda name, shape, dt=F32: nc.dram_tensor(name, shape, dt, kind="ExternalOutput").ap()
    dbg = debug is not None

    def scratch(name, shape, dt=F32):
        if name in ext_in:
            return di(name, shape, dt)
        return do(name, shape, dt) if dbg else nc.dram_tensor(name, shape, dt).ap()
    cx.xT = di("xT", [D, TS])
    cx.xtok = di("xtok", [TS, D])
    cx.w_in = di("w_in", [DEPTH, D, NROW])
    cx.w_out = di("w_out", [DEPTH, D, D])
    cx.ln_g, cx.ln_b = di("ln_g", [DEPTH, D]), di("ln_b", [DEPTH, D])
    cx.identb = di("identb", [128, 128], BF16)
    cx.pp = di("pp", [DEPTH, 128, NPP])
    cx.pool_w = di("pool_w", [DEPTH, 4, 128, 128])
    cx.dbufT = di("dbufT", [DEPTH, 512, 15])
    cx.o_dbuf = do("o_dbuf", [DEPTH, 2, 512, 15])
    cx.cf32 = di("cf32", [128, 256])
    cx.cb16 = di("cb16", [128, 128 + 4 * 512], BF16)
    cx.b_f = di("b_f", [DEPTH, 8])
    cx.o_clf = do("o_clf", [DEPTH, 8, TS])
    cx.o_ckv = do("o_ckv", [DEPTH, 1024, TS])
    cx.cscr = scratch("cscr", [6, T], BF16)
    cx.cb2 = di("cb2", [128, 256], BF16)
    cx.cb3 = di("cb3", [128, 512], BF16)
    cx.cf3 = di("cf3", [128, 2048])
    cx.sel8 = di("sel8", [8, 512])
    cx.aconvT = di("aconvT", [DEPTH, 1536, 3])
    cx.sA_in = di("sA_in", [DEPTH, 64, 8, 64])
    cx.sB_in = di("sB_in", [DEPTH, 64, 8, 64])
    cx.gn_g, cx.gn_b = di("gn_g", [DEPTH, 512]), di("gn_b", [DEPTH, 512])
    cx.cf32s = di("cf32s", [128, 384])
    cx.bmask = di("bmask", [8, 512])
    cx.ptab = di("ptab", [128, 1], I32)
    cx.cache_k = di("cache_k", [DEPTH, 1280, 128, 8, 64])
    cx.cache_v = di("cache_v", [DEPTH, 1280, 128, 8, 64])
    cx.cache_lf = di("cache_lf", [DEPTH, 1280, 128, 8])
    cx.srow = scratch("srow", [1, 2056])
    cx.lora_w = di("lora_w", [DEPTH, 128, 512])
    cx.bshiftT = di("bshiftT", [DEPTH, 128, 13])
    cx.o_bshift = do("o_bshift", [DEPTH, 2, 128, 13])
    cx.o_aconv = do("o_aconv", [DEPTH, 2, 1536, 3])
    cx.o_AS = do("o_AS", [DEPTH, 2, 64, 8, 64])
    cx.o_BS = do("o_BS", [DEPTH, 2, 64, 8, 64])
    cx.A_sc = {k: scratch("A_" + k, [512, TP], BF16) for k in A_KINDS}
    cx.B_sc = {k: scratch("B_" + k, [512, TP], BF16) for k in B_KINDS}
    cx.A_erows = scratch("A_erows", [4, 2, 8, TP])
    cx.A_gc, cx.B_gc = scratch("A_gc", [512, NCH]), scratch("B_gc", [512, NCH])
    cx.A_mrows = scratch("A_mrows", [5, 8, TP])
    cx.B_bonus = scratch("B_bonus", [512, TP])
    cx.y = do("y", [TS, D])
    cx.projT = scratch("projT", [NROW, TS])
    cx.gatedT = scratch("gatedT", [D, TS], BF16)
    cx.y1 = scratch("y1", [TS, D])
    cx.x2T = scratch("x2T", [D, TS], BF16)
    cx.ps = nc.alloc_psum_tensor("ps", [128, 8, 512], F32)
    if phases is None:
        phases = [('in', 0), ('out', 0), ('in', 1), ('out', 1)]
    for ph, l in phases:
        if ph == 'in':
            phase_inproj(nc, cx, l)
        elif ph == 'out':
            phase_outproj(nc, cx, l)
        elif ph == 'mixD':
            phase_mixD(nc, cx, l)
        elif ph == 'mixC':
            phase_mixC(nc, cx, l)
        elif ph == 'preA':
            phase_rowsA(nc, cx, l)
            phase_preA(nc, cx, l)
        elif ph == 'preB':
            phase_preB(nc, cx, l)
        elif ph == 'scanA':
            phase_scan(nc, cx, l, 'A')
        elif ph == 'scanA_few':
            phase_scan(nc, cx, l, 'A', chunks=[0, 1, 32])
        elif ph == 'scanB_few':
            phase_scan(nc, cx, l, 'B', chunks=[0, 1, 32])
        elif ph == 'scanB':
            phase_scan(nc, cx, l, 'B')
        elif ph == 'mixCs':
            phase_mixCs(nc, cx, l)
        elif ph == 'mixC1':
            phase_mixC(nc, cx, l, heads=[0], do_sample=False)
    return nc


def _cf32():
    a = np.zeros((128, 256), np.float32)
    a[:, 0:128] = np.eye(128)
    a[:, 128:256] = np.triu(np.ones((128, 128)), 1)
    return a


def _cb16():
    import ml_dtypes
    a = np.zeros((128, 128 + 4 * 512), np.float32)
    a[:, 0:128] = np.eye(128)
    k, q = np.arange(128)[:, None], np.arange(512)[None, :]
    for j in range(4):
        a[:, 128 + j * 512:128 + (j + 1) * 512] = np.where(j * 128 + k > q, -30000.0, 0.0)
    return a.astype(ml_dtypes.bfloat16)


def _cb2():
    import ml_dtypes
    a = np.zeros((128, 256), np.float32)
    a[:, 0:128] = np.eye(128)
    a[:, 128:256] = np.kron(np.eye(2), np.ones((64, 64)))
    return a.astype(ml_dtypes.bfloat16)


def _cb3():
    import ml_dtypes
    p, f = np.arange(128)[:, None], np.arange(128)[None, :]
    a = np.zeros((128, 512), np.float32)
    a[:, 0:128] = np.eye(128)
    a[:, 128:256] = np.where(f >= p, -30000.0, 0.0)
    a[:, 256:384] = np.where(p >= f, -30000.0, 0.0)
    a[:, 384:512] = np.where(p > f, -30000.0, 0.0)
    return a.astype(ml_dtypes.bfloat16)


def _cf3():
    p, f = np.arange(128)[:, None], np.arange(128)[None, :]
    mats = [np.eye(128), (f < p) * 1.0, (p < f) * 1.0, (p <= f) * 1.0]
    return np.concatenate([np.tile(m, (1, 4)) for m in mats], axis=1).astype(np.float32)


def _lora_w(I):
    a = np.zeros((DEPTH, 128, 512), np.float32)
    a[:, 32:64] = I['w_up_B']
    a[:, 64:96] = I['a_up_B']
    return a


def _bshiftT(I, c):
    sh = I['state_B_shift'][:, c]
    a = np.zeros((DEPTH, 128, 13), np.float32)
    a[:, :, 0:12] = sh[:, 0:1536].reshape(DEPTH, 12, 128).transpose(0, 2, 1)
    a[:, 32:64, 12] = sh[:, 1536:1568]
    a[:, 64:96, 12] = sh[:, 1568:1600]
    return a


def host_inputs(c, I):
    import ml_dtypes
    n = c % 2
    perm = _col_perm()
    w = I['w_in']
    w_p = np.zeros((DEPTH, D, NROW), np.float32)
    w_p[:, :, perm >= 0] = w[:, :, perm[perm >= 0]]
    xtok = np.concatenate([I['x_prompt'][n], I['x_sample'][c]], axis=0)
    return {"xT": np.ascontiguousarray(xtok.T), "xtok": np.ascontiguousarray(xtok), "w_in": w_p, "w_out": I['w_out'],
            "ln_g": I['ln_g'], "ln_b": I['ln_b'], "identb": np.eye(128, dtype=np.float32).astype(ml_dtypes.bfloat16),
            "pp": pack_pp(I), "pool_w": I['pool_w_D'], "cf32": _cf32(), "cb16": _cb16(), "b_f": I['b_f_C'],
            "cb2": _cb2(), "cb3": _cb3(), "cf3": _cf3(), "sel8": np.kron(np.eye(8, dtype=np.float32), np.ones((1, 64), np.float32)),
            "aconvT": np.ascontiguousarray(I['state_A_conv'][:, c].transpose(0, 2, 1)),
            "sA_in": np.ascontiguousarray(I['state_A_S'][:, c].transpose(0, 2, 1, 3)),
            "sB_in": np.ascontiguousarray(I['state_B_S'][:, c].transpose(0, 3, 1, 2)),
            "cf32s": np.concatenate([_cf32(), np.tril(np.ones((128, 128), np.float32), -1)], axis=1),
            "bmask": np.kron(np.eye(8, dtype=np.float32), np.ones((1, 64), np.float32)),
            "ptab": np.ascontiguousarray(I['page_table'][c].reshape(128, 1).astype(np.int32)),
            "cache_k": I['cache_C_k'], "cache_v": I['cache_C_v'], "cache_lf": I['cache_C_logf'],
            "gn_g": I['gn_g_B'], "gn_b": I['gn_b_B'], "lora_w": _lora_w(I), "bshiftT": _bshiftT(I, c),
            "dbufT": np.ascontiguousarray(I['state_D_buf'][:, c].transpose(0, 2, 1))}


FULL_PHASES = [(ph, l) for l in range(DEPTH) for ph in ('in', 'preA', 'scanA', 'preB', 'scanB', 'mixC', 'mixCs', 'mixD', 'out')]


def _unpack_bshift(a):
    o = np.zeros(1600, np.float32)
    o[0:1536] = a[:, 0:12].T.reshape(1536)
    o[1536:1568] = a[32:64, 12]
    o[1568:1600] = a[64:96, 12]
    return o


def kernel(**I):
    I = {k: np.asarray(v) for k, v in I.items()}
    nc = build(phases=FULL_PHASES)
    shared = None
    in_maps = []
    for c in range(8):
        m = host_inputs(c, I)
        if shared is None:
            shared = m
        else:
            for k in ('w_in', 'w_out', 'ln_g', 'ln_b', 'identb', 'pp', 'pool_w', 'cf32', 'cb16', 'b_f', 'cb2', 'cb3', 'cf3', 'sel8',
                      'cf32s', 'bmask', 'cache_k', 'cache_v', 'cache_lf', 'gn_g', 'gn_b', 'lora_w'):
                m[k] = shared[k]
        in_maps.append(m)
    res = run_bass_kernel_spmd(nc, in_maps, core_ids=list(range(8)))
    R = res.results
    f = np.float32
    NB, ND = 2, 8
    y_prompt = np.stack([R[n]['y'][0:T] for n in range(NB)]).astype(f)
    y_sample = np.stack([R[c]['y'][T:T + 1] for c in range(ND)]).astype(f)

    def per(fn, j):
        cores = range(NB) if j == 0 else range(ND)
        return np.stack([np.stack([fn(R[c], l) for c in cores]) for l in range(DEPTH)]).astype(f)
    outs = {}
    for j, pre in ((0, 'p'), (1, 's')):
        outs[pre + '_A_S'] = per(lambda r, l: r['o_AS'][l, j].transpose(1, 0, 2), j)
        outs[pre + '_A_conv'] = per(lambda r, l: r['o_aconv'][l, j].T, j)
        outs[pre + '_B_S'] = per(lambda r, l: r['o_BS'][l, j].transpose(1, 2, 0), j)
        outs[pre + '_B_shift'] = per(lambda r, l: _unpack_bshift(r['o_bshift'][l, j]), j)
        outs[pre + '_D_buf'] = per(lambda r, l: r['o_dbuf'][l, j].T, j)
    cs = {0: slice(0, T), 1: slice(T, T + 1)}
    for j, pre in ((0, 'p'), (1, 's')):
        n = T if j == 0 else 1
        outs[pre + '_C_k'] = per(lambda r, l: r['o_ckv'][l, 0:512, cs[j]].T.reshape(n, NH, HD), j)
        outs[pre + '_C_v'] = per(lambda r, l: r['o_ckv'][l, 512:1024, cs[j]].T.reshape(n, NH, HD), j)
        outs[pre + '_C_logf'] = per(lambda r, l: r['o_clf'][l, :, cs[j]].T, j)
    order = ['p_A_S', 'p_A_conv', 'p_B_S', 'p_B_shift', 'p_C_k', 'p_C_v', 'p_C_logf', 'p_D_buf',
             's_A_S', 's_A_conv', 's_B_S', 's_B_shift', 's_C_k', 's_C_v', 's_C_logf', 's_D_buf']
    return (y_prompt, y_sample) + tuple(np.ascontiguousarray(outs[k]) for k in order)
```

```python
import contextlib
import numpy as np
import concourse.bass as bass
import concourse.mybir as mybir
from concourse.bass_utils import run_bass_kernel_spmd

F32, BF16, I32 = mybir.dt.float32, mybir.dt.bfloat16, mybir.dt.int32
AF, ALU, AX = mybir.ActivationFunctionType, mybir.AluOpType, mybir.AxisListType

D, T, TS = 2048, 4096, 4097
DEPTH, NH, HD = 2, 8, 64
NBLK, NROW = 57, 57 * 128
KC = 16
R_AQ, R_AK, R_AV, R_AZ = 0, 512, 1024, 1536
R_BR, R_BK, R_BV, R_BZ = 2048, 2560, 3072, 3584
R_CQ, R_CK, R_CV, R_CZ = 4096, 4608, 5120, 5632
R_DU, R_DZ, R_SM = 6144, 6656, 7168
SM_A, SM_B, SM_F, SM_WL, SM_AL = 0, 8, 16, 32, 64


def _col_perm():
    ref = dict(qkvA=0, aA=1536, bA=1544, zA=1552, rB=2064, kB=2576, vB=3088, wl=3600, al=3632, zB=3664,
               qkvC=4176, fC=5712, zC=5720, uD=6232, zD=6744)
    p = np.full(NROW, -1, np.int64)
    p[0:1536] = ref['qkvA'] + np.arange(1536)
    p[1536:2048] = ref['zA'] + np.arange(512)
    p[2048:3584] = ref['rB'] + np.arange(1536)
    p[3584:4096] = ref['zB'] + np.arange(512)
    p[4096:5632] = ref['qkvC'] + np.arange(1536)
    p[5632:6144] = ref['zC'] + np.arange(512)
    p[6144:6656] = ref['uD'] + np.arange(512)
    p[6656:7168] = ref['zD'] + np.arange(512)
    p[R_SM + SM_A:R_SM + SM_A + 8] = ref['aA'] + np.arange(8)
    p[R_SM + SM_B:R_SM + SM_B + 8] = ref['bA'] + np.arange(8)
    p[R_SM + SM_F:R_SM + SM_F + 8] = ref['fC'] + np.arange(8)
    p[R_SM + SM_WL:R_SM + SM_WL + 32] = ref['wl'] + np.arange(32)
    p[R_SM + SM_AL:R_SM + SM_AL + 32] = ref['al'] + np.arange(32)
    return p


PE, ACT, DVE, POOL, SP = 'pe', 'act', 'dve', 'pool', 'sp'
ENGS = (PE, ACT, DVE, POOL, SP)


class _Op:
    __slots__ = ('eng', 'fn', 'deps', 'is_dma', 'tag', 'sig', 'semkey', 'semval')

    def __init__(self, eng, fn, is_dma, tag):
        self.eng, self.fn, self.is_dma, self.tag = eng, fn, is_dma, tag
        self.deps, self.sig, self.semkey, self.semval = set(), False, None, 0


class Prog:
    n_emitted = 0

    def __init__(self, nc, dma_rot=6):
        self.nc, self.dma_rot = nc, dma_rot
        self.ops = {e: [] for e in ENGS}
        self.lastw, self.readers, self.all_ops = {}, {}, []

    def op(self, eng, fn, reads=(), writes=(), dma=False, tag=None):
        o = _Op(eng, fn, dma, tag)
        wr = list(writes)
        for r in reads:
            if isinstance(r, tuple) and r[0] == 'ps':
                wr.append(r)
                continue
            w = self.lastw.get(r)
            if w is not None:
                o.deps.add(w)
        for r in wr:
            w = self.lastw.get(r)
            if w is not None:
                o.deps.add(w)
            o.deps.update(self.readers.get(r, ()))
        for r in reads:
            if not (isinstance(r, tuple) and r[0] == 'ps'):
                self.readers.setdefault(r, []).append(o)
        for r in wr:
            self.lastw[r] = o
            self.readers[r] = []
        o.deps.discard(o)
        self.ops[eng].append(o)
        self.all_ops.append(o)
        return o

    def dma(self, eng, fn, reads=(), writes=(), tag='d'):
        return self.op(eng, fn, reads, writes, dma=True, tag=tag)

    def mm(self, out, lhsT, rhs, start=True, stop=True, reads=(), writes=()):
        return self.op(PE, lambda e: e.matmul(out, lhsT=lhsT, rhs=rhs, start=start, stop=stop), reads, writes)

    def tr(self, out, in_, identity, reads=(), writes=()):
        return self.op(PE, lambda e: e.transpose(out=out, in_=in_, identity=identity), reads, writes)

    def act(self, out, in_, func, reads=(), writes=(), **kw):
        return self.op(ACT, lambda e: e.activation(out=out, in_=in_, func=func, **kw), reads, writes)

    def cp(self, eng, out, in_, reads=(), writes=()):
        if eng == ACT:
            return self.op(ACT, lambda e: e.activation(out=out, in_=in_, func=AF.Copy), reads, writes)
        return self.op(eng, lambda e: e.tensor_copy(out=out, in_=in_), reads, writes)

    def tt(self, eng, out, in0, in1, op, reads=(), writes=()):
        return self.op(eng, lambda e: e.tensor_tensor(out=out, in0=in0, in1=in1, op=op), reads, writes)

    def ts(self, eng, out, in0, s1, op0, s2=None, op1=None, reads=(), writes=()):
        if op1 is None:
            return self.op(eng, lambda e: e.tensor_scalar(out=out, in0=in0, scalar1=s1, scalar2=None, op0=op0), reads, writes)
        return self.op(eng, lambda e: e.tensor_scalar(out=out, in0=in0, scalar1=s1, scalar2=s2, op0=op0, op1=op1), reads, writes)

    def stt(self, out, in0, scalar, in1, op0, op1, reads=(), writes=()):
        return self.op(DVE, lambda e: e.scalar_tensor_tensor(out=out, in0=in0, scalar=scalar, in1=in1, op0=op0, op1=op1), reads, writes)

    def ms(self, eng, ap, val, reads=(), writes=()):
        return self.op(eng, lambda e: e.memset(ap, val), reads, writes)

    def ld(self, out, in_, reads=(), writes=(), tag='d', slow=False, eng=SP):
        return self.dma(eng, lambda e: e.dma_start(out=out, in_=in_, allow_slow_non_contiguous=slow), reads, writes, tag=tag)

    def emit(self):
        nc = self.nc
        for o in self.all_ops:
            o.deps = {d for d in o.deps if not (d.eng == PE and o.eng == PE and not d.is_dma and not o.is_dma)}
            for d in o.deps:
                d.sig = True
            if o.is_dma:
                o.sig = True
        lasts = [ops[-1] for ops in self.ops.values() if ops]
        for o in lasts:
            o.sig = True
        counters, dma_seq, order = {}, {}, []
        for o in self.all_ops:
            if not o.sig:
                continue
            if o.is_dma:
                k = dma_seq.get((o.eng, o.tag), 0)
                dma_seq[(o.eng, o.tag)] = k + 1
                key, inc = ('dma', o.eng, o.tag, k % self.dma_rot), 16
            else:
                key, inc = ('cmp', o.eng), 1
            if key not in counters:
                order.append(key)
            counters[key] = counters.get(key, 0) + inc
            o.semkey, o.semval = key, counters[key]
        Prog.n_emitted += 1
        sems = {key: nc.alloc_semaphore('p%ds%d' % (Prog.n_emitted, i)) for i, key in enumerate(order)}
        with contextlib.ExitStack() as st:
            block = st.enter_context(nc.Block())
            engmap = {PE: block.tensor, ACT: block.scalar, DVE: block.vector, POOL: block.gpsimd, SP: block.sync}

            def make(eng):
                ops = self.ops[eng]

                def body(e):
                    waited = {}

                    def wait(key, val):
                        if waited.get(key, 0) < val:
                            e.wait_ge(sems[key], val)
                            waited[key] = val
                    for o in ops:
                        need = {}
                        for d in o.deps:
                            if need.get(d.semkey, 0) < d.semval:
                                need[d.semkey] = d.semval
                        for key, val in need.items():
                            wait(key, val)
                        if o.is_dma and o.semval > 16:
                            wait(o.semkey, o.semval - 16)
                        ins = o.fn(e)
                        if o.sig:
                            ins.then_inc(sems[o.semkey], 16 if o.is_dma else 1)
                    for key, val in counters.items():
                        wait(key, val)
                return body
            for eng in ENGS:
                if self.ops[eng]:
                    engmap[eng](make(eng))
        nc.all_engine_barrier()
        nc.clear_and_free_semaphores(list(sems.values()))
        nc.all_engine_barrier()


class Ctx:
    pass


def phase_inproj(nc, cx, l):
    P = Prog(nc)
    ps = cx.ps
    with contextlib.ExitStack() as st:
        sb = lambda name, shape, dt: st.enter_context(nc.sbuf_tensor('i%d_' % l + name, shape, dt))
        xtb = sb('xtb', [128, KC, TS], BF16)
        wst = [sb('wst%d' % i, [128, KC, 128], F32) for i in range(2)]
        wbf = [sb('wbf%d' % i, [128, KC, 128], BF16) for i in range(2)]
        stg = [sb('stg%d' % i, [128, 512], F32) for i in range(4)]
        if l == 0:
            xst = [sb('xst%d' % i, [128, TS], F32) for i in range(2)]
            for kc in range(KC):
                b = kc % 2
                P.dma(SP, lambda e, kc=kc, b=b: e.dma_start(out=xst[b][:], in_=cx.xT[kc * 128:(kc + 1) * 128, :]),
                      writes=[('xst', b)], tag='x')
                eng = (ACT, DVE, POOL)[kc % 3]
                if eng == ACT:
                    P.op(ACT, lambda e, kc=kc, b=b: e.activation(out=xtb[:, kc, :], in_=xst[b][:], func=AF.Copy),
                         reads=[('xst', b)], writes=[('xtb', kc)])
                else:
                    P.op(eng, lambda e, kc=kc, b=b: e.tensor_copy(out=xtb[:, kc, :], in_=xst[b][:]),
                         reads=[('xst', b)], writes=[('xtb', kc)])
        else:
            for kc in range(KC):
                P.dma(SP, lambda e, kc=kc: e.dma_start(out=xtb[:, kc, :], in_=cx.x2T[kc * 128:(kc + 1) * 128, :]),
                      writes=[('xtb', kc)], tag='x')
        xres = [('xtb', kc) for kc in range(KC)]
        ntile = 0
        for blk in range(NBLK):
            b = blk % 2
            P.dma(SP, lambda e, blk=blk, b=b: e.dma_start(
                out=wst[b][:], in_=cx.w_in[l, :, blk * 128:(blk + 1) * 128].rearrange("(kc p) c -> p kc c", p=128)),
                writes=[('wst', b)], tag='w')
            P.op(POOL, lambda e, b=b: e.tensor_copy(out=wbf[b][:], in_=wst[b][:]), reads=[('wst', b)], writes=[('wbf', b)])
            for tt in range(9):
                t0, n = tt * 512, (512 if tt < 8 else 1)
                bank = ntile % 4
                for kc in range(KC):
                    P.op(PE, lambda e, kc=kc, b=b, bank=bank, t0=t0, n=n: e.matmul(
                        ps[:, bank, 0:n], lhsT=wbf[b][:, kc, :], rhs=xtb[:, kc, t0:t0 + n], start=(kc == 0), stop=(kc == KC - 1)),
                        reads=[('wbf', b)] + (xres if kc == 0 else []), writes=[('ps', bank)])
                s = ntile % 4
                if ntile % 2 == 0:
                    P.op(ACT, lambda e, bank=bank, s=s, n=n: e.activation(out=stg[s][:, 0:n], in_=ps[:, bank, 0:n], func=AF.Copy),
                         reads=[('ps', bank)], writes=[('stg', s)])
                else:
                    P.op(DVE, lambda e, bank=bank, s=s, n=n: e.tensor_copy(out=stg[s][:, 0:n], in_=ps[:, bank, 0:n]),
                         reads=[('ps', bank)], writes=[('stg', s)])
                P.dma(SP, lambda e, blk=blk, s=s, t0=t0, n=n: e.dma_start(
                    out=cx.projT[blk * 128:(blk + 1) * 128, t0:t0 + n], in_=stg[s][:, 0:n], allow_slow_non_contiguous=(n == 1)),
                    reads=[('stg', s)], writes=[('projT', blk)], tag='po')
                ntile += 1
        P.emit()


ALPHA_DN = (2.0 * DEPTH) ** 0.25
LN_EPS = 1e-5


def load_bcast(P, nc, eng, dst, src_row, key):
    P.dma(eng, lambda e: e.dma_start(out=dst, in_=src_row.partition_broadcast(dst.shape[0])), writes=[key], tag='bc')


def phase_outproj(nc, cx, l):
    P = Prog(nc)
    ps = cx.ps
    last = (l == DEPTH - 1)
    with contextlib.ExitStack() as st:
        sb = lambda name, shape, dt: st.enter_context(nc.sbuf_tensor('o%d_' % l + name, shape, dt))
        wob = sb('wob', [128, KC, D], BF16)
        wst = [sb('wost%d' % i, [128, D], F32) for i in range(2)]
        lng, lnb = sb('lng', [128, D], F32), sb('lnb', [128, D], F32)
        identb = sb('identb', [128, 128], BF16)
        gT = [sb('gT%d' % i, [128, KC, 128], BF16) for i in range(2)]
        xr = [sb('xr%d' % i, [128, D], F32) for i in range(2)]
        zt = [sb('zt%d' % i, [128, D], F32) for i in range(2)]
        yb = [sb('yb%d' % i, [128, D], BF16) for i in range(2)]
        xT2 = [sb('xT2%d' % i, [128, KC, 128], BF16) for i in range(2)]
        st1 = [sb('st1%d' % i, [128, 4], F32) for i in range(2)]
        sq = sb('sqjunk', [128, D], F32)
        P.dma(SP, lambda e: e.dma_start(out=identb[:], in_=cx.identb), writes=['identb'], tag='c')
        load_bcast(P, nc, SP, lng[:], cx.ln_g[l:l + 1, :], 'lng')
        load_bcast(P, nc, SP, lnb[:], cx.ln_b[l:l + 1, :], 'lnb')
        for kc in range(KC):
            b = kc % 2
            P.dma(SP, lambda e, kc=kc, b=b: e.dma_start(out=wst[b][:], in_=cx.w_out[l, kc * 128:(kc + 1) * 128, :]), writes=[('wst', b)], tag='w')
            if kc % 2 == 0:
                P.op(POOL, lambda e, kc=kc, b=b: e.tensor_copy(out=wob[:, kc, :], in_=wst[b][:]), reads=[('wst', b)], writes=[('wob', kc)])
            else:
                P.op(ACT, lambda e, kc=kc, b=b: e.activation(out=wob[:, kc, :], in_=wst[b][:], func=AF.Copy), reads=[('wst', b)], writes=[('wob', kc)])
        wres = [('wob', kc) for kc in range(KC)]
        xsrc = cx.xtok if l == 0 else cx.y1
        ydst = cx.y if last else cx.y1
        deferred = []
        for tb in range(33):
            t0, n = tb * 128, (128 if tb < 32 else 1)
            b = tb % 2
            P.dma(SP, lambda e, b=b, t0=t0, n=n: e.dma_start(
                out=gT[b][:, :, 0:n], in_=cx.gatedT[:, t0:t0 + n].rearrange("(kc p) t -> p kc t", p=128), allow_slow_non_contiguous=(n == 1)),
                writes=[('gT', b)], tag='g')
            P.dma(SP, lambda e, b=b, t0=t0, n=n: e.dma_start(out=xr[b][0:n, :], in_=xsrc[t0:t0 + n, :]), writes=[('xr', b)], tag='xr')
            for cg in range(4):
                for kc in range(KC):
                    P.op(PE, lambda e, b=b, n=n, cg=cg, kc=kc: e.matmul(
                        ps[0:n, cg, :], lhsT=gT[b][:, kc, 0:n], rhs=wob[:, kc, cg * 512:(cg + 1) * 512], start=(kc == 0), stop=(kc == KC - 1)),
                        reads=[('gT', b)] + (wres if kc == 0 and cg == 0 and tb == 0 else []), writes=[('ps', cg)])
                P.op(DVE, lambda e, b=b, n=n, cg=cg: e.scalar_tensor_tensor(
                    out=zt[b][0:n, cg * 512:(cg + 1) * 512], in0=xr[b][0:n, cg * 512:(cg + 1) * 512], scalar=ALPHA_DN,
                    in1=ps[0:n, cg, :], op0=ALU.mult, op1=ALU.add), reads=[('xr', b), ('ps', cg)], writes=[('zt', b)])
            s = st1[b]
            P.op(DVE, lambda e, b=b, n=n, s=s: e.tensor_reduce(out=s[0:n, 0:1], in_=zt[b][0:n, :], axis=AX.X, op=ALU.add),
                 reads=[('zt', b)], writes=[('st1', b)])
            P.op(DVE, lambda e, n=n, s=s: e.tensor_scalar(out=s[0:n, 1:2], in0=s[0:n, 0:1], scalar1=-1.0 / D, scalar2=None, op0=ALU.mult),
                 reads=[('st1', b)], writes=[('st1', b)])
            P.op(ACT, lambda e, b=b, n=n, s=s: e.activation(out=sq[0:n, :], in_=zt[b][0:n, :], func=AF.Square, bias=s[0:n, 1:2], accum_out=s[0:n, 2:3]),
                 reads=[('zt', b), ('st1', b)], writes=[('st1', b), 'sq'])
            P.op(DVE, lambda e, n=n, s=s: e.tensor_scalar(out=s[0:n, 3:4], in0=s[0:n, 2:3], scalar1=1.0 / D, scalar2=LN_EPS, op0=ALU.mult, op1=ALU.add),
                 reads=[('st1', b)], writes=[('st1', b)])
            P.op(ACT, lambda e, n=n, s=s: e.activation(out=s[0:n, 3:4], in_=s[0:n, 3:4], func=AF.Ln), reads=[('st1', b)], writes=[('st1', b)])
            P.op(ACT, lambda e, n=n, s=s: e.activation(out=s[0:n, 3:4], in_=s[0:n, 3:4], func=AF.Exp, scale=-0.5), reads=[('st1', b)], writes=[('st1', b)])
            P.op(DVE, lambda e, b=b, n=n, s=s: e.tensor_scalar(out=zt[b][0:n, :], in0=zt[b][0:n, :], scalar1=s[0:n, 1:2], scalar2=s[0:n, 3:4],
                                                               op0=ALU.add, op1=ALU.mult), reads=[('zt', b), ('st1', b)], writes=[('zt', b)])
            P.op(DVE, lambda e, b=b, n=n: e.tensor_tensor(out=zt[b][0:n, :], in0=zt[b][0:n, :], in1=lng[0:n, :], op=ALU.mult),
                 reads=[('zt', b), 'lng'], writes=[('zt', b)])
            P.op(POOL, lambda e, b=b, n=n: e.tensor_tensor(out=zt[b][0:n, :], in0=zt[b][0:n, :], in1=lnb[0:n, :], op=ALU.add),
                 reads=[('zt', b), 'lnb'], writes=[('zt', b)])
            P.dma(SP, lambda e, b=b, t0=t0, n=n: e.dma_start(out=ydst[t0:t0 + n, :], in_=zt[b][0:n, :]), reads=[('zt', b)], writes=['ydst'], tag='yo')
            if not last:
                P.op(ACT, lambda e, b=b, n=n: e.activation(out=yb[b][0:n, :], in_=zt[b][0:n, :], func=AF.Copy), reads=[('zt', b)], writes=[('yb', b)])

                def stage2(b=b, n=n, t0=t0):
                    psb = [ps[:, 4 + h, :].bitcast(BF16) for h in range(2)]
                    for kc in range(KC):
                        h, j = kc // 8, kc % 8
                        P.op(PE, lambda e, kc=kc, h=h, j=j: e.transpose(out=psb[h][:, j * 128:j * 128 + n], in_=yb[b][0:n, kc * 128:(kc + 1) * 128],
                                                                      identity=identb[0:n, 0:n]),
                             reads=[('yb', b), 'identb'], writes=[('ps', 4 + h)])
                    for h in range(2):
                        P.op(DVE if h == 0 else ACT, (lambda e, h=h: e.tensor_copy(
                            out=xT2[b][:, h * 8:(h + 1) * 8, 0:n], in_=psb[h].rearrange("p (j t) -> p j t", t=128)[:, :, 0:n])) if h == 0 else
                            (lambda e, h=h: e.activation(
                                out=xT2[b][:, h * 8:(h + 1) * 8, 0:n], in_=psb[h].rearrange("p (j t) -> p j t", t=128)[:, :, 0:n], func=AF.Copy)),
                            reads=[('ps', 4 + h)], writes=[('xT2', b)])
                    P.dma(SP, lambda e: e.dma_start(
                        out=cx.x2T[:, t0:t0 + n].rearrange("(kc p) t -> p kc t", p=128), in_=xT2[b][:, :, 0:n], allow_slow_non_contiguous=(n == 1)),
                        reads=[('xT2', b)], writes=['x2T'], tag='x2')
                if deferred:
                    deferred.pop()()
                deferred.append(stage2)
        while deferred:
            deferred.pop()()
        P.emit()


PP_CONV, PP_NORMA, PP_ALOG, PP_DTB, PP_MU, PP_MUS = 0, 48, 49, 50, 51, 63
PP_W0, PP_A0, PP_XI, PP_ALPHA, PP_RHO, PP_GNG, PP_GNB, PP_BF, PP_PSC, NPP = 64, 68, 72, 76, 80, 84, 88, 92, 93, 100
POOL_W = (2, 4, 8, 16)


def pack_pp(I):
    pp = np.zeros((DEPTH, 128, NPP), np.float32)
    p = np.arange(128)
    for l in range(DEPTH):
        for b12 in range(12):
            for i in range(4):
                pp[l, :, PP_CONV + b12 * 4 + i] = I['conv_A'][l][i, b12 * 128 + p]
            pp[l, :, PP_MU + b12] = I['mu_B'][l][b12 * 128 + p]
        pp[l, :, PP_NORMA] = I['norm_A'][l][p % 64]
        pp[l, 0:8, PP_ALOG] = I['A_log'][l]
        pp[l, 0:8, PP_DTB] = I['dt_bias'][l]
        pp[l, 32:64, PP_MUS] = I['mu_B'][l][1536:1568]
        pp[l, 64:96, PP_MUS] = I['mu_B'][l][1568:1600]
        for j in range(4):
            for col, nm in ((PP_W0, 'w0_B'), (PP_A0, 'a0_B'), (PP_XI, 'xi_B'), (PP_ALPHA, 'alpha_B'), (PP_RHO, 'rho_B'),
                            (PP_GNG, 'gn_g_B'), (PP_GNB, 'gn_b_B'), (PP_PSC, 'pool_scale_D')):
                pp[l, :, col + j] = I[nm][l][j * 128 + p]
        pp[l, 16:24, PP_BF] = I['b_f_C'][l]
    return pp


def phase_mixD(nc, cx, l):
    P = Prog(nc)
    ps = cx.ps
    with contextlib.ExitStack() as st:
        sb = lambda name, shape, dt: st.enter_context(nc.sbuf_tensor('d%d_' % l + name, shape, dt))
        pp = sb('pp', [128, NPP], F32)
        pwf, pwb = sb('pwf', [128, 4, 128], F32), sb('pwb', [128, 4, 128], BF16)
        ext, zt = sb('ext', [128, 15 + T], F32), sb('zt', [128, T], F32)
        sA, sB = sb('sA', [128, 15 + T], F32), sb('sB', [128, 15 + T], F32)
        pb, go = sb('pb', [128, T], BF16), sb('go', [128, T], BF16)
        exs, zs, sAs, sBs = sb('exs', [128, 16], F32), sb('zs', [128, 1], F32), sb('sAs', [128, 16], F32), sb('sBs', [128, 16], F32)
        pbs, gos = sb('pbs', [128, 1], BF16), sb('gos', [128, 1], BF16)
        P.dma(SP, lambda e: e.dma_start(out=pp[:], in_=cx.pp[l]), writes=['pp'], tag='c')
        P.dma(SP, lambda e: e.dma_start(out=pwf[:], in_=cx.pool_w[l].rearrange("g c d -> c g d")), writes=['pwf'], tag='c')
        P.op(POOL, lambda e: e.tensor_copy(out=pwb[:], in_=pwf[:]), reads=['pwf'], writes=['pwb'])
        P.op(POOL, lambda e: e.memset(ext[:, 0:15], 0.0), writes=['ext'])

        def group(g, n, ext, zt, sA, sB, pb, go, fix, sfx):
            w = POOL_W[g]
            r0 = g * 128
            k = lambda nm: nm + sfx
            if n > 1:
                P.dma(SP, lambda e: e.dma_start(out=ext[:, 15:15 + n], in_=cx.projT[R_DU + r0:R_DU + r0 + 128, 0:n]), writes=[k('ext')], tag='m')
                P.dma(SP, lambda e: e.dma_start(out=zt[:, 0:n], in_=cx.projT[R_DZ + r0:R_DZ + r0 + 128, 0:n]), writes=[k('zt')], tag='m')
            else:
                P.dma(SP, lambda e: e.dma_start(out=ext[:, 0:15], in_=cx.dbufT[l, r0:r0 + 128, :]), writes=[k('ext')], tag='m')
                P.dma(SP, lambda e: e.dma_start(out=ext[:, 15:16], in_=cx.projT[R_DU + r0:R_DU + r0 + 128, T:T + 1], allow_slow_non_contiguous=True), writes=[k('ext')], tag='m')
                P.dma(SP, lambda e: e.dma_start(out=zt[:, 0:1], in_=cx.projT[R_DZ + r0:R_DZ + r0 + 128, T:T + 1], allow_slow_non_contiguous=True), writes=[k('zt')], tag='m')
            P.op(ACT, lambda e: e.activation(out=zt[:, 0:n], in_=zt[:, 0:n], func=AF.Silu), reads=[k('zt')], writes=[k('zt')])
            src, srck, sh, E = ext, k('ext'), 1, 15 + n
            for step in range(g + 1):
                dst, dstk = (sA, k('sA')) if step % 2 == 0 else (sB, k('sB'))
                lo = 2 * sh - 1
                P.op(DVE if step % 2 == 0 else POOL, lambda e, src=src, dst=dst, sh=sh, lo=lo: e.tensor_tensor(
                    out=dst[:, lo:E], in0=src[:, lo:E], in1=src[:, lo - sh:E - sh], op=ALU.add), reads=[srck], writes=[dstk])
                src, srck, sh = dst, dstk, sh * 2
            P.op(DVE, lambda e, src=src: e.scalar_tensor_tensor(out=pb[:, 0:n], in0=src[:, 15:15 + n], scalar=1.0 / w, in1=ext[:, 15:15 + n],
                                                              op0=ALU.mult, op1=ALU.subtract), reads=[srck, k('ext')], writes=[k('pb')])
            if fix:
                for t in range(w - 1):
                    P.op(DVE, lambda e, src=src, t=t: e.scalar_tensor_tensor(out=pb[:, t:t + 1], in0=src[:, 15 + t:16 + t], scalar=1.0 / (t + 1),
                                                                            in1=ext[:, 15 + t:16 + t], op0=ALU.mult, op1=ALU.subtract),
                         reads=[srck, k('ext')], writes=[k('pb')])
            for tt in range((n + 511) // 512):
                t0, m = tt * 512, min(512, n - tt * 512)
                bank = tt % 4
                P.op(PE, lambda e, t0=t0, m=m, bank=bank: e.matmul(ps[:, bank, 0:m], lhsT=pwb[:, g, :], rhs=pb[:, t0:t0 + m], start=True, stop=True),
                     reads=['pwb', k('pb')], writes=[('ps', bank)])
                P.op(DVE, lambda e, t0=t0, m=m, bank=bank: e.scalar_tensor_tensor(
                    out=go[:, t0:t0 + m], in0=ps[:, bank, 0:m], scalar=pp[:, PP_PSC + g:PP_PSC + g + 1], in1=zt[:, t0:t0 + m], op0=ALU.mult, op1=ALU.mult),
                    reads=[('ps', bank), 'pp', k('zt')], writes=[k('go')])
            c0 = 0 if n > 1 else T
            P.dma(SP, lambda e: e.dma_start(out=cx.gatedT[1536 + r0:1536 + r0 + 128, c0:c0 + n], in_=go[:, 0:n], allow_slow_non_contiguous=(n == 1)),
                  reads=[k('go')], writes=['gatedT'], tag='go')
        for g in range(4):
            group(g, T, ext, zt, sA, sB, pb, go, True, '')
            group(g, 1, exs, zs, sAs, sBs, pbs, gos, False, 's')
        P.dma(SP, lambda e: e.dma_start(out=cx.o_dbuf[l, 0], in_=cx.projT[R_DU:R_DU + 512, T - 15:T], allow_slow_non_contiguous=True), writes=['o_dbuf0'], tag='so')
        P.dma(SP, lambda e: e.dma_start(out=cx.o_dbuf[l, 1, :, 0:14], in_=cx.dbufT[l, :, 1:15], allow_slow_non_contiguous=True), writes=['o_dbuf1'], tag='so')
        P.dma(SP, lambda e: e.dma_start(out=cx.o_dbuf[l, 1, :, 14:15], in_=cx.projT[R_DU:R_DU + 512, T:T + 1], allow_slow_non_contiguous=True), writes=['o_dbuf2'], tag='so')
        P.emit()


def phase_mixC(nc, cx, l, heads=range(NH), do_sample=True):
    P = Prog(nc)
    ps = cx.ps
    with contextlib.ExitStack() as st:
        sb = lambda name, shape, dt: st.enter_context(nc.sbuf_tensor('c%d_' % l + name, shape, dt))
        cf = sb('cf', [128, 256], F32)
        cb = sb('cb', [128, 128 + 4 * 512], BF16)
        identf, tri = cf[:, 0:128], cf[:, 128:256]
        identb = cb[:, 0:128]
        negb = sb('negb', [128, 8], F32)
        ones32 = sb('ones32', [128, 64], F32)
        onesj = sb('onesj', [128, 32], F32)
        qf, kf, vf, zf = (sb(n, [64, T], F32) for n in ('qf', 'kf', 'vf', 'zf'))
        Qa, Ka = sb('Qa', [70, T], BF16), sb('Ka', [70, T], BF16)
        Vt = sb('Vt', [128, 32, 65], BF16)
        PT = [sb('PT%d' % i, [128, 512], BF16) for i in range(3)]
        f32t, e1, lf, cw, hif, r1 = (sb(n, [128, 32], F32) for n in ('f32t', 'e1', 'lf', 'cw', 'hif', 'r1'))
        offs = sb('offs', [128, 1], F32)
        c3 = sb('c3', [128, 6, 32], BF16)
        oT, t1 = sb('oT', [65, 512], F32), sb('t1', [64, 512], F32)
        rden = sb('rden', [65, 512], F32)
        go = sb('go', [64, T], BF16)
        P.dma(SP, lambda e: e.dma_start(out=cf[:], in_=cx.cf32), writes=['cf'], tag='c')
        P.dma(SP, lambda e: e.dma_start(out=cb[:], in_=cx.cb16), writes=['cb'], tag='c')
        P.dma(SP, lambda e: e.dma_start(out=negb[:], in_=cx.b_f[l:l + 1, :].partition_broadcast(128)), writes=['negb'], tag='c')
        P.op(POOL, lambda e: e.tensor_scalar(out=negb[:], in0=negb[:], scalar1=-1.0, scalar2=None, op0=ALU.mult), reads=['negb'], writes=['negb'])
        P.op(POOL, lambda e: e.memset(ones32[:], 1.0), writes=['ones32'])
        P.op(POOL, lambda e: e.memset(onesj[:], 1.0), writes=['onesj'])
        P.op(POOL, lambda e: e.memset(Vt[:, :, 64:65], 1.0), writes=['Vt'])
        P.op(POOL, lambda e: e.memset(Qa[64:70, :], 1.0), writes=['Qa'])
        P.op(POOL, lambda e: e.memset(Ka[64:70, :], 1.0), writes=['Ka'])
        zfs = [zf, sb('zf1', [64, T], F32)]

        def load_head(h):
            zfh, zk = zfs[h % 2], ('zf', h % 2)
            P.dma(SP, lambda e, h=h: e.dma_start(out=f32t[:], in_=cx.projT[R_SM + SM_F + h, 0:T].rearrange("(p j) -> p j", j=32)), writes=['f32t'], tag='m')
            for tl, r0, key in ((qf, R_CQ, 'qf'), (kf, R_CK, 'kf'), (vf, R_CV, 'vf'), (zfh, R_CZ, zk)):
                P.dma(SP, lambda e, tl=tl, r0=r0, h=h: e.dma_start(out=tl[:], in_=cx.projT[r0 + h * 64:r0 + (h + 1) * 64, 0:T]), writes=[key], tag='m')
            P.op(ACT, lambda e, h=h: e.activation(out=e1[:], in_=f32t[:], func=AF.Exp, scale=-1.0, bias=negb[:, h:h + 1]), reads=['f32t', 'negb'], writes=['e1'])
            P.op(ACT, lambda e: e.activation(out=e1[:], in_=e1[:], func=AF.Ln, bias=1.0), reads=['e1'], writes=['e1'])
            P.op(DVE, lambda e: e.tensor_scalar(out=lf[:], in0=e1[:], scalar1=-1.0, scalar2=None, op0=ALU.mult), reads=['e1'], writes=['lf'])
            P.dma(SP, lambda e, h=h: e.dma_start(out=cx.o_clf[l, h, 0:T].rearrange("(p j) -> p j", j=32), in_=lf[:]), reads=['lf'], writes=['o_clf'], tag='so')
            P.op(DVE, lambda e: e.tensor_tensor_scan(out=cw[:], data0=onesj[:], data1=lf[:], initial=0.0, op0=ALU.mult, op1=ALU.add),
                 reads=['lf', 'onesj'], writes=['cw'])
            P.op(PE, lambda e: e.matmul(ps[:, 4, 0:1], lhsT=tri, rhs=cw[:, 31:32], start=True, stop=True), reads=['cf', 'cw'], writes=[('ps', 4)])
            P.op(DVE, lambda e: e.tensor_copy(out=offs[:], in_=ps[:, 4, 0:1]), reads=[('ps', 4)], writes=['offs'])
            P.op(DVE, lambda e: e.tensor_scalar(out=cw[:], in0=cw[:], scalar1=offs[:, 0:1], scalar2=None, op0=ALU.add), reads=['cw', 'offs'], writes=['cw'])
            src = cw
            for pc in range(3):
                P.op(DVE, lambda e, pc=pc, src=src: e.tensor_copy(out=c3[:, pc, :], in_=src[:]), reads=['cw', 'r1'], writes=['c3'])
                P.op(POOL, lambda e, pc=pc, src=src: e.tensor_scalar(out=c3[:, 3 + pc, :], in0=src[:], scalar1=-1.0, scalar2=None, op0=ALU.mult),
                     reads=['cw', 'r1'], writes=['c3'])
                if pc < 2:
                    P.op(DVE, lambda e, pc=pc: e.tensor_copy(out=hif[:], in_=c3[:, pc, :]), reads=['c3'], writes=['hif'])
                    P.op(DVE, lambda e, src=src: e.tensor_tensor(out=r1[:], in0=src[:], in1=hif[:], op=ALU.subtract), reads=['cw', 'r1', 'hif'], writes=['r1'])
                    src = r1
            P.dma(SP, lambda e: e.dma_start(out=cx.cscr.rearrange("r (p j) -> p r j", j=32), in_=c3[:]), reads=['c3'], writes=['cscr'], tag='cs')

        def conv_head(h):
            zfh, zk = zfs[h % 2], ('zf', h % 2)
            P.dma(SP, lambda e: e.dma_start(out=Qa[64:67, :], in_=cx.cscr[0:3, :]), reads=['cscr'], writes=['Qa'], tag='cs')
            P.dma(SP, lambda e: e.dma_start(out=Ka[67:70, :], in_=cx.cscr[3:6, :]), reads=['cscr'], writes=['Ka'], tag='cs')
            P.op(ACT, lambda e: e.activation(out=Qa[0:64, :], in_=qf[:], func=AF.Copy, scale=0.125), reads=['qf'], writes=['Qa'])
            P.op(POOL, lambda e: e.tensor_copy(out=Ka[0:64, :], in_=kf[:]), reads=['kf'], writes=['Ka'])
            P.op(ACT, lambda e: e.activation(out=zfh[:], in_=zfh[:], func=AF.Silu), reads=[zk], writes=[zk])
            for r in range(4):
                bank = 5 + r % 2
                for j in range(8):
                    blk = r * 8 + j
                    P.op(PE, lambda e, bank=bank, j=j, blk=blk: e.transpose(out=ps[:, bank, j * 64:(j + 1) * 64], in_=vf[:, blk * 128:(blk + 1) * 128],
                                                                        identity=identf[0:64, 0:64]), reads=['vf', 'cf'], writes=[('ps', bank)])
                P.op(DVE, lambda e, bank=bank, r=r: e.tensor_copy(out=Vt[:, r * 8:(r + 1) * 8, 0:64], in_=ps[:, bank, :].rearrange("p (j d) -> p j d", d=64)),
                     reads=[('ps', bank)], writes=['Vt'])

        itc = [0]

        def attend(h):
            zfh, zk = zfs[h % 2], ('zf', h % 2)
            units = [(qt, kb) for qt in range(8) for kb in range(4 * qt + 4)]
            LOOK = 2

            def qk(i):
                qt, kb = units[i]
                bank = (itc[0] + i) % 3
                diag = kb >= 4 * qt
                P.op(PE, lambda e: e.matmul(ps[:, bank, :], lhsT=Ka[0:70, kb * 128:(kb + 1) * 128], rhs=Qa[0:70, qt * 512:(qt + 1) * 512], start=True, stop=not diag),
                     reads=['Ka', 'Qa'], writes=[('ps', bank)])
                if diag:
                    j = kb - 4 * qt
                    P.op(PE, lambda e: e.matmul(ps[:, bank, :], lhsT=identb, rhs=cb[:, 128 + j * 512:128 + (j + 1) * 512], start=False, stop=True),
                         reads=['cb'], writes=[('ps', bank)])

            def pv(i):
                qt, kb = units[i]
                nkb = 4 * qt + 4
                bank = r = (itc[0] + i) % 3
                ob = 3 if qt % 2 == 0 else 7
                P.op(ACT, lambda e: e.activation(out=PT[r][:], in_=ps[:, bank, :], func=AF.Exp), reads=[('ps', bank)], writes=[('PT', r)])
                P.op(PE, lambda e: e.matmul(ps[0:65, ob, :], lhsT=Vt[:, kb, :], rhs=PT[r][:], start=(kb == 0), stop=(kb == nkb - 1)),
                     reads=['Vt', ('PT', r)], writes=[('ps', ob)])
                if kb == nkb - 1:
                    P.op(DVE, lambda e: e.tensor_copy(out=oT[:], in_=ps[0:65, ob, :]), reads=[('ps', ob)], writes=['oT'])
                    P.op(DVE, lambda e: e.reciprocal(out=rden[64:65, :], in_=oT[64:65, :]), reads=['oT'], writes=['rden'])

                    def epi():
                        P.op(PE, lambda e: e.matmul(ps[0:64, 4, :], lhsT=ones32[64:65, 0:64], rhs=rden[64:65, :], start=True, stop=True),
                             reads=['ones32', 'rden'], writes=[('ps', 4)])
                        P.op(DVE, lambda e: e.tensor_tensor(out=t1[:], in0=oT[0:64, :], in1=ps[0:64, 4, :], op=ALU.mult), reads=['oT', ('ps', 4)], writes=['t1'])
                        P.op(POOL, lambda e: e.tensor_tensor(out=go[:, qt * 512:(qt + 1) * 512], in0=t1[:], in1=zfh[:, qt * 512:(qt + 1) * 512], op=ALU.mult),
                             reads=['t1', zk], writes=['go'])
                    pending.append((i + 2, epi))
            pending = []
            for i in range(len(units) + LOOK):
                if i < len(units):
                    qk(i)
                if i >= LOOK:
                    pv(i - LOOK)
                    while pending and pending[0][0] <= i - LOOK:
                        pending.pop(0)[1]()
            while pending:
                pending.pop(0)[1]()
            itc[0] += len(units)
            P.dma(SP, lambda e, h=h: e.dma_start(out=cx.gatedT[1024 + h * 64:1024 + (h + 1) * 64, 0:T], in_=go[:]), reads=['go'], writes=['gatedT'], tag='go')

        hl = list(heads)
        load_head(hl[0])
        conv_head(hl[0])
        for i, h in enumerate(hl):
            if i + 1 < len(hl):
                load_head(hl[i + 1])
            attend(h)
            if i + 1 < len(hl):
                conv_head(hl[i + 1])
        for j in range(8):
            P.dma(SP, lambda e, j=j: e.dma_start(out=cx.o_ckv[l, j * 128:(j + 1) * 128, :], in_=cx.projT[R_CK + j * 128:R_CK + (j + 1) * 128, :]),
                  writes=[('o_ckv', j)], tag='so')
        P.emit()


TP, NCH = 33 * 128, 33
A_KINDS = ('k', 'q', 'bz', 'rz', 'k2', 'a2', 'v')
B_KINDS = ('ah', 'bh', 'kh', 'rh', 'k2', 'a2', 'v')
RWKV_DS = float(np.exp(-0.5))


def _l2_rn(P, nc, ps, sqb, onesbd, n, rn, keys, bank0=0):
    for j, t0 in enumerate(range(0, n, 512)):
        m = min(512, n - t0)
        bank = bank0 + j % 2
        P.op(PE, lambda e, t0=t0, m=m, bank=bank: e.matmul(ps[:, bank, 0:m], lhsT=onesbd, rhs=sqb[:, t0:t0 + m], start=True, stop=True),
             reads=[keys[0], 'cb'], writes=[('ps', bank)])
        P.op(ACT, lambda e, t0=t0, m=m, bank=bank: e.activation(out=rn[:, t0:t0 + m], in_=ps[:, bank, 0:m], func=AF.Ln, bias=1e-6),
             reads=[('ps', bank)], writes=[keys[1]])
    P.op(ACT, lambda e: e.activation(out=rn[:, 0:n], in_=rn[:, 0:n], func=AF.Exp, scale=-0.5), reads=[keys[1]], writes=[keys[1]])


def phase_rowsA(nc, cx, l):
    P = Prog(nc)
    with contextlib.ExitStack() as st:
        sb = lambda name, shape, dt: st.enter_context(nc.sbuf_tensor('ra%d_' % l + name, shape, dt))
        pp = sb('pp', [128, NPP], F32)
        P.dma(SP, lambda e: e.dma_start(out=pp[:], in_=cx.pp[l]), writes=['pp'], tag='c')
        g, lb, u1, u0, gl, tmp, rmask, onesr, mA, mB = (sb(n, [8, TP], F32) for n in ('g', 'lb', 'u1', 'u0', 'gl', 'tmp', 'rmask', 'onesr', 'mA', 'mB'))
        nega = sb('nega', [8, 1], F32)
        P.op(POOL, lambda e: e.memset(g[:], 0.0), writes=['g'])
        P.op(POOL, lambda e: e.memset(lb[:], 0.0), writes=['lb'])
        P.op(POOL, lambda e: e.memset(onesr[:], 1.0), writes=['onesr'])
        P.op(POOL, lambda e: e.memset(rmask[:], 1.0), writes=['rmask'])
        P.op(POOL, lambda e: e.memset(rmask[:].rearrange("p (c i) -> p c i", i=128)[:, :, 0:1], 0.0), reads=['rmask'], writes=['rmask'])
        P.dma(SP, lambda e: e.dma_start(out=g[:, 0:TS], in_=cx.projT[R_SM + SM_A:R_SM + SM_A + 8, :]), reads=['g'], writes=['g'], tag='m')
        P.dma(SP, lambda e: e.dma_start(out=lb[:, 0:TS], in_=cx.projT[R_SM + SM_B:R_SM + SM_B + 8, :]), reads=['lb'], writes=['lb'], tag='m')
        P.op(ACT, lambda e: e.activation(out=nega[:], in_=pp[0:8, PP_ALOG:PP_ALOG + 1], func=AF.Exp), reads=['pp'], writes=['nega'])
        P.op(DVE, lambda e: e.tensor_scalar(out=nega[:], in0=nega[:], scalar1=-1.0, scalar2=None, op0=ALU.mult), reads=['nega'], writes=['nega'])
        P.op(ACT, lambda e: e.activation(out=g[:], in_=g[:], func=AF.Exp, bias=pp[0:8, PP_DTB:PP_DTB + 1]), reads=['g', 'pp'], writes=['g'])
        P.op(ACT, lambda e: e.activation(out=g[:], in_=g[:], func=AF.Ln, bias=1.0), reads=['g'], writes=['g'])
        P.op(DVE, lambda e: e.tensor_scalar(out=g[:], in0=g[:], scalar1=nega[:, 0:1], scalar2=None, op0=ALU.mult), reads=['g', 'nega'], writes=['g'])
        P.op(POOL, lambda e: e.memset(g[:, TS:TP], 0.0), reads=['g'], writes=['g'])
        P.op(ACT, lambda e: e.activation(out=lb[:], in_=lb[:], func=AF.Exp, scale=-1.0), reads=['lb'], writes=['lb'])
        P.op(ACT, lambda e: e.activation(out=lb[:], in_=lb[:], func=AF.Ln, bias=1.0), reads=['lb'], writes=['lb'])
        P.op(DVE, lambda e: e.tensor_scalar(out=lb[:], in0=lb[:], scalar1=-1.0, scalar2=None, op0=ALU.mult), reads=['lb'], writes=['lb'])
        P.op(DVE, lambda e: e.tensor_tensor_scan(out=u1[:], data0=rmask[:], data1=g[:], initial=0.0, op0=ALU.mult, op1=ALU.add),
             reads=['g', 'rmask'], writes=['u1'])
        P.op(POOL, lambda e: e.tensor_tensor(out=u0[:], in0=u1[:], in1=g[:], op=ALU.subtract), reads=['u1', 'g'], writes=['u0'])
        P.op(DVE, lambda e: e.tensor_copy(out=gl[:].rearrange("p (c i) -> p c i", i=128),
                                          in_=u1[:].rearrange("p (c i) -> p c i", i=128)[:, :, 127:128].to_broadcast([8, NCH, 128])),
             reads=['u1'], writes=['gl'])
        P.op(POOL, lambda e: e.tensor_tensor(out=tmp[:], in0=lb[:], in1=gl[:], op=ALU.add), reads=['lb', 'gl'], writes=['tmp'])
        stage = [(mA, 'mA'), (mB, 'mB')]
        for i in range(5):
            m, mk = stage[i % 2]
            if i == 0:
                P.op(ACT, lambda e, m=m: e.activation(out=m[:], in_=u0[:], func=AF.Exp), reads=['u0'], writes=[mk])
            elif i == 1:
                P.op(ACT, lambda e, m=m: e.activation(out=m[:], in_=u1[:], func=AF.Exp), reads=['u1'], writes=[mk])
            elif i == 4:
                P.op(ACT, lambda e, m=m: e.activation(out=m[:], in_=gl[:], func=AF.Exp), reads=['gl'], writes=[mk])
            else:
                src = u1 if i == 2 else u0
                P.op(POOL, lambda e, m=m, src=src: e.tensor_tensor(out=m[:], in0=tmp[:], in1=src[:], op=ALU.subtract), reads=['tmp', 'u1', 'u0'], writes=[mk])
                P.op(ACT, lambda e, m=m: e.activation(out=m[:], in_=m[:], func=AF.Exp), reads=[mk], writes=[mk])
            P.dma(SP, lambda e, i=i, m=m: e.dma_start(out=cx.A_mrows[i], in_=m[:]), reads=[mk], writes=['A_mrows'], tag='er')
        P.op(POOL, lambda e: e.tensor_tensor(out=tmp[:], in0=lb[:], in1=u0[:], op=ALU.subtract), reads=['lb', 'u0', 'mA', 'mB'], writes=['tmp'])
        P.op(POOL, lambda e: e.tensor_tensor(out=gl[:], in0=lb[:], in1=u1[:], op=ALU.subtract), reads=['lb', 'u1', 'gl', 'mA', 'mB'], writes=['gl'])
        hbl = mA[:].bitcast(BF16)
        hb, lob = hbl[:, 0:TP], hbl[:, TP:2 * TP]
        onesb = onesr[:].bitcast(BF16)[:, 0:TP]
        P.op(POOL, lambda e: e.memset(onesb, 1.0), reads=['onesr'], writes=['onesr'])
        for kind, r0 in ((0, 2), (3, 2), (1, 0), (2, 0)):
            for r in (r0, r0 + 1):
                P.dma(SP, lambda e, kind=kind, r=r: e.dma_start(out=cx.A_erb[kind, r], in_=onesb), reads=['onesr'], writes=[('A_erb', kind, r)], tag='er')
        for kind, r0, xt, xk in ((0, 0, u0, 'u0'), (3, 0, u1, 'u1'), (1, 2, tmp, 'tmp'), (2, 2, gl, 'gl')):
            P.op(DVE, lambda e, xt=xt: e.tensor_copy(out=hb, in_=xt[:]), reads=[xk, 'mA'], writes=['mA'])
            P.op(DVE, lambda e: e.tensor_copy(out=mB[:], in_=hb), reads=['mA', 'mB'], writes=['mB'])
            P.op(POOL, lambda e, xt=xt: e.tensor_tensor(out=g[:], in0=xt[:], in1=mB[:], op=ALU.subtract), reads=[xk, 'mB', 'g'], writes=['g'])
            P.op(DVE, lambda e: e.tensor_copy(out=lob, in_=g[:]), reads=['g', 'mA'], writes=['mA'])
            P.dma(SP, lambda e, kind=kind, r0=r0: e.dma_start(out=cx.A_erb[kind, r0], in_=hb), reads=['mA'], writes=[('A_erb', kind, r0)], tag='er')
            P.dma(SP, lambda e, kind=kind, r0=r0: e.dma_start(out=cx.A_erb[kind, r0 + 1], in_=lob), reads=['mA'], writes=[('A_erb', kind, r0 + 1)], tag='er')
        P.emit()


def phase_preA(nc, cx, l):
    P = Prog(nc)
    ps = cx.ps
    with contextlib.ExitStack() as st:
        sb = lambda name, shape, dt: st.enter_context(nc.sbuf_tensor('pa%d_' % l + name, shape, dt))
        pp = sb('pp', [128, NPP], F32)
        cb = sb('cb', [128, 256], BF16)
        onesbd = cb[:, 128:256]
        sel = sb('sel', [8, 512], F32)
        P.dma(SP, lambda e: e.dma_start(out=pp[:], in_=cx.pp[l]), writes=['pp'], tag='c')
        P.dma(SP, lambda e: e.dma_start(out=cb[:], in_=cx.cb2), writes=['cb'], tag='c')
        P.dma(SP, lambda e: e.dma_start(out=sel[:], in_=cx.sel8), writes=['sel'], tag='c')
        W = 1024
        raw, acc, rn, kn, qn = (sb(n, [128, W + 3], F32) for n in ('raw', 'acc', 'rn', 'kn', 'qn'))
        sqb, ob = sb('sqb', [128, W], BF16), [sb('ob%d' % i, [128, W], BF16) for i in range(2)]
        gct = sb('gct', [128, NCH], F32)
        mr = sb('mr', [8, 4, W], F32)
        m4c = sb('m4c', [8, NCH], F32)
        P.dma(SP, lambda e: e.dma_start(out=m4c[:], in_=cx.A_mrows[4].rearrange("h (c i) -> h c i", i=128)[:, :, 0], allow_slow_non_contiguous=True), writes=['m4c'], tag='c')
        nob = [0]

        def store(kind, pb, c0, n, src_f32, mul_row=None, srckey=None):
            o = ob[nob[0] % 2]
            ok = ('ob', nob[0] % 2)
            nob[0] += 1
            if mul_row is None:
                P.op(POOL, lambda e: e.tensor_copy(out=o[:, 0:n], in_=src_f32[:, 0:n]), reads=[srckey], writes=[ok])
            else:
                for j, t0 in enumerate(range(0, n, 512)):
                    m = min(512, n - t0)
                    bank = 2 + j % 2
                    P.op(PE, lambda e, t0=t0, m=m, bank=bank: e.matmul(ps[:, bank, 0:m], lhsT=sel[:, pb * 128:(pb + 1) * 128],
                                                                    rhs=mr[:, mul_row, t0:t0 + m], start=True, stop=True),
                         reads=['sel', 'mr'], writes=[('ps', bank)])
                    P.op(DVE, lambda e, t0=t0, m=m, bank=bank: e.tensor_tensor(out=o[:, t0:t0 + m], in0=src_f32[:, t0:t0 + m], in1=ps[:, bank, 0:m], op=ALU.mult),
                         reads=[srckey, ('ps', bank)], writes=[ok])
            P.dma(SP, lambda e: e.dma_start(out=cx.A_sc[kind][pb * 128:(pb + 1) * 128, c0:c0 + n], in_=o[:, 0:n], allow_slow_non_contiguous=(n == 1)),
                  reads=[ok], writes=[('A_sc', kind)], tag='sc')

        def conv_block(br, pb, c0, n, dst, dstkey):
            r0 = br * 512 + pb * 128
            b12 = br * 4 + pb
            if c0 == 0:
                P.op(POOL, lambda e: e.memset(raw[:, 0:3], 0.0), reads=['raw'], writes=['raw'])
                P.dma(SP, lambda e: e.dma_start(out=raw[:, 3:3 + n], in_=cx.projT[r0:r0 + 128, 0:n]), writes=['raw'], tag='m')
            elif c0 == T:
                P.dma(SP, lambda e: e.dma_start(out=raw[:, 0:3], in_=cx.aconvT[l, r0:r0 + 128, :]), writes=['raw'], tag='m')
                P.dma(SP, lambda e: e.dma_start(out=raw[:, 3:4], in_=cx.projT[r0:r0 + 128, T:T + 1], allow_slow_non_contiguous=True), writes=['raw'], tag='m')
            else:
                P.dma(SP, lambda e: e.dma_start(out=raw[:, 0:3 + n], in_=cx.projT[r0:r0 + 128, c0 - 3:c0 + n]), writes=['raw'], tag='m')
            cw = lambda i: pp[:, PP_CONV + b12 * 4 + i:PP_CONV + b12 * 4 + i + 1]
            P.op(DVE, lambda e: e.tensor_scalar(out=acc[:, 0:n], in0=raw[:, 0:n], scalar1=cw(0), scalar2=None, op0=ALU.mult), reads=['raw', 'pp'], writes=['acc'])
            for i in range(1, 4):
                P.op(DVE, lambda e, i=i: e.scalar_tensor_tensor(out=acc[:, 0:n], in0=raw[:, i:i + n], scalar=cw(i), in1=acc[:, 0:n], op0=ALU.mult, op1=ALU.add),
                     reads=['raw', 'acc', 'pp'], writes=['acc'])
            P.op(ACT, lambda e: e.activation(out=dst[:, 0:n], in_=acc[:, 0:n], func=AF.Silu), reads=['acc'], writes=[dstkey])

        for pb in range(4):
            for (c0, n) in [(i * W, W) for i in range(T // W)] + [(T, 1)]:
                P.dma(SP, lambda e, c0=c0, n=n: e.dma_start(out=mr[:, :, 0:n], in_=cx.A_mrows[0:4, :, c0:c0 + n].rearrange("k h t -> h k t"),
                                                          allow_slow_non_contiguous=(n == 1)), writes=['mr'], tag='m')
                for br, dst, dk, sc in ((0, qn, 'qn', 0.125), (1, kn, 'kn', 1.0)):
                    conv_block(br, pb, c0, n, dst, dk)
                    P.act(sqb[:, 0:n], dst[:, 0:n], AF.Square, reads=[dk], writes=['sqb'])
                    _l2_rn(P, nc, ps, sqb, onesbd, n, rn, ('sqb', 'rn'))
                    P.stt(dst[:, 0:n], dst[:, 0:n], sc, rn[:, 0:n], ALU.mult, ALU.mult, reads=[dk, 'rn'], writes=[dk])
                store('q', pb, c0, n, qn, None, 'qn')
                store('rz', pb, c0, n, qn, 1, 'qn')
                store('k', pb, c0, n, kn, None, 'kn')
                store('bz', pb, c0, n, kn, 0, 'kn')
                store('k2', pb, c0, n, kn, 2, 'kn')
                store('a2', pb, c0, n, kn, 3, 'kn')
                conv_block(2, pb, c0, n, qn, 'qn')
                store('v', pb, c0, n, qn, None, 'qn')
            for j in range(NCH):
                pass
            P.op(PE, lambda e, pb=pb: e.matmul(ps[:, 4, 0:NCH], lhsT=sel[:, pb * 128:(pb + 1) * 128],
                                               rhs=m4c[:], start=True, stop=True),
                 reads=['sel', 'm4c'], writes=[('ps', 4)])
            P.op(DVE, lambda e: e.tensor_copy(out=gct[:], in_=ps[:, 4, 0:NCH]), reads=[('ps', 4)], writes=['gct'])
            P.dma(SP, lambda e, pb=pb: e.dma_start(out=cx.A_gc[pb * 128:(pb + 1) * 128, :], in_=gct[:]), reads=['gct'], writes=['A_gc'], tag='sc')
        P.dma(SP, lambda e: e.dma_start(out=cx.o_aconv[l, 0], in_=cx.projT[0:1536, T - 3:T], allow_slow_non_contiguous=True), writes=['oac0'], tag='so')
        P.dma(SP, lambda e: e.dma_start(out=cx.o_aconv[l, 1, :, 0:2], in_=cx.aconvT[l, :, 1:3], allow_slow_non_contiguous=True), writes=['oac1'], tag='so')
        P.dma(SP, lambda e: e.dma_start(out=cx.o_aconv[l, 1, :, 2:3], in_=cx.projT[0:1536, T:T + 1], allow_slow_non_contiguous=True), writes=['oac2'], tag='so')
        P.emit()


def phase_preB(nc, cx, l):
    P = Prog(nc)
    ps = cx.ps
    W = 1024
    with contextlib.ExitStack() as st:
        sb = lambda name, shape, dt: st.enter_context(nc.sbuf_tensor('pb%d_' % l + name, shape, dt))
        pp = sb('pp', [128, NPP], F32)
        cb = sb('cb', [128, 256], BF16)
        onesbd = cb[:, 128:256]
        lwf, lwb = sb('lwf', [128, 512], F32), sb('lwb', [128, 512], BF16)
        oma = sb('oma', [128, 4], F32)
        rmask = sb('rmask', [128, W], F32)
        P.ld(pp[:], cx.pp[l], writes=['pp'], tag='c')
        P.ld(cb[:], cx.cb2, writes=['cb'], tag='c')
        P.ld(lwf[:], cx.lora_w[l], writes=['lwf'], tag='c')
        P.cp(POOL, lwb[:], lwf[:], reads=['lwf'], writes=['lwb'])
        P.ts(DVE, oma[:], pp[:, PP_ALPHA:PP_ALPHA + 4], -1.0, ALU.mult, 1.0, ALU.add, reads=['pp'], writes=['oma'])
        P.ms(POOL, rmask[:], 1.0, writes=['rmask'])
        P.ms(POOL, rmask[:].rearrange("p (c i) -> p c i", i=128)[:, :, 0:1], 0.0, reads=['rmask'], writes=['rmask'])
        sm, smd = sb('sm', [128, W + 1], F32), sb('smd', [128, W], F32)
        lwin = sb('lwin', [128, W], BF16)
        raw = {k: sb('raw_' + k, [128, W + 1], F32) for k in 'rkv'}
        sh = {k: sb('sh_' + k, [128, W], F32) for k in 'rkv'}
        dtmp = sb('dtmp', [128, W], F32)
        lw, av, kx, rn, kt, t1, Lc, G, Gi, Gm, Ah, Kh = (sb(n, [128, W], F32) for n in ('lw', 'av', 'kx', 'rn', 'kt', 't1', 'Lc', 'G', 'Gi', 'Gm', 'Ah', 'Kh'))
        sqb = sb('sqb', [128, W], BF16)
        ob = [sb('ob%d' % i, [128, W], BF16) for i in range(3)]
        bo = sb('bo', [128, W], F32)
        gct = [sb('gct%d' % i, [128, NCH], F32) for i in range(4)]
        nob = [0]

        def store(kind, pb, c0, n, fn):
            j = nob[0] % 3
            nob[0] += 1
            fn(ob[j][:, 0:n], ('ob', j))
            P.ld(cx.B_sc[kind][pb * 128:(pb + 1) * 128, c0:c0 + n], ob[j][:, 0:n], reads=[('ob', j)], writes=[('B_sc', kind)], tag='sc', slow=(n == 1))

        def halo_load(tile, key, r0, c0, n, col12):
            if c0 == 0:
                P.ms(POOL, tile[:, 0:1], 0.0, reads=[key], writes=[key])
                P.ld(tile[:, 1:1 + n], cx.projT[r0:r0 + 128, 0:n], writes=[key], tag='m')
            elif c0 == T:
                P.ld(tile[:, 0:1], cx.bshiftT[l, :, col12:col12 + 1], writes=[key], tag='m', slow=True)
                P.ld(tile[:, 1:2], cx.projT[r0:r0 + 128, T:T + 1], writes=[key], tag='m', slow=True)
            else:
                P.ld(tile[:, 0:1 + n], cx.projT[r0:r0 + 128, c0 - 1:c0 + n], writes=[key], tag='m')

        def shift(dst, dkey, tile, key, n, mucol):
            P.tt(POOL, dtmp[:, 0:n], tile[:, 0:n], tile[:, 1:1 + n], ALU.subtract, reads=[key], writes=['dtmp'])
            P.stt(dst[:, 0:n], dtmp[:, 0:n], pp[:, mucol:mucol + 1], tile[:, 1:1 + n], ALU.mult, ALU.add, reads=['dtmp', key, 'pp'], writes=[dkey])

        for (c0, n) in [(i * W, W) for i in range(T // W)] + [(T, 1)]:
            nch = max(1, n // 128)
            halo_load(sm, 'sm', R_SM, c0, n, 12)
            shift(smd, 'smd', sm, 'sm', n, PP_MUS)
            P.act(lwin[32:64, 0:n], smd[32:64, 0:n], AF.Tanh, reads=['smd'], writes=['lwin'])
            P.cp(POOL, lwin[64:96, 0:n], smd[64:96, 0:n], reads=['smd'], writes=['lwin'])
            for pb in range(4):
                for j, k in enumerate('rkv'):
                    halo_load(raw[k], 'raw_' + k, R_BR + j * 512 + pb * 128, c0, n, j * 4 + pb)
                    shift(sh[k], 'sh_' + k, raw[k], 'raw_' + k, n, PP_MU + j * 4 + pb)
                rs, ks, vs = sh['r'], sh['k'], sh['v']
                for jt, t0 in enumerate(range(0, n, 512)):
                    m = min(512, n - t0)
                    cs = slice(t0, t0 + m)
                    P.mm(ps[:, 2, 0:m], lwb[32:64, pb * 128:(pb + 1) * 128], lwin[32:64, cs], reads=['lwb', 'lwin'], writes=[('ps', 2)])
                    P.mm(ps[:, 3, 0:m], lwb[64:96, pb * 128:(pb + 1) * 128], lwin[64:96, cs], reads=['lwb', 'lwin'], writes=[('ps', 3)])
                    P.act(lw[:, cs], ps[:, 2, 0:m], AF.Sigmoid, reads=[('ps', 2), 'pp'], writes=['lw'], bias=pp[:, PP_W0 + pb:PP_W0 + pb + 1])
                    P.act(av[:, cs], ps[:, 3, 0:m], AF.Sigmoid, reads=[('ps', 3), 'pp'], writes=['av'], bias=pp[:, PP_A0 + pb:PP_A0 + pb + 1])
                P.ts(DVE, lw[:, 0:n], lw[:, 0:n], -RWKV_DS, ALU.mult, reads=['lw'], writes=['lw'])
                P.ts(DVE, kx[:, 0:n], ks[:, 0:n], pp[:, PP_XI + pb:PP_XI + pb + 1], ALU.mult, reads=['sh_k', 'pp'], writes=['kx'])
                P.act(sqb[:, 0:n], kx[:, 0:n], AF.Square, reads=['kx'], writes=['sqb'])
                _l2_rn(P, nc, ps, sqb, onesbd, n, rn, ('sqb', 'rn'))
                P.tt(DVE, kx[:, 0:n], kx[:, 0:n], rn[:, 0:n], ALU.mult, reads=['kx', 'rn'], writes=['kx'])
                P.ts(DVE, t1[:, 0:n], av[:, 0:n], pp[:, PP_ALPHA + pb:PP_ALPHA + pb + 1], ALU.mult, oma[:, pb:pb + 1], ALU.add, reads=['av', 'pp', 'oma'], writes=['t1'])
                P.tt(POOL, kt[:, 0:n], ks[:, 0:n], t1[:, 0:n], ALU.mult, reads=['sh_k', 't1'], writes=['kt'])
                P.tt(POOL, t1[:, 0:n], rs[:, 0:n], kt[:, 0:n], ALU.mult, reads=['sh_r', 'kt', 't1'], writes=['t1'])
                P.ts(DVE, sqb[:, 0:n], t1[:, 0:n], pp[:, PP_RHO + pb:PP_RHO + pb + 1], ALU.mult, reads=['t1', 'pp', 'rn'], writes=['sqb'])
                for jt, t0 in enumerate(range(0, n, 512)):
                    m = min(512, n - t0)
                    cs = slice(t0, t0 + m)
                    P.mm(ps[:, 4 + jt % 2, 0:m], onesbd, sqb[:, cs], reads=['sqb', 'cb'], writes=[('ps', 4 + jt % 2)])
                    P.tt(DVE, bo[:, cs], ps[:, 4 + jt % 2, 0:m], vs[:, cs], ALU.mult, reads=[('ps', 4 + jt % 2), 'sh_v'], writes=['bo'])
                P.ld(cx.B_bonus[pb * 128:(pb + 1) * 128, c0:c0 + n], bo[:, 0:n], reads=['bo'], writes=['B_bonus'], tag='sc', slow=(n == 1))
                P.op(DVE, lambda e, n=n: e.tensor_tensor_scan(out=Lc[:, 0:n], data0=rmask[:, 0:n], data1=lw[:, 0:n], initial=0.0, op0=ALU.mult, op1=ALU.add),
                     reads=['lw', 'rmask'], writes=['Lc'])
                P.act(G[:, 0:n], Lc[:, 0:n], AF.Exp, reads=['Lc'], writes=['G'])
                P.act(Gi[:, 0:n], Lc[:, 0:n], AF.Exp, reads=['Lc'], writes=['Gi'], scale=-1.0)
                P.tt(POOL, Gm[:, 0:n], Lc[:, 0:n], lw[:, 0:n], ALU.subtract, reads=['Lc', 'lw'], writes=['Gm'])
                P.act(Gm[:, 0:n], Gm[:, 0:n], AF.Exp, reads=['Gm'], writes=['Gm'])
                if n > 1:
                    Gv = G[:, 0:n].rearrange("p (c i) -> p c i", i=128)
                    GCb = Gv[:, :, 127:128].to_broadcast([128, nch, 128])
                    P.cp(POOL, gct[pb][:, c0 // 128:c0 // 128 + nch], Gv[:, :, 127], reads=['G'], writes=[('gct', pb)])
                    v3 = lambda ap: ap.rearrange("p (c i) -> p c i", i=128)
                else:
                    GCb = G[:, 0:1]
                    P.cp(POOL, gct[pb][:, 32:33], G[:, 0:1], reads=['G'], writes=[('gct', pb)])
                    v3 = lambda ap: ap
                P.tt(POOL, t1[:, 0:n], av[:, 0:n], kx[:, 0:n], ALU.mult, reads=['av', 'kx', 'sqb'], writes=['t1'])
                P.tt(DVE, Ah[:, 0:n], t1[:, 0:n], Gi[:, 0:n], ALU.mult, reads=['t1', 'Gi'], writes=['Ah'])
                P.tt(DVE, Kh[:, 0:n], kt[:, 0:n], Gi[:, 0:n], ALU.mult, reads=['kt', 'Gi'], writes=['Kh'])
                store('ah', pb, c0, n, lambda o, ok: P.cp(POOL, o, Ah[:, 0:n], reads=['Ah'], writes=[ok]))
                store('kh', pb, c0, n, lambda o, ok: P.cp(ACT, o, Kh[:, 0:n], reads=['Kh'], writes=[ok]))
                store('bh', pb, c0, n, lambda o, ok: P.tt(DVE, o, kx[:, 0:n], Gm[:, 0:n], ALU.mult, reads=['kx', 'Gm'], writes=[ok]))
                store('rh', pb, c0, n, lambda o, ok: P.tt(POOL, o, rs[:, 0:n], G[:, 0:n], ALU.mult, reads=['sh_r', 'G'], writes=[ok]))
                store('k2', pb, c0, n, lambda o, ok: P.tt(DVE, v3(o), v3(Kh[:, 0:n]), GCb, ALU.mult, reads=['Kh', 'G'], writes=[ok]))
                store('a2', pb, c0, n, lambda o, ok: P.tt(POOL, v3(o), v3(Ah[:, 0:n]), GCb, ALU.mult, reads=['Ah', 'G'], writes=[ok]))
                store('v', pb, c0, n, lambda o, ok: P.cp(ACT, o, vs[:, 0:n], reads=['sh_v'], writes=[ok]))
        for pb in range(4):
            P.ld(cx.B_gc[pb * 128:(pb + 1) * 128, :], gct[pb][:], reads=[('gct', pb)], writes=['B_gc'], tag='sc')
        for j, col in ((0, T - 1), (1, T)):
            P.ld(cx.o_bshift[l, j, :, 0:12], cx.projT[R_BR:R_BR + 1536, col:col + 1].rearrange("(b p) o -> p (b o)", p=128), writes=[('obs', j)], tag='so', slow=True)
            P.ld(cx.o_bshift[l, j, :, 12:13], cx.projT[R_SM:R_SM + 128, col:col + 1], writes=[('obs2', j)], tag='so', slow=True)
        P.emit()


def phase_scan(nc, cx, l, br, chunks=range(NCH)):
    P = Prog(nc)
    ps = cx.ps
    isA = br == 'A'
    SC = cx.A_sc if isA else cx.B_sc
    GC = cx.A_gc if isA else cx.B_gc
    names = dict(XA='k', XB='k', XK='k', XR='q', XBz='bz', XRz='rz', K2='k2', A2='a2', V='v') if isA else \
        dict(XA='ah', XB='bh', XK='kh', XR='rh', XBz='bh', XRz='rh', K2='k2', A2='a2', V='v')
    kinds = sorted(set(names.values()))
    with contextlib.ExitStack() as st:
        sb = lambda name, shape, dt: st.enter_context(nc.sbuf_tensor('s%s%d_' % (br, l) + name, shape, dt))
        cb = sb('cb', [128, 128 + 3 * 128], BF16)
        identb = cb[:, 0:128]
        cf = sb('cf', [128, 4 * 4 * 128], F32)
        ident4 = cf[:, 0:512].rearrange("p (h t) -> p h t", h=4)
        k01 = [cf[:, 512 * (1 + i):512 * (2 + i)].rearrange("p (h t) -> p h t", h=4) for i in range(3)]
        pp = sb('pp', [128, NPP], F32)
        P.dma(SP, lambda e: e.dma_start(out=cb[:], in_=cx.cb3), writes=['cb'], tag='c')
        P.dma(SP, lambda e: e.dma_start(out=cf[:], in_=cx.cf3), writes=['cf'], tag='c')
        P.dma(SP, lambda e: e.dma_start(out=pp[:], in_=cx.pp[l]), writes=['pp'], tag='c')
        op_t = {k: [sb('x_%s%d' % (k, i), [64, 8, 128], BF16) for i in range(2)] for k in kinds}
        tok = {k: [sb('t_%s%d' % (k, i), [128, 8, 64], BF16) for i in range(2)] for k in ('k2', 'a2', 'v')}
        er = [sb('er%d' % i, [4, 4, 8, 128], BF16) for i in range(2)] if isA else None
        E = [[sb('E%d_%d' % (k, g), [128, 4, 128], F32) for g in range(2)] for k in range(5)] if isA else None
        NDT = F32
        Lb = [[sb('Lb%d_%d' % (g, i), [128, 4, 128], NDT) for i in range(2)] for g in range(2)]
        Uf = [sb('Uf%d' % g, [128, 4, 128], F32) for g in range(2)]
        Ub = [[Uf[g] if i == 0 else sb('Ub%d_%d' % (g, i), [128, 4, 128], NDT) for i in range(3)] for g in range(2)]
        Pf = [sb('Pf%d' % g, [128, 4, 128], F32) for g in range(2)]
        Pb = [sb('Pb%d' % g, [128, 4, 128], BF16) for g in range(2)]
        Mkb, RKb, RAb = ([sb('%s%d' % (n, g), [128, 4, 128], BF16) for g in range(2)] for n in ('Mkb', 'RKb', 'RAb'))
        Xb, Pnb = ([sb('%s%d' % (n, g), [128, 4, 64], BF16) for g in range(2)] for n in ('Xb', 'Pnb'))
        Zf, Zb = sb('Zf', [64, 8, 64], F32), sb('Zb', [64, 8, 64], BF16)
        Zt = sb('Ztmp', [64, 8, 64], F32)
        gct = sb('gct', [64, 8, NCH], F32)
        oT = [sb('oT%d' % i, [64, 8, 128], F32) for i in range(2)]
        zt = [sb('zt%d' % i, [64, 8, 128], F32) for i in range(2)]
        w1, w2 = sb('w1', [64, 8, 128], F32), sb('w2', [64, 8, 128], F32)
        wb = sb('wb', [64, 8, 128], BF16)
        gob = [sb('gob%d' % i, [64, 8, 128], BF16) for i in range(2)]
        ones64b = sb('ones64b', [64, 64], BF16)
        ones64f = sb('ones64f', [64, 64], F32)
        P.op(POOL, lambda e: e.memset(ones64b[:], 1.0), writes=['ones64b'])
        P.op(POOL, lambda e: e.memset(ones64f[:], 1.0), writes=['ones64f'])
        P.dma(SP, lambda e: e.dma_start(out=gct[:], in_=GC.rearrange("(h d) c -> d h c", d=64)), writes=['gct'], tag='c')
        if not isA:
            bon = [sb('bon%d' % i, [64, 8, 128], F32) for i in range(2)]
            gng, gnb = sb('gng', [64, 8], F32), sb('gnb', [64, 8], F32)
            P.dma(SP, lambda e: e.dma_start(out=gng[:], in_=cx.gn_g[l].rearrange("(h d) -> d h", d=64), allow_slow_non_contiguous=True), writes=['gng'], tag='c')
            P.dma(SP, lambda e: e.dma_start(out=gnb[:], in_=cx.gn_b[l].rearrange("(h d) -> d h", d=64), allow_slow_non_contiguous=True), writes=['gnb'], tag='c')
        P.op(POOL, lambda e: e.memset(Zf[:], 0.0), writes=['Zf'])
        P.op(POOL, lambda e: e.memset(Zb[:], 0.0), writes=['Zb'])
        zrow = R_AZ if isA else R_BZ
        grow = 0 if isA else 512
        st_in = cx.sA_in if isA else cx.sB_in
        st_out = cx.o_AS if isA else cx.o_BS
        flat = lambda ap: ap.rearrange("p h t -> p (h t)")
        b4 = lambda ap: ap.rearrange("p (h t) -> p h t", h=4)
        for c in chunks:
            b = c % 2
            c0 = c * 128
            n = 128 if c < 32 else 1
            slow = (n == 1)
            for k in kinds:
                if n == 1:
                    P.ms(POOL, op_t[k][b][:], 0.0, writes=[('x', k, b)])
                P.ld(op_t[k][b][:, :, 0:n], SC[k][:, c0:c0 + n].rearrange("(h d) t -> d h t", d=64), writes=[('x', k, b)], tag='ld', slow=slow)
            if n == 1:
                P.ms(POOL, zt[b][:], 0.0, writes=[('zt', b)])
            P.ld(zt[b][:, :, 0:n], cx.projT[zrow:zrow + 512, c0:c0 + n].rearrange("(h d) t -> d h t", d=64), writes=[('zt', b)], tag='ld', slow=slow)
            if not isA:
                if n == 1:
                    P.ms(POOL, bon[b][:], 0.0, writes=[('bon', b)])
                P.ld(bon[b][:, :, 0:n], cx.B_bonus[:, c0:c0 + n].rearrange("(h d) t -> d h t", d=64), writes=[('bon', b)], tag='ld', slow=slow)
            if isA:
                for kd in range(4):
                    P.ld(er[b][:, kd], cx.A_erb[kd, :, :, c0:c0 + 128], writes=[('er', b)], tag='ld')
            if c == 32:
                P.ld(st_out[l, 0], Zf[:], reads=['Zf'], writes=['st_out0'], tag='so')
                P.ld(Zf[:], st_in[l], reads=['st_out0'], writes=['Zf'], tag='ld')
                P.cp(ACT, Zb[:], Zf[:], reads=['Zf'], writes=['Zb'])
            X = {role: op_t[k][b] for role, k in names.items()}
            xk = {role: ('x', k, b) for role, k in names.items()}
            for j, k in enumerate(('k2', 'a2', 'v')):
                bank = 5 + j % 2
                pst = ps[:, bank, :].bitcast(BF16)[:, 0:512].rearrange("p (h d) -> p h d", d=64)
                for h in range(8):
                    P.tr(pst[:, h, :], op_t[k][b][:, h, :], identb[0:64, 0:64], reads=[('x', k, b), 'cb'], writes=[('ps', bank)])
                P.cp(ACT if j != 1 else DVE, tok[k][b][:], pst, reads=[('ps', bank)], writes=[('tok', k, b)])
            K2t, A2t, Vt = tok['k2'][b], tok['a2'][b], tok['v'][b]
            tkk = [('tok', 'k2', b), ('tok', 'a2', b), ('tok', 'v', b)]
            for g in range(2):
                hs = list(range(4 * g, 4 * g + 4))
                if isA:
                    combos = ((0, 1, 1), (1, 0, 2), (2, 0, 2), (2, 3, 3), (1, 3, 3))
                    for k5, (ka, kb_, mi) in enumerate(combos):
                        bank = k5 % 2
                        for hh, h in enumerate(hs):
                            o_ = ps[:, bank, hh * 128:(hh + 1) * 128]
                            P.mm(o_, er[b][0:4, ka, h, :], er[b][0:4, kb_, h, :], True, False, reads=[('er', b)], writes=[('ps', bank)])
                            P.mm(o_, identb, cb[:, 128 * mi:128 * (mi + 1)], False, True, reads=['cb'], writes=[('ps', bank)])
                        P.act(E[k5][g][:], b4(ps[:, bank, :]), AF.Exp, reads=[('ps', bank)], writes=[('E', k5, g)])
                    mult = [E[k5][g][:] for k5 in range(5)]
                    mkeys = [('E', k5, g) for k5 in range(5)]
                else:
                    mult = [k01[0], k01[1], k01[1], k01[2], k01[2]]
                    mkeys = ['cf'] * 5
                specs = (('XB', 'XA'), ('XA', 'XB'), ('XK', 'XB'), ('XK', 'XR'), ('XA', 'XR'))
                dsts = ((Lb[g][0], ('Lb', g, 0)), (Uf[g], ('Uf', g)), (Mkb[g], ('Mkb', g)), (RKb[g], ('RKb', g)), (RAb[g], ('RAb', g)))
                for k5, ((ra, rb), (dst, dkey)) in enumerate(zip(specs, dsts)):
                    bank = 2 + k5 % 3
                    for hh, h in enumerate(hs):
                        P.mm(ps[:, bank, hh * 128:(hh + 1) * 128], X[ra][:, h, :], X[rb][:, h, :], reads=[xk[ra], xk[rb]], writes=[('ps', bank)])
                    P.tt(DVE, dst[:], b4(ps[:, bank, :]), mult[k5], ALU.mult, reads=[('ps', bank), mkeys[k5]], writes=[dkey])
                P.tt(POOL, Pf[g][:], ident4, Uf[g][:], ALU.subtract, reads=['cf', ('Uf', g)], writes=[('Pf', g)])
            ukey = lambda g, i: ('Uf', g) if i == 0 else ('Ub', g, i)
            for lvl in range(1, 7):
                i0, i1 = (lvl - 1) % 2, lvl % 2
                u0i, u1i = (0 if lvl == 1 else 1 + lvl % 2), 1 + (lvl + 1) % 2
                for g in range(2):
                    for hh in range(4):
                        P.mm(ps[:, g, hh * 128:(hh + 1) * 128], Ub[g][u0i][:, hh, :], Lb[g][i0][:, hh, :], reads=[ukey(g, u0i), ('Lb', g, i0)], writes=[('ps', g)])
                    if lvl < 6:
                        for hh in range(4):
                            P.mm(ps[:, 2 + g, hh * 128:(hh + 1) * 128], Lb[g][i0][:, hh, :], Ub[g][u0i][:, hh, :], reads=[ukey(g, u0i), ('Lb', g, i0)], writes=[('ps', 2 + g)])
                    P.cp(ACT, Lb[g][i1][:], b4(ps[:, g, :]), reads=[('ps', g)], writes=[('Lb', g, i1)])
                    if lvl < 6:
                        P.cp(ACT, Ub[g][u1i][:], b4(ps[:, 2 + g, :]), reads=[('ps', 2 + g)], writes=[ukey(g, u1i)])
                for g in range(2):
                    for hh in range(4):
                        P.mm(ps[:, 4 + g, hh * 128:(hh + 1) * 128], Lb[g][i1][:, hh, :], Pf[g][:, hh, :], reads=[('Lb', g, i1), ('Pf', g)], writes=[('ps', 4 + g)])
                    P.tt(DVE, Pf[g][:], Pf[g][:], b4(ps[:, 4 + g, :]), ALU.add, reads=[('ps', 4 + g), ('Pf', g)], writes=[('Pf', g)])
            for g in range(2):
                P.cp(POOL, Pb[g][:], Pf[g][:], reads=[('Pf', g)], writes=[('Pb', g)])
            for g in range(2):
                hs = list(range(4 * g, 4 * g + 4))
                gs = slice(4 * g, 4 * g + 4)
                bx = 6 + g
                psX = ps[:, bx, 0:256].rearrange("p (h v) -> p h v", v=64)
                for hh, h in enumerate(hs):
                    P.mm(psX[:, hh, :], X['XBz'][:, h, :], Zb[:, h, :], True, False, reads=[xk['XBz'], 'Zb'], writes=[('ps', bx)])
                    P.mm(psX[:, hh, :], Mkb[g][:, hh, :], Vt[:, h, :], False, True, reads=[('Mkb', g), tkk[2]], writes=[('ps', bx)])
                P.cp(ACT, Xb[g][:], psX, reads=[('ps', bx)], writes=[('Xb', g)])
                psP = ps[:, bx, 256:512].rearrange("p (h v) -> p h v", v=64)
                for hh in range(4):
                    P.mm(psP[:, hh, :], Pb[g][:, hh, :], Xb[g][:, hh, :], reads=[('Pb', g), ('Xb', g)], writes=[('ps', bx)])
                P.act(Pnb[g][:], psP, AF.Copy, reads=[('ps', bx)], writes=[('Pnb', g)], scale=-1.0)
                bo = g
                psO = b4(ps[0:64, bo, :])
                for hh, h in enumerate(hs):
                    P.mm(psO[:, hh, :], Zb[:, h, :], X['XRz'][:, h, :], True, False, reads=['Zb', xk['XRz']], writes=[('ps', bo)])
                    P.mm(psO[:, hh, :], Vt[:, h, :], RKb[g][:, hh, :], False, False, reads=[tkk[2], ('RKb', g)], writes=[('ps', bo)])
                    P.mm(psO[:, hh, :], Pnb[g][:, hh, :], RAb[g][:, hh, :], False, True, reads=[('Pnb', g), ('RAb', g)], writes=[('ps', bo)])
                P.cp(DVE, oT[b][:, gs, :], psO, reads=[('ps', bo)], writes=[('oT', b)])
                bz = 2 + g
                psZ = ps[0:64, bz, 0:256].rearrange("p (h v) -> p h v", v=64)
                for hh, h in enumerate(hs):
                    P.mm(psZ[:, hh, :], K2t[:, h, :], Vt[:, h, :], True, False, reads=[tkk[0], tkk[2]], writes=[('ps', bz)])
                    P.mm(psZ[:, hh, :], A2t[:, h, :], Pnb[g][:, hh, :], False, True, reads=[tkk[1], ('Pnb', g)], writes=[('ps', bz)])
                P.tt(DVE, Zt[:, gs, :], Zf[:, gs, :], gct[:, gs, c:c + 1].to_broadcast([64, 4, 64]), ALU.mult, reads=['Zf', 'gct'], writes=['Zt'])
                P.tt(DVE, Zf[:, gs, :], Zt[:, gs, :], psZ, ALU.add, reads=['Zt', ('ps', bz)], writes=['Zf'])
            P.cp(ACT, Zb[:], Zf[:], reads=['Zf'], writes=['Zb'])
            o = oT[b]
            P.act(zt[b][:], zt[b][:], AF.Silu, reads=[('zt', b)], writes=[('zt', b)])
            if isA:
                P.act(wb[:], o[:], AF.Square, reads=[('oT', b)], writes=['wb'])
                for j in range(2):
                    js = slice(j * 512, (j + 1) * 512)
                    P.mm(ps[0:64, 4 + j, :], ones64b[:], flat(wb[:])[:, js], reads=['wb', 'ones64b'], writes=[('ps', 4 + j)])
                    P.act(flat(w1[:])[:, js], ps[0:64, 4 + j, :], AF.Ln, reads=[('ps', 4 + j)], writes=['w1'], scale=1.0 / 64, bias=1e-6)
                P.act(w1[:], w1[:], AF.Exp, reads=['w1'], writes=['w1'], scale=-0.5)
                P.stt(w2[:], o[:], pp[0:64, PP_NORMA:PP_NORMA + 1], w1[:], ALU.mult, ALU.mult, reads=[('oT', b), 'pp', 'w1'], writes=['w2'])
            else:
                for j in range(2):
                    js = slice(j * 512, (j + 1) * 512)
                    P.mm(ps[0:64, 4 + j, :], ones64f[:], flat(o[:])[:, js], reads=[('oT', b), 'ones64f'], writes=[('ps', 4 + j)])
                    P.stt(flat(w2[:])[:, js], ps[0:64, 4 + j, :], -1.0 / 64, flat(o[:])[:, js], ALU.mult, ALU.add, reads=[('ps', 4 + j), ('oT', b)], writes=['w2'])
                P.act(w1[:], w2[:], AF.Square, reads=['w2'], writes=['w1'])
                for j in range(2):
                    js = slice(j * 512, (j + 1) * 512)
                    P.mm(ps[0:64, 4 + j, :], ones64f[:], flat(w1[:])[:, js], reads=['w1', 'ones64f'], writes=[('ps', 4 + j)])
                    P.act(flat(w1[:])[:, js], ps[0:64, 4 + j, :], AF.Ln, reads=[('ps', 4 + j)], writes=['w1'], scale=1.0 / 64, bias=64e-5)
                P.act(w1[:], w1[:], AF.Exp, reads=['w1'], writes=['w1'], scale=-0.5)
                P.tt(DVE, w2[:], w2[:], w1[:], ALU.mult, reads=['w2', 'w1'], writes=['w2'])
                P.tt(POOL, w2[:], w2[:], gng[:].unsqueeze(2).to_broadcast([64, 8, 128]), ALU.mult, reads=['w2', 'gng'], writes=['w2'])
                P.tt(POOL, w2[:], w2[:], gnb[:].unsqueeze(2).to_broadcast([64, 8, 128]), ALU.add, reads=['w2', 'gnb'], writes=['w2'])
                P.tt(POOL, w2[:], w2[:], bon[b][:], ALU.add, reads=['w2', ('bon', b)], writes=['w2'])
            P.tt(DVE, gob[b][:], w2[:], zt[b][:], ALU.mult, reads=['w2', ('zt', b)], writes=[('gob', b)])
            P.ld(cx.gatedT[grow:grow + 512, c0:c0 + n].rearrange("(h d) t -> d h t", d=64), gob[b][:, :, 0:n], reads=[('gob', b)], writes=['gatedT'], tag='go', slow=slow)
        if NCH - 1 in chunks:
            P.dma(SP, lambda e: e.dma_start(out=st_out[l, 1], in_=Zf[:]), reads=['Zf'], writes=['st_out1'], tag='so')
        P.emit()


def phase_mixCs(nc, cx, l):
    P = Prog(nc)
    ps = cx.ps
    NPG = 128
    with contextlib.ExitStack() as st:
        sb = lambda name, shape, dt: st.enter_context(nc.sbuf_tensor('cs%d_' % l + name, shape, dt))
        cf = sb('cf', [128, 384], F32)
        identf, tri_gt = cf[:, 0:128], cf[:, 256:384]
        bm = sb('bm', [8, 512], F32)
        idx = sb('idx', [128, 1], I32)
        rows = sb('rows', [128, 2056], F32)
        qb, kb, vb, fb = rows[:, 0:512], rows[:, 512:1024], rows[:, 1024:1536], rows[:, 2048:2056]
        negb = sb('negb', [128, 8], F32)
        ones = sb('ones', [128, 128], F32)
        lfp, Cw, sc, pr = (sb(n, [128, 128, 8], F32) for n in ('lfp', 'Cw', 'sc', 'pr'))
        kch = [sb('kch%d' % i, [128, 16, 512], F32) for i in range(2)]
        lfn, tot, base, mloc, snew, pnew, den, rden, mxb = (sb(n, [128, 8], F32) for n in ('lfn', 'tot', 'base', 'mloc', 'snew', 'pnew', 'den', 'rden', 'mxb'))
        tmp512 = sb('tmp512', [128, 512], F32)
        mx8, D8 = sb('mx8', [8, 1], F32), sb('D8', [8, 8], F32)
        accs = sb('accs', [8, 512], F32)
        oc, zc = sb('oc', [128, 4], F32), sb('zc', [128, 4], F32)
        gb = sb('gb', [128, 4], BF16)
        P.ld(cf[:], cx.cf32s, writes=['cf'], tag='c')
        P.ld(bm[:], cx.bmask, writes=['bm'], tag='c')
        P.ld(idx[:], cx.ptab, writes=['idx'], tag='c')
        P.ld(negb[:], cx.b_f[l:l + 1, :].partition_broadcast(128), writes=['negb'], tag='c')
        P.ts(POOL, negb[:], negb[:], -1.0, ALU.mult, reads=['negb'], writes=['negb'])
        P.ms(POOL, ones[:], 1.0, writes=['ones'])
        for j, r0 in enumerate((R_CQ, R_CK, R_CV, R_CZ)):
            P.ld(cx.srow[0:1, j * 512:(j + 1) * 512].rearrange("o c -> c o"), cx.projT[r0:r0 + 512, T:T + 1], writes=[('srow', j)], tag='sr', slow=True)
        P.ld(cx.srow[0:1, 2048:2056].rearrange("o c -> c o"), cx.projT[R_SM + SM_F:R_SM + SM_F + 8, T:T + 1], writes=[('srow', 4)], tag='sr', slow=True)
        P.ld(rows[:], cx.srow[0:1, :].partition_broadcast(128), reads=[('srow', j) for j in range(5)], writes=['rows'], tag='c')
        P.tt(DVE, lfn[:], fb, negb[:], ALU.subtract, reads=['rows', 'negb'], writes=['lfn'])
        P.act(lfn[:], lfn[:], AF.Exp, reads=['lfn'], writes=['lfn'], scale=-1.0)
        P.act(lfn[:], lfn[:], AF.Ln, reads=['lfn'], writes=['lfn'], bias=1.0)
        P.ts(DVE, lfn[:], lfn[:], -1.0, ALU.mult, reads=['lfn'], writes=['lfn'])
        P.ld(cx.o_clf[l, :, T:T + 1].rearrange("h o -> o h"), lfn[0:1, :], reads=['lfn'], writes=['o_clf_s'], tag='so', slow=True)
        ioff = bass.IndirectOffsetOnAxis(ap=idx[:, 0:1], axis=0)
        P.dma(POOL, lambda e: e.indirect_dma_start(out=lfp[:].rearrange("p t h -> p (t h)"), out_offset=None,
                                                   in_=cx.cache_lf.rearrange("l n t h -> (l n) (t h)"), in_offset=ioff,
                                                   element_offset=l * 1280 * 1024), reads=['idx'], writes=['lfp'], tag='ig')
        for h in range(8):
            P.op(DVE, lambda e, h=h: e.tensor_tensor_scan(out=Cw[:, :, h], data0=ones[:, :], data1=lfp[:, :, h], initial=0.0, op0=ALU.mult, op1=ALU.add),
                 reads=['lfp', 'ones'], writes=['Cw'])
        P.cp(DVE, tot[:], Cw[:, 127, :], reads=['Cw'], writes=['tot'])
        P.mm(ps[:, 0, 0:8], tri_gt, tot[:], reads=['cf', 'tot'], writes=[('ps', 0)])
        P.tt(DVE, base[:], tot[:], ps[:, 0, 0:8], ALU.add, reads=['tot', ('ps', 0)], writes=['base'])
        P.tt(DVE, base[:], base[:], lfn[:], ALU.add, reads=['base', 'lfn'], writes=['base'])
        P.tt(DVE, Cw[:], base[:].unsqueeze(1).to_broadcast([128, 128, 8]), Cw[:], ALU.subtract, reads=['base', 'Cw'], writes=['Cw'])
        ck = cx.cache_k.rearrange("l n t h d -> (l n) (t h d)")
        cv = cx.cache_v.rearrange("l n t h d -> (l n) (t h d)")
        ng = [0]

        def gather(src, tc):
            b = ng[0] % 2
            ng[0] += 1
            P.dma(POOL, lambda e: e.indirect_dma_start(out=kch[b][:].rearrange("p t c -> p (t c)"), out_offset=None, in_=src, in_offset=ioff,
                                                       element_offset=l * 1280 * 65536 + tc * 8192),
                  reads=['idx'], writes=[('kch', b)], tag='ig')
            return b
        for tc in range(8):
            b = gather(ck, tc)
            eng = DVE if tc % 2 == 0 else POOL
            P.tt(eng, kch[b][:], kch[b][:], qb.unsqueeze(1).to_broadcast([128, 16, 512]), ALU.mult, reads=[('kch', b), 'rows'], writes=[('kch', b)])
            P.op(DVE, lambda e, b=b, tc=tc: e.tensor_reduce(out=sc[:, tc * 16:(tc + 1) * 16, :].rearrange("p t h -> p (t h)"),
                                                           in_=kch[b][:].rearrange("p t (h d) -> p (t h) d", d=64), axis=AX.X, op=ALU.add),
                 reads=[('kch', b)], writes=['sc'])
        P.stt(sc[:], sc[:], 0.125, Cw[:], ALU.mult, ALU.add, reads=['sc', 'Cw'], writes=['sc'])
        P.tt(DVE, tmp512[:], qb, kb, ALU.mult, reads=['rows'], writes=['tmp512'])
        P.op(DVE, lambda e: e.tensor_reduce(out=snew[:], in_=tmp512[:].rearrange("p (h d) -> p h d", d=64), axis=AX.X, op=ALU.add), reads=['tmp512'], writes=['snew'])
        P.ts(DVE, snew[:], snew[:], 0.125, ALU.mult, reads=['snew'], writes=['snew'])
        P.op(DVE, lambda e: e.tensor_reduce(out=mloc[:], in_=sc[:].rearrange("p t h -> p h t"), axis=AX.X, op=ALU.max), reads=['sc'], writes=['mloc'])
        P.tt(DVE, mloc[:], mloc[:], snew[:], ALU.max, reads=['mloc', 'snew'], writes=['mloc'])
        P.tr(ps[0:8, 1, 0:128], mloc[:], identf, reads=['mloc', 'cf'], writes=[('ps', 1)])
        P.op(DVE, lambda e: e.tensor_reduce(out=mx8[:], in_=ps[0:8, 1, 0:128], axis=AX.X, op=ALU.max), reads=[('ps', 1)], writes=['mx8'])
        P.ts(DVE, D8[:], identf[0:8, 0:8], mx8[:, 0:1], ALU.mult, reads=['cf', 'mx8'], writes=['D8'])
        P.mm(ps[:, 2, 0:8], ones[0:8, :], D8[:], reads=['ones', 'D8'], writes=[('ps', 2)])
        P.cp(DVE, mxb[:], ps[:, 2, 0:8], reads=[('ps', 2)], writes=['mxb'])
        P.tt(DVE, sc[:], sc[:], mxb[:].unsqueeze(1).to_broadcast([128, 128, 8]), ALU.subtract, reads=['sc', 'mxb'], writes=['sc'])
        P.act(pr[:], sc[:], AF.Exp, reads=['sc'], writes=['pr'])
        P.tt(DVE, pnew[:], snew[:], mxb[:], ALU.subtract, reads=['snew', 'mxb'], writes=['pnew'])
        P.act(pnew[:], pnew[:], AF.Exp, reads=['pnew'], writes=['pnew'])
        P.op(DVE, lambda e: e.tensor_reduce(out=den[:], in_=pr[:].rearrange("p t h -> p h t"), axis=AX.X, op=ALU.add), reads=['pr'], writes=['den'])
        P.mm(ps[:, 3, 0:8], ones[:, :], den[:], reads=['ones', 'den'], writes=[('ps', 3)])
        P.tt(DVE, den[:], ps[:, 3, 0:8], pnew[:], ALU.add, reads=[('ps', 3), 'pnew', 'den'], writes=['den'])
        P.op(DVE, lambda e: e.reciprocal(out=rden[:], in_=den[:]), reads=['den'], writes=['rden'])
        P.tt(DVE, pr[:], pr[:], rden[:].unsqueeze(1).to_broadcast([128, 128, 8]), ALU.mult, reads=['pr', 'rden'], writes=['pr'])
        P.tt(DVE, pnew[:], pnew[:], rden[:], ALU.mult, reads=['pnew', 'rden'], writes=['pnew'])
        first = True
        for tc in range(8):
            b = gather(cv, tc)
            for t in range(16):
                P.mm(ps[0:8, 4, :], pr[:, tc * 16 + t, :], kch[b][:, t, :], first, False, reads=['pr', ('kch', b)], writes=[('ps', 4)])
                first = False
        P.mm(ps[0:8, 4, :], pnew[0:1, :], vb[0:1, :], False, True, reads=['pnew', 'rows'], writes=[('ps', 4)])
        P.tt(DVE, accs[:], ps[0:8, 4, :], bm[:], ALU.mult, reads=[('ps', 4), 'bm'], writes=['accs'])
        for j in range(4):
            P.mm(ps[:, 5, j:j + 1], accs[:, j * 128:(j + 1) * 128], ones[0:8, 0:1], reads=['accs', 'ones'], writes=[('ps', 5)])
        P.cp(DVE, oc[:], ps[:, 5, 0:4], reads=[('ps', 5)], writes=['oc'])
        P.ld(zc[:], cx.projT[R_CZ:R_CZ + 512, T:T + 1].rearrange("(j p) o -> p (j o)", p=128), writes=['zc'], tag='c', slow=True)
        P.act(zc[:], zc[:], AF.Silu, reads=['zc'], writes=['zc'])
        P.tt(DVE, gb[:], oc[:], zc[:], ALU.mult, reads=['oc', 'zc'], writes=['gb'])
        P.ld(cx.gatedT[1024:1536, T:T + 1].rearrange("(j p) o -> p (j o)", p=128), gb[:], reads=['gb'], writes=['gatedT'], tag='go', slow=True)
        P.emit()


def build(debug=None, phases=None, ext_in=()):
    nc = bass.Bass("TRN2", target_bir_lowering=False)
    cx = Ctx()
    di = lambda name, shape, dt=F32: nc.dram_tensor(name, shape, dt, kind="ExternalInput").ap()
    do = lambda name, shape, dt=F32: nc.dram_tensor(name, shape, dt, kind="ExternalOutput").ap()
    dbg = debug is not None

    def scratch(name, shape, dt=F32):
        if name in ext_in:
            return di(name, shape, dt)
        return do(name, shape, dt) if dbg else nc.dram_tensor(name, shape, dt).ap()
    cx.xT = di("xT", [D, TS])
    cx.xtok = di("xtok", [TS, D])
    cx.w_in = di("w_in", [DEPTH, D, NROW])
    cx.w_out = di("w_out", [DEPTH, D, D])
    cx.ln_g, cx.ln_b = di("ln_g", [DEPTH, D]), di("ln_b", [DEPTH, D])
    cx.identb = di("identb", [128, 128], BF16)
    cx.pp = di("pp", [DEPTH, 128, NPP])
    cx.pool_w = di("pool_w", [DEPTH, 4, 128, 128])
    cx.dbufT = di("dbufT", [DEPTH, 512, 15])
    cx.o_dbuf = do("o_dbuf", [DEPTH, 2, 512, 15])
    cx.cf32 = di("cf32", [128, 256])
    cx.cb16 = di("cb16", [128, 128 + 4 * 512], BF16)
    cx.b_f = di("b_f", [DEPTH, 8])
    cx.o_clf = do("o_clf", [DEPTH, 8, TS])
    cx.o_ckv = do("o_ckv", [DEPTH, 1024, TS])
    cx.cscr = scratch("cscr", [6, T], BF16)
    cx.cb2 = di("cb2", [128, 256], BF16)
    cx.cb3 = di("cb3", [128, 512], BF16)
    cx.cf3 = di("cf3", [128, 2048])
    cx.sel8 = di("sel8", [8, 512])
    cx.aconvT = di("aconvT", [DEPTH, 1536, 3])
    cx.sA_in = di("sA_in", [DEPTH, 64, 8, 64])
    cx.sB_in = di("sB_in", [DEPTH, 64, 8, 64])
    cx.gn_g, cx.gn_b = di("gn_g", [DEPTH, 512]), di("gn_b", [DEPTH, 512])
    cx.cf32s = di("cf32s", [128, 384])
    cx.bmask = di("bmask", [8, 512])
    cx.ptab = di("ptab", [128, 1], I32)
    cx.cache_k = di("cache_k", [DEPTH, 1280, 128, 8, 64])
    cx.cache_v = di("cache_v", [DEPTH, 1280, 128, 8, 64])
    cx.cache_lf = di("cache_lf", [DEPTH, 1280, 128, 8])
    cx.srow = scratch("srow", [1, 2056])
    cx.lora_w = di("lora_w", [DEPTH, 128, 512])
    cx.bshiftT = di("bshiftT", [DEPTH, 128, 13])
    cx.o_bshift = do("o_bshift", [DEPTH, 2, 128, 13])
    cx.o_aconv = do("o_aconv", [DEPTH, 2, 1536, 3])
    cx.o_AS = do("o_AS", [DEPTH, 2, 64, 8, 64])
    cx.o_BS = do("o_BS", [DEPTH, 2, 64, 8, 64])
    cx.A_sc = {k: scratch("A_" + k, [512, TP], BF16) for k in A_KINDS}
    cx.B_sc = {k: scratch("B_" + k, [512, TP], BF16) for k in B_KINDS}
    cx.A_erows = scratch("A_erows", [4, 2, 8, TP])
    cx.A_erb = scratch("A_erb", [4, 4, 8, TP], BF16)
    cx.A_gc, cx.B_gc = scratch("A_gc", [512, NCH]), scratch("B_gc", [512, NCH])
    cx.A_mrows = scratch("A_mrows", [5, 8, TP])
    cx.B_bonus = scratch("B_bonus", [512, TP])
    cx.y = do("y", [TS, D])
    cx.projT = scratch("projT", [NROW, TS])
    cx.gatedT = scratch("gatedT", [D, TS], BF16)
    cx.y1 = scratch("y1", [TS, D])
    cx.x2T = scratch("x2T", [D, TS], BF16)
    cx.ps = nc.alloc_psum_tensor("ps", [128, 8, 512], F32)
    if phases is None:
        phases = [('in', 0), ('out', 0), ('in', 1), ('out', 1)]
    for ph, l in phases:
        if ph == 'in':
            phase_inproj(nc, cx, l)
        elif ph == 'out':
            phase_outproj(nc, cx, l)
        elif ph == 'mixD':
            phase_mixD(nc, cx, l)
        elif ph == 'mixC':
            phase_mixC(nc, cx, l)
        elif ph == 'preA':
            phase_rowsA(nc, cx, l)
            phase_preA(nc, cx, l)
        elif ph == 'preB':
            phase_preB(nc, cx, l)
        elif ph == 'scanA':
            phase_scan(nc, cx, l, 'A')
        elif ph == 'scanA_few':
            phase_scan(nc, cx, l, 'A', chunks=[0, 1, 32])
        elif ph == 'scanB_few':
            phase_scan(nc, cx, l, 'B', chunks=[0, 1, 32])
        elif ph == 'scanB':
            phase_scan(nc, cx, l, 'B')
        elif ph == 'mixCs':
            phase_mixCs(nc, cx, l)
        elif ph == 'mixC1':
            phase_mixC(nc, cx, l, heads=[0], do_sample=False)
    return nc


def _cf32():
    a = np.zeros((128, 256), np.float32)
    a[:, 0:128] = np.eye(128)
    a[:, 128:256] = np.triu(np.ones((128, 128)), 1)
    return a


def _cb16():
    import ml_dtypes
    a = np.zeros((128, 128 + 4 * 512), np.float32)
    a[:, 0:128] = np.eye(128)
    k, q = np.arange(128)[:, None], np.arange(512)[None, :]
    for j in range(4):
        a[:, 128 + j * 512:128 + (j + 1) * 512] = np.where(j * 128 + k > q, -30000.0, 0.0)
    return a.astype(ml_dtypes.bfloat16)


def _cb2():
    import ml_dtypes
    a = np.zeros((128, 256), np.float32)
    a[:, 0:128] = np.eye(128)
    a[:, 128:256] = np.kron(np.eye(2), np.ones((64, 64)))
    return a.astype(ml_dtypes.bfloat16)


def _cb3():
    import ml_dtypes
    p, f = np.arange(128)[:, None], np.arange(128)[None, :]
    a = np.zeros((128, 512), np.float32)
    a[:, 0:128] = np.eye(128)
    a[:, 128:256] = np.where(f >= p, -30000.0, 0.0)
    a[:, 256:384] = np.where(p >= f, -30000.0, 0.0)
    a[:, 384:512] = np.where(p > f, -30000.0, 0.0)
    return a.astype(ml_dtypes.bfloat16)


def _cf3():
    p, f = np.arange(128)[:, None], np.arange(128)[None, :]
    mats = [np.eye(128), (f < p) * 1.0, (p < f) * 1.0, (p <= f) * 1.0]
    return np.concatenate([np.tile(m, (1, 4)) for m in mats], axis=1).astype(np.float32)


def _lora_w(I):
    a = np.zeros((DEPTH, 128, 512), np.float32)
    a[:, 32:64] = I['w_up_B']
    a[:, 64:96] = I['a_up_B']
    return a


def _bshiftT(I, c):
    sh = I['state_B_shift'][:, c]
    a = np.zeros((DEPTH, 128, 13), np.float32)
    a[:, :, 0:12] = sh[:, 0:1536].reshape(DEPTH, 12, 128).transpose(0, 2, 1)
    a[:, 32:64, 12] = sh[:, 1536:1568]
    a[:, 64:96, 12] = sh[:, 1568:1600]
    return a


def host_inputs(c, I):
    import ml_dtypes
    n = c % 2
    perm = _col_perm()
    w = I['w_in']
    w_p = np.zeros((DEPTH, D, NROW), np.float32)
    w_p[:, :, perm >= 0] = w[:, :, perm[perm >= 0]]
    xtok = np.concatenate([I['x_prompt'][n], I['x_sample'][c]], axis=0)
    return {"xT": np.ascontiguousarray(xtok.T), "xtok": np.ascontiguousarray(xtok), "w_in": w_p, "w_out": I['w_out'],
            "ln_g": I['ln_g'], "ln_b": I['ln_b'], "identb": np.eye(128, dtype=np.float32).astype(ml_dtypes.bfloat16),
            "pp": pack_pp(I), "pool_w": I['pool_w_D'], "cf32": _cf32(), "cb16": _cb16(), "b_f": I['b_f_C'],
            "cb2": _cb2(), "cb3": _cb3(), "cf3": _cf3(), "sel8": np.kron(np.eye(8, dtype=np.float32), np.ones((1, 64), np.float32)),
            "aconvT": np.ascontiguousarray(I['state_A_conv'][:, c].transpose(0, 2, 1)),
            "sA_in": np.ascontiguousarray(I['state_A_S'][:, c].transpose(0, 2, 1, 3)),
            "sB_in": np.ascontiguousarray(I['state_B_S'][:, c].transpose(0, 3, 1, 2)),
            "cf32s": np.concatenate([_cf32(), np.tril(np.ones((128, 128), np.float32), -1)], axis=1),
            "bmask": np.kron(np.eye(8, dtype=np.float32), np.ones((1, 64), np.float32)),
            "ptab": np.ascontiguousarray(I['page_table'][c].reshape(128, 1).astype(np.int32)),
            "cache_k": I['cache_C_k'], "cache_v": I['cache_C_v'], "cache_lf": I['cache_C_logf'],
            "gn_g": I['gn_g_B'], "gn_b": I['gn_b_B'], "lora_w": _lora_w(I), "bshiftT": _bshiftT(I, c),
            "dbufT": np.ascontiguousarray(I['state_D_buf'][:, c].transpose(0, 2, 1))}


FULL_PHASES = [(ph, l) for l in range(DEPTH) for ph in ('in', 'preA', 'scanA', 'preB', 'scanB', 'mixC', 'mixCs', 'mixD', 'out')]


def _unpack_bshift(a):
    o = np.zeros(1600, np.float32)
    o[0:1536] = a[:, 0:12].T.reshape(1536)
    o[1536:1568] = a[32:64, 12]
    o[1568:1600] = a[64:96, 12]
    return o


def kernel(**I):
    I = {k: np.asarray(v) for k, v in I.items()}
    nc = build(phases=FULL_PHASES)
    shared = None
    in_maps = []
    for c in range(8):
        m = host_inputs(c, I)
        if shared is None:
            shared = m
        else:
            for k in ('w_in', 'w_out', 'ln_g', 'ln_b', 'identb', 'pp', 'pool_w', 'cf32', 'cb16', 'b_f', 'cb2', 'cb3', 'cf3', 'sel8',
                      'cf32s', 'bmask', 'cache_k', 'cache_v', 'cache_lf', 'gn_g', 'gn_b', 'lora_w'):
                m[k] = shared[k]
        in_maps.append(m)
    res = run_bass_kernel_spmd(nc, in_maps, core_ids=list(range(8)))
    R = res.results
    f = np.float32
    NB, ND = 2, 8
    y_prompt = np.stack([R[n]['y'][0:T] for n in range(NB)]).astype(f)
    y_sample = np.stack([R[c]['y'][T:T + 1] for c in range(ND)]).astype(f)

    def per(fn, j):
        cores = range(NB) if j == 0 else range(ND)
        return np.stack([np.stack([fn(R[c], l) for c in cores]) for l in range(DEPTH)]).astype(f)
    outs = {}
    for j, pre in ((0, 'p'), (1, 's')):
        outs[pre + '_A_S'] = per(lambda r, l: r['o_AS'][l, j].transpose(1, 0, 2), j)
        outs[pre + '_A_conv'] = per(lambda r, l: r['o_aconv'][l, j].T, j)
        outs[pre + '_B_S'] = per(lambda r, l: r['o_BS'][l, j].transpose(1, 2, 0), j)
        outs[pre + '_B_shift'] = per(lambda r, l: _unpack_bshift(r['o_bshift'][l, j]), j)
        outs[pre + '_D_buf'] = per(lambda r, l: r['o_dbuf'][l, j].T, j)
    cs = {0: slice(0, T), 1: slice(T, T + 1)}
    for j, pre in ((0, 'p'), (1, 's')):
        n = T if j == 0 else 1
        outs[pre + '_C_k'] = per(lambda r, l: r['o_ckv'][l, 0:512, cs[j]].T.reshape(n, NH, HD), j)
        outs[pre + '_C_v'] = per(lambda r, l: r['o_ckv'][l, 512:1024, cs[j]].T.reshape(n, NH, HD), j)
        outs[pre + '_C_logf'] = per(lambda r, l: r['o_clf'][l, :, cs[j]].T, j)
    order = ['p_A_S', 'p_A_conv', 'p_B_S', 'p_B_shift', 'p_C_k', 'p_C_v', 'p_C_logf', 'p_D_buf',
             's_A_S', 's_A_conv', 's_B_S', 's_B_shift', 's_C_k', 's_C_v', 's_C_logf', 's_D_buf']
    return (y_prompt, y_sample) + tuple(np.ascontiguousarray(outs[k]) for k in order)
```
